# Optimizing a Trainium2 kernel written in Bass

```python
import math
import jax, jax.numpy as jnp
from jax import lax
import numpy as np

D_MODEL = 1024
BATCH = 32
SEQ = 256
DEPTH = 4
DEC_BATCH = 2
DEC_SEQ = 2048
PAST_LEN = 512

GRID_W = 64
N_EVEN = (DEPTH + 1) // 2
N_ODD = DEPTH // 2
A_WIDTH = D_MODEL // 2
A_HEADS = 4
A_DV = A_WIDTH // A_HEADS
A_DK = A_DV // 2
A_QK = A_HEADS * A_DK
A_RANK = 16
A_GATE_NORM = 16.0
GLA_CHUNK = 16
B_WIDTH = D_MODEL - A_WIDTH
POOL_WINDOWS = (2, 4, 8, 16)
B_GROUPS = len(POOL_WINDOWS)
B_GW = B_WIDTH // B_GROUPS
A_SPLITS = (A_QK, A_QK, A_WIDTH, A_WIDTH, 2 * A_RANK, B_WIDTH)
C_HEADS = 8
C_DK = D_MODEL // C_HEADS
C_DV = C_DK
C_WIDTH = C_HEADS * C_DV
SHORT_CONV = 3
DN_CHUNK = 64
C_SPLITS = (3 * C_WIDTH, C_WIDTH, 2 * C_HEADS, 2 * C_HEADS)
D_FF = 2816
FFN_CONV = 3
N_MOD = 6
ALPHA = (2 * DEPTH) ** 0.25
BETA_INIT = (8 * DEPTH) ** -0.25
EPS = 1e-6

kernel_name = "bidir_gla_pool_deltanet_convffn_diffusion_step"

F32 = jnp.float32


def _split(h, sizes):
    idx, acc = [], 0
    for s in sizes[:-1]:
        acc += s
        idx.append(acc)
    return jnp.split(h, idx, axis=-1)


def _layernorm(x, g, b):
    xf = x.astype(F32)
    mu = xf.mean(-1, keepdims=True)
    var = jnp.mean(jnp.square(xf - mu), -1, keepdims=True)
    return ((xf - mu) * lax.rsqrt(var + EPS) * g.astype(F32) + b.astype(F32)).astype(x.dtype)


def _rmsnorm_f32(x, g):
    xf = x.astype(F32)
    return xf * lax.rsqrt(jnp.mean(xf * xf, -1, keepdims=True) + EPS) * g.astype(F32)


def _l2norm(x):
    return x * lax.rsqrt(jnp.sum(x * x, -1, keepdims=True) + EPS)


def _dwconv(x, w):
    k = w.shape[0]
    p = k // 2
    L = x.shape[1]
    xp = jnp.pad(x, ((0, 0), (p, p), (0, 0)))
    return sum(xp[:, i:i + L] * w[i] for i in range(k))


def _grid_pos_embed(n_tokens):
    rows = n_tokens // GRID_W
    r = jnp.broadcast_to(jnp.arange(rows, dtype=F32)[:, None], (rows, GRID_W)).reshape(-1)
    col = jnp.broadcast_to(jnp.arange(GRID_W, dtype=F32)[None, :], (rows, GRID_W)).reshape(-1)
    quarter = D_MODEL // 4
    freq = jnp.exp(-math.log(10000.0) * jnp.arange(quarter, dtype=F32) / quarter)
    ra, ca = r[:, None] * freq, col[:, None] * freq
    return jnp.concatenate([jnp.sin(ra), jnp.cos(ra), jnp.sin(ca), jnp.cos(ca)], -1)


def _gla_chunk(q, k, v, g, s0):
    B_, H, L, dk = q.shape
    dv = v.shape[-1]
    n = L // GLA_CHUNK
    sp = lambda t: t.reshape(B_, H, n, GLA_CHUNK, t.shape[-1])
    q, k, v, g = sp(q), sp(k), sp(v), sp(g)
    b = jnp.cumsum(g, axis=3)
    b_last = b[:, :, :, -1]
    mask = jnp.tril(jnp.ones((GLA_CHUNK, GLA_CHUNK), bool))[:, :, None]
    dec = jnp.exp(jnp.where(mask, b[:, :, :, :, None, :] - b[:, :, :, None, :, :], -jnp.inf))
    att = jnp.einsum('bhnid,bhnjd,bhnijd->bhnij', q, k, dec)
    o_intra = jnp.einsum('bhnij,bhnjv->bhniv', att, v)
    qg = q * jnp.exp(b)
    kd = k * jnp.exp(b_last[:, :, :, None, :] - b)

    def step(s, xs):
        qg_c, kd_c, v_c, d_c = xs
        o = jnp.einsum('bhid,bhdv->bhiv', qg_c, s)
        s = s * d_c[..., None] + jnp.einsum('bhid,bhiv->bhdv', kd_c, v_c)
        return s, o

    xs = (jnp.moveaxis(qg, 2, 0), jnp.moveaxis(kd, 2, 0), jnp.moveaxis(v, 2, 0),
          jnp.moveaxis(jnp.exp(b_last), 2, 0))
    s_fin, o_inter = lax.scan(step, s0, xs)
    o = o_intra + jnp.moveaxis(o_inter, 0, 2)
    return o.reshape(B_, H, L, dv), s_fin


def _delta_chunk(q, k, v, g, beta, s0):
    B_, H, L, dk = q.shape
    dv = v.shape[-1]
    C = DN_CHUNK
    n = L // C
    q = q.reshape(B_, H, n, C, dk)
    k = k.reshape(B_, H, n, C, dk)
    v = v.reshape(B_, H, n, C, dv)
    g = g.reshape(B_, H, n, C)
    beta = beta.reshape(B_, H, n, C)
    b = jnp.cumsum(g, axis=-1)
    incl = jnp.tril(jnp.ones((C, C), bool))
    strict = jnp.tril(jnp.ones((C, C), bool), k=-1)
    ld = jnp.exp(jnp.where(incl, b[..., :, None] - b[..., None, :], -jnp.inf))
    kb = k * beta[..., None]
    kk = jnp.where(strict, jnp.einsum('bhnid,bhnjd->bhnij', kb, k) * ld, 0.0)
    eye = jnp.eye(C, dtype=F32)
    t = lax.linalg.triangular_solve(eye + kk, jnp.broadcast_to(eye, kk.shape),
                                    left_side=True, lower=True, unit_diagonal=True)
    u = jnp.einsum('bhnij,bhnjv->bhniv', t, v * beta[..., None])
    w = jnp.einsum('bhnij,bhnjd->bhnid', t, kb * jnp.exp(b)[..., None])
    aqk = jnp.einsum('bhnid,bhnjd->bhnij', q, k) * ld
    qd = q * jnp.exp(b)[..., None]
    kd = k * jnp.exp(b[..., -1:] - b)[..., None]
    dec = jnp.exp(b[..., -1])

    def step(s, xs):
        u_c, w_c, q_c, a_c, k_c, d_c = xs
        v_new = u_c - jnp.einsum('bhid,bhdv->bhiv', w_c, s)
        o = jnp.einsum('bhid,bhdv->bhiv', q_c, s) + jnp.einsum('bhij,bhjv->bhiv', a_c, v_new)
        s = s * d_c[..., None, None] + jnp.einsum('bhid,bhiv->bhdv', k_c, v_new)
        return s, o

    xs = tuple(jnp.moveaxis(a, 2, 0) for a in (u, w, qd, aqk, kd, dec))
    s_fin, o = lax.scan(step, s0, xs)
    return jnp.moveaxis(o, 0, 2).reshape(B_, H, L, dv), s_fin


def _pool_mix(u, w_grp, scale):
    B_, L, _ = u.shape
    uf = u.astype(F32).reshape(B_, L, B_GROUPS, B_GW)
    cs = jnp.pad(jnp.cumsum(uf, axis=1), ((0, 0), (1, 0), (0, 0), (0, 0)))
    pos = jnp.arange(L)
    outs = []
    for gi, win in enumerate(POOL_WINDOWS):
        lo = win // 2
        hi = win - 1 - lo
        start = jnp.clip(pos - lo, 0, L)
        end = jnp.clip(pos + hi + 1, 0, L)
        csg = cs[:, :, gi]
        cnt = (end - start).astype(F32)[None, :, None]
        outs.append((csg[:, end] - csg[:, start]) / cnt - uf[:, :, gi])
    pooled = jnp.stack(outs, axis=2).astype(u.dtype)
    mixed = jnp.einsum('blgc,gcd->blgd', pooled, w_grp)
    return mixed.reshape(B_, L, B_WIDTH) * scale


def _heads(t, d):
    B_, L = t.shape[0], t.shape[1]
    return t.reshape(B_, L, -1, d).transpose(0, 2, 1, 3).astype(F32)


def _gla_pool_mixer(u, s0, w_in, w_gate, b_gate, norm_g, pool_w, pool_s, w_out):
    B_, L, _ = u.shape
    q, k, v, r, lr, pz = _split(u @ w_in, A_SPLITS)
    qh = _heads(q, A_DK) * (A_DK ** -0.5)
    kh = _heads(k, A_DK)
    vh = _heads(v, A_DV)
    lr = lr.astype(F32).reshape(B_, L, 2, A_RANK)
    glog = jax.nn.log_sigmoid(jnp.einsum('blzr,zrk->blzk', lr, w_gate.astype(F32)) + b_gate.astype(F32)) / A_GATE_NORM
    g_f = _heads(glog[:, :, 0], A_DK)
    g_b = _heads(glog[:, :, 1], A_DK)
    fl = lambda t: jnp.flip(t, axis=2)
    o_f, s_f = _gla_chunk(qh, kh, vh, g_f, s0[:, 0])
    o_b, s_b = _gla_chunk(fl(qh), fl(kh), fl(vh), fl(g_b), s0[:, 1])
    o = (o_f + fl(o_b)).transpose(0, 2, 1, 3)
    o = _rmsnorm_f32(o, norm_g).reshape(B_, L, A_WIDTH).astype(u.dtype) * jax.nn.silu(r)
    p = _pool_mix(pz, pool_w, pool_s)
    y = jnp.concatenate([o, p], axis=-1) @ w_out
    return y, jnp.stack([s_f, s_b], axis=1)


def _deltanet_mixer(u, s0, w_in, conv_w, a_log, dt_bias, norm_g, w_out):
    B_, L, _ = u.shape
    qkv, z, a, bt = _split(u @ w_in, C_SPLITS)
    qkv = jax.nn.silu(_dwconv(qkv, conv_w))
    q, k, v = jnp.split(qkv, 3, axis=-1)
    qh = _l2norm(_heads(q, C_DK)) * (C_DK ** -0.5)
    kh = _l2norm(_heads(k, C_DK))
    vh = _heads(v, C_DV)
    a = a.astype(F32).reshape(B_, L, 2, C_HEADS)
    g = -jnp.exp(a_log.astype(F32)) * jax.nn.softplus(a + dt_bias.astype(F32))
    beta = jax.nn.sigmoid(bt.astype(F32).reshape(B_, L, 2, C_HEADS))
    g = g.transpose(2, 0, 3, 1)
    beta = beta.transpose(2, 0, 3, 1)
    fl = lambda t: jnp.flip(t, axis=2)
    o_f, s_f = _delta_chunk(qh, kh, vh, g[0], beta[0], s0[:, 0])
    o_b, s_b = _delta_chunk(fl(qh), fl(kh), fl(vh), fl(g[1]), fl(beta[1]), s0[:, 1])
    o = (o_f + fl(o_b)).transpose(0, 2, 1, 3)
    o = _rmsnorm_f32(o, norm_g).reshape(B_, L, C_WIDTH).astype(u.dtype) * jax.nn.silu(z)
    return o @ w_out, jnp.stack([s_f, s_b], axis=1)


def _conv_ffn(u, w_up, conv_w, w_down):
    h = _dwconv(u @ w_up, conv_w)
    a, gt = jnp.split(h, 2, axis=-1)
    return (jax.nn.silu(gt) * a) @ w_down


def _trunk(x, cond, gla_s0, dn_s0, w):
    sc = jax.nn.silu(cond.astype(F32)).astype(x.dtype)
    gla_out, dn_out = [], []
    for l in range(DEPTH):
        mod = (sc @ w['w_mod'][l] + w['b_mod'][l])[:, None, :]
        sh1, sc1, g1, sh2, sc2, g2 = jnp.split(mod, N_MOD, axis=-1)
        u = x * (1 + sc1) + sh1
        if l % 2 == 0:
            e = l // 2
            y, st = _gla_pool_mixer(u, gla_s0[:, e], w['a_w_in'][e], w['a_w_gate'][e], w['a_b_gate'][e],
                                    w['a_norm'][e], w['b_proj'][e], w['b_scale'][e], w['a_w_out'][e])
            gla_out.append(st)
        else:
            o_ = l // 2
            y, st = _deltanet_mixer(u, dn_s0[:, o_], w['c_w_in'][o_], w['c_conv'][o_], w['c_a_log'][o_],
                                    w['c_dt_bias'][o_], w['c_norm'][o_], w['c_w_out'][o_])
            dn_out.append(st)
        x = _layernorm(ALPHA * x + g1 * y.astype(x.dtype), w['ln1_g'][l], w['ln1_b'][l])
        u = x * (1 + sc2) + sh2
        f = _conv_ffn(u, w['f_w_up'][l], w['f_conv'][l], w['f_w_down'][l])
        x = _layernorm(ALPHA * x + g2 * f, w['ln2_g'][l], w['ln2_b'][l])
    return x, jnp.stack(gla_out, axis=1), jnp.stack(dn_out, axis=1)


def setup_inputs(seed: int = 0) -> dict:
    key = jax.random.key(seed)
    ks = jax.random.split(key, 32)
    nrm = lambda i, shape, s: jax.random.normal(ks[i], shape, F32) * s
    d_in_a = sum(A_SPLITS)
    d_in_c = sum(C_SPLITS)
    dt = jnp.exp(jax.random.uniform(ks[21], (N_ODD, 2, C_HEADS), F32) * (math.log(0.1) - math.log(0.001)) + math.log(0.001))
    return {
        "x_prompt": nrm(0, (BATCH, SEQ, D_MODEL), 1.0),
        "x_sample": nrm(1, (DEC_BATCH, DEC_SEQ, D_MODEL), 1.0),
        "state_gla": nrm(2, (DEC_BATCH, N_EVEN, 2, A_HEADS, A_DK, A_DV), 0.5),
        "state_dn": nrm(3, (DEC_BATCH, N_ODD, 2, C_HEADS, C_DK, C_DV), 1.0),
        "c": nrm(4, (DEC_BATCH, D_MODEL), 1.0),
        "c_ctx": nrm(5, (D_MODEL,), 1.0),
        "w_mod": nrm(6, (DEPTH, D_MODEL, N_MOD * D_MODEL), 0.5 * D_MODEL ** -0.5),
        "b_mod": nrm(7, (DEPTH, N_MOD * D_MODEL), 0.02),
        "ln1_g": 1.0 + nrm(8, (DEPTH, D_MODEL), 0.02),
        "ln1_b": nrm(9, (DEPTH, D_MODEL), 0.02),
        "ln2_g": 1.0 + nrm(10, (DEPTH, D_MODEL), 0.02),
        "ln2_b": nrm(11, (DEPTH, D_MODEL), 0.02),
        "a_w_in": nrm(12, (N_EVEN, D_MODEL, d_in_a), D_MODEL ** -0.5),
        "a_w_gate": nrm(13, (N_EVEN, 2, A_RANK, A_QK), A_RANK ** -0.5),
        "a_b_gate": nrm(14, (N_EVEN, 2, A_QK), 0.02),
        "a_norm": 1.0 + nrm(15, (N_EVEN, A_DV), 0.02),
        "b_proj": nrm(16, (N_EVEN, B_GROUPS, B_GW, B_GW), B_GW ** -0.5),
        "b_scale": 1.0 + nrm(17, (N_EVEN, B_WIDTH), 0.02),
        "a_w_out": nrm(18, (N_EVEN, A_WIDTH + B_WIDTH, D_MODEL), BETA_INIT * (A_WIDTH + B_WIDTH) ** -0.5),
        "c_w_in": nrm(19, (N_ODD, D_MODEL, d_in_c), D_MODEL ** -0.5),
        "c_conv": nrm(20, (N_ODD, SHORT_CONV, 3 * C_WIDTH), SHORT_CONV ** -0.5),
        "c_a_log": jnp.log(jax.random.uniform(ks[22], (N_ODD, 2, C_HEADS), F32, 1.0, 16.0)),
        "c_dt_bias": dt + jnp.log(-jnp.expm1(-dt)),
        "c_norm": 1.0 + nrm(23, (N_ODD, C_DV), 0.02),
        "c_w_out": nrm(24, (N_ODD, C_WIDTH, D_MODEL), BETA_INIT * C_WIDTH ** -0.5),
        "f_w_up": nrm(25, (DEPTH, D_MODEL, 2 * D_FF), D_MODEL ** -0.5),
        "f_conv": nrm(26, (DEPTH, FFN_CONV, 2 * D_FF), FFN_CONV ** -0.5),
        "f_w_down": nrm(27, (DEPTH, D_FF, D_MODEL), BETA_INIT * D_FF ** -0.5),
    }


def reference(x_prompt, x_sample, state_gla, state_dn, c, c_ctx, w_mod, b_mod, ln1_g, ln1_b, ln2_g, ln2_b,
              a_w_in, a_w_gate, a_b_gate, a_norm, b_proj, b_scale, a_w_out,
              c_w_in, c_conv, c_a_log, c_dt_bias, c_norm, c_w_out, f_w_up, f_conv, f_w_down):
    w = dict(w_mod=w_mod, b_mod=b_mod, ln1_g=ln1_g, ln1_b=ln1_b, ln2_g=ln2_g, ln2_b=ln2_b,
             a_w_in=a_w_in, a_w_gate=a_w_gate, a_b_gate=a_b_gate, a_norm=a_norm, b_proj=b_proj,
             b_scale=b_scale, a_w_out=a_w_out, c_w_in=c_w_in, c_conv=c_conv, c_a_log=c_a_log,
             c_dt_bias=c_dt_bias, c_norm=c_norm, c_w_out=c_w_out, f_w_up=f_w_up, f_conv=f_conv,
             f_w_down=f_w_down)
    nb = x_prompt.shape[0]
    g0 = jnp.zeros((nb, N_EVEN, 2, A_HEADS, A_DK, A_DV), F32)
    d0 = jnp.zeros((nb, N_ODD, 2, C_HEADS, C_DK, C_DV), F32)
    y_prompt, new_gla, new_dn = _trunk(x_prompt, c_ctx[None, :], g0, d0, w)
    pe = _grid_pos_embed(x_sample.shape[1]).astype(x_sample.dtype)
    y_sample, _, _ = _trunk(x_sample + pe, c, state_gla.astype(F32), state_dn.astype(F32), w)
    return (y_prompt, y_sample, new_gla.astype(x_prompt.dtype), new_dn.astype(x_prompt.dtype))
```

```python
import math
import numpy as np
from concourse.bass_utils import run_bass_kernel_spmd
import concourse.bass as bass
import concourse.mybir as mybir

F32 = mybir.dt.float32
BF16 = mybir.dt.bfloat16
I32 = mybir.dt.int32
AF = mybir.ActivationFunctionType
ALU = mybir.AluOpType
AX = mybir.AxisListType

CELL = 256
_DT_SIZE = {F32: 4, BF16: 2, I32: 4}


class Region:
    def __init__(self, fw, name, handle, nbytes, cell=CELL):
        self.fw = fw
        self.name = name
        self.h = handle
        self.cell = cell
        self.ncell = (nbytes + cell - 1) // cell
        self.w = [None] * self.ncell
        self.r = [dict() for _ in range(self.ncell)]


class V:
    def __init__(self, region, ap):
        self.region = region
        self.ap = ap
        self._cells = None

    def __getitem__(self, key):
        return V(self.region, self.ap[key])

    def rr(self, pattern_, **kw):
        return V(self.region, self.ap.rearrange(pattern_, **kw))

    def bitcast(self, dt):
        return V(self.region, self.ap.bitcast(dt))

    def bc(self, shape):
        return V(self.region, self.ap.to_broadcast(shape))

    @property
    def shape(self):
        return self.ap.shape

    def cells(self):
        if self._cells is None:
            ap = self.ap
            esz = _DT_SIZE[ap.dtype]
            dims = list(ap.ap)[1:]
            base = int(ap.offset) if not isinstance(ap.offset, int) else ap.offset
            pstep = list(ap.ap)[0][0]
            if pstep > 0:
                base = base % pstep
            base_b = base * esz
            cs = set()
            CELL = self.region.cell
            dims = [(s, n) for (s, n) in dims if n > 1 or True]
            if not dims:
                dims = [(1, 1)]
            *outer, (ls, ln) = dims
            if ls in (0, 1):
                run = (esz * (ln if ls == 1 else 1))
                inner_iter = [0]
            else:
                run = esz
                inner_iter = [i * ls * esz for i in range(ln)]
            offs = [0]
            for (s, n) in outer:
                if s == 0:
                    continue
                offs = [o + i * s * esz for o in offs for i in range(n)]
            for o in offs:
                for ii in inner_iter:
                    a = base_b + o + ii
                    for c in range(a // CELL, (a + run - 1) // CELL + 1):
                        cs.add(c)
            self._cells = sorted(cs)
            assert self._cells[-1] < self.region.ncell, (self.region.name, self._cells[-1], self.region.ncell, ap)
        return self._cells


class Op:
    __slots__ = ("eng", "fn", "waits", "signal", "semval", "dma_sem", "is_dma")

    def __init__(self, eng, fn):
        self.eng = eng
        self.fn = fn
        self.waits = []
        self.signal = False
        self.semval = None
        self.dma_sem = None
        self.is_dma = False


ENGS = ("pe", "dve", "act", "pool", "sp")
N_DMA_SEMS = 12


class FW:
    def __init__(self, nc):
        self.nc = nc
        self.ops = {e: [] for e in ENGS}
        self.regions = []
        self.dma_rr = {"sp": 0, "pool": 0, "act": 0}
        self.dma_last = {}
        self._ctx = []
        self.psum_ptr = 0
        self.nops = 0

    def sbuf(self, name, shape, dt):
        g = self.nc.sbuf_tensor(name, list(shape), dt)
        h = g.__enter__()
        self._ctx.append(g)
        nb = int(np.prod(shape[1:])) * _DT_SIZE[dt]
        reg = Region(self, name, h, nb)
        self.regions.append(reg)
        return V(reg, h[:] if hasattr(h, "__getitem__") else h.ap())

    def psum_banks(self):
        self.banks = []
        for i in range(8):
            g = self.nc.psum_tensor(f"psb{i}", [128, 512], F32)
            h = g.__enter__()
            self._ctx.append(g)
            reg = Region(self, f"psb{i}", h, 2048, cell=2048)
            self.banks.append(V(reg, h[:]))

    def psum(self, ncols=512, parts=128, dt=F32):
        b = self.psum_ptr
        self.psum_ptr = (b + 1) % 8
        bank = self.banks[b] if dt == F32 else self.banks[b].bitcast(dt)
        return bank[0:parts, 0:ncols]

    def _deps(self, op, reads, writes):
        deps = {}
        for v in reads:
            reg = v.region
            for c in v.cells():
                w = reg.w[c]
                if w is not None:
                    deps[id(w)] = w
        for v in writes:
            reg = v.region
            for c in v.cells():
                w = reg.w[c]
                if w is not None:
                    deps[id(w)] = w
                for t in reg.r[c].values():
                    deps[id(t)] = t
        for t in deps.values():
            if t is op:
                continue
            if t.eng == "pe" and op.eng == "pe" and not t.is_dma and not op.is_dma:
                continue
            op.waits.append(t)
            t.signal = True
        for v in reads:
            reg = v.region
            key = op.dma_sem if op.is_dma else op.eng
            for c in v.cells():
                reg.r[c][key] = op
        for v in writes:
            reg = v.region
            for c in v.cells():
                reg.w[c] = op
                reg.r[c] = {}

    def op(self, eng, fn, reads=(), writes=()):
        if getattr(self, "halted", False):
            return None
        o = Op(eng, fn)
        self._deps(o, reads, writes)
        self.ops[eng].append(o)
        self.nops += 1
        return o

    def dma(self, queue, out, in_, reads=(), writes=(), **kw):
        if getattr(self, "halted", False):
            return None
        o = Op(queue, None)
        o.is_dma = True
        o.signal = True
        oap = out.ap if isinstance(out, V) else out
        iap = in_.ap if isinstance(in_, V) else in_
        rd = list(reads) + ([in_] if isinstance(in_, V) else [])
        wr = list(writes) + ([out] if isinstance(out, V) else [])
        k = self.dma_rr[queue]
        self.dma_rr[queue] = (k + 1) % N_DMA_SEMS
        o.dma_sem = (queue, k)
        prev = self.dma_last.get((queue, k))
        self._deps(o, rd, wr)
        if prev is not None:
            o.waits.append(prev)
        self.dma_last[(queue, k)] = o
        o.fn = lambda e: e.dma_start(out=oap, in_=iap, **kw)
        self.ops[queue].append(o)
        self.nops += 1
        return o

    def emit(self):
        nc = self.nc
        sems = {}
        semctx = []
        for e in ENGS:
            g = nc.semaphore(f"s_{e}")
            sems[e] = g.__enter__()
            semctx.append(g)
        dsems = {}
        for q in ("sp", "pool"):
            for k in range(N_DMA_SEMS):
                g = nc.semaphore(f"d_{q}{k}")
                dsems[(q, k)] = g.__enter__()
                semctx.append(g)
        for e in ENGS:
            cnt = 0
            for o in self.ops[e]:
                if o.is_dma:
                    continue
                if o.signal:
                    cnt += 1
                    o.semval = cnt
            self.maxsem = max(getattr(self, "maxsem", 0), cnt)
        dcnt = {}
        for e in ENGS:
            for o in self.ops[e]:
                if o.is_dma:
                    dcnt[o.dma_sem] = dcnt.get(o.dma_sem, 0) + 16
                    o.semval = dcnt[o.dma_sem]

        def semof(t):
            return dsems[t.dma_sem] if t.is_dma else sems[t.eng]

        def run(engname, engobj):
            known = {}
            for o in self.ops[engname]:
                need = {}
                for t in o.waits:
                    s = t.dma_sem if t.is_dma else t.eng
                    if t.semval > need.get(s, (0, None))[0]:
                        need[s] = (t.semval, t)
                for s, (val, t) in need.items():
                    if known.get(s, 0) >= val:
                        continue
                    engobj.wait_ge(semof(t), val)
                    known[s] = val
                ins = o.fn(engobj)
                if o.is_dma:
                    ins.then_inc(dsems[o.dma_sem], 16)
                elif o.signal:
                    ins.then_inc(sems[engname], 1)
            if engname in ("sp", "pool"):
                for k in range(N_DMA_SEMS):
                    if dcnt.get((engname, k), 0) > 0:
                        engobj.wait_ge(dsems[(engname, k)], dcnt[(engname, k)])

        with nc.Block() as block:
            @block.tensor
            def _(e):
                run("pe", e)

            @block.vector
            def _(e):
                run("dve", e)

            @block.scalar
            def _(e):
                run("act", e)

            @block.gpsimd
            def _(e):
                run("pool", e)

            @block.sync
            def _(e):
                run("sp", e)
        for g in reversed(semctx):
            g.__exit__(None, None, None)
        for g in reversed(self._ctx):
            g.__exit__(None, None, None)

D = 1024
NL = 4
DFF = 2816
NCH = DFF // 128
ALPHA = float(8 ** 0.25)
EPS = 1e-6
KC = 8
A_IN = 2080
C_IN = 4128
PI = float(np.pi)

DBG_SPECS = {}


class _Stop(Exception):
    pass


def build_program(debug=(), stop=None):
    nc = bass.Bass("TRN2", target_bir_lowering=False)
    fw = FW(nc)

    def din(name, shape):
        return nc.dram_tensor(name, list(shape), F32, kind="ExternalInput").ap()

    def dout(name, shape):
        return nc.dram_tensor(name, list(shape), F32, kind="ExternalOutput").ap()

    xp_d = din("xp", [1024, D])
    xs_d = din("xs", [2048, D])
    cond_d = din("cond2", [2, D])
    sgla_d = din("sgla", [16, 64, 128])
    sdn_d = din("sdn", [32, 128, 128])
    w_mod_d = din("w_mod", [NL, D, 6 * D])
    b_mod_d = din("b_mod", [NL, 6 * D])
    ln_d = {k: din(k, [NL, D]) for k in ("ln1_g", "ln1_b", "ln2_g", "ln2_b")}
    a_w_in_d = din("a_w_in", [2, D, A_IN])
    a_w_gate_d = din("a_w_gate", [2, 2, 16, 256])
    a_b_gate_d = din("a_b_gate", [2, 2, 256])
    a_norm_d = din("a_norm", [2, 128])
    b_proj_d = din("b_proj", [2, 4, 128, 128])
    b_scale_d = din("b_scale", [2, 512])
    a_w_out_d = din("a_w_out", [2, D, D])
    c_w_in_d = din("c_w_in", [2, D, C_IN])
    c_conv_d = din("c_conv", [2, 3, 3072])
    c_a_log_d = din("c_a_log", [2, 2, 8])
    c_dt_bias_d = din("c_dt_bias", [2, 2, 8])
    c_norm_d = din("c_norm", [2, 128])
    c_w_out_d = din("c_w_out", [2, D, D])
    f_w_up_d = din("f_w_up", [NL, D, 2 * DFF])
    f_conv_d = din("f_conv", [NL, 3, 2 * DFF])
    f_w_down_d = din("f_w_down", [NL, DFF, D])

    yp_d = dout("yp", [1024, D])
    ys_d = dout("ys", [2048, D])
    ngla_d = dout("ngla", [64, 64, 128])
    ndn_d = dout("ndn", [128, 128, 128])
    dbg_d = {}

    def A(v):
        return v.ap if isinstance(v, V) else v

    def rds(*xs):
        return [x for x in xs if isinstance(x, V)]

    def mm(out, lhsT, rhs, start=True, stop=True):
        fw.op("pe", lambda e: e.matmul(out.ap, lhsT=lhsT.ap, rhs=rhs.ap, start=start, stop=stop),
              reads=[lhsT, rhs], writes=[out])

    def tr(out, in_, idn):
        fw.op("pe", lambda e: e.transpose(out=out.ap, in_=in_.ap, identity=idn.ap),
              reads=[in_, idn], writes=[out])

    def act(out, in_, func, bias=None, scale=None, accum=None):
        kw = {}
        if bias is not None:
            kw["bias"] = A(bias)
        if scale is not None:
            kw["scale"] = A(scale)
        if accum is not None:
            kw["accum_out"] = accum.ap
        fw.op("act", lambda e: e.activation(out=out.ap, in_=in_.ap, func=func, **kw),
              reads=rds(in_, bias, scale), writes=rds(out, accum))

    def ts(eng, out, in0, s1, op0, s2=None, op1=None):
        kw = {"op1": op1} if op1 is not None else {}
        fw.op(eng, lambda e: e.tensor_scalar(out=out.ap, in0=in0.ap, scalar1=A(s1),
                                             scalar2=(A(s2) if s2 is not None else None), op0=op0, **kw),
              reads=rds(in0, s1, s2), writes=[out])

    def tt(eng, out, in0, in1, op):
        fw.op(eng, lambda e: e.tensor_tensor(out=out.ap, in0=in0.ap, in1=in1.ap, op=op),
              reads=[in0, in1], writes=[out])

    def stt(eng, out, in0, scalar, in1, op0, op1):
        fw.op(eng, lambda e: e.scalar_tensor_tensor(out=out.ap, in0=in0.ap, scalar=A(scalar), in1=in1.ap,
                                                    op0=op0, op1=op1),
              reads=rds(in0, scalar, in1), writes=[out])

    def cp(eng, out, in_):
        if eng == "act":
            fw.op("act", lambda e: e.copy(out=out.ap, in_=in_.ap), reads=[in_], writes=[out])
        else:
            fw.op(eng, lambda e: e.tensor_copy(out=out.ap, in_=in_.ap), reads=[in_], writes=[out])

    def memset(eng, out, val):
        fw.op(eng, lambda e: e.memset(out.ap, val), writes=[out])

    def scan(out, d0, d1, init=0.0):
        fw.op("dve", lambda e: e.tensor_tensor_scan(out=out.ap, data0=d0.ap, data1=d1.ap, initial=init,
                                                    op0=ALU.mult, op1=ALU.add),
              reads=[d0, d1], writes=[out])

    def recip(out, in_):
        fw.op("dve", lambda e: e.reciprocal(out=out.ap, in_=in_.ap), reads=[in_], writes=[out])

    def rsum(out, in_):
        fw.op("dve", lambda e: e.reduce_sum(out=out.ap, in_=in_.ap, axis=AX.X), reads=[in_], writes=[out])

    def load(q, out, src, **kw):
        fw.dma(q, out, src, **kw)

    def dbg(name, v, shape):
        if name in debug:
            d = dout("dbg_" + name, list(shape))
            dbg_d[name] = d
            fw.dma("sp" if v.ap.dtype == F32 else "pool", d, v)

    def stop_at(name):
        if stop == name:
            fw.halted = True

    rr = [0]

    def evac_eng():
        rr[0] ^= 1
        return "act" if rr[0] else "dve"

    fw.psum_banks()
    X = fw.sbuf("X", [128, 16, D], F32)
    ARENA = fw.sbuf("ARENA", [128, 39936], BF16)
    ARENA_F = ARENA.bitcast(F32)
    WR = fw.sbuf("WR", [128, 3, 4096], BF16)
    LNB = fw.sbuf("LNB", [128, 2, D], F32)
    GB = fw.sbuf("GB", [128, D], F32)
    ones_f = fw.sbuf("ones_f", [128, 128], F32)
    ident_f = fw.sbuf("ident_f", [128, 128], F32)
    ident_b = fw.sbuf("ident_b", [128, 128], BF16)
    ones_b = fw.sbuf("ones_b", [128, 128], BF16)
    MU = fw.sbuf("MU", [128, 128], F32)
    MUs = fw.sbuf("MUs", [128, 128], F32)
    ML = fw.sbuf("ML", [128, 128], F32)
    MLs = fw.sbuf("MLs", [128, 128], F32)
    pidx = fw.sbuf("pidx", [64, 128], F32)
    BMK = fw.sbuf("BMK", [128, 4, 128], BF16)
    selc = fw.sbuf("selc", [64, 128], F32)
    modcol = fw.sbuf("modcol", [128, NL, 48, 2], F32)
    mscale = fw.sbuf("mscale", [128, NL, 2, 8, 2], F32)
    scT = fw.sbuf("scT", [128, 8, 2], F32)
    small = fw.sbuf("small", [128, 512], F32)
    S_f = fw.sbuf("S_f", [128, 2, 128], F32)
    S_b = fw.sbuf("S_b", [128, 2, 128], BF16)
    wsm = fw.sbuf("wsm", [128, 8, 128], BF16)
    wg = fw.sbuf("wg", [48, 256], BF16)
    bproj = fw.sbuf("bproj", [128, 4, 128], BF16)
    convc = fw.sbuf("convc", [128, 160], F32)
    normB = fw.sbuf("normB", [128, 128], F32)
    junk = fw.sbuf("junk", [128, D], BF16)
    pe_c = fw.sbuf("pe_c", [128, 512], F32)
    freq = fw.sbuf("freq", [128, 256], F32)
    pcol = fw.sbuf("pcol", [128, 8], F32)
    ptmp = fw.sbuf("ptmp", [128, 3, 256], F32)
    ptmpi = fw.sbuf("ptmpi", [128, 256], I32)
    efix = fw.sbuf("efix", [128, 4, 16], F32)

    memset("pool", ones_f, 1.0)
    memset("pool", ones_b, 1.0)

    def asel(out, in_, pattern, cmp, cm, base=0):
        fw.op("pool", lambda e: e.affine_select(out=out.ap, in_=in_.ap, pattern=pattern, compare_op=cmp, fill=0.0,
                                                base=base, channel_multiplier=cm), reads=[in_], writes=[out])

    asel(ident_f, ones_f, [[-1, 128]], ALU.is_equal, 1)
    cp("dve", ident_b, ident_f)
    asel(MU, ones_f, [[1, 128]], ALU.is_ge, -1)
    asel(MUs, ones_f, [[1, 128]], ALU.is_gt, -1)
    asel(ML, ones_f, [[-1, 128]], ALU.is_ge, 1)
    asel(MLs, ones_f, [[-1, 128]], ALU.is_gt, 1)
    fw.op("pool", lambda e: e.iota(pidx.ap, [[0, 128]], base=0, channel_multiplier=1,
                                   allow_small_or_imprecise_dtypes=True), writes=[pidx])
    mdt = ptmp.rr("p a n -> p (a n)")
    for bi_, bsz in enumerate((16, 32, 64)):
        nb_ = 128 // bsz
        Eb = ptmp[0:8, 0, 0:128]
        asel(Eb, ones_f[0:8, :], [[1, 128]], ALU.is_ge, -bsz, base=0)
        asel(Eb, Eb, [[-1, 128]], ALU.is_ge, bsz, base=bsz - 1)
        ps = fw.psum()
        mm(ps[:, 0:128], Eb[0:nb_, :], Eb[0:nb_, :])
        cp("dve", mdt[:, 256 + bi_ * 128:256 + (bi_ + 1) * 128], ps[:, 0:128])
    md16, md32, md64 = (mdt[:, 256 + i * 128:256 + (i + 1) * 128] for i in range(3))
    cp("dve", BMK[:, 0, :], md16)
    tt("dve", BMK[:, 1, :], md32, md16, ALU.subtract)
    tt("dve", BMK[:, 2, :], md64, md32, ALU.subtract)
    ts("dve", BMK[:, 3, :], md64, -1.0, ALU.mult, 1.0, ALU.add)

    stop_at("c1")
    condt = ARENA_F[0:2, 0:1024]
    load("sp", condt, cond_d)
    act(condt, condt, AF.Silu)
    ps = fw.psum()
    for kc in range(8):
        tr(ps[:, kc * 2:(kc + 1) * 2], condt[:, kc * 128:(kc + 1) * 128], ident_f[0:2, 0:2])
    cp("dve", scT.rr("p k c -> p (k c)"), ps[:, 0:16])

    stop_at("c2")
    bmr = ARENA_F[0:48, 1024:1152]
    bmT = small[:, 0:48]
    wm_slots = [ARENA_F[:, 2048 + i * 4096: 2048 + (i + 1) * 4096].rr("p (k n) -> p k n", k=8) for i in range(2)]
    for l in range(NL):
        load("sp", bmr, b_mod_d[l].rearrange("(b p) -> b p", p=128))
        ps = fw.psum()
        tr(ps[:, 0:48], bmr, ident_f[0:48, 0:48])
        cp("act", bmT, ps[:, 0:48])
        for g in range(12):
            slot = wm_slots[g % 2]
            load("sp", slot, w_mod_d[l].rearrange("(k p) n -> p k n", p=128)[:, :, g * 512:(g + 1) * 512])
            ps = fw.psum()
            for b4 in range(4):
                for kc in range(8):
                    mm(ps[:, b4 * 2:(b4 + 1) * 2], slot[:, kc, b4 * 128:(b4 + 1) * 128], scT[:, kc, :],
                       start=(kc == 0), stop=(kc == 7))
            ps3 = ps[:, 0:8].rr("p (b c) -> p b c", c=2)
            for c in range(2):
                tt("dve", modcol[:, l, 4 * g:4 * g + 4, c], ps3[:, :, c], bmT[:, 4 * g:4 * g + 4], ALU.add)
        for w, blk0 in enumerate((8, 32)):
            ts("dve", mscale[:, l, w], modcol[:, l, blk0:blk0 + 8, :], 1.0, ALU.add, 1.0 / ALPHA, ALU.mult)
    dbg("modcol", modcol.rr("p l b c -> p (l b c)"), [128, NL * 96])

    stop_at("c3")
    POOLW = (2, 4, 8, 16)
    for gi, w in enumerate(POOLW):
        lo = w // 2
        hi = w - 1 - lo
        fw.op("pool", lambda e, gi=gi, lo=lo, hi=hi: e.iota(efix[:, gi, 0:lo].ap, [[1, lo]], base=hi + 1, channel_multiplier=0,
                                                           allow_small_or_imprecise_dtypes=True), writes=[efix[:, gi, 0:lo]])
        if hi > 0:
            fw.op("pool", lambda e, gi=gi, hi=hi, w=w: e.iota(efix[:, gi, 8:8 + hi].ap, [[-1, hi]], base=w - 1, channel_multiplier=0,
                                                              allow_small_or_imprecise_dtypes=True), writes=[efix[:, gi, 8:8 + hi]])
        for (a, n) in ((0, lo), (8, hi)):
            if n > 0:
                recip(efix[:, gi, a:a + n], efix[:, gi, a:a + n])
                ts("dve", efix[:, gi, a:a + n], efix[:, gi, a:a + n], float(w), ALU.mult)

    stop_at("c4")
    fw.op("pool", lambda e: e.iota(freq.ap, [[1, 256]], base=0, channel_multiplier=0, allow_small_or_imprecise_dtypes=True),
          writes=[freq])
    act(freq, freq, AF.Exp, scale=-math.log(10000.0) / 256.0)
    fw.op("pool", lambda e: e.iota(pcol[:, 0:1].ap, [[0, 1]], base=0, channel_multiplier=1, allow_small_or_imprecise_dtypes=True),
          writes=[pcol[:, 0:1]])
    fw.op("dve", lambda e: e.tensor_single_scalar(out=pcol[:, 1:2].ap, in_=pcol[:, 0:1].ap, scalar=64.0, op=ALU.is_ge),
          reads=[pcol[:, 0:1]], writes=[pcol[:, 1:2]])
    stt("dve", pcol[:, 2:3], pcol[:, 1:2], -64.0, pcol[:, 0:1], ALU.mult, ALU.add)

    def sincos(out_sin, out_cos, theta):
        for out, shift in ((out_sin, 0.0), (out_cos, 0.25)):
            t = ptmp[:, 1, :]
            gq = ptmp[:, 2, :]
            ts("dve", t, theta, 1.0 / (2 * PI), ALU.mult, shift, ALU.add)
            cp("dve", ptmpi, t)
            tt("dve", t, t, ptmpi, ALU.subtract)
            fw.op("dve", lambda e, t=t, gq=gq: e.tensor_single_scalar(out=gq.ap, in_=t.ap, scalar=0.5, op=ALU.is_ge),
                  reads=[t], writes=[gq])
            tt("dve", t, t, gq, ALU.subtract)
            fw.op("dve", lambda e, t=t, gq=gq: e.tensor_single_scalar(out=gq.ap, in_=t.ap, scalar=-0.5, op=ALU.is_lt),
                  reads=[t], writes=[gq])
            tt("dve", t, t, gq, ALU.add)
            act(out, t, AF.Sin, scale=2 * PI)

    ts("dve", ptmp[:, 0, :], freq, pcol[:, 2:3], ALU.mult)
    sincos(pe_c[:, 0:256], pe_c[:, 256:512], ptmp[:, 0, :])

    stop_at("c5")
    def seg512(n):
        out = []
        a = 0
        while a < n:
            out.append((a, min(512, n - a)))
            a += 512
        return out

    def load_x(pass_id, T):
        src = xp_d if pass_id == 0 else xs_d
        for t in range(T):
            load("sp", X[:, t, :], src[t * 128:(t + 1) * 128, :])
            if pass_id == 1:
                ts("dve", pcol[:, 3:4], pcol[:, 1:2], float(2 * t), ALU.add)
                ts("dve", ptmp[:, 0, :], freq, pcol[:, 3:4], ALU.mult)
                pe_r = junk.bitcast(F32)
                sincos(pe_r[:, 0:256], pe_r[:, 256:512], ptmp[:, 0, :])
                tt("dve", X[:, t, 0:512], X[:, t, 0:512], pe_r, ALU.add)
                tt("dve", X[:, t, 512:1024], X[:, t, 512:1024], pe_c, ALU.add)
            ts("pool", X[:, t, :], X[:, t, :], ALPHA, ALU.mult)
        stop_at("c6")

    def make_ut(UT, l, which, cond, tiles, col0, halo=None):
        shb = 0 if which == 0 else 24
        jobs = [(t, col0 + i * 128, None) for i, t in enumerate(tiles)]
        if halo is not None:
            jobs.append((halo[0], halo[2], halo[1]))
        for (t, c0, hc) in jobs:
            for half in range(2):
                ps = fw.psum()
                for q in range(4):
                    kc = half * 4 + q
                    tr(ps[:, q * 128:(q + 1) * 128], X[:, t, kc * 128:(kc + 1) * 128], ident_f)
                if which == 1:
                    stop_at("u1")
                eng_b = evac_eng()
                for q in range(4):
                    kc = half * 4 + q
                    sc = mscale[:, l, which, kc, cond:cond + 1]
                    sh = modcol[:, l, shb + kc, cond:cond + 1]
                    if hc is None:
                        src = ps[:, q * 128:(q + 1) * 128]
                        dst = UT[:, kc, c0:c0 + 128]
                    else:
                        src = ps[:, q * 128 + hc:q * 128 + hc + 1]
                        dst = UT[:, kc, c0:c0 + 1]
                    if eng_b == "act":
                        act(dst, src, AF.Identity, bias=sh, scale=sc)
                    else:
                        ts("dve", dst, src, sc, ALU.mult, sh, ALU.add)
                    if which == 1:
                        stop_at("u2")
                if which == 1:
                    stop_at("u3")
            if which == 1:
                stop_at("u4")
        if stop == "ut":
            dbg("ut", UT[:, 0, 0:1024], [128, 1024])
            for kc_ in range(8):
                dbg("ut%d" % kc_, UT[:, kc_, 0:256], [128, 256])
            dbg("x0", X[:, 0, :], [128, D])
            dbg("mscale", mscale.rr("p l w k c -> p (l w k c)"), [128, NL * 32])
            raise _Stop()

    def gbcast(l, blk0, cond):
        for half in range(2):
            ps = fw.psum()
            for q in range(4):
                kc = half * 4 + q
                dg = ptmp[:, 0, 0:128]
                ts("dve", dg, ident_f, modcol[:, l, blk0 + kc, cond:cond + 1], ALU.mult)
                mm(ps[:, q * 128:(q + 1) * 128], ones_f, dg)
            cp("act", GB[:, half * 512:(half + 1) * 512], ps)

    def layernorm(l, which, T, last):
        gname, bname = ("ln1_g", "ln1_b") if which == 0 else ("ln2_g", "ln2_b")
        load("sp", LNB[:, 0, :], ln_d[gname][l:l + 1, :].to_broadcast([128, D]))
        load("sp", LNB[:, 1, :], ln_d[bname][l:l + 1, :].to_broadcast([128, D]))
        stop_at("lna")
        if not last:
            ts("pool", LNB.rr("p a d -> p (a d)"), LNB.rr("p a d -> p (a d)"), ALPHA, ALU.mult)
        stop_at("lnb")
        st = small[:, 64:72]
        for t in range(T):
            xt = X[:, t, :]
            rsum(st[:, 0:1], xt)
            stop_at("lnc")
            ts("dve", st[:, 1:2], st[:, 0:1], -1.0 / D, ALU.mult)
            act(junk, xt, AF.Square, bias=st[:, 1:2], accum=st[:, 2:3])
            stop_at("lnd")
            ts("dve", st[:, 3:4], st[:, 2:3], 1.0 / D, ALU.mult, EPS, ALU.add)
            act(st[:, 3:4], st[:, 3:4], AF.Sqrt)
            recip(st[:, 4:5], st[:, 3:4])
            ts("dve", xt, xt, st[:, 1:2], ALU.add, st[:, 4:5], ALU.mult)
            tt("pool", xt, xt, LNB[:, 0, :], ALU.mult)
            tt("dve", xt, xt, LNB[:, 1, :], ALU.add)

    wr_i = [0]

    def wslot(nring=3):
        s = WR[:, wr_i[0] % nring, :]
        wr_i[0] += 1
        return s

    WO = WR[:, 2, :]

    def xacc(t, psA, psB):
        tt("dve", X[:, t, 0:512], X[:, t, 0:512], psA, ALU.add)
        tt("dve", X[:, t, 512:1024], X[:, t, 512:1024], psB, ALU.add)

    def ffn(l, pass_id, T, cond):
        cr = ARENA_F[0:44, 0:128]
        for k in range(3):
            load("sp", cr, f_conv_d[l][k].rearrange("(c p) -> c p", p=128))
            ps = fw.psum()
            tr(ps[:, 0:44], cr, ident_f[0:44, 0:44])
            cp("act", convc[:, k * 44:(k + 1) * 44], ps[:, 0:44])
        stop_at("fa")
        gbcast(l, 40, cond)
        stop_at("fb")
        UTs = ARENA[:, 0:8224].rr("p (k n) -> p k n", k=8)
        actT = ARENA[:, 8224:8224 + 22528].rr("p (j n) -> p j n", j=NCH)
        o0 = 8224 + 22528
        hpre = [[ARENA[:, o0 + (2 * i + h) * 1032: o0 + (2 * i + h) * 1032 + 1028] for h in range(2)] for i in range(2)]
        o1 = (o0 + 4 * 1032 + 1) // 2 + 8
        acc = [ARENA_F[:, o1 + h * 1024: o1 + (h + 1) * 1024] for h in range(2)]
        assert (o1 + 2048) <= 19968
        groups = [(0, 8, None)] if pass_id == 0 else [(0, 8, "R"), (8, 8, "L")]
        wup = f_w_up_d[l].rearrange("(k p) n -> p k n", p=128)
        wdn = f_w_down_d[l].rearrange("(c p) n -> p c n", p=128)
        for (t0, nt, hal) in groups:
            ntok = nt * 128
            halo = None
            if hal == "R":
                halo = (t0 + nt, 0, 1026)
            elif hal == "L":
                halo = (t0 - 1, 127, 1)
            make_ut(UTs, l, 1, cond, list(range(t0, t0 + nt)), 2, halo)
            stop_at("f0")
            for i in range(2):
                for h in range(2):
                    if hal != "L":
                        memset("pool", hpre[i][h][:, 0:2], 0.0)
                    if hal != "R":
                        memset("pool", hpre[i][h][:, 1026:1028], 0.0)
            segs = [(2, 512), (514, 512)]
            if hal == "R":
                segs.append((1026, 1))
            if hal == "L":
                segs.append((1, 1))
            jgs = [(j0, min(4, NCH - j0)) for j0 in range(0, NCH, 4)]
            for (j0, nj) in jgs:
                sa = wslot()[:, 0:8 * 128 * nj].rr("p (k n) -> p k n", k=8)
                sg = wslot()[:, 0:8 * 128 * nj].rr("p (k n) -> p k n", k=8)
                load("pool", sa, wup[:, :, j0 * 128:(j0 + nj) * 128])
                load("pool", sg, wup[:, :, DFF + j0 * 128:DFF + (j0 + nj) * 128])
                for jj in range(nj):
                    j = j0 + jj
                    hp = hpre[j % 2]
                    for h, sw in enumerate((sa, sg)):
                        for (c0, n) in segs:
                            ps = fw.psum()
                            for kc in range(8):
                                mm(ps[:, 0:n], sw[:, kc, jj * 128:(jj + 1) * 128], UTs[:, kc, c0:c0 + n],
                                   start=(kc == 0), stop=(kc == 7))
                            if n > 1:
                                cp("act", hp[h][:, c0:c0 + n], ps[:, 0:n])
                            else:
                                cp("dve", hp[h][:, c0:c0 + n], ps[:, 0:n])
                        cc = h * 22 + j
                        w0 = convc[:, cc:cc + 1]
                        w1 = convc[:, 44 + cc:44 + cc + 1]
                        w2 = convc[:, 88 + cc:88 + cc + 1]
                        hh = hp[h]
                        ts("pool", acc[h], hh[:, 2:1026], w1, ALU.mult)
                        if pass_id == 0:
                            a3 = acc[h].rr("p (s t) -> p s t", s=4)
                            h3 = hh[:, 2:1026].rr("p (s t) -> p s t", s=4)
                            stt("dve", a3[:, :, 1:256], h3[:, :, 0:255], w0, a3[:, :, 1:256], ALU.mult, ALU.add)
                            stt("dve", a3[:, :, 0:255], h3[:, :, 1:256], w2, a3[:, :, 0:255], ALU.mult, ALU.add)
                        else:
                            stt("dve", acc[h], hh[:, 1:1025], w0, acc[h], ALU.mult, ALU.add)
                            stt("dve", acc[h], hh[:, 3:1027], w2, acc[h], ALU.mult, ALU.add)
                    act(acc[1], acc[1], AF.Silu)
                    tt("pool", actT[:, j, :], acc[1], acc[0], ALU.mult)
                    stop_at("f0b")
            if l == 0 and t0 == 0:
                dbg("actT%d" % pass_id, actT[:, 0, :], [128, 1024])
            stop_at("f1")
            for q0 in range(0, nt, 4):
                for (j0, nj) in jgs:
                    sd = wslot()[:, 0:1024 * nj].rr("p (c n) -> p c n", c=nj)
                    load("pool", sd, wdn[:, j0:j0 + nj, :])
                    for c in range(nj):
                        tt("dve" if c % 2 else "pool", sd[:, c, :], sd[:, c, :], GB, ALU.mult)
                    for ti in range(4):
                        for jj in range(nj):
                            j = j0 + jj
                            for hf in range(2):
                                mm(fw.banks[ti * 2 + hf], actT[:, j, (q0 + ti) * 128:(q0 + ti + 1) * 128],
                                   sd[:, jj, hf * 512:(hf + 1) * 512], start=(j == 0), stop=(j == NCH - 1))
                for ti in range(4):
                    xacc(t0 + q0 + ti, fw.banks[ti * 2], fw.banks[ti * 2 + 1])
                stop_at("f2")

    def gla_mixer(l, pass_id, T, seqs, cond):
        e_ = l // 2
        win = a_w_in_d[e_].rearrange("(k p) n -> p k n", p=128)
        wout = a_w_out_d[e_].rearrange("(c p) n -> p c n", p=128)
        UT = ARENA[:, 0:16384].rr("p (k n) -> p k n", k=8)
        TR = ARENA[:, 16384:39936]
        TRF = ARENA_F[:, 8192:19968]
        make_ut(UT, l, 0, cond, list(range(T)), 0)
        gbcast(l, 16, cond)
        memset("pool", wsm, 0.0)
        load("pool", wsm[:, :, 0:16], win[:, :, 1536:1552])
        load("pool", wsm[:, :, 32:48], win[:, :, 1552:1568])
        load("pool", wg[0:16, :], a_w_gate_d[e_, 0])
        load("pool", wg[32:48, :], a_w_gate_d[e_, 1])
        load("pool", bproj, b_proj_d[e_].rearrange("g c d -> c g d"))
        load("sp", normB, a_norm_d[e_:e_ + 1, :].to_broadcast([128, 128]))
        bg = TRF[0:8, 0:128]
        load("sp", bg[0:8, 0:64], a_b_gate_d[e_].rearrange("z (h d) -> (z h) d", d=64))
        ps = fw.psum()
        tr(ps[0:64, 0:8], bg[0:8, 0:64], ident_f[0:8, 0:8])
        ts("dve", small[0:64, 80:88], ps[0:64, 0:8], -1.0, ALU.mult)
        ps = fw.psum()
        bg2 = TRF[0:4, 128:256]
        load("sp", bg2, b_scale_d[e_].rearrange("(g d) -> g d", d=128))
        tr(ps[:, 0:4], bg2, ident_f[0:4, 0:4])
        cp("act", small[:, 96:100], ps[:, 0:4])
        wo_h = WO.rr("p (c n) -> p c n", c=4)
        load("pool", wo_h, wout[:, 0:4, :])
        for c in range(4):
            tt("dve" if c % 2 else "pool", wo_h[:, c, :], wo_h[:, c, :], GB, ALU.mult)
        o_f = LNB.rr("p a d -> p (a d)").rr("p (t v) -> p t v", v=128)
        stop_at("g1")

        for si, (t0, nt) in enumerate(seqs):
            L = nt * 128
            c0 = t0 * 128
            qT = TR[0:64, 0:2048]
            kT = TR[0:64, 2048:4096]
            v_tok = TR[:, 4096:6144].rr("p (t v) -> p t v", v=128)
            r_tok = TR[:, 6144:8192].rr("p (t v) -> p t v", v=128)
            lrT = TR[0:64, 8192:10240]
            qe = TR[0:64, 10240:11264]
            ke = TR[0:64, 11264:12288]
            kd = TR[0:64, 12288:13312]
            kd_tok = TR[:, 13312:13824].rr("p (c d) -> p c d", d=64)
            ATb = [TR[:, 13824 + i * 128:13824 + (i + 1) * 128] for i in range(2)]
            ogb = TR[:, 14080:14208]
            oTb = TR[:, 14208:14336]
            fo = 14336 // 2
            cpos = TRF[0:64, fo:fo + 1024]
            tmp = TRF[0:64, fo + 1024:fo + 2048]
            dcol = TRF[0:64, fo + 2048:fo + 2056]
            osum = TRF[:, fo + 2056:fo + 2184]
            ost = TRF[:, fo + 2184:fo + 2192]
            totc = TRF[0:64, fo + 2192:fo + 2200]
            for (a, n) in seg512(L):
                ps = fw.psum()
                for kc in range(8):
                    mm(ps[0:64, 0:n], wsm[:, kc, 0:64], UT[:, kc, c0 + a:c0 + a + n], start=(kc == 0), stop=(kc == 7))
                cp("act", lrT[:, a:a + n], ps[0:64, 0:n])
            stop_at("g2")
            for h in range(4):
                sw = wslot(2)[:, 0:8 * 384].rr("p (k n) -> p k n", k=8)
                load("pool", sw[:, :, 0:64], win[:, :, h * 64:(h + 1) * 64])
                load("pool", sw[:, :, 64:128], win[:, :, 256 + h * 64:256 + (h + 1) * 64])
                load("pool", sw[:, :, 128:256], win[:, :, 512 + h * 128:512 + (h + 1) * 128])
                load("pool", sw[:, :, 256:384], win[:, :, 1024 + h * 128:1024 + (h + 1) * 128])
                stop_at("g2a")
                for (a, n) in seg512(L):
                    ps = fw.psum()
                    for kc in range(8):
                        mm(ps[0:64, 0:n], sw[:, kc, 0:64], UT[:, kc, c0 + a:c0 + a + n], start=(kc == 0), stop=(kc == 7))
                    act(qT[:, a:a + n], ps[0:64, 0:n], AF.Identity, scale=0.125)
                    stop_at("g2b")
                    ps = fw.psum()
                    for kc in range(8):
                        mm(ps[0:64, 0:n], sw[:, kc, 64:128], UT[:, kc, c0 + a:c0 + a + n], start=(kc == 0), stop=(kc == 7))
                    cp("dve", kT[:, a:a + n], ps[0:64, 0:n])
                    stop_at("g2c")
                for t in range(nt):
                    ps = fw.psum()
                    for kc in range(8):
                        mm(ps[:, 0:128], UT[:, kc, c0 + t * 128:c0 + (t + 1) * 128], sw[:, kc, 128:256], start=(kc == 0), stop=(kc == 7))
                    cp("dve", v_tok[:, t, :], ps[:, 0:128])
                    stop_at("g2d")
                    ps = fw.psum()
                    for kc in range(8):
                        mm(ps[:, 0:128], UT[:, kc, c0 + t * 128:c0 + (t + 1) * 128], sw[:, kc, 256:384], start=(kc == 0), stop=(kc == 7))
                    act(r_tok[:, t, :], ps[:, 0:128], AF.Silu)
                    stop_at("g2e")
                stop_at("g3")
                for z in range(2):
                    S = S_f[0:64, z, :]
                    Sb = S_b[0:64, z, :]
                    if pass_id == 0:
                        memset("pool", S, 0.0)
                    else:
                        load("sp", S, sgla_d[(e_ * 2 + z) * 4 + h])
                    cp("act", Sb, S)
                    blocks = [(b0, min(8, nt - b0)) for b0 in range(0, nt, 8)]
                    if z == 1:
                        blocks = blocks[::-1]
                    for (b0, nb) in blocks:
                        n = nb * 128
                        for (a, m) in seg512(n):
                            ps = fw.psum()
                            mm(ps[0:64, 0:m], wg[z * 32:z * 32 + 16, h * 64:(h + 1) * 64],
                               lrT[z * 32:z * 32 + 16, b0 * 128 + a:b0 * 128 + a + m])
                            act(tmp[:, a:a + m], ps[0:64, 0:m], AF.Exp, bias=small[0:64, 80 + z * 4 + h:81 + z * 4 + h], scale=-1.0)
                        act(tmp[:, 0:n], tmp[:, 0:n], AF.Ln, bias=1.0)
                        for c in range(nb):
                            cs = slice(c * 128, (c + 1) * 128)
                            scan(cpos[:, cs], ones_f[0:64, :], tmp[:, cs])
                        if z == 1:
                            tt("dve", tmp[:, 0:n], tmp[:, 0:n], cpos[:, 0:n], ALU.subtract)
                            cp("dve", totc[:, 0:nb], cpos[:, 0:n].rr("p (c i) -> p c i", i=128)[:, :, 127])
                            for c in range(nb):
                                cs = slice(c * 128, (c + 1) * 128)
                                ts("dve", cpos[:, cs], tmp[:, cs], totc[:, c:c + 1], ALU.add)
                        lastc = (lambda c: c * 128 + 127) if z == 0 else (lambda c: c * 128)
                        act(tmp[:, 0:n], cpos[:, 0:n], AF.Exp, scale=-1.0 / 16)
                        tt("dve", qe[:, 0:n], qT[:, b0 * 128:b0 * 128 + n], tmp[:, 0:n], ALU.mult)
                        tmp3 = tmp[:, 0:n].rr("p (c i) -> p c i", i=128)
                        cp("dve", dcol[:, 0:nb], tmp3[:, :, 127 if z == 0 else 0])
                        act(tmp[:, 0:n], cpos[:, 0:n], AF.Exp, scale=1.0 / 16)
                        tt("pool", ke[:, 0:n], kT[:, b0 * 128:b0 * 128 + n], tmp[:, 0:n], ALU.mult)
                        for c in range(nb):
                            cs = slice(c * 128, (c + 1) * 128)
                            ts("dve", tmp[:, cs], cpos[:, cs], cpos[:, lastc(c):lastc(c) + 1], ALU.subtract)
                        act(tmp[:, 0:n], tmp[:, 0:n], AF.Exp, scale=1.0 / 16)
                        tt("pool", kd[:, 0:n], kT[:, b0 * 128:b0 * 128 + n], tmp[:, 0:n], ALU.mult)
                        psk = fw.psum(dt=BF16)
                        for c in range(nb):
                            tr(psk[:, c * 64:(c + 1) * 64], kd[:, c * 128:(c + 1) * 128], ident_b[0:64, 0:64])
                        cp("act", kd_tok[:, 0:nb, :].rr("p c d -> p (c d)"), psk[:, 0:nb * 64])
                        stop_at("g4")
                        corder = range(nb) if z == 0 else range(nb - 1, -1, -1)
                        for c in corder:
                            t = b0 + c
                            cs = slice(c * 128, (c + 1) * 128)
                            ps = fw.psum()
                            mm(ps[:, 0:128], ke[:, cs], qe[:, cs])
                            AT = ATb[c % 2]
                            tt("dve", AT, ps[:, 0:128], MU if z == 0 else ML, ALU.mult)
                            ps2 = fw.psum()
                            mm(ps2[:, 0:128], AT, v_tok[:, t, :], start=True, stop=False)
                            mm(ps2[:, 0:128], qe[:, cs], Sb, start=False, stop=True)
                            ps3 = fw.psum()
                            mm(ps3[0:64, 0:128], kd_tok[:, c, :], v_tok[:, t, :])
                            stt("dve", S, S, dcol[:, c:c + 1], ps3[0:64, 0:128], ALU.mult, ALU.add)
                            cp("act", Sb, S)
                            if z == 0:
                                cp("act", o_f[:, t, :], ps2[:, 0:128])
                            else:
                                tt("dve", osum, ps2[:, 0:128], o_f[:, t, :], ALU.add)
                                act(junk[:, 0:128], osum, AF.Square, accum=ost[:, 0:1])
                                ts("dve", ost[:, 1:2], ost[:, 0:1], 1.0 / 128, ALU.mult, EPS, ALU.add)
                                act(ost[:, 1:2], ost[:, 1:2], AF.Sqrt)
                                recip(ost[:, 2:3], ost[:, 1:2])
                                stt("dve", osum, osum, ost[:, 2:3], normB, ALU.mult, ALU.mult)
                                tt("pool", ogb, osum, r_tok[:, t, :], ALU.mult)
                                pst = fw.psum(dt=BF16)
                                tr(pst[:, 0:128], ogb, ident_b)
                                cp("act", oTb, pst[:, 0:128])
                                psA = fw.psum()
                                psB = fw.psum()
                                mm(psA, oTb, wo_h[:, h, 0:512])
                                mm(psB, oTb, wo_h[:, h, 512:1024])
                                xacc(t0 + t, psA, psB)
                            stop_at("g5")
                        stop_at("g6")
                    if pass_id == 0:
                        s_glob = si
                        load("sp", ngla_d[((s_glob * 2 + e_) * 2 + z) * 4 + h], S)
        wo_p = WO.rr("p (c n) -> p c n", c=4)
        load("pool", wo_p, wout[:, 4:8, :])
        for c in range(4):
            tt("dve" if c % 2 else "pool", wo_p[:, c, :], wo_p[:, c, :], GB, ALU.mult)
        swp = wslot(2).rr("p (k n) -> p k n", k=8)
        load("pool", swp, win[:, :, 1568:2080])
        for si, (t0, nt) in enumerate(seqs):
            L = nt * 128
            c0 = t0 * 128
            pm = TR[:, 0:8192].rr("p (g n) -> p g n", g=4)
            xpad = TRF[:, 4096:4096 + 2080]
            sA = TRF[:, 6176:6176 + 2080]
            sB = TRF[:, 8256:8256 + 2080]
            pooled = TR[:, 20672:20672 + 2048]
            assert 20672 + 2048 <= 23552 and (8256 + 2080) * 2 <= 20672
            for gi, w in enumerate(POOLW):
                lo = w // 2
                hi = w - 1 - lo
                memset("pool", xpad[:, 0:16], 0.0)
                memset("pool", xpad[:, 16 + L:32 + L], 0.0)
                for (a, n) in seg512(L):
                    ps = fw.psum()
                    for kc in range(8):
                        mm(ps[:, 0:n], swp[:, kc, gi * 128:(gi + 1) * 128], UT[:, kc, c0 + a:c0 + a + n], start=(kc == 0), stop=(kc == 7))
                    cp("act", xpad[:, 16 + a:16 + a + n], ps[:, 0:n])
                src = xpad
                k = 1
                bufs = [sA, sB]
                bi = 0
                while k < w:
                    dst = bufs[bi]
                    bi ^= 1
                    n = L + 32 - 2 * k
                    tt("dve", dst[:, 0:n], src[:, 0:n], src[:, k:k + n], ALU.add)
                    src = dst
                    k *= 2
                wsum = src[:, 16 - lo:16 - lo + L]
                tt("dve", wsum[:, 0:lo], wsum[:, 0:lo], efix[:, gi, 0:lo], ALU.mult)
                if hi > 0:
                    tt("dve", wsum[:, L - hi:L], wsum[:, L - hi:L], efix[:, gi, 8:8 + hi], ALU.mult)
                stt("dve", pooled[:, 0:L], wsum, 1.0 / w, xpad[:, 16:16 + L], ALU.mult, ALU.subtract)
                for (a, n) in seg512(L):
                    ps = fw.psum()
                    mm(ps[:, 0:n], bproj[:, gi, :], pooled[:, a:a + n])
                    act(pm[:, gi, a:a + n], ps[:, 0:n], AF.Identity, scale=small[:, 96 + gi:97 + gi])
            for t in range(nt):
                psA = fw.psum()
                psB = fw.psum()
                for gi in range(4):
                    mm(psA, pm[:, gi, t * 128:(t + 1) * 128], wo_p[:, gi, 0:512], start=(gi == 0), stop=(gi == 3))
                for gi in range(4):
                    mm(psB, pm[:, gi, t * 128:(t + 1) * 128], wo_p[:, gi, 512:1024], start=(gi == 0), stop=(gi == 3))
                xacc(t0 + t, psA, psB)

    def dn_mixer(l, pass_id, T, seqs, cond):
        o_ = l // 2
        win = c_w_in_d[o_].rearrange("(k p) n -> p k n", p=128)
        wout = c_w_out_d[o_].rearrange("(c p) n -> p c n", p=128)
        UT = ARENA[:, 0:16384].rr("p (k n) -> p k n", k=8)
        TR = ARENA[:, 16384:39936]
        TRF = ARENA_F[:, 8192:19968]
        make_ut(UT, l, 0, cond, list(range(T)), 0)
        gbcast(l, 16, cond)
        memset("pool", wsm, 0.0)
        for i, off in enumerate((0, 32, 64, 96)):
            load("pool", wsm[:, :, off:off + 8], win[:, :, 4096 + i * 8:4096 + (i + 1) * 8])
        load("sp", normB, c_norm_d[o_:o_ + 1, :].to_broadcast([128, 128]))
        memset("pool", small[0:40, 104:106], 0.0)
        for z in range(2):
            load("sp", small[z * 32:z * 32 + 8, 104:105], c_a_log_d[o_, z].rearrange("(h o) -> h o", o=1))
            load("sp", small[z * 32:z * 32 + 8, 105:106], c_dt_bias_d[o_, z].rearrange("(h o) -> h o", o=1))
        act(small[0:40, 104:105], small[0:40, 104:105], AF.Exp)
        cr = TRF[0:72, 0:128]
        load("sp", cr, c_conv_d[o_].rearrange("k (c p) -> (k c) p", p=128))
        ps = fw.psum()
        tr(ps[:, 0:72], cr, ident_f[0:72, 0:72])
        cp("act", convc[:, 0:72], ps[:, 0:72])
        o_f = LNB.rr("p a d -> p (a d)").rr("p (t v) -> p t v", v=128)
        wo = [None, None]

        for si, (t0, nt) in enumerate(seqs):
            L = nt * 128
            c0 = t0 * 128
            R1 = TRF[0:128, 0:2048]
            R2 = TRF[0:64, 2048:4096]
            bo = 8192
            hpre = TR[:, bo:bo + 2050]
            qT = TR[:, bo + 2056:bo + 2056 + 2048]
            kT = TR[:, bo + 4104:bo + 4104 + 2048]
            k_tok = TR[:, bo + 6152:bo + 6152 + 2048].rr("p (t v) -> p t v", v=128)
            v_tok = TR[:, bo + 8200:bo + 8200 + 2048].rr("p (t v) -> p t v", v=128)
            z_tok = TR[:, bo + 10248:bo + 10248 + 2048].rr("p (t v) -> p t v", v=128)
            so = bo + 12296
            sb_ = [TR[:, so + i * 128:so + (i + 1) * 128] for i in range(16)]
            fo = (so + 16 * 128 + 1) // 2 + 4
            f_ = [TRF[:, fo + i * 128:fo + (i + 1) * 128] for i in range(3)]
            cols = small[:, 128:232]
            cold = small[:, 232:272]
            ccol = small[:, 272:288]
            ost = small[:, 288:296]
            totr = small[0:40, 296:312]
            assert fo + 384 <= 11776, fo
            memset("pool", R1[:, 0:L], 0.0)
            memset("pool", R2[:, 0:L], 0.0)
            for (a, n) in seg512(L):
                ps = fw.psum()
                for kc in range(8):
                    mm(ps[0:128, 0:n], wsm[:, kc, 0:128], UT[:, kc, c0 + a:c0 + a + n], start=(kc == 0), stop=(kc == 7))
                act(R1[0:40, a:a + n], ps[0:40, 0:n], AF.Exp, bias=small[0:40, 105:106])
                act(R1[0:40, a:a + n], R1[0:40, a:a + n], AF.Ln, bias=1.0)
                ts("dve", R2[0:40, a:a + n], R1[0:40, a:a + n], small[0:40, 104:105], ALU.mult)
                act(R1[64:104, a:a + n], ps[64:104, 0:n], AF.Sigmoid)
            for c in range(nt):
                cs = slice(c * 128, (c + 1) * 128)
                scan(R1[0:40, cs], ones_f[0:40, :], R2[0:40, cs])
            tt("dve", R2[32:40, 0:L], R2[32:40, 0:L], R1[32:40, 0:L], ALU.subtract)
            cp("dve", totr[32:40, 0:nt], R1[32:40, 0:L].rr("p (c i) -> p c i", i=128)[:, :, 127])
            for c in range(nt):
                cs = slice(c * 128, (c + 1) * 128)
                ts("dve", R1[32:40, cs], R2[32:40, cs], totr[32:40, c:c + 1], ALU.add)
            for c in range(nt):
                cs = slice(c * 128, (c + 1) * 128)
                ts("dve", R2[0:8, cs], R1[0:8, cs], R1[0:8, c * 128 + 127:c * 128 + 128], ALU.subtract)
                ts("dve", R2[32:40, cs], R1[32:40, cs], R1[32:40, c * 128:c * 128 + 1], ALU.subtract)
            if l == 1 and si == 0:
                dbg("R1_%d" % pass_id, R1[0:104, 0:256], [104, 256])
                dbg("R2_%d" % pass_id, R2[0:40, 0:256], [40, 256])

            for h in range(8):
                if h % 4 == 0:
                    wo_h = WO.rr("p (c n) -> p c n", c=4)
                    load("pool", wo_h, wout[:, h:h + 4, :])
                    for c in range(4):
                        tt("dve" if c % 2 else "pool", wo_h[:, c, :], wo_h[:, c, :], GB, ALU.mult)
                sw = wslot(2).rr("p (k n) -> p k n", k=8)
                for i in range(4):
                    load("pool", sw[:, :, i * 128:(i + 1) * 128], win[:, :, i * 1024 + h * 128:i * 1024 + (h + 1) * 128])
                memset("pool", hpre[:, 0:1], 0.0)
                memset("pool", hpre[:, L + 1:L + 2], 0.0)
                for i, dstT in enumerate((qT, kT, None)):
                    for (a, n) in seg512(L):
                        ps = fw.psum()
                        for kc in range(8):
                            mm(ps[:, 0:n], sw[:, kc, i * 128:(i + 1) * 128], UT[:, kc, c0 + a:c0 + a + n], start=(kc == 0), stop=(kc == 7))
                        cp("act", hpre[:, 1 + a:1 + a + n], ps[:, 0:n])
                    cc = i * 8 + h
                    for (a, n) in seg512(L):
                        accv = junk.bitcast(F32)[:, 0:n]
                        ts("pool", accv, hpre[:, 1 + a:1 + a + n], convc[:, 24 + cc:25 + cc], ALU.mult)
                        stt("dve", accv, hpre[:, a:a + n], convc[:, cc:cc + 1], accv, ALU.mult, ALU.add)
                        stt("dve", accv, hpre[:, 2 + a:2 + a + n], convc[:, 48 + cc:49 + cc], accv, ALU.mult, ALU.add)
                        if dstT is not None:
                            act(dstT[:, a:a + n], accv, AF.Silu)
                            sqv = TR[:, so:so + 512]
                            tt("pool", sqv[:, 0:n], dstT[:, a:a + n], dstT[:, a:a + n], ALU.mult)
                            ps = fw.psum()
                            mm(ps[:, 0:n], ones_b, sqv[:, 0:n])
                            rn = junk.bitcast(F32)[:, 0:n]
                            ts("dve", rn, ps[:, 0:n], EPS, ALU.add)
                            act(rn, rn, AF.Sqrt)
                            recip(rn, rn)
                            if i == 0:
                                stt("dve", dstT[:, a:a + n], dstT[:, a:a + n], float(128 ** -0.5), rn, ALU.mult, ALU.mult)
                            else:
                                tt("dve", dstT[:, a:a + n], dstT[:, a:a + n], rn, ALU.mult)
                        else:
                            vT = TR[:, so:so + 512]
                            act(vT[:, 0:n], accv, AF.Silu)
                            for tq in range(n // 128):
                                pst = fw.psum(dt=BF16)
                                tr(pst[:, 0:128], vT[:, tq * 128:(tq + 1) * 128], ident_b)
                                cp("dve", v_tok[:, a // 128 + tq, :], pst[:, 0:128])
                for t in range(nt):
                    pst = fw.psum(dt=BF16)
                    tr(pst[:, 0:128], kT[:, t * 128:(t + 1) * 128], ident_b)
                    cp("act", k_tok[:, t, :], pst[:, 0:128])
                    ps = fw.psum()
                    for kc in range(8):
                        mm(ps[:, 0:128], UT[:, kc, c0 + t * 128:c0 + (t + 1) * 128], sw[:, kc, 384:512], start=(kc == 0), stop=(kc == 7))
                    act(z_tok[:, t, :], ps[:, 0:128], AF.Silu)
                if l == 1 and si == 0 and h == 0:
                    dbg("qT_%d" % pass_id, qT[:, 0:256], [128, 256])
                    dbg("kT_%d" % pass_id, kT[:, 0:256], [128, 256])
                    dbg("vtok_%d" % pass_id, v_tok[:, 0, :], [128, 128])

                for z in range(2):
                    S = S_f[:, z, :]
                    Sb = S_b[:, z, :]
                    if pass_id == 0:
                        memset("pool", S, 0.0)
                    else:
                        load("sp", S, sdn_d[(o_ * 2 + z) * 8 + h])
                    cp("act", Sb, S)
                    rb = z * 32 + h
                    fw.op("dve", lambda e, rb=rb: e.tensor_single_scalar(out=selc.ap, in_=pidx.ap, scalar=float(rb), op=ALU.is_equal),
                          reads=[pidx], writes=[selc])
                    Msk_s = MLs if z == 0 else MUs
                    Msk_i = ML if z == 0 else MU
                    corder = range(nt) if z == 0 else range(nt - 1, -1, -1)
                    for c in corder:
                        cs = slice(c * 128, (c + 1) * 128)
                        ps = fw.psum()
                        tr(ps[:, 0:128], R1[0:128, cs], ident_f)
                        cp("act", cols, ps[:, 0:104])
                        ps = fw.psum()
                        tr(ps[:, 0:64], R2[0:64, cs], ident_f[0:64, 0:64])
                        cp("dve", cold, ps[:, 0:40])
                        bcol = cols[:, rb:rb + 1]
                        betac = cols[:, 64 + rb:65 + rb]
                        act(ccol[:, 0:1], bcol, AF.Exp, scale=-1.0)
                        tt("dve", ccol[:, 1:2], ccol[:, 0:1], betac, ALU.mult)
                        ts("dve", ccol[:, 2:3], betac, -1.0, ALU.mult)
                        act(ccol[:, 3:4], cold[:, rb:rb + 1], AF.Exp)
                        psB = fw.psum()
                        mm(psB[:, 0:128], selc, R1[0:64, cs])
                        EB = f_[1]
                        act(EB, psB[:, 0:128], AF.Exp, scale=-1.0)
                        ld = f_[0]
                        fw.op("dve", lambda e, ld=ld, psB=psB, bcol=bcol: e.tensor_scalar(
                            out=ld.ap, in0=psB[:, 0:128].ap, scalar1=bcol.ap, scalar2=0.0, op0=ALU.subtract, op1=ALU.min),
                            reads=[psB[:, 0:128], bcol, EB], writes=[ld])
                        act(ld, ld, AF.Exp)
                        psG = fw.psum()
                        mm(psG[:, 0:128], kT[:, cs], kT[:, cs])
                        t1 = f_[2]
                        tt("dve", t1, psG[:, 0:128], ld, ALU.mult)
                        P0 = sb_[0]
                        stt("dve", P0, t1, ccol[:, 2:3], Msk_s, ALU.mult, ALU.mult)
                        P = sb_[1]
                        tt("pool", P, P0, BMK[:, 0, :], ALU.mult)
                        pst = fw.psum(dt=BF16)
                        tr(pst[:, 0:128], P, ident_b)
                        PT = sb_[2]
                        cp("act", PT, pst[:, 0:128])
                        TT = sb_[7]
                        tt("pool", TT, ident_b, PT, ALU.add)
                        for it in range(3):
                            Pn = sb_[3 + (it % 2) * 2]
                            PTn = sb_[4 + (it % 2) * 2]
                            ps1 = fw.psum()
                            mm(ps1[:, 0:128], PT, P)
                            cp("act", Pn, ps1[:, 0:128])
                            if it < 2:
                                ps2 = fw.psum()
                                mm(ps2[:, 0:128], P, PT)
                                cp("dve", PTn, ps2[:, 0:128])
                            ps3 = fw.psum()
                            mm(ps3[:, 0:128], Pn, TT)
                            TTn = sb_[8 - (it % 2)]
                            tt("dve", TTn, ps3[:, 0:128], TT, ALU.add)
                            P, PT, TT = Pn, PTn, TTn
                        for lv in range(3):
                            pst = fw.psum(dt=BF16)
                            tr(pst[:, 0:128], TT, ident_b)
                            Tn = sb_[3]
                            cp("act", Tn, pst[:, 0:128])
                            Bk = sb_[4]
                            tt("pool", Bk, P0, BMK[:, 1 + lv, :], ALU.mult)
                            psz = fw.psum()
                            mm(psz[:, 0:128], Bk, TT)
                            Zb = sb_[5]
                            cp("dve", Zb, psz[:, 0:128])
                            psw = fw.psum()
                            mm(psw[:, 0:128], Tn, Zb)
                            TTn = sb_[7] if TT is sb_[8] else sb_[8]
                            tt("dve", TTn, psw[:, 0:128], TT, ALU.add)
                            TT = TTn
                        psQ = fw.psum()
                        mm(psQ[:, 0:128], qT[:, cs], kT[:, cs])
                        tt("dve", t1, psQ[:, 0:128], ld, ALU.mult)
                        aq = sb_[9]
                        tt("pool", aq, t1, Msk_i, ALU.mult)
                        pst = fw.psum(dt=BF16)
                        tr(pst[:, 0:128], aq, ident_b)
                        aqT = sb_[10]
                        cp("act", aqT, pst[:, 0:128])
                        qdT = sb_[11]
                        tt("pool", qdT, qT[:, cs], EB, ALU.mult)
                        kbe = sb_[12]
                        vb = sb_[13]
                        kdk = sb_[14]
                        ts("dve", kbe, k_tok[:, c, :], ccol[:, 1:2], ALU.mult)
                        ts("pool", vb, v_tok[:, c, :], betac, ALU.mult)
                        ts("pool", kdk, k_tok[:, c, :], ccol[:, 3:4], ALU.mult)
                        psU = fw.psum()
                        mm(psU[:, 0:128], TT, vb)
                        u = f_[2]
                        cp("act", u, psU[:, 0:128])
                        psW = fw.psum()
                        mm(psW[:, 0:128], kbe, TT)
                        wT = sb_[15]
                        cp("dve", wT, psW[:, 0:128])
                        psS = fw.psum()
                        mm(psS[:, 0:128], wT, Sb)
                        vnew = sb_[9]
                        tt("dve", vnew, u, psS[:, 0:128], ALU.subtract)
                        psO = fw.psum()
                        mm(psO[:, 0:128], qdT, Sb, start=True, stop=False)
                        mm(psO[:, 0:128], aqT, vnew, start=False, stop=True)
                        psN = fw.psum()
                        mm(psN[:, 0:128], kdk, vnew)
                        dec = EB[:, 127:128] if z == 0 else EB[:, 0:1]
                        stt("dve", S, S, dec, psN[:, 0:128], ALU.mult, ALU.add)
                        cp("act", Sb, S)
                        t = c
                        if z == 0:
                            cp("act", o_f[:, t, :], psO[:, 0:128])
                        else:
                            osum = f_[0]
                            tt("dve", osum, psO[:, 0:128], o_f[:, t, :], ALU.add)
                            act(junk[:, 0:128], osum, AF.Square, accum=ost[:, 0:1])
                            ts("dve", ost[:, 1:2], ost[:, 0:1], 1.0 / 128, ALU.mult, EPS, ALU.add)
                            act(ost[:, 1:2], ost[:, 1:2], AF.Sqrt)
                            recip(ost[:, 2:3], ost[:, 1:2])
                            stt("dve", osum, osum, ost[:, 2:3], normB, ALU.mult, ALU.mult)
                            ogb = sb_[10]
                            tt("pool", ogb, osum, z_tok[:, t, :], ALU.mult)
                            pst = fw.psum(dt=BF16)
                            tr(pst[:, 0:128], ogb, ident_b)
                            oTb = sb_[11]
                            cp("act", oTb, pst[:, 0:128])
                            psA = fw.psum()
                            psBk = fw.psum()
                            mm(psA, oTb, wo_h[:, h % 4, 0:512])
                            mm(psBk, oTb, wo_h[:, h % 4, 512:1024])
                            xacc(t0 + t, psA, psBk)
                    if pass_id == 0:
                        load("sp", ndn_d[((si * 2 + o_) * 2 + z) * 8 + h], S)

    passes = [(0, 8, [(0, 2), (2, 2), (4, 2), (6, 2)], 0), (1, 16, [(0, 16)], 1)]

    def _main():
        for (pass_id, T, seqs, cond) in passes:
            load_x(pass_id, T)
            if pass_id == 1:
                dbg("x0s", X[:, 0, :], [128, D])
            for l in range(NL):
                if l % 2 == 0:
                    gla_mixer(l, pass_id, T, seqs, cond)
                else:
                    dn_mixer(l, pass_id, T, seqs, cond)
                dbg("xmix%d_%d" % (l, pass_id), X[:, 0, :], [128, D])
                if stop == "mix%d_%d" % (l, pass_id):
                    raise _Stop()
                layernorm(l, 0, T, False)
                dbg("xln%d_%d" % (l, pass_id), X[:, 0, :], [128, D])
                stop_at("ln%d_%d" % (l, pass_id))
                ffn(l, pass_id, T, cond)
                dbg("xffn%d_%d" % (l, pass_id), X[:, 0, :], [128, D])
                stop_at("ffn%d_%d" % (l, pass_id))
                layernorm(l, 1, T, l == NL - 1)
                dbg("xl%d_%d" % (l, pass_id), X[:, 0, :], [128, D])
                if stop == "l%d_%d" % (l, pass_id):
                    raise _Stop()
            dst = yp_d if pass_id == 0 else ys_d
            for t in range(T):
                load("sp", dst[t * 128:(t + 1) * 128, :], X[:, t, :])

    try:
        _main()
    except _Stop:
        pass
    fw.emit()
    return nc, fw, dbg_d


_CACHE = {}


def _inputs_per_core(inp, core):
    b = core % 2
    m = {}
    m["xp"] = np.ascontiguousarray(inp["x_prompt"][core * 4:(core + 1) * 4].reshape(1024, D))
    m["xs"] = np.ascontiguousarray(inp["x_sample"][b])
    m["cond2"] = np.ascontiguousarray(np.stack([inp["c_ctx"], inp["c"][b]], 0))
    m["sgla"] = np.ascontiguousarray(inp["state_gla"][b].reshape(16, 64, 128))
    m["sdn"] = np.ascontiguousarray(inp["state_dn"][b].reshape(32, 128, 128))
    for k in ("w_mod", "b_mod", "ln1_g", "ln1_b", "ln2_g", "ln2_b", "a_w_in", "a_w_gate", "a_b_gate", "a_norm",
              "b_proj", "b_scale", "a_w_out", "c_w_in", "c_conv", "c_a_log", "c_dt_bias", "c_norm", "c_w_out",
              "f_w_up", "f_conv", "f_w_down"):
        m[k] = np.ascontiguousarray(inp[k])
    return m


def kernel(**inputs):
    inp = {k: np.asarray(v, dtype=np.float32) for k, v in inputs.items()}
    if "nc" not in _CACHE:
        _CACHE["nc"] = build_program()[0]
    nc = _CACHE["nc"]
    n = 8
    in_maps = [_inputs_per_core(inp, c) for c in range(n)]
    res = run_bass_kernel_spmd(nc, in_maps, core_ids=list(range(n)))
    R = res.results
    y_prompt = np.concatenate([R[c]["yp"].reshape(4, 256, D) for c in range(n)], 0)
    y_sample = np.stack([R[0]["ys"], R[1]["ys"]], 0)
    ngla = np.concatenate([R[c]["ngla"].reshape(4, 2, 2, 4, 64, 128) for c in range(n)], 0)
    ndn = np.concatenate([R[c]["ndn"].reshape(4, 2, 2, 8, 128, 128) for c in range(n)], 0)
    return (y_prompt.astype(np.float32), y_sample.astype(np.float32), ngla.astype(np.float32), ndn.astype(np.float32))
```

```python
import math
import numpy as np
from concourse.bass_utils import run_bass_kernel_spmd
import concourse.bass as bass
import concourse.mybir as mybir

F32 = mybir.dt.float32
BF16 = mybir.dt.bfloat16
I32 = mybir.dt.int32
AF = mybir.ActivationFunctionType
ALU = mybir.AluOpType
AX = mybir.AxisListType

CELL = 256
_DT_SIZE = {F32: 4, BF16: 2, I32: 4}


class Region:
    def __init__(self, fw, name, handle, nbytes, cell=CELL):
        self.fw = fw
        self.name = name
        self.h = handle
        self.cell = cell
        self.ncell = (nbytes + cell - 1) // cell
        self.w = [None] * self.ncell
        self.r = [dict() for _ in range(self.ncell)]


class V:
    def __init__(self, region, ap):
        self.region = region
        self.ap = ap
        self._cells = None

    def __getitem__(self, key):
        return V(self.region, self.ap[key])

    def rr(self, pattern_, **kw):
        return V(self.region, self.ap.rearrange(pattern_, **kw))

    def bitcast(self, dt):
        return V(self.region, self.ap.bitcast(dt))

    def bc(self, shape):
        return V(self.region, self.ap.to_broadcast(shape))

    @property
    def shape(self):
        return self.ap.shape

    def cells(self):
        if self._cells is None:
            ap = self.ap
            esz = _DT_SIZE[ap.dtype]
            dims = list(ap.ap)[1:]
            base = int(ap.offset) if not isinstance(ap.offset, int) else ap.offset
            pstep = list(ap.ap)[0][0]
            if pstep > 0:
                base = base % pstep
            base_b = base * esz
            cs = set()
            CELL = self.region.cell
            dims = [(s, n) for (s, n) in dims if n > 1 or True]
            if not dims:
                dims = [(1, 1)]
            *outer, (ls, ln) = dims
            if ls in (0, 1):
                run = (esz * (ln if ls == 1 else 1))
                inner_iter = [0]
            else:
                run = esz
                inner_iter = [i * ls * esz for i in range(ln)]
            offs = [0]
            for (s, n) in outer:
                if s == 0:
                    continue
                offs = [o + i * s * esz for o in offs for i in range(n)]
            for o in offs:
                for ii in inner_iter:
                    a = base_b + o + ii
                    for c in range(a // CELL, (a + run - 1) // CELL + 1):
                        cs.add(c)
            self._cells = sorted(cs)
            assert self._cells[-1] < self.region.ncell, (self.region.name, self._cells[-1], self.region.ncell, ap)
        return self._cells


class Op:
    __slots__ = ("eng", "fn", "waits", "signal", "semval", "dma_sem", "is_dma")

    def __init__(self, eng, fn):
        self.eng = eng
        self.fn = fn
        self.waits = []
        self.signal = False
        self.semval = None
        self.dma_sem = None
        self.is_dma = False


ENGS = ("pe", "dve", "act", "pool", "sp")
N_DMA_SEMS = 12


class FW:
    def __init__(self, nc):
        self.nc = nc
        self.ops = {e: [] for e in ENGS}
        self.regions = []
        self.dma_rr = {"sp": 0, "pool": 0, "act": 0}
        self.dma_last = {}
        self._ctx = []
        self.psum_ptr = 0
        self.nops = 0

    def sbuf(self, name, shape, dt):
        g = self.nc.sbuf_tensor(name, list(shape), dt)
        h = g.__enter__()
        self._ctx.append(g)
        nb = int(np.prod(shape[1:])) * _DT_SIZE[dt]
        reg = Region(self, name, h, nb)
        self.regions.append(reg)
        return V(reg, h[:] if hasattr(h, "__getitem__") else h.ap())

    def psum_banks(self):
        self.banks = []
        for i in range(8):
            g = self.nc.psum_tensor(f"psb{i}", [128, 512], F32)
            h = g.__enter__()
            self._ctx.append(g)
            reg = Region(self, f"psb{i}", h, 2048, cell=2048)
            self.banks.append(V(reg, h[:]))

    def psum(self, ncols=512, parts=128, dt=F32):
        b = self.psum_ptr
        self.psum_ptr = (b + 1) % 8
        bank = self.banks[b] if dt == F32 else self.banks[b].bitcast(dt)
        return bank[0:parts, 0:ncols]

    def _deps(self, op, reads, writes):
        deps = {}
        for v in reads:
            reg = v.region
            for c in v.cells():
                w = reg.w[c]
                if w is not None:
                    deps[id(w)] = w
        for v in writes:
            reg = v.region
            for c in v.cells():
                w = reg.w[c]
                if w is not None:
                    deps[id(w)] = w
                for t in reg.r[c].values():
                    deps[id(t)] = t
        for t in deps.values():
            if t is op:
                continue
            if t.eng == "pe" and op.eng == "pe" and not t.is_dma and not op.is_dma:
                continue
            op.waits.append(t)
            t.signal = True
        for v in reads:
            reg = v.region
            key = op.dma_sem if op.is_dma else op.eng
            for c in v.cells():
                reg.r[c][key] = op
        for v in writes:
            reg = v.region
            for c in v.cells():
                reg.w[c] = op
                reg.r[c] = {}

    def op(self, eng, fn, reads=(), writes=()):
        if getattr(self, "halted", False):
            return None
        o = Op(eng, fn)
        self._deps(o, reads, writes)
        self.ops[eng].append(o)
        self.nops += 1
        return o

    def dma(self, queue, out, in_, reads=(), writes=(), **kw):
        if getattr(self, "halted", False):
            return None
        o = Op(queue, None)
        o.is_dma = True
        o.signal = True
        oap = out.ap if isinstance(out, V) else out
        iap = in_.ap if isinstance(in_, V) else in_
        rd = list(reads) + ([in_] if isinstance(in_, V) else [])
        wr = list(writes) + ([out] if isinstance(out, V) else [])
        k = self.dma_rr[queue]
        self.dma_rr[queue] = (k + 1) % N_DMA_SEMS
        o.dma_sem = (queue, k)
        prev = self.dma_last.get((queue, k))
        self._deps(o, rd, wr)
        if prev is not None:
            o.waits.append(prev)
        self.dma_last[(queue, k)] = o
        o.fn = lambda e: e.dma_start(out=oap, in_=iap, **kw)
        self.ops[queue].append(o)
        self.nops += 1
        return o

    def emit(self):
        nc = self.nc
        sems = {}
        semctx = []
        for e in ENGS:
            g = nc.semaphore(f"s_{e}")
            sems[e] = g.__enter__()
            semctx.append(g)
        dsems = {}
        for q in ("sp", "pool"):
            for k in range(N_DMA_SEMS):
                g = nc.semaphore(f"d_{q}{k}")
                dsems[(q, k)] = g.__enter__()
                semctx.append(g)
        for e in ENGS:
            cnt = 0
            for o in self.ops[e]:
                if o.is_dma:
                    continue
                if o.signal:
                    cnt += 1
                    o.semval = cnt
            self.maxsem = max(getattr(self, "maxsem", 0), cnt)
        dcnt = {}
        for e in ENGS:
            for o in self.ops[e]:
                if o.is_dma:
                    dcnt[o.dma_sem] = dcnt.get(o.dma_sem, 0) + 16
                    o.semval = dcnt[o.dma_sem]

        def semof(t):
            return dsems[t.dma_sem] if t.is_dma else sems[t.eng]

        def run(engname, engobj):
            known = {}
            for o in self.ops[engname]:
                need = {}
                for t in o.waits:
                    s = t.dma_sem if t.is_dma else t.eng
                    if t.semval > need.get(s, (0, None))[0]:
                        need[s] = (t.semval, t)
                for s, (val, t) in need.items():
                    if known.get(s, 0) >= val:
                        continue
                    engobj.wait_ge(semof(t), val)
                    known[s] = val
                ins = o.fn(engobj)
                if o.is_dma:
                    ins.then_inc(dsems[o.dma_sem], 16)
                elif o.signal:
                    ins.then_inc(sems[engname], 1)
            if engname in ("sp", "pool"):
                for k in range(N_DMA_SEMS):
                    if dcnt.get((engname, k), 0) > 0:
                        engobj.wait_ge(dsems[(engname, k)], dcnt[(engname, k)])

        with nc.Block() as block:
            @block.tensor
            def _(e):
                run("pe", e)

            @block.vector
            def _(e):
                run("dve", e)

            @block.scalar
            def _(e):
                run("act", e)

            @block.gpsimd
            def _(e):
                run("pool", e)

            @block.sync
            def _(e):
                run("sp", e)
        for g in reversed(semctx):
            g.__exit__(None, None, None)
        for g in reversed(self._ctx):
            g.__exit__(None, None, None)

D = 1024
NL = 4
DFF = 2816
NCH = DFF // 128
ALPHA = float(8 ** 0.25)
EPS = 1e-6
KC = 8
A_IN = 2080
C_IN = 4128
PI = float(np.pi)

DBG_SPECS = {}


class _Stop(Exception):
    pass


def build_program(debug=(), stop=None):
    nc = bass.Bass("TRN2", target_bir_lowering=False)
    fw = FW(nc)

    def din(name, shape):
        return nc.dram_tensor(name, list(shape), F32, kind="ExternalInput").ap()

    def dout(name, shape):
        return nc.dram_tensor(name, list(shape), F32, kind="ExternalOutput").ap()

    xp_d = din("xp", [1024, D])
    xs_d = din("xs", [2048, D])
    cond_d = din("cond2", [2, D])
    sgla_d = din("sgla", [16, 64, 128])
    sdn_d = din("sdn", [32, 128, 128])
    w_mod_d = din("w_mod", [NL, D, 6 * D])
    b_mod_d = din("b_mod", [NL, 6 * D])
    ln_d = {k: din(k, [NL, D]) for k in ("ln1_g", "ln1_b", "ln2_g", "ln2_b")}
    a_w_in_d = din("a_w_in", [2, D, A_IN])
    a_w_gate_d = din("a_w_gate", [2, 2, 16, 256])
    a_b_gate_d = din("a_b_gate", [2, 2, 256])
    a_norm_d = din("a_norm", [2, 128])
    b_proj_d = din("b_proj", [2, 4, 128, 128])
    b_scale_d = din("b_scale", [2, 512])
    a_w_out_d = din("a_w_out", [2, D, D])
    c_w_in_d = din("c_w_in", [2, D, C_IN])
    c_conv_d = din("c_conv", [2, 3, 3072])
    c_a_log_d = din("c_a_log", [2, 2, 8])
    c_dt_bias_d = din("c_dt_bias", [2, 2, 8])
    c_norm_d = din("c_norm", [2, 128])
    c_w_out_d = din("c_w_out", [2, D, D])
    f_w_up_d = din("f_w_up", [NL, D, 2 * DFF])
    f_conv_d = din("f_conv", [NL, 3, 2 * DFF])
    f_w_down_d = din("f_w_down", [NL, DFF, D])

    yp_d = dout("yp", [1024, D])
    ys_d = dout("ys", [2048, D])
    ngla_d = dout("ngla", [64, 64, 128])
    ndn_d = dout("ndn", [128, 128, 128])
    dbg_d = {}

    def A(v):
        return v.ap if isinstance(v, V) else v

    def rds(*xs):
        return [x for x in xs if isinstance(x, V)]

    def mm(out, lhsT, rhs, start=True, stop=True):
        fw.op("pe", lambda e: e.matmul(out.ap, lhsT=lhsT.ap, rhs=rhs.ap, start=start, stop=stop),
              reads=[lhsT, rhs], writes=[out])

    def tr(out, in_, idn):
        fw.op("pe", lambda e: e.transpose(out=out.ap, in_=in_.ap, identity=idn.ap),
              reads=[in_, idn], writes=[out])

    def act(out, in_, func, bias=None, scale=None, accum=None):
        kw = {}
        if bias is not None:
            kw["bias"] = A(bias)
        if scale is not None:
            kw["scale"] = A(scale)
        if accum is not None:
            kw["accum_out"] = accum.ap
        fw.op("act", lambda e: e.activation(out=out.ap, in_=in_.ap, func=func, **kw),
              reads=rds(in_, bias, scale), writes=rds(out, accum))

    def ts(eng, out, in0, s1, op0, s2=None, op1=None):
        kw = {"op1": op1} if op1 is not None else {}
        fw.op(eng, lambda e: e.tensor_scalar(out=out.ap, in0=in0.ap, scalar1=A(s1),
                                             scalar2=(A(s2) if s2 is not None else None), op0=op0, **kw),
              reads=rds(in0, s1, s2), writes=[out])

    def tt(eng, out, in0, in1, op):
        fw.op(eng, lambda e: e.tensor_tensor(out=out.ap, in0=in0.ap, in1=in1.ap, op=op),
              reads=[in0, in1], writes=[out])

    def stt(eng, out, in0, scalar, in1, op0, op1):
        fw.op(eng, lambda e: e.scalar_tensor_tensor(out=out.ap, in0=in0.ap, scalar=A(scalar), in1=in1.ap,
                                                    op0=op0, op1=op1),
              reads=rds(in0, scalar, in1), writes=[out])

    def cp(eng, out, in_):
        if eng == "act":
            fw.op("act", lambda e: e.copy(out=out.ap, in_=in_.ap), reads=[in_], writes=[out])
        else:
            fw.op(eng, lambda e: e.tensor_copy(out=out.ap, in_=in_.ap), reads=[in_], writes=[out])

    def memset(eng, out, val):
        fw.op(eng, lambda e: e.memset(out.ap, val), writes=[out])

    def scan(out, d0, d1, init=0.0):
        fw.op("dve", lambda e: e.tensor_tensor_scan(out=out.ap, data0=d0.ap, data1=d1.ap, initial=init,
                                                    op0=ALU.mult, op1=ALU.add),
              reads=[d0, d1], writes=[out])

    def recip(out, in_):
        fw.op("dve", lambda e: e.reciprocal(out=out.ap, in_=in_.ap), reads=[in_], writes=[out])

    def rsum(out, in_):
        fw.op("dve", lambda e: e.reduce_sum(out=out.ap, in_=in_.ap, axis=AX.X), reads=[in_], writes=[out])

    def load(q, out, src, **kw):
        fw.dma(q, out, src, **kw)

    def dbg(name, v, shape):
        if name in debug:
            d = dout("dbg_" + name, list(shape))
            dbg_d[name] = d
            fw.dma("sp" if v.ap.dtype == F32 else "pool", d, v)

    def stop_at(name):
        if stop == name:
            fw.halted = True

    fw.marks = []

    def mark(name):
        fw.marks.append((name, len(fw.ops["dve"]), len(fw.ops["pe"])))

    rr = [0]

    def evac_eng():
        rr[0] ^= 1
        return "act" if rr[0] else "dve"

    fw.psum_banks()
    X = fw.sbuf("X", [128, 16, D], F32)
    ARENA = fw.sbuf("ARENA", [128, 44032], BF16)
    ARENA_F = ARENA.bitcast(F32)
    ARENA_I = ARENA.bitcast(I32)
    WR = fw.sbuf("WR", [128, 3, 4096], BF16)
    LNB = fw.sbuf("LNB", [128, 2, D], F32)
    GB = fw.sbuf("GB", [128, D], F32)
    ones_f = fw.sbuf("ones_f", [128, 128], F32)
    ident_f = fw.sbuf("ident_f", [128, 128], F32)
    ident_b = fw.sbuf("ident_b", [128, 128], BF16)
    ones_b = fw.sbuf("ones_b", [128, 128], BF16)
    MU = fw.sbuf("MU", [128, 128], F32)
    MUs = fw.sbuf("MUs", [128, 128], F32)
    ML = fw.sbuf("ML", [128, 128], F32)
    MLs = fw.sbuf("MLs", [128, 128], F32)
    pidx = fw.sbuf("pidx", [64, 128], F32)
    BMK = fw.sbuf("BMK", [128, 4, 128], BF16)
    selc = fw.sbuf("selc", [64, 256], F32)
    dncol = fw.sbuf("dncol", [128, 3, 160], F32)
    modcol = fw.sbuf("modcol", [128, NL, 48, 2], F32)
    mscale = fw.sbuf("mscale", [128, NL, 2, 8, 2], F32)
    scT = fw.sbuf("scT", [128, 8, 2], F32)
    small = fw.sbuf("small", [128, 512], F32)
    S_f = fw.sbuf("S_f", [128, 2, 128], F32)
    S_b = fw.sbuf("S_b", [128, 2, 128], BF16)
    wsm = fw.sbuf("wsm", [128, 8, 128], BF16)
    wg = fw.sbuf("wg", [48, 256], BF16)
    bproj = fw.sbuf("bproj", [128, 4, 128], BF16)
    convc = fw.sbuf("convc", [128, 160], F32)
    normB = fw.sbuf("normB", [128, 128], F32)
    junk = fw.sbuf("junk", [128, D], BF16)
    pe_c = ARENA_F[:, 0:512]
    freq = ARENA_F[:, 512:768]
    ptmp = ARENA_F[:, 768:1536].rr("p (a n) -> p a n", a=3)
    ptmpi = ARENA_I[:, 1536:1792]
    pcol = fw.sbuf("pcol", [128, 8], F32)
    dgs = fw.sbuf("dgs", [128, 128], F32)
    efix = fw.sbuf("efix", [128, 4, 16], F32)

    memset("pool", ones_f, 1.0)
    memset("pool", ones_b, 1.0)

    def asel(out, in_, pattern, cmp, cm, base=0):
        fw.op("pool", lambda e: e.affine_select(out=out.ap, in_=in_.ap, pattern=pattern, compare_op=cmp, fill=0.0,
                                                base=base, channel_multiplier=cm), reads=[in_], writes=[out])

    asel(ident_f, ones_f, [[-1, 128]], ALU.is_equal, 1)
    cp("dve", ident_b, ident_f)
    asel(MU, ones_f, [[1, 128]], ALU.is_ge, -1)
    asel(MUs, ones_f, [[1, 128]], ALU.is_gt, -1)
    asel(ML, ones_f, [[-1, 128]], ALU.is_ge, 1)
    asel(MLs, ones_f, [[-1, 128]], ALU.is_gt, 1)
    fw.op("pool", lambda e: e.iota(pidx.ap, [[0, 128]], base=0, channel_multiplier=1,
                                   allow_small_or_imprecise_dtypes=True), writes=[pidx])
    mdt = ptmp.rr("p a n -> p (a n)")
    for bi_, bsz in enumerate((16, 32, 64)):
        nb_ = 128 // bsz
        Eb = ptmp[0:8, 0, 0:128]
        asel(Eb, ones_f[0:8, :], [[1, 128]], ALU.is_ge, -bsz, base=0)
        asel(Eb, Eb, [[-1, 128]], ALU.is_ge, bsz, base=bsz - 1)
        ps = fw.psum()
        mm(ps[:, 0:128], Eb[0:nb_, :], Eb[0:nb_, :])
        cp("dve", mdt[:, 256 + bi_ * 128:256 + (bi_ + 1) * 128], ps[:, 0:128])
    md16, md32, md64 = (mdt[:, 256 + i * 128:256 + (i + 1) * 128] for i in range(3))
    cp("dve", BMK[:, 0, :], md16)
    tt("dve", BMK[:, 1, :], md32, md16, ALU.subtract)
    tt("dve", BMK[:, 2, :], md64, md32, ALU.subtract)
    ts("dve", BMK[:, 3, :], md64, -1.0, ALU.mult, 1.0, ALU.add)

    stop_at("c1")
    condt = ARENA_F[0:2, 0:1024]
    load("sp", condt, cond_d)
    act(condt, condt, AF.Silu)
    ps = fw.psum()
    for kc in range(8):
        tr(ps[:, kc * 2:(kc + 1) * 2], condt[:, kc * 128:(kc + 1) * 128], ident_f[0:2, 0:2])
    cp("dve", scT.rr("p k c -> p (k c)"), ps[:, 0:16])

    stop_at("c2")
    bmr = ARENA_F[0:48, 1024:1152]
    bmT = small[:, 0:48]
    wm_slots = [ARENA_F[:, 2048 + i * 4096: 2048 + (i + 1) * 4096].rr("p (k n) -> p k n", k=8) for i in range(2)]
    for l in range(NL):
        load("sp", bmr, b_mod_d[l].rearrange("(b p) -> b p", p=128))
        ps = fw.psum()
        tr(ps[:, 0:48], bmr, ident_f[0:48, 0:48])
        cp("act", bmT, ps[:, 0:48])
        for g in range(12):
            slot = wm_slots[g % 2]
            load("sp", slot, w_mod_d[l].rearrange("(k p) n -> p k n", p=128)[:, :, g * 512:(g + 1) * 512])
            ps = fw.psum()
            for b4 in range(4):
                for kc in range(8):
                    mm(ps[:, b4 * 2:(b4 + 1) * 2], slot[:, kc, b4 * 128:(b4 + 1) * 128], scT[:, kc, :],
                       start=(kc == 0), stop=(kc == 7))
            ps3 = ps[:, 0:8].rr("p (b c) -> p b c", c=2)
            for c in range(2):
                tt("dve", modcol[:, l, 4 * g:4 * g + 4, c], ps3[:, :, c], bmT[:, 4 * g:4 * g + 4], ALU.add)
        for w, blk0 in enumerate((8, 32)):
            ts("dve", mscale[:, l, w], modcol[:, l, blk0:blk0 + 8, :], 1.0, ALU.add, 1.0 / ALPHA, ALU.mult)
    dbg("modcol", modcol.rr("p l b c -> p (l b c)"), [128, NL * 96])

    stop_at("c3")
    POOLW = (2, 4, 8, 16)
    for gi, w in enumerate(POOLW):
        lo = w // 2
        hi = w - 1 - lo
        fw.op("pool", lambda e, gi=gi, lo=lo, hi=hi: e.iota(efix[:, gi, 0:lo].ap, [[1, lo]], base=hi + 1, channel_multiplier=0,
                                                           allow_small_or_imprecise_dtypes=True), writes=[efix[:, gi, 0:lo]])
        if hi > 0:
            fw.op("pool", lambda e, gi=gi, hi=hi, w=w: e.iota(efix[:, gi, 8:8 + hi].ap, [[-1, hi]], base=w - 1, channel_multiplier=0,
                                                              allow_small_or_imprecise_dtypes=True), writes=[efix[:, gi, 8:8 + hi]])
        for (a, n) in ((0, lo), (8, hi)):
            if n > 0:
                recip(efix[:, gi, a:a + n], efix[:, gi, a:a + n])
                ts("dve", efix[:, gi, a:a + n], efix[:, gi, a:a + n], float(w), ALU.mult)

    def sincos(out_sin, out_cos, theta):
        for out, shift in ((out_sin, 0.0), (out_cos, 0.25)):
            t = ptmp[:, 1, :]
            gq = ptmp[:, 2, :]
            ts("dve", t, theta, 1.0 / (2 * PI), ALU.mult, shift, ALU.add)
            cp("dve", ptmpi, t)
            tt("dve", t, t, ptmpi, ALU.subtract)
            fw.op("dve", lambda e, t=t, gq=gq: e.tensor_single_scalar(out=gq.ap, in_=t.ap, scalar=0.5, op=ALU.is_ge),
                  reads=[t], writes=[gq])
            tt("dve", t, t, gq, ALU.subtract)
            fw.op("dve", lambda e, t=t, gq=gq: e.tensor_single_scalar(out=gq.ap, in_=t.ap, scalar=-0.5, op=ALU.is_lt),
                  reads=[t], writes=[gq])
            tt("dve", t, t, gq, ALU.add)
            act(out, t, AF.Sin, scale=2 * PI)

    def pe_consts():
        fw.op("pool", lambda e: e.iota(freq.ap, [[1, 256]], base=0, channel_multiplier=0, allow_small_or_imprecise_dtypes=True),
              writes=[freq])
        act(freq, freq, AF.Exp, scale=-math.log(10000.0) / 256.0)
        fw.op("pool", lambda e: e.iota(pcol[:, 0:1].ap, [[0, 1]], base=0, channel_multiplier=1, allow_small_or_imprecise_dtypes=True),
              writes=[pcol[:, 0:1]])
        fw.op("dve", lambda e: e.tensor_single_scalar(out=pcol[:, 1:2].ap, in_=pcol[:, 0:1].ap, scalar=64.0, op=ALU.is_ge),
              reads=[pcol[:, 0:1]], writes=[pcol[:, 1:2]])
        stt("dve", pcol[:, 2:3], pcol[:, 1:2], -64.0, pcol[:, 0:1], ALU.mult, ALU.add)

        ts("dve", ptmp[:, 0, :], freq, pcol[:, 2:3], ALU.mult)
        sincos(pe_c[:, 0:256], pe_c[:, 256:512], ptmp[:, 0, :])


    stop_at("c5")
    def seg512(n):
        out = []
        a = 0
        while a < n:
            out.append((a, min(512, n - a)))
            a += 512
        return out

    def load_x(pass_id, T):
        src = xp_d if pass_id == 0 else xs_d
        if pass_id == 1:
            pe_consts()
        for t in range(T):
            load("sp", X[:, t, :], src[t * 128:(t + 1) * 128, :])
            if pass_id == 1:
                ts("dve", pcol[:, 3:4], pcol[:, 1:2], float(2 * t), ALU.add)
                ts("dve", ptmp[:, 0, :], freq, pcol[:, 3:4], ALU.mult)
                pe_r = junk.bitcast(F32)
                sincos(pe_r[:, 0:256], pe_r[:, 256:512], ptmp[:, 0, :])
                tt("dve", X[:, t, 0:512], X[:, t, 0:512], pe_r, ALU.add)
                tt("dve", X[:, t, 512:1024], X[:, t, 512:1024], pe_c, ALU.add)
            ts("pool", X[:, t, :], X[:, t, :], ALPHA, ALU.mult)
        stop_at("c6")

    def make_ut(UT, l, which, cond, tiles, col0, halo=None):
        shb = 0 if which == 0 else 24
        jobs = [(t, col0 + i * 128, None) for i, t in enumerate(tiles)]
        if halo is not None:
            jobs.append((halo[0], halo[2], halo[1]))
        for (t, c0, hc) in jobs:
            for half in range(2):
                ps = fw.psum()
                for q in range(4):
                    kc = half * 4 + q
                    tr(ps[:, q * 128:(q + 1) * 128], X[:, t, kc * 128:(kc + 1) * 128], ident_f)
                if which == 1:
                    stop_at("u1")
                eng_b = evac_eng()
                for q in range(4):
                    kc = half * 4 + q
                    sc = mscale[:, l, which, kc, cond:cond + 1]
                    sh = modcol[:, l, shb + kc, cond:cond + 1]
                    if hc is None:
                        src = ps[:, q * 128:(q + 1) * 128]
                        dst = UT[:, kc, c0:c0 + 128]
                    else:
                        src = ps[:, q * 128 + hc:q * 128 + hc + 1]
                        dst = UT[:, kc, c0:c0 + 1]
                    if eng_b == "act":
                        act(dst, src, AF.Identity, bias=sh, scale=sc)
                    else:
                        ts("dve", dst, src, sc, ALU.mult, sh, ALU.add)
                    if which == 1:
                        stop_at("u2")
                if which == 1:
                    stop_at("u3")
            if which == 1:
                stop_at("u4")
        if stop == "ut":
            dbg("ut", UT[:, 0, 0:1024], [128, 1024])
            for kc_ in range(8):
                dbg("ut%d" % kc_, UT[:, kc_, 0:256], [128, 256])
            dbg("x0", X[:, 0, :], [128, D])
            dbg("mscale", mscale.rr("p l w k c -> p (l w k c)"), [128, NL * 32])
            raise _Stop()

    def gbcast(l, blk0, cond):
        for half in range(2):
            ps = fw.psum()
            for q in range(4):
                kc = half * 4 + q
                dg = dgs
                ts("dve", dg, ident_f, modcol[:, l, blk0 + kc, cond:cond + 1], ALU.mult)
                mm(ps[:, q * 128:(q + 1) * 128], ones_f, dg)
            cp("act", GB[:, half * 512:(half + 1) * 512], ps)

    def layernorm(l, which, T, last):
        gname, bname = ("ln1_g", "ln1_b") if which == 0 else ("ln2_g", "ln2_b")
        load("sp", LNB[:, 0, :], ln_d[gname][l:l + 1, :].to_broadcast([128, D]))
        load("sp", LNB[:, 1, :], ln_d[bname][l:l + 1, :].to_broadcast([128, D]))
        stop_at("lna")
        if not last:
            ts("pool", LNB.rr("p a d -> p (a d)"), LNB.rr("p a d -> p (a d)"), ALPHA, ALU.mult)
        stop_at("lnb")
        st = small[:, 64:72]
        for t in range(T):
            xt = X[:, t, :]
            rsum(st[:, 0:1], xt)
            stop_at("lnc")
            ts("dve", st[:, 1:2], st[:, 0:1], -1.0 / D, ALU.mult)
            act(junk, xt, AF.Square, bias=st[:, 1:2], accum=st[:, 2:3])
            stop_at("lnd")
            ts("dve", st[:, 3:4], st[:, 2:3], 1.0 / D, ALU.mult, EPS, ALU.add)
            act(st[:, 3:4], st[:, 3:4], AF.Sqrt)
            recip(st[:, 4:5], st[:, 3:4])
            ts("dve", xt, xt, st[:, 1:2], ALU.add, st[:, 4:5], ALU.mult)
            tt("pool", xt, xt, LNB[:, 0, :], ALU.mult)
            tt("dve", xt, xt, LNB[:, 1, :], ALU.add)

    wr_i = [0]

    def wslot(nring=3):
        s = WR[:, wr_i[0] % nring, :]
        wr_i[0] += 1
        return s

    WO = WR[:, 2, :]

    def xacc(t, psA, psB):
        tt("dve", X[:, t, 0:512], X[:, t, 0:512], psA, ALU.add)
        tt("dve", X[:, t, 512:1024], X[:, t, 512:1024], psB, ALU.add)

    def ffn(l, pass_id, T, cond):
        cr = ARENA_F[0:44, 0:128]
        for k in range(3):
            load("sp", cr, f_conv_d[l][k].rearrange("(c p) -> c p", p=128))
            ps = fw.psum()
            tr(ps[:, 0:44], cr, ident_f[0:44, 0:44])
            cp("act", convc[:, k * 44:(k + 1) * 44], ps[:, 0:44])
        stop_at("fa")
        gbcast(l, 40, cond)
        stop_at("fb")
        UTs = ARENA[:, 0:8224].rr("p (k n) -> p k n", k=8)
        actT = ARENA[:, 8224:8224 + 22528].rr("p (j n) -> p j n", j=NCH)
        o0 = 8224 + 22528
        hpre = [[ARENA[:, o0 + (2 * i + h) * 1032: o0 + (2 * i + h) * 1032 + 1028] for h in range(2)] for i in range(2)]
        o1 = (o0 + 4 * 1032 + 1) // 2 + 8
        accs = [[ARENA_F[:, o1 + (2 * i + h) * 1024: o1 + (2 * i + h + 1) * 1024] for h in range(2)] for i in range(2)]
        assert (o1 + 4096) <= 22016
        groups = [(0, 8, None)] if pass_id == 0 else [(0, 8, "R"), (8, 8, "L")]
        wup = f_w_up_d[l].rearrange("(k p) n -> p k n", p=128)
        wdn = f_w_down_d[l].rearrange("(c p) n -> p c n", p=128)
        for (t0, nt, hal) in groups:
            ntok = nt * 128
            halo = None
            if hal == "R":
                halo = (t0 + nt, 0, 1026)
            elif hal == "L":
                halo = (t0 - 1, 127, 1)
            make_ut(UTs, l, 1, cond, list(range(t0, t0 + nt)), 2, halo)
            stop_at("f0")
            for i in range(2):
                for h in range(2):
                    if hal != "L":
                        memset("pool", hpre[i][h][:, 0:2], 0.0)
                    if hal != "R":
                        memset("pool", hpre[i][h][:, 1026:1028], 0.0)
            segs = [(2, 512), (514, 512)]
            if hal == "R":
                segs.append((1026, 1))
            if hal == "L":
                segs.append((1, 1))
            jgs = [(j0, min(4, NCH - j0)) for j0 in range(0, NCH, 4)]
            for (j0, nj) in jgs:
                sa = wslot()[:, 0:8 * 128 * nj].rr("p (k n) -> p k n", k=8)
                sg = wslot()[:, 0:8 * 128 * nj].rr("p (k n) -> p k n", k=8)
                load("pool", sa, wup[:, :, j0 * 128:(j0 + nj) * 128])
                load("pool", sg, wup[:, :, DFF + j0 * 128:DFF + (j0 + nj) * 128])
                for jj in range(nj):
                    j = j0 + jj
                    hp = hpre[j % 2]
                    acc = accs[j % 2]
                    for h, sw in enumerate((sa, sg)):
                        cc = h * 22 + j
                        w0 = convc[:, cc:cc + 1]
                        w1 = convc[:, 44 + cc:44 + cc + 1]
                        w2 = convc[:, 88 + cc:88 + cc + 1]
                        for (c0, n) in segs:
                            ps = fw.psum()
                            for kc in range(8):
                                mm(ps[:, 0:n], sw[:, kc, jj * 128:(jj + 1) * 128], UTs[:, kc, c0:c0 + n],
                                   start=(kc == 0), stop=(kc == 7))
                            if n > 1:
                                cp("act", hp[h][:, c0:c0 + n], ps[:, 0:n])
                                act(acc[h][:, c0 - 2:c0 - 2 + n], ps[:, 0:n], AF.Identity, scale=w1)
                            else:
                                cp("act", hp[h][:, c0:c0 + n], ps[:, 0:n])
                        hh = hp[h]
                        if pass_id == 0:
                            a3 = acc[h].rr("p (s t) -> p s t", s=4)
                            h3 = hh[:, 2:1026].rr("p (s t) -> p s t", s=4)
                            stt("dve", a3[:, :, 1:256], h3[:, :, 0:255], w0, a3[:, :, 1:256], ALU.mult, ALU.add)
                            stt("dve", a3[:, :, 0:255], h3[:, :, 1:256], w2, a3[:, :, 0:255], ALU.mult, ALU.add)
                        else:
                            stt("dve", acc[h], hh[:, 1:1025], w0, acc[h], ALU.mult, ALU.add)
                            stt("dve", acc[h], hh[:, 3:1027], w2, acc[h], ALU.mult, ALU.add)
                    act(acc[1], acc[1], AF.Silu)
                    tt("dve", actT[:, j, :], acc[1], acc[0], ALU.mult)
                    stop_at("f0b")
            if l == 0 and t0 == 0:
                dbg("actT%d" % pass_id, actT[:, 0, :], [128, 1024])
            stop_at("f1")
            for q0 in range(0, nt, 4):
                for (j0, nj) in jgs:
                    sd = wslot()[:, 0:1024 * nj].rr("p (c n) -> p c n", c=nj)
                    load("pool", sd, wdn[:, j0:j0 + nj, :])
                    for ti in range(4):
                        for jj in range(nj):
                            j = j0 + jj
                            for hf in range(2):
                                mm(fw.banks[ti * 2 + hf], actT[:, j, (q0 + ti) * 128:(q0 + ti + 1) * 128],
                                   sd[:, jj, hf * 512:(hf + 1) * 512], start=(j == 0), stop=(j == NCH - 1))
                for ti in range(4):
                    t_ = t0 + q0 + ti
                    for hf in range(2):
                        tmpx = accs[ti % 2][hf][:, 0:512]
                        tt("dve", tmpx, fw.banks[ti * 2 + hf], GB[:, hf * 512:(hf + 1) * 512], ALU.mult)
                        tt("dve", X[:, t_, hf * 512:(hf + 1) * 512], X[:, t_, hf * 512:(hf + 1) * 512], tmpx, ALU.add)
                stop_at("f2")

    def gla_mixer(l, pass_id, T, seqs, cond):
        e_ = l // 2
        win = a_w_in_d[e_].rearrange("(k p) n -> p k n", p=128)
        wout = a_w_out_d[e_].rearrange("(c p) n -> p c n", p=128)
        UT = ARENA[:, 0:16384].rr("p (k n) -> p k n", k=8)
        TR = ARENA[:, 16384:39936]
        TRF = ARENA_F[:, 8192:19968]
        make_ut(UT, l, 0, cond, list(range(T)), 0)
        gbcast(l, 16, cond)
        memset("pool", wsm, 0.0)
        load("pool", wsm[:, :, 0:16], win[:, :, 1536:1552])
        load("pool", wsm[:, :, 32:48], win[:, :, 1552:1568])
        load("pool", wg[0:16, :], a_w_gate_d[e_, 0])
        load("pool", wg[32:48, :], a_w_gate_d[e_, 1])
        load("pool", bproj, b_proj_d[e_].rearrange("g c d -> c g d"))
        load("sp", normB, a_norm_d[e_:e_ + 1, :].to_broadcast([128, 128]))
        bg = TRF[0:8, 0:128]
        load("sp", bg[0:8, 0:64], a_b_gate_d[e_].rearrange("z (h d) -> (z h) d", d=64))
        ps = fw.psum()
        tr(ps[0:64, 0:8], bg[0:8, 0:64], ident_f[0:8, 0:8])
        ts("dve", small[0:64, 80:88], ps[0:64, 0:8], -1.0, ALU.mult)
        ps = fw.psum()
        bg2 = TRF[0:4, 128:256]
        load("sp", bg2, b_scale_d[e_].rearrange("(g d) -> g d", d=128))
        tr(ps[:, 0:4], bg2, ident_f[0:4, 0:4])
        cp("act", small[:, 96:100], ps[:, 0:4])
        wo_h = WO.rr("p (c n) -> p c n", c=4)
        load("pool", wo_h, wout[:, 0:4, :])
        for c in range(4):
            tt("dve" if c % 2 else "pool", wo_h[:, c, :], wo_h[:, c, :], GB, ALU.mult)
        o_f = LNB.rr("p a d -> p (a d)").rr("p (t v) -> p t v", v=128)
        stop_at("g1")

        for si, (t0, nt) in enumerate(seqs):
            L = nt * 128
            c0 = t0 * 128
            qT = TR[0:64, 0:2048]
            kT = TR[0:64, 2048:4096]
            v_tok = TR[:, 4096:6144].rr("p (t v) -> p t v", v=128)
            r_tok = TR[:, 6144:8192].rr("p (t v) -> p t v", v=128)
            lrT = TR[0:64, 8192:10240]
            qe = TR[0:64, 10240:11264]
            ke = TR[0:64, 11264:12288]
            kd = TR[0:64, 12288:13312]
            kd_tok = TR[:, 13312:13824].rr("p (c d) -> p c d", d=64)
            ATb = [TR[:, 13824 + i * 128:13824 + (i + 1) * 128] for i in range(2)]
            ogb = TR[:, 14080:14208]
            oTb = TR[:, 14208:14336]
            fo = 14336 // 2
            cpos = TRF[0:64, fo:fo + 1024]
            tmp = TRF[0:64, fo + 1024:fo + 2048]
            dcol = TRF[0:64, fo + 2048:fo + 2056]
            osum = TRF[:, fo + 2056:fo + 2184]
            ost = TRF[:, fo + 2184:fo + 2192]
            totc = TRF[0:64, fo + 2192:fo + 2200]
            for (a, n) in seg512(L):
                ps = fw.psum()
                for kc in range(8):
                    mm(ps[0:64, 0:n], wsm[:, kc, 0:64], UT[:, kc, c0 + a:c0 + a + n], start=(kc == 0), stop=(kc == 7))
                cp("act", lrT[:, a:a + n], ps[0:64, 0:n])
            stop_at("g2")
            for h in range(4):
                sw = wslot(2)[:, 0:8 * 384].rr("p (k n) -> p k n", k=8)
                load("pool", sw[:, :, 0:64], win[:, :, h * 64:(h + 1) * 64])
                load("pool", sw[:, :, 64:128], win[:, :, 256 + h * 64:256 + (h + 1) * 64])
                load("pool", sw[:, :, 128:256], win[:, :, 512 + h * 128:512 + (h + 1) * 128])
                load("pool", sw[:, :, 256:384], win[:, :, 1024 + h * 128:1024 + (h + 1) * 128])
                stop_at("g2a")
                for (a, n) in seg512(L):
                    ps = fw.psum()
                    for kc in range(8):
                        mm(ps[0:64, 0:n], sw[:, kc, 0:64], UT[:, kc, c0 + a:c0 + a + n], start=(kc == 0), stop=(kc == 7))
                    act(qT[:, a:a + n], ps[0:64, 0:n], AF.Identity, scale=0.125)
                    stop_at("g2b")
                    ps = fw.psum()
                    for kc in range(8):
                        mm(ps[0:64, 0:n], sw[:, kc, 64:128], UT[:, kc, c0 + a:c0 + a + n], start=(kc == 0), stop=(kc == 7))
                    cp("dve", kT[:, a:a + n], ps[0:64, 0:n])
                    stop_at("g2c")
                for t in range(nt):
                    ps = fw.psum()
                    for kc in range(8):
                        mm(ps[:, 0:128], UT[:, kc, c0 + t * 128:c0 + (t + 1) * 128], sw[:, kc, 128:256], start=(kc == 0), stop=(kc == 7))
                    cp("dve", v_tok[:, t, :], ps[:, 0:128])
                    stop_at("g2d")
                    ps = fw.psum()
                    for kc in range(8):
                        mm(ps[:, 0:128], UT[:, kc, c0 + t * 128:c0 + (t + 1) * 128], sw[:, kc, 256:384], start=(kc == 0), stop=(kc == 7))
                    act(r_tok[:, t, :], ps[:, 0:128], AF.Silu)
                    stop_at("g2e")
                stop_at("g3")
                for z in range(2):
                    S = S_f[0:64, z, :]
                    Sb = S_b[0:64, z, :]
                    if pass_id == 0:
                        memset("pool", S, 0.0)
                    else:
                        load("sp", S, sgla_d[(e_ * 2 + z) * 4 + h])
                    cp("act", Sb, S)
                    blocks = [(b0, min(8, nt - b0)) for b0 in range(0, nt, 8)]
                    if z == 1:
                        blocks = blocks[::-1]
                    for (b0, nb) in blocks:
                        n = nb * 128
                        for (a, m) in seg512(n):
                            ps = fw.psum()
                            mm(ps[0:64, 0:m], wg[z * 32:z * 32 + 16, h * 64:(h + 1) * 64],
                               lrT[z * 32:z * 32 + 16, b0 * 128 + a:b0 * 128 + a + m])
                            act(tmp[:, a:a + m], ps[0:64, 0:m], AF.Exp, bias=small[0:64, 80 + z * 4 + h:81 + z * 4 + h], scale=-1.0)
                        act(tmp[:, 0:n], tmp[:, 0:n], AF.Ln, bias=1.0)
                        for c in range(nb):
                            cs = slice(c * 128, (c + 1) * 128)
                            scan(cpos[:, cs], ones_f[0:64, :], tmp[:, cs])
                        if z == 1:
                            tt("dve", tmp[:, 0:n], tmp[:, 0:n], cpos[:, 0:n], ALU.subtract)
                            cp("dve", totc[:, 0:nb], cpos[:, 0:n].rr("p (c i) -> p c i", i=128)[:, :, 127])
                            for c in range(nb):
                                cs = slice(c * 128, (c + 1) * 128)
                                ts("dve", cpos[:, cs], tmp[:, cs], totc[:, c:c + 1], ALU.add)
                        lastc = (lambda c: c * 128 + 127) if z == 0 else (lambda c: c * 128)
                        act(tmp[:, 0:n], cpos[:, 0:n], AF.Exp, scale=-1.0 / 16)
                        tt("dve", qe[:, 0:n], qT[:, b0 * 128:b0 * 128 + n], tmp[:, 0:n], ALU.mult)
                        tmp3 = tmp[:, 0:n].rr("p (c i) -> p c i", i=128)
                        cp("dve", dcol[:, 0:nb], tmp3[:, :, 127 if z == 0 else 0])
                        act(tmp[:, 0:n], cpos[:, 0:n], AF.Exp, scale=1.0 / 16)
                        tt("pool", ke[:, 0:n], kT[:, b0 * 128:b0 * 128 + n], tmp[:, 0:n], ALU.mult)
                        for c in range(nb):
                            cs = slice(c * 128, (c + 1) * 128)
                            ts("dve", tmp[:, cs], cpos[:, cs], cpos[:, lastc(c):lastc(c) + 1], ALU.subtract)
                        act(tmp[:, 0:n], tmp[:, 0:n], AF.Exp, scale=1.0 / 16)
                        tt("pool", kd[:, 0:n], kT[:, b0 * 128:b0 * 128 + n], tmp[:, 0:n], ALU.mult)
                        psk = fw.psum(dt=BF16)
                        for c in range(nb):
                            tr(psk[:, c * 64:(c + 1) * 64], kd[:, c * 128:(c + 1) * 128], ident_b[0:64, 0:64])
                        cp("act", kd_tok[:, 0:nb, :].rr("p c d -> p (c d)"), psk[:, 0:nb * 64])
                        stop_at("g4")
                        corder = range(nb) if z == 0 else range(nb - 1, -1, -1)
                        for c in corder:
                            t = b0 + c
                            cs = slice(c * 128, (c + 1) * 128)
                            ps = fw.psum()
                            mm(ps[:, 0:128], ke[:, cs], qe[:, cs])
                            AT = ATb[c % 2]
                            tt("dve", AT, ps[:, 0:128], MU if z == 0 else ML, ALU.mult)
                            ps2 = fw.psum()
                            mm(ps2[:, 0:128], AT, v_tok[:, t, :], start=True, stop=False)
                            mm(ps2[:, 0:128], qe[:, cs], Sb, start=False, stop=True)
                            ps3 = fw.psum()
                            mm(ps3[0:64, 0:128], kd_tok[:, c, :], v_tok[:, t, :])
                            stt("dve", S, S, dcol[:, c:c + 1], ps3[0:64, 0:128], ALU.mult, ALU.add)
                            cp("act", Sb, S)
                            if z == 0:
                                cp("act", o_f[:, t, :], ps2[:, 0:128])
                            else:
                                tt("dve", osum, ps2[:, 0:128], o_f[:, t, :], ALU.add)
                                act(junk[:, 0:128], osum, AF.Square, accum=ost[:, 0:1])
                                ts("dve", ost[:, 1:2], ost[:, 0:1], 1.0 / 128, ALU.mult, EPS, ALU.add)
                                act(ost[:, 1:2], ost[:, 1:2], AF.Sqrt)
                                recip(ost[:, 2:3], ost[:, 1:2])
                                stt("dve", osum, osum, ost[:, 2:3], normB, ALU.mult, ALU.mult)
                                tt("pool", ogb, osum, r_tok[:, t, :], ALU.mult)
                                pst = fw.psum(dt=BF16)
                                tr(pst[:, 0:128], ogb, ident_b)
                                cp("act", oTb, pst[:, 0:128])
                                psA = fw.psum()
                                psB = fw.psum()
                                mm(psA, oTb, wo_h[:, h, 0:512])
                                mm(psB, oTb, wo_h[:, h, 512:1024])
                                xacc(t0 + t, psA, psB)
                            stop_at("g5")
                        stop_at("g6")
                    if pass_id == 0:
                        s_glob = si
                        load("sp", ngla_d[((s_glob * 2 + e_) * 2 + z) * 4 + h], S)
        wo_p = WO.rr("p (c n) -> p c n", c=4)
        load("pool", wo_p, wout[:, 4:8, :])
        for c in range(4):
            tt("dve" if c % 2 else "pool", wo_p[:, c, :], wo_p[:, c, :], GB, ALU.mult)
        swp = wslot(2).rr("p (k n) -> p k n", k=8)
        load("pool", swp, win[:, :, 1568:2080])
        for si, (t0, nt) in enumerate(seqs):
            L = nt * 128
            c0 = t0 * 128
            pm = TR[:, 0:8192].rr("p (g n) -> p g n", g=4)
            xpad = TRF[:, 4096:4096 + 2080]
            sA = TRF[:, 6176:6176 + 2080]
            sB = TRF[:, 8256:8256 + 2080]
            pooled = TR[:, 20672:20672 + 2048]
            assert 20672 + 2048 <= 23552 and (8256 + 2080) * 2 <= 20672
            for gi, w in enumerate(POOLW):
                lo = w // 2
                hi = w - 1 - lo
                memset("pool", xpad[:, 0:16], 0.0)
                memset("pool", xpad[:, 16 + L:32 + L], 0.0)
                for (a, n) in seg512(L):
                    ps = fw.psum()
                    for kc in range(8):
                        mm(ps[:, 0:n], swp[:, kc, gi * 128:(gi + 1) * 128], UT[:, kc, c0 + a:c0 + a + n], start=(kc == 0), stop=(kc == 7))
                    cp("act", xpad[:, 16 + a:16 + a + n], ps[:, 0:n])
                src = xpad
                k = 1
                bufs = [sA, sB]
                bi = 0
                while k < w:
                    dst = bufs[bi]
                    bi ^= 1
                    n = L + 32 - 2 * k
                    tt("dve", dst[:, 0:n], src[:, 0:n], src[:, k:k + n], ALU.add)
                    src = dst
                    k *= 2
                wsum = src[:, 16 - lo:16 - lo + L]
                tt("dve", wsum[:, 0:lo], wsum[:, 0:lo], efix[:, gi, 0:lo], ALU.mult)
                if hi > 0:
                    tt("dve", wsum[:, L - hi:L], wsum[:, L - hi:L], efix[:, gi, 8:8 + hi], ALU.mult)
                stt("dve", pooled[:, 0:L], wsum, 1.0 / w, xpad[:, 16:16 + L], ALU.mult, ALU.subtract)
                for (a, n) in seg512(L):
                    ps = fw.psum()
                    mm(ps[:, 0:n], bproj[:, gi, :], pooled[:, a:a + n])
                    act(pm[:, gi, a:a + n], ps[:, 0:n], AF.Identity, scale=small[:, 96 + gi:97 + gi])
            for t in range(nt):
                psA = fw.psum()
                psB = fw.psum()
                for gi in range(4):
                    mm(psA, pm[:, gi, t * 128:(t + 1) * 128], wo_p[:, gi, 0:512], start=(gi == 0), stop=(gi == 3))
                for gi in range(4):
                    mm(psB, pm[:, gi, t * 128:(t + 1) * 128], wo_p[:, gi, 512:1024], start=(gi == 0), stop=(gi == 3))
                xacc(t0 + t, psA, psB)

    def dn_mixer(l, pass_id, T, seqs, cond):
        o_ = l // 2
        win = c_w_in_d[o_].rearrange("(k p) n -> p k n", p=128)
        wout = c_w_out_d[o_].rearrange("(c p) n -> p c n", p=128)
        UT = ARENA[:, 0:16384].rr("p (k n) -> p k n", k=8)
        TR = ARENA[:, 16384:39936]
        TRF = ARENA_F[:, 8192:19968]
        make_ut(UT, l, 0, cond, list(range(T)), 0)
        gbcast(l, 16, cond)
        memset("pool", wsm, 0.0)
        for i, off in enumerate((0, 32, 64, 96)):
            load("pool", wsm[:, :, off:off + 8], win[:, :, 4096 + i * 8:4096 + (i + 1) * 8])
        load("sp", normB, c_norm_d[o_:o_ + 1, :].to_broadcast([128, 128]))
        memset("pool", small[0:40, 104:106], 0.0)
        for z in range(2):
            load("sp", small[z * 32:z * 32 + 8, 104:105], c_a_log_d[o_, z].rearrange("(h o) -> h o", o=1))
            load("sp", small[z * 32:z * 32 + 8, 105:106], c_dt_bias_d[o_, z].rearrange("(h o) -> h o", o=1))
        act(small[0:40, 104:105], small[0:40, 104:105], AF.Exp)
        cr = TRF[0:72, 0:128]
        load("sp", cr, c_conv_d[o_].rearrange("k (c p) -> (k c) p", p=128))
        ps = fw.psum()
        tr(ps[:, 0:72], cr, ident_f[0:72, 0:72])
        cp("act", convc[:, 0:72], ps[:, 0:72])
        o_f = LNB.rr("p a d -> p (a d)").rr("p (t v) -> p t v", v=128)
        wo = [None, None]

        for si, (t0, nt) in enumerate(seqs):
            L = nt * 128
            c0 = t0 * 128
            R1 = TRF[0:128, 0:2048]
            R2 = TRF[0:64, 2048:4096]
            bo = 8192
            hpre = TR[:, bo:bo + 2050]
            qT = TR[:, bo + 2056:bo + 2056 + 2048]
            kT = TR[:, bo + 4104:bo + 4104 + 2048]
            k_tok = TR[:, bo + 6152:bo + 6152 + 2048].rr("p (t v) -> p t v", v=128)
            v_tok = TR[:, bo + 8200:bo + 8200 + 2048].rr("p (t v) -> p t v", v=128)
            z_tok = TR[:, bo + 10248:bo + 10248 + 2048].rr("p (t v) -> p t v", v=128)
            so = bo + 12296
            sb_ = [TR[:, so + i * 128:so + (i + 1) * 128] for i in range(4)]
            NSLOT = 3
            slots = []
            for s_ in range(NSLOT):
                base = 16384 + so + s_ * 2176
                sbk = [ARENA[:, base + i * 128:base + (i + 1) * 128] for i in range(11)]
                fb = (base + 11 * 128) // 2
                fk = [ARENA_F[:, fb + i * 128:fb + (i + 1) * 128] for i in range(3)]
                assert base + 2176 <= 44032
                slots.append((sbk, fk, dncol[:, s_, 0:104], dncol[:, s_, 104:144], dncol[:, s_, 144:160],
                              (fw.banks[2 * s_], fw.banks[2 * s_ + 1])))
            totr = small[0:40, 296:312]
            memset("pool", R1[:, 0:L], 0.0)
            memset("pool", R2[:, 0:L], 0.0)
            for (a, n) in seg512(L):
                ps = fw.psum()
                for kc in range(8):
                    mm(ps[0:128, 0:n], wsm[:, kc, 0:128], UT[:, kc, c0 + a:c0 + a + n], start=(kc == 0), stop=(kc == 7))
                act(R1[0:40, a:a + n], ps[0:40, 0:n], AF.Exp, bias=small[0:40, 105:106])
                act(R1[0:40, a:a + n], R1[0:40, a:a + n], AF.Ln, bias=1.0)
                ts("dve", R2[0:40, a:a + n], R1[0:40, a:a + n], small[0:40, 104:105], ALU.mult)
                act(R1[64:104, a:a + n], ps[64:104, 0:n], AF.Sigmoid)
            for c in range(nt):
                cs = slice(c * 128, (c + 1) * 128)
                scan(R1[0:40, cs], ones_f[0:40, :], R2[0:40, cs])
            tt("dve", R2[32:40, 0:L], R2[32:40, 0:L], R1[32:40, 0:L], ALU.subtract)
            cp("dve", totr[32:40, 0:nt], R1[32:40, 0:L].rr("p (c i) -> p c i", i=128)[:, :, 127])
            for c in range(nt):
                cs = slice(c * 128, (c + 1) * 128)
                ts("dve", R1[32:40, cs], R2[32:40, cs], totr[32:40, c:c + 1], ALU.add)
            for c in range(nt):
                cs = slice(c * 128, (c + 1) * 128)
                ts("dve", R2[0:8, cs], R1[0:8, cs], R1[0:8, c * 128 + 127:c * 128 + 128], ALU.subtract)
                ts("dve", R2[32:40, cs], R1[32:40, cs], R1[32:40, c * 128:c * 128 + 1], ALU.subtract)
            if l == 1 and si == 0:
                dbg("R1_%d" % pass_id, R1[0:104, 0:256], [104, 256])
                dbg("R2_%d" % pass_id, R2[0:40, 0:256], [40, 256])

            for h in range(8):
                if h % 4 == 0:
                    wo_h = WO.rr("p (c n) -> p c n", c=4)
                    load("pool", wo_h, wout[:, h:h + 4, :])
                    for c in range(4):
                        tt("dve" if c % 2 else "pool", wo_h[:, c, :], wo_h[:, c, :], GB, ALU.mult)
                sw = wslot(2).rr("p (k n) -> p k n", k=8)
                for i in range(4):
                    load("pool", sw[:, :, i * 128:(i + 1) * 128], win[:, :, i * 1024 + h * 128:i * 1024 + (h + 1) * 128])
                memset("pool", hpre[:, 0:1], 0.0)
                memset("pool", hpre[:, L + 1:L + 2], 0.0)
                for i, dstT in enumerate((qT, kT, None)):
                    for (a, n) in seg512(L):
                        ps = fw.psum()
                        for kc in range(8):
                            mm(ps[:, 0:n], sw[:, kc, i * 128:(i + 1) * 128], UT[:, kc, c0 + a:c0 + a + n], start=(kc == 0), stop=(kc == 7))
                        cp("act", hpre[:, 1 + a:1 + a + n], ps[:, 0:n])
                    cc = i * 8 + h
                    for (a, n) in seg512(L):
                        accv = junk.bitcast(F32)[:, 0:n]
                        ts("pool", accv, hpre[:, 1 + a:1 + a + n], convc[:, 24 + cc:25 + cc], ALU.mult)
                        stt("dve", accv, hpre[:, a:a + n], convc[:, cc:cc + 1], accv, ALU.mult, ALU.add)
                        stt("dve", accv, hpre[:, 2 + a:2 + a + n], convc[:, 48 + cc:49 + cc], accv, ALU.mult, ALU.add)
                        if dstT is not None:
                            act(dstT[:, a:a + n], accv, AF.Silu)
                            sqv = TR[:, so:so + 512]
                            tt("pool", sqv[:, 0:n], dstT[:, a:a + n], dstT[:, a:a + n], ALU.mult)
                            ps = fw.psum()
                            mm(ps[:, 0:n], ones_b, sqv[:, 0:n])
                            rn = junk.bitcast(F32)[:, 0:n]
                            ts("dve", rn, ps[:, 0:n], EPS, ALU.add)
                            act(rn, rn, AF.Sqrt)
                            recip(rn, rn)
                            if i == 0:
                                stt("dve", dstT[:, a:a + n], dstT[:, a:a + n], float(128 ** -0.5), rn, ALU.mult, ALU.mult)
                            else:
                                tt("dve", dstT[:, a:a + n], dstT[:, a:a + n], rn, ALU.mult)
                        else:
                            vT = TR[:, so:so + 512]
                            act(vT[:, 0:n], accv, AF.Silu)
                            for tq in range(n // 128):
                                pst = fw.psum(dt=BF16)
                                tr(pst[:, 0:128], vT[:, tq * 128:(tq + 1) * 128], ident_b)
                                cp("dve", v_tok[:, a // 128 + tq, :], pst[:, 0:128])
                for t in range(nt):
                    pst = fw.psum(dt=BF16)
                    tr(pst[:, 0:128], kT[:, t * 128:(t + 1) * 128], ident_b)
                    cp("act", k_tok[:, t, :], pst[:, 0:128])
                    ps = fw.psum()
                    for kc in range(8):
                        mm(ps[:, 0:128], UT[:, kc, c0 + t * 128:c0 + (t + 1) * 128], sw[:, kc, 384:512], start=(kc == 0), stop=(kc == 7))
                    act(z_tok[:, t, :], ps[:, 0:128], AF.Silu)
                if l == 1 and si == 0 and h == 0:
                    dbg("qT_%d" % pass_id, qT[:, 0:256], [128, 256])
                    dbg("kT_%d" % pass_id, kT[:, 0:256], [128, 256])
                    dbg("vtok_%d" % pass_id, v_tok[:, 0, :], [128, 128])

                for z in range(2):
                    S = S_f[:, z, :]
                    Sb = S_b[:, z, :]
                    if pass_id == 0:
                        memset("pool", S, 0.0)
                    else:
                        load("sp", S, sdn_d[(o_ * 2 + z) * 8 + h])
                    cp("act", Sb, S)
                    rb = z * 32 + h
                    fw.op("dve", lambda e, rb=rb, z=z: e.tensor_single_scalar(out=selc[:, z * 128:(z + 1) * 128].ap, in_=pidx.ap,
                                                                           scalar=float(rb), op=ALU.is_equal),
                          reads=[pidx], writes=[selc[:, z * 128:(z + 1) * 128]])
                rec_turn = [0, 0]
                stored = set()

                def chunk_task(z, c, zi, sl):
                    sbk, fk, cols, cold, ccol, bankpair = sl
                    nal = [0]

                    def palloc(dt=F32):
                        bnk = bankpair[nal[0] % 2]
                        nal[0] += 1
                        return bnk if dt == F32 else bnk.bitcast(dt)
                    S = S_f[:, z, :]
                    Sb = S_b[:, z, :]
                    rb = z * 32 + h
                    Msk_s = MLs if z == 0 else MUs
                    Msk_i = ML if z == 0 else MU
                    cs = slice(c * 128, (c + 1) * 128)
                    ps = palloc()
                    tr(ps[:, 0:128], R1[0:128, cs], ident_f)
                    cp("act", cols, ps[:, 0:104])
                    ps = palloc()
                    tr(ps[:, 0:64], R2[0:64, cs], ident_f[0:64, 0:64])
                    cp("dve", cold, ps[:, 0:40])
                    bcol = cols[:, rb:rb + 1]
                    betac = cols[:, 64 + rb:65 + rb]
                    act(ccol[:, 0:1], bcol, AF.Exp, scale=-1.0)
                    tt("dve", ccol[:, 1:2], ccol[:, 0:1], betac, ALU.mult)
                    ts("dve", ccol[:, 2:3], betac, -1.0, ALU.mult)
                    act(ccol[:, 3:4], cold[:, rb:rb + 1], AF.Exp)
                    psBGQ = palloc()
                    psB = psBGQ[:, 0:128]
                    psG = psBGQ[:, 128:256]
                    psQ = psBGQ[:, 256:384]
                    mm(psB[:, 0:128], selc[:, z * 128:(z + 1) * 128], R1[0:64, cs])
                    mm(psG[:, 0:128], kT[:, cs], kT[:, cs])
                    mm(psQ[:, 0:128], qT[:, cs], kT[:, cs])
                    yield
                    EB = fk[1]
                    act(EB, psB[:, 0:128], AF.Exp, scale=-1.0)
                    ld = fk[0]
                    fw.op("dve", lambda e, ld=ld, psB=psB, bcol=bcol: e.tensor_scalar(
                        out=ld.ap, in0=psB[:, 0:128].ap, scalar1=bcol.ap, scalar2=0.0, op0=ALU.subtract, op1=ALU.min),
                        reads=[psB[:, 0:128], bcol, EB], writes=[ld])
                    act(ld, ld, AF.Exp)
                    cp("act", ccol[:, 4:5], EB[:, 127:128] if z == 0 else EB[:, 0:1])
                    yield
                    t1 = fk[2]
                    tt("dve", t1, psG[:, 0:128], ld, ALU.mult)
                    P0 = sbk[0]
                    stt("dve", P0, t1, ccol[:, 2:3], Msk_s, ALU.mult, ALU.mult)
                    P = sbk[1]
                    tt("pool", P, P0, BMK[:, 0, :], ALU.mult)
                    yield
                    pst = palloc(BF16)
                    tr(pst[:, 0:128], P, ident_b)
                    PT = sbk[2]
                    cp("act", PT, pst[:, 0:128])
                    TT = sbk[7]
                    tt("dve", TT, ident_b, PT, ALU.add)
                    tt("dve", t1, psQ[:, 0:128], ld, ALU.mult)
                    yield
                    for it in range(3):
                        Pn = sbk[3 + (it % 2) * 2]
                        PTn = sbk[4 + (it % 2) * 2]
                        ps1 = palloc()
                        mm(ps1[:, 0:128], PT, P)
                        if it < 2:
                            ps2 = palloc()
                            mm(ps2[:, 0:128], P, PT)
                        yield
                        cp("act", Pn, ps1[:, 0:128])
                        if it < 2:
                            cp("dve", PTn, ps2[:, 0:128])
                        yield
                        ps3 = palloc()
                        mm(ps3[:, 0:128], Pn, TT)
                        yield
                        TTn = sbk[8] if TT is sbk[7] else sbk[7]
                        tt("dve", TTn, ps3[:, 0:128], TT, ALU.add)
                        P, PT, TT = Pn, PTn, TTn
                        yield
                    for lv in range(3):
                        Bk = sbk[4]
                        tt("pool", Bk, P0, BMK[:, 1 + lv, :], ALU.mult)
                        pst = palloc(BF16)
                        tr(pst[:, 0:128], TT, ident_b)
                        psz = palloc()
                        mm(psz[:, 0:128], Bk, TT)
                        yield
                        Tn = sbk[3]
                        cp("act", Tn, pst[:, 0:128])
                        Zb = sbk[5]
                        cp("dve", Zb, psz[:, 0:128])
                        yield
                        psw = palloc()
                        mm(psw[:, 0:128], Tn, Zb)
                        yield
                        TTn = sbk[8] if TT is sbk[7] else sbk[7]
                        tt("dve", TTn, psw[:, 0:128], TT, ALU.add)
                        TT = TTn
                        yield
                    aq = sbk[2]
                    tt("dve", aq, t1, Msk_i, ALU.mult)
                    pst = palloc(BF16)
                    tr(pst[:, 0:128], aq, ident_b)
                    aqT = sbk[9]
                    cp("act", aqT, pst[:, 0:128])
                    qdT = sbk[10]
                    tt("dve", qdT, qT[:, cs], EB, ALU.mult)
                    kbe = sbk[3]
                    vb = sbk[4]
                    kdk = sbk[6]
                    ts("dve", kbe, k_tok[:, c, :], ccol[:, 1:2], ALU.mult)
                    ts("pool", vb, v_tok[:, c, :], betac, ALU.mult)
                    ts("pool", kdk, k_tok[:, c, :], ccol[:, 3:4], ALU.mult)
                    yield
                    psU = palloc()
                    mm(psU[:, 0:128], TT, vb)
                    psW = palloc()
                    mm(psW[:, 0:128], kbe, TT)
                    yield
                    u = fk[2]
                    cp("act", u, psU[:, 0:128])
                    wT = sbk[1]
                    cp("dve", wT, psW[:, 0:128])
                    yield
                    while rec_turn[z] != zi:
                        yield
                    psS = palloc()
                    mm(psS[:, 0:128], wT, Sb)
                    yield
                    vnew = sbk[2]
                    tt("dve", vnew, u, psS[:, 0:128], ALU.subtract)
                    yield
                    psO = palloc()
                    mm(psO[:, 0:128], qdT, Sb, start=True, stop=False)
                    mm(psO[:, 0:128], aqT, vnew, start=False, stop=True)
                    psN = palloc()
                    mm(psN[:, 0:128], kdk, vnew)
                    yield
                    stt("dve", S, S, ccol[:, 4:5], psN[:, 0:128], ALU.mult, ALU.add)
                    cp("act", Sb, S)
                    rec_turn[z] += 1
                    t = c
                    second = (z == 1 and 2 * t < nt) or (z == 0 and 2 * t >= nt)
                    if not second:
                        cp("act", o_f[:, t, :], psO[:, 0:128])
                        stored.add(t)
                    else:
                        while t not in stored:
                            yield
                        osum = fk[0]
                        ost = ccol[:, 8:16]
                        tt("dve", osum, psO[:, 0:128], o_f[:, t, :], ALU.add)
                        act(junk[:, 0:128], osum, AF.Square, accum=ost[:, 0:1])
                        ts("dve", ost[:, 1:2], ost[:, 0:1], 1.0 / 128, ALU.mult, EPS, ALU.add)
                        act(ost[:, 1:2], ost[:, 1:2], AF.Sqrt)
                        recip(ost[:, 2:3], ost[:, 1:2])
                        stt("dve", osum, osum, ost[:, 2:3], normB, ALU.mult, ALU.mult)
                        yield
                        ogb = sbk[9]
                        tt("pool", ogb, osum, z_tok[:, t, :], ALU.mult)
                        pst = palloc(BF16)
                        tr(pst[:, 0:128], ogb, ident_b)
                        yield
                        oTb = sbk[10]
                        cp("act", oTb, pst[:, 0:128])
                        psA = palloc()
                        psBk = palloc()
                        mm(psA, oTb, wo_h[:, h % 4, 0:512])
                        mm(psBk, oTb, wo_h[:, h % 4, 512:1024])
                        yield
                        xacc(t0 + t, psA, psBk)

                items = []
                for i in range(nt):
                    items.append((0, i, i))
                    items.append((1, nt - 1 - i, i))
                active = []
                free = list(range(len(slots)))
                qi = 0
                while qi < len(items) or active:
                    while free and qi < len(items):
                        z_, c_, zi_ = items[qi]
                        qi += 1
                        s_ = free.pop(0)
                        active.append([chunk_task(z_, c_, zi_, slots[s_]), s_])
                    for a_ in list(active):
                        try:
                            next(a_[0])
                        except StopIteration:
                            active.remove(a_)
                            free.append(a_[1])
                if pass_id == 0:
                    for z in range(2):
                        load("sp", ndn_d[((si * 2 + o_) * 2 + z) * 8 + h], S_f[:, z, :])

    passes = [(0, 8, [(0, 2), (2, 2), (4, 2), (6, 2)], 0), (1, 16, [(0, 16)], 1)]

    def _main():
        for (pass_id, T, seqs, cond) in passes:
            mark("P%d load" % pass_id)
            load_x(pass_id, T)
            if pass_id == 1:
                dbg("x0s", X[:, 0, :], [128, D])
            for l in range(NL):
                mark("P%d L%d mixer" % (pass_id, l))
                if l % 2 == 0:
                    gla_mixer(l, pass_id, T, seqs, cond)
                else:
                    dn_mixer(l, pass_id, T, seqs, cond)
                mark("P%d L%d ln1" % (pass_id, l))
                dbg("xmix%d_%d" % (l, pass_id), X[:, 0, :], [128, D])
                if stop == "mix%d_%d" % (l, pass_id):
                    raise _Stop()
                layernorm(l, 0, T, False)
                dbg("xln%d_%d" % (l, pass_id), X[:, 0, :], [128, D])
                stop_at("ln%d_%d" % (l, pass_id))
                mark("P%d L%d ffn" % (pass_id, l))
                ffn(l, pass_id, T, cond)
                mark("P%d L%d ln2" % (pass_id, l))
                dbg("xffn%d_%d" % (l, pass_id), X[:, 0, :], [128, D])
                stop_at("ffn%d_%d" % (l, pass_id))
                layernorm(l, 1, T, l == NL - 1)
                dbg("xl%d_%d" % (l, pass_id), X[:, 0, :], [128, D])
                if stop == "l%d_%d" % (l, pass_id):
                    raise _Stop()
            mark("P%d store" % pass_id)
            dst = yp_d if pass_id == 0 else ys_d
            for t in range(T):
                load("sp", dst[t * 128:(t + 1) * 128, :], X[:, t, :])

    try:
        _main()
    except _Stop:
        pass
    fw.emit()
    return nc, fw, dbg_d


_CACHE = {}


def _inputs_per_core(inp, core):
    b = core % 2
    m = {}
    m["xp"] = np.ascontiguousarray(inp["x_prompt"][core * 4:(core + 1) * 4].reshape(1024, D))
    m["xs"] = np.ascontiguousarray(inp["x_sample"][b])
    m["cond2"] = np.ascontiguousarray(np.stack([inp["c_ctx"], inp["c"][b]], 0))
    m["sgla"] = np.ascontiguousarray(inp["state_gla"][b].reshape(16, 64, 128))
    m["sdn"] = np.ascontiguousarray(inp["state_dn"][b].reshape(32, 128, 128))
    for k in ("w_mod", "b_mod", "ln1_g", "ln1_b", "ln2_g", "ln2_b", "a_w_in", "a_w_gate", "a_b_gate", "a_norm",
              "b_proj", "b_scale", "a_w_out", "c_w_in", "c_conv", "c_a_log", "c_dt_bias", "c_norm", "c_w_out",
              "f_w_up", "f_conv", "f_w_down"):
        m[k] = np.ascontiguousarray(inp[k])
    return m


def kernel(**inputs):
    inp = {k: np.asarray(v, dtype=np.float32) for k, v in inputs.items()}
    if "nc" not in _CACHE:
        _CACHE["nc"] = build_program()[0]
    nc = _CACHE["nc"]
    n = 8
    in_maps = [_inputs_per_core(inp, c) for c in range(n)]
    res = run_bass_kernel_spmd(nc, in_maps, core_ids=list(range(n)))
    R = res.results
    y_prompt = np.concatenate([R[c]["yp"].reshape(4, 256, D) for c in range(n)], 0)
    y_sample = np.stack([R[0]["ys"], R[1]["ys"]], 0)
    ngla = np.concatenate([R[c]["ngla"].reshape(4, 2, 2, 4, 64, 128) for c in range(n)], 0)
    ndn = np.concatenate([R[c]["ndn"].reshape(4, 2, 2, 8, 128, 128) for c in range(n)], 0)
    return (y_prompt.astype(np.float32), y_sample.astype(np.float32), ngla.astype(np.float32), ndn.astype(np.float32))
```

```python
import math
import numpy as np
from concourse.bass_utils import run_bass_kernel_spmd
import concourse.bass as bass
import concourse.mybir as mybir

F32 = mybir.dt.float32
BF16 = mybir.dt.bfloat16
I32 = mybir.dt.int32
AF = mybir.ActivationFunctionType
ALU = mybir.AluOpType
AX = mybir.AxisListType

CELL = 256
_DT_SIZE = {F32: 4, BF16: 2, I32: 4}


class Region:
    def __init__(self, fw, name, handle, nbytes, cell=CELL):
        self.fw = fw
        self.name = name
        self.h = handle
        self.cell = cell
        self.ncell = (nbytes + cell - 1) // cell
        self.w = [None] * self.ncell
        self.r = [dict() for _ in range(self.ncell)]


class V:
    def __init__(self, region, ap):
        self.region = region
        self.ap = ap
        self._cells = None

    def __getitem__(self, key):
        return V(self.region, self.ap[key])

    def rr(self, pattern_, **kw):
        return V(self.region, self.ap.rearrange(pattern_, **kw))

    def bitcast(self, dt):
        return V(self.region, self.ap.bitcast(dt))

    def bc(self, shape):
        return V(self.region, self.ap.to_broadcast(shape))

    @property
    def shape(self):
        return self.ap.shape

    def cells(self):
        if self._cells is None:
            ap = self.ap
            esz = _DT_SIZE[ap.dtype]
            dims = list(ap.ap)[1:]
            base = int(ap.offset) if not isinstance(ap.offset, int) else ap.offset
            pstep = list(ap.ap)[0][0]
            if pstep > 0:
                base = base % pstep
            base_b = base * esz
            cs = set()
            CELL = self.region.cell
            dims = [(s, n) for (s, n) in dims if n > 1 or True]
            if not dims:
                dims = [(1, 1)]
            *outer, (ls, ln) = dims
            if ls in (0, 1):
                run = (esz * (ln if ls == 1 else 1))
                inner_iter = [0]
            else:
                run = esz
                inner_iter = [i * ls * esz for i in range(ln)]
            offs = [0]
            for (s, n) in outer:
                if s == 0:
                    continue
                offs = [o + i * s * esz for o in offs for i in range(n)]
            for o in offs:
                for ii in inner_iter:
                    a = base_b + o + ii
                    for c in range(a // CELL, (a + run - 1) // CELL + 1):
                        cs.add(c)
            self._cells = sorted(cs)
            assert self._cells[-1] < self.region.ncell, (self.region.name, self._cells[-1], self.region.ncell, ap)
        return self._cells


class Op:
    __slots__ = ("eng", "fn", "waits", "signal", "semval", "dma_sem", "is_dma")

    def __init__(self, eng, fn):
        self.eng = eng
        self.fn = fn
        self.waits = []
        self.signal = False
        self.semval = None
        self.dma_sem = None
        self.is_dma = False


ENGS = ("pe", "dve", "act", "pool", "sp")
N_DMA_SEMS = 12


class FW:
    def __init__(self, nc):
        self.nc = nc
        self.ops = {e: [] for e in ENGS}
        self.regions = []
        self.dma_rr = {"sp": 0, "pool": 0, "act": 0}
        self.dma_last = {}
        self._ctx = []
        self.psum_ptr = 0
        self.nops = 0

    def sbuf(self, name, shape, dt):
        g = self.nc.sbuf_tensor(name, list(shape), dt)
        h = g.__enter__()
        self._ctx.append(g)
        nb = int(np.prod(shape[1:])) * _DT_SIZE[dt]
        reg = Region(self, name, h, nb)
        self.regions.append(reg)
        return V(reg, h[:] if hasattr(h, "__getitem__") else h.ap())

    def psum_banks(self):
        self.banks = []
        for i in range(8):
            g = self.nc.psum_tensor(f"psb{i}", [128, 512], F32)
            h = g.__enter__()
            self._ctx.append(g)
            reg = Region(self, f"psb{i}", h, 2048, cell=2048)
            self.banks.append(V(reg, h[:]))

    def psum(self, ncols=512, parts=128, dt=F32):
        b = self.psum_ptr
        self.psum_ptr = (b + 1) % 8
        bank = self.banks[b] if dt == F32 else self.banks[b].bitcast(dt)
        return bank[0:parts, 0:ncols]

    def _deps(self, op, reads, writes):
        deps = {}
        for v in reads:
            reg = v.region
            for c in v.cells():
                w = reg.w[c]
                if w is not None:
                    deps[id(w)] = w
        for v in writes:
            reg = v.region
            for c in v.cells():
                w = reg.w[c]
                if w is not None:
                    deps[id(w)] = w
                for t in reg.r[c].values():
                    deps[id(t)] = t
        for t in deps.values():
            if t is op:
                continue
            if t.eng == "pe" and op.eng == "pe" and not t.is_dma and not op.is_dma:
                continue
            op.waits.append(t)
            t.signal = True
        for v in reads:
            reg = v.region
            key = op.dma_sem if op.is_dma else op.eng
            for c in v.cells():
                reg.r[c][key] = op
        for v in writes:
            reg = v.region
            for c in v.cells():
                reg.w[c] = op
                reg.r[c] = {}

    def op(self, eng, fn, reads=(), writes=()):
        if getattr(self, "halted", False):
            return None
        o = Op(eng, fn)
        self._deps(o, reads, writes)
        self.ops[eng].append(o)
        self.nops += 1
        return o

    def dma(self, queue, out, in_, reads=(), writes=(), **kw):
        if getattr(self, "halted", False):
            return None
        o = Op(queue, None)
        o.is_dma = True
        o.signal = True
        oap = out.ap if isinstance(out, V) else out
        iap = in_.ap if isinstance(in_, V) else in_
        rd = list(reads) + ([in_] if isinstance(in_, V) else [])
        wr = list(writes) + ([out] if isinstance(out, V) else [])
        k = self.dma_rr[queue]
        self.dma_rr[queue] = (k + 1) % N_DMA_SEMS
        o.dma_sem = (queue, k)
        prev = self.dma_last.get((queue, k))
        self._deps(o, rd, wr)
        if prev is not None:
            o.waits.append(prev)
        self.dma_last[(queue, k)] = o
        o.fn = lambda e: e.dma_start(out=oap, in_=iap, **kw)
        self.ops[queue].append(o)
        self.nops += 1
        return o

    def emit(self):
        nc = self.nc
        sems = {}
        semctx = []
        for e in ENGS:
            g = nc.semaphore(f"s_{e}")
            sems[e] = g.__enter__()
            semctx.append(g)
        dsems = {}
        for q in ("sp", "pool"):
            for k in range(N_DMA_SEMS):
                g = nc.semaphore(f"d_{q}{k}")
                dsems[(q, k)] = g.__enter__()
                semctx.append(g)
        for e in ENGS:
            cnt = 0
            for o in self.ops[e]:
                if o.is_dma:
                    continue
                if o.signal:
                    cnt += 1
                    o.semval = cnt
            self.maxsem = max(getattr(self, "maxsem", 0), cnt)
        dcnt = {}
        for e in ENGS:
            for o in self.ops[e]:
                if o.is_dma:
                    dcnt[o.dma_sem] = dcnt.get(o.dma_sem, 0) + 16
                    o.semval = dcnt[o.dma_sem]

        def semof(t):
            return dsems[t.dma_sem] if t.is_dma else sems[t.eng]

        def run(engname, engobj):
            known = {}
            for o in self.ops[engname]:
                need = {}
                for t in o.waits:
                    s = t.dma_sem if t.is_dma else t.eng
                    if t.semval > need.get(s, (0, None))[0]:
                        need[s] = (t.semval, t)
                for s, (val, t) in need.items():
                    if known.get(s, 0) >= val:
                        continue
                    engobj.wait_ge(semof(t), val)
                    known[s] = val
                ins = o.fn(engobj)
                if o.is_dma:
                    ins.then_inc(dsems[o.dma_sem], 16)
                elif o.signal:
                    ins.then_inc(sems[engname], 1)
            if engname in ("sp", "pool"):
                for k in range(N_DMA_SEMS):
                    if dcnt.get((engname, k), 0) > 0:
                        engobj.wait_ge(dsems[(engname, k)], dcnt[(engname, k)])

        with nc.Block() as block:
            @block.tensor
            def _(e):
                run("pe", e)

            @block.vector
            def _(e):
                run("dve", e)

            @block.scalar
            def _(e):
                run("act", e)

            @block.gpsimd
            def _(e):
                run("pool", e)

            @block.sync
            def _(e):
                run("sp", e)
        for g in reversed(semctx):
            g.__exit__(None, None, None)
        for g in reversed(self._ctx):
            g.__exit__(None, None, None)

D = 1024
NL = 4
DFF = 2816
NCH = DFF // 128
ALPHA = float(8 ** 0.25)
EPS = 1e-6
KC = 8
A_IN = 2080
C_IN = 4128
PI = float(np.pi)

DBG_SPECS = {}


class _Stop(Exception):
    pass


def build_program(debug=(), stop=None):
    nc = bass.Bass("TRN2", target_bir_lowering=False)
    fw = FW(nc)

    def din(name, shape):
        return nc.dram_tensor(name, list(shape), F32, kind="ExternalInput").ap()

    def dout(name, shape):
        return nc.dram_tensor(name, list(shape), F32, kind="ExternalOutput").ap()

    xp_d = din("xp", [1024, D])
    xs_d = din("xs", [2048, D])
    cond_d = din("cond2", [2, D])
    sgla_d = din("sgla", [16, 64, 128])
    sdn_d = din("sdn", [32, 128, 128])
    w_mod_d = din("w_mod", [NL, D, 6 * D])
    b_mod_d = din("b_mod", [NL, 6 * D])
    ln_d = {k: din(k, [NL, D]) for k in ("ln1_g", "ln1_b", "ln2_g", "ln2_b")}
    a_w_in_d = din("a_w_in", [2, D, A_IN])
    a_w_gate_d = din("a_w_gate", [2, 2, 16, 256])
    a_b_gate_d = din("a_b_gate", [2, 2, 256])
    a_norm_d = din("a_norm", [2, 128])
    b_proj_d = din("b_proj", [2, 4, 128, 128])
    b_scale_d = din("b_scale", [2, 512])
    a_w_out_d = din("a_w_out", [2, D, D])
    c_w_in_d = din("c_w_in", [2, D, C_IN])
    c_conv_d = din("c_conv", [2, 3, 3072])
    c_a_log_d = din("c_a_log", [2, 2, 8])
    c_dt_bias_d = din("c_dt_bias", [2, 2, 8])
    c_norm_d = din("c_norm", [2, 128])
    c_w_out_d = din("c_w_out", [2, D, D])
    f_w_up_d = din("f_w_up", [NL, D, 2 * DFF])
    f_conv_d = din("f_conv", [NL, 3, 2 * DFF])
    f_w_down_d = din("f_w_down", [NL, DFF, D])

    yp_d = dout("yp", [1024, D])
    ys_d = dout("ys", [2048, D])
    ngla_d = dout("ngla", [64, 64, 128])
    ndn_d = dout("ndn", [128, 128, 128])
    dbg_d = {}

    def A(v):
        return v.ap if isinstance(v, V) else v

    def rds(*xs):
        return [x for x in xs if isinstance(x, V)]

    def mm(out, lhsT, rhs, start=True, stop=True):
        fw.op("pe", lambda e: e.matmul(out.ap, lhsT=lhsT.ap, rhs=rhs.ap, start=start, stop=stop),
              reads=[lhsT, rhs], writes=[out])

    def tr(out, in_, idn):
        fw.op("pe", lambda e: e.transpose(out=out.ap, in_=in_.ap, identity=idn.ap),
              reads=[in_, idn], writes=[out])

    def act(out, in_, func, bias=None, scale=None, accum=None):
        kw = {}
        if bias is not None:
            kw["bias"] = A(bias)
        if scale is not None:
            kw["scale"] = A(scale)
        if accum is not None:
            kw["accum_out"] = accum.ap
        fw.op("act", lambda e: e.activation(out=out.ap, in_=in_.ap, func=func, **kw),
              reads=rds(in_, bias, scale), writes=rds(out, accum))

    def ts(eng, out, in0, s1, op0, s2=None, op1=None):
        kw = {"op1": op1} if op1 is not None else {}
        fw.op(eng, lambda e: e.tensor_scalar(out=out.ap, in0=in0.ap, scalar1=A(s1),
                                             scalar2=(A(s2) if s2 is not None else None), op0=op0, **kw),
              reads=rds(in0, s1, s2), writes=[out])

    def tt(eng, out, in0, in1, op):
        fw.op(eng, lambda e: e.tensor_tensor(out=out.ap, in0=in0.ap, in1=in1.ap, op=op),
              reads=[in0, in1], writes=[out])

    def stt(eng, out, in0, scalar, in1, op0, op1):
        fw.op(eng, lambda e: e.scalar_tensor_tensor(out=out.ap, in0=in0.ap, scalar=A(scalar), in1=in1.ap,
                                                    op0=op0, op1=op1),
              reads=rds(in0, scalar, in1), writes=[out])

    def cp(eng, out, in_):
        if eng == "act":
            fw.op("act", lambda e: e.copy(out=out.ap, in_=in_.ap), reads=[in_], writes=[out])
        else:
            fw.op(eng, lambda e: e.tensor_copy(out=out.ap, in_=in_.ap), reads=[in_], writes=[out])

    def memset(eng, out, val):
        fw.op(eng, lambda e: e.memset(out.ap, val), writes=[out])

    def scan(out, d0, d1, init=0.0):
        fw.op("dve", lambda e: e.tensor_tensor_scan(out=out.ap, data0=d0.ap, data1=d1.ap, initial=init,
                                                    op0=ALU.mult, op1=ALU.add),
              reads=[d0, d1], writes=[out])

    def recip(out, in_):
        fw.op("dve", lambda e: e.reciprocal(out=out.ap, in_=in_.ap), reads=[in_], writes=[out])

    def rsum(out, in_):
        fw.op("dve", lambda e: e.reduce_sum(out=out.ap, in_=in_.ap, axis=AX.X), reads=[in_], writes=[out])

    def load(q, out, src, **kw):
        fw.dma(q, out, src, **kw)

    def dbg(name, v, shape):
        if name in debug:
            d = dout("dbg_" + name, list(shape))
            dbg_d[name] = d
            fw.dma("sp" if v.ap.dtype == F32 else "pool", d, v)

    def stop_at(name):
        if stop == name:
            fw.halted = True

    fw.marks = []

    def mark(name):
        fw.marks.append((name, len(fw.ops["dve"]), len(fw.ops["pe"])))

    rr = [0]

    def evac_eng():
        rr[0] ^= 1
        return "act" if rr[0] else "dve"

    fw.psum_banks()
    X = fw.sbuf("X", [128, 16, D], F32)
    ARENA = fw.sbuf("ARENA", [128, 44032], BF16)
    ARENA_F = ARENA.bitcast(F32)
    ARENA_I = ARENA.bitcast(I32)
    WR = fw.sbuf("WR", [128, 3, 4096], BF16)
    LNB = fw.sbuf("LNB", [128, 2, D], F32)
    GB = fw.sbuf("GB", [128, D], F32)
    ones_f = fw.sbuf("ones_f", [128, 128], F32)
    ident_f = fw.sbuf("ident_f", [128, 128], F32)
    ident_b = fw.sbuf("ident_b", [128, 128], BF16)
    ones_b = fw.sbuf("ones_b", [128, 128], BF16)
    MU = fw.sbuf("MU", [128, 128], F32)
    MUs = fw.sbuf("MUs", [128, 128], F32)
    ML = fw.sbuf("ML", [128, 128], F32)
    MLs = fw.sbuf("MLs", [128, 128], F32)
    pidx = fw.sbuf("pidx", [64, 128], F32)
    BMK = fw.sbuf("BMK", [128, 4, 128], BF16)
    MDS = fw.sbuf("MDS", [128, 2, 128], BF16)
    selc = fw.sbuf("selc", [64, 256], F32)
    dncol = fw.sbuf("dncol", [128, 4, 160], F32)
    modcol = fw.sbuf("modcol", [128, NL, 48, 2], F32)
    mscale = fw.sbuf("mscale", [128, NL, 2, 8, 2], F32)
    scT = fw.sbuf("scT", [128, 8, 2], F32)
    small = fw.sbuf("small", [128, 320], F32)
    S_f = fw.sbuf("S_f", [128, 2, 128], F32)
    S_b = fw.sbuf("S_b", [128, 2, 128], BF16)
    wsm = fw.sbuf("wsm", [128, 8, 128], BF16)
    wg = fw.sbuf("wg", [48, 256], BF16)
    bproj = fw.sbuf("bproj", [128, 4, 128], BF16)
    convc = fw.sbuf("convc", [128, 160], F32)
    normB = fw.sbuf("normB", [128, 128], F32)
    junk = fw.sbuf("junk", [128, D], BF16)
    pe_c = ARENA_F[:, 0:512]
    freq = ARENA_F[:, 512:768]
    ptmp = ARENA_F[:, 768:1536].rr("p (a n) -> p a n", a=3)
    ptmpi = ARENA_I[:, 1536:1792]
    pcol = fw.sbuf("pcol", [128, 8], F32)
    dgs = fw.sbuf("dgs", [128, 128], F32)
    efix = fw.sbuf("efix", [128, 4, 16], F32)

    fw.sbuf_free = nc.sbuf_bytes_remaining
    memset("pool", ones_f, 1.0)
    memset("pool", ones_b, 1.0)

    def asel(out, in_, pattern, cmp, cm, base=0):
        fw.op("pool", lambda e: e.affine_select(out=out.ap, in_=in_.ap, pattern=pattern, compare_op=cmp, fill=0.0,
                                                base=base, channel_multiplier=cm), reads=[in_], writes=[out])

    asel(ident_f, ones_f, [[-1, 128]], ALU.is_equal, 1)
    cp("dve", ident_b, ident_f)
    asel(MU, ones_f, [[1, 128]], ALU.is_ge, -1)
    asel(MUs, ones_f, [[1, 128]], ALU.is_gt, -1)
    asel(ML, ones_f, [[-1, 128]], ALU.is_ge, 1)
    asel(MLs, ones_f, [[-1, 128]], ALU.is_gt, 1)
    fw.op("pool", lambda e: e.iota(pidx.ap, [[0, 128]], base=0, channel_multiplier=1,
                                   allow_small_or_imprecise_dtypes=True), writes=[pidx])
    mdt = ptmp.rr("p a n -> p (a n)")
    for bi_, bsz in enumerate((16, 32, 64)):
        nb_ = 128 // bsz
        Eb = ptmp[0:8, 0, 0:128]
        asel(Eb, ones_f[0:8, :], [[1, 128]], ALU.is_ge, -bsz, base=0)
        asel(Eb, Eb, [[-1, 128]], ALU.is_ge, bsz, base=bsz - 1)
        ps = fw.psum()
        mm(ps[:, 0:128], Eb[0:nb_, :], Eb[0:nb_, :])
        cp("dve", mdt[:, 256 + bi_ * 128:256 + (bi_ + 1) * 128], ps[:, 0:128])
    md16, md32, md64 = (mdt[:, 256 + i * 128:256 + (i + 1) * 128] for i in range(3))
    cp("dve", BMK[:, 0, :], md16)
    tt("dve", BMK[:, 1, :], md32, md16, ALU.subtract)
    tt("dve", BMK[:, 2, :], md64, md32, ALU.subtract)
    ts("dve", BMK[:, 3, :], md64, -1.0, ALU.mult, 1.0, ALU.add)
    tt("dve", MDS[:, 0, :], md16, MLs, ALU.mult)
    tt("dve", MDS[:, 1, :], md16, MUs, ALU.mult)

    stop_at("c1")
    condt = ARENA_F[0:2, 0:1024]
    load("sp", condt, cond_d)
    act(condt, condt, AF.Silu)
    ps = fw.psum()
    for kc in range(8):
        tr(ps[:, kc * 2:(kc + 1) * 2], condt[:, kc * 128:(kc + 1) * 128], ident_f[0:2, 0:2])
    cp("dve", scT.rr("p k c -> p (k c)"), ps[:, 0:16])

    stop_at("c2")
    bmr = ARENA_F[0:48, 1024:1152]
    bmT = small[:, 0:48]
    wm_slots = [ARENA_F[:, 2048 + i * 4096: 2048 + (i + 1) * 4096].rr("p (k n) -> p k n", k=8) for i in range(2)]
    for l in range(NL):
        load("sp", bmr, b_mod_d[l].rearrange("(b p) -> b p", p=128))
        ps = fw.psum()
        tr(ps[:, 0:48], bmr, ident_f[0:48, 0:48])
        cp("act", bmT, ps[:, 0:48])
        for g in range(12):
            slot = wm_slots[g % 2]
            load("sp", slot, w_mod_d[l].rearrange("(k p) n -> p k n", p=128)[:, :, g * 512:(g + 1) * 512])
            ps = fw.psum()
            for b4 in range(4):
                for kc in range(8):
                    mm(ps[:, b4 * 2:(b4 + 1) * 2], slot[:, kc, b4 * 128:(b4 + 1) * 128], scT[:, kc, :],
                       start=(kc == 0), stop=(kc == 7))
            ps3 = ps[:, 0:8].rr("p (b c) -> p b c", c=2)
            for c in range(2):
                tt("dve", modcol[:, l, 4 * g:4 * g + 4, c], ps3[:, :, c], bmT[:, 4 * g:4 * g + 4], ALU.add)
        for w, blk0 in enumerate((8, 32)):
            ts("dve", mscale[:, l, w], modcol[:, l, blk0:blk0 + 8, :], 1.0, ALU.add, 1.0 / ALPHA, ALU.mult)
    dbg("modcol", modcol.rr("p l b c -> p (l b c)"), [128, NL * 96])

    stop_at("c3")
    POOLW = (2, 4, 8, 16)
    for gi, w in enumerate(POOLW):
        lo = w // 2
        hi = w - 1 - lo
        fw.op("pool", lambda e, gi=gi, lo=lo, hi=hi: e.iota(efix[:, gi, 0:lo].ap, [[1, lo]], base=hi + 1, channel_multiplier=0,
                                                           allow_small_or_imprecise_dtypes=True), writes=[efix[:, gi, 0:lo]])
        if hi > 0:
            fw.op("pool", lambda e, gi=gi, hi=hi, w=w: e.iota(efix[:, gi, 8:8 + hi].ap, [[-1, hi]], base=w - 1, channel_multiplier=0,
                                                              allow_small_or_imprecise_dtypes=True), writes=[efix[:, gi, 8:8 + hi]])
        for (a, n) in ((0, lo), (8, hi)):
            if n > 0:
                recip(efix[:, gi, a:a + n], efix[:, gi, a:a + n])
                ts("dve", efix[:, gi, a:a + n], efix[:, gi, a:a + n], float(w), ALU.mult)

    def sincos(out_sin, out_cos, theta):
        for out, shift in ((out_sin, 0.0), (out_cos, 0.25)):
            t = ptmp[:, 1, :]
            gq = ptmp[:, 2, :]
            ts("dve", t, theta, 1.0 / (2 * PI), ALU.mult, shift, ALU.add)
            cp("dve", ptmpi, t)
            tt("dve", t, t, ptmpi, ALU.subtract)
            fw.op("dve", lambda e, t=t, gq=gq: e.tensor_single_scalar(out=gq.ap, in_=t.ap, scalar=0.5, op=ALU.is_ge),
                  reads=[t], writes=[gq])
            tt("dve", t, t, gq, ALU.subtract)
            fw.op("dve", lambda e, t=t, gq=gq: e.tensor_single_scalar(out=gq.ap, in_=t.ap, scalar=-0.5, op=ALU.is_lt),
                  reads=[t], writes=[gq])
            tt("dve", t, t, gq, ALU.add)
            act(out, t, AF.Sin, scale=2 * PI)

    def pe_consts():
        fw.op("pool", lambda e: e.iota(freq.ap, [[1, 256]], base=0, channel_multiplier=0, allow_small_or_imprecise_dtypes=True),
              writes=[freq])
        act(freq, freq, AF.Exp, scale=-math.log(10000.0) / 256.0)
        fw.op("pool", lambda e: e.iota(pcol[:, 0:1].ap, [[0, 1]], base=0, channel_multiplier=1, allow_small_or_imprecise_dtypes=True),
              writes=[pcol[:, 0:1]])
        fw.op("dve", lambda e: e.tensor_single_scalar(out=pcol[:, 1:2].ap, in_=pcol[:, 0:1].ap, scalar=64.0, op=ALU.is_ge),
              reads=[pcol[:, 0:1]], writes=[pcol[:, 1:2]])
        stt("dve", pcol[:, 2:3], pcol[:, 1:2], -64.0, pcol[:, 0:1], ALU.mult, ALU.add)

        ts("dve", ptmp[:, 0, :], freq, pcol[:, 2:3], ALU.mult)
        sincos(pe_c[:, 0:256], pe_c[:, 256:512], ptmp[:, 0, :])


    stop_at("c5")
    def seg512(n):
        out = []
        a = 0
        while a < n:
            out.append((a, min(512, n - a)))
            a += 512
        return out

    def load_x(pass_id, T):
        src = xp_d if pass_id == 0 else xs_d
        if pass_id == 1:
            pe_consts()
        for t in range(T):
            load("sp", X[:, t, :], src[t * 128:(t + 1) * 128, :])
            if pass_id == 1:
                ts("dve", pcol[:, 3:4], pcol[:, 1:2], float(2 * t), ALU.add)
                ts("dve", ptmp[:, 0, :], freq, pcol[:, 3:4], ALU.mult)
                pe_r = junk.bitcast(F32)
                sincos(pe_r[:, 0:256], pe_r[:, 256:512], ptmp[:, 0, :])
                tt("dve", X[:, t, 0:512], X[:, t, 0:512], pe_r, ALU.add)
                tt("dve", X[:, t, 512:1024], X[:, t, 512:1024], pe_c, ALU.add)
            act(X[:, t, :], X[:, t, :], AF.Identity, scale=ALPHA)
        stop_at("c6")

    def make_ut(UT, l, which, cond, tiles, col0, halo=None):
        shb = 0 if which == 0 else 24
        jobs = [(t, col0 + i * 128, None) for i, t in enumerate(tiles)]
        if halo is not None:
            jobs.append((halo[0], halo[2], halo[1]))
        for (t, c0, hc) in jobs:
            for half in range(2):
                ps = fw.psum()
                for q in range(4):
                    kc = half * 4 + q
                    tr(ps[:, q * 128:(q + 1) * 128], X[:, t, kc * 128:(kc + 1) * 128], ident_f)
                if which == 1:
                    stop_at("u1")
                eng_b = evac_eng()
                for q in range(4):
                    kc = half * 4 + q
                    sc = mscale[:, l, which, kc, cond:cond + 1]
                    sh = modcol[:, l, shb + kc, cond:cond + 1]
                    if hc is None:
                        src = ps[:, q * 128:(q + 1) * 128]
                        dst = UT[:, kc, c0:c0 + 128]
                    else:
                        src = ps[:, q * 128 + hc:q * 128 + hc + 1]
                        dst = UT[:, kc, c0:c0 + 1]
                    if eng_b == "act":
                        act(dst, src, AF.Identity, bias=sh, scale=sc)
                    else:
                        ts("dve", dst, src, sc, ALU.mult, sh, ALU.add)
                    if which == 1:
                        stop_at("u2")
                if which == 1:
                    stop_at("u3")
            if which == 1:
                stop_at("u4")
        if stop == "ut":
            dbg("ut", UT[:, 0, 0:1024], [128, 1024])
            for kc_ in range(8):
                dbg("ut%d" % kc_, UT[:, kc_, 0:256], [128, 256])
            dbg("x0", X[:, 0, :], [128, D])
            dbg("mscale", mscale.rr("p l w k c -> p (l w k c)"), [128, NL * 32])
            raise _Stop()

    def gbcast(l, blk0, cond):
        for half in range(2):
            ps = fw.psum()
            for q in range(4):
                kc = half * 4 + q
                dg = dgs
                ts("dve", dg, ident_f, modcol[:, l, blk0 + kc, cond:cond + 1], ALU.mult)
                mm(ps[:, q * 128:(q + 1) * 128], ones_f, dg)
            cp("act", GB[:, half * 512:(half + 1) * 512], ps)

    def layernorm(l, which, T, last):
        gname, bname = ("ln1_g", "ln1_b") if which == 0 else ("ln2_g", "ln2_b")
        load("sp", LNB[:, 0, :], ln_d[gname][l:l + 1, :].to_broadcast([128, D]))
        load("sp", LNB[:, 1, :], ln_d[bname][l:l + 1, :].to_broadcast([128, D]))
        stop_at("lna")
        if not last:
            act(LNB.rr("p a d -> p (a d)"), LNB.rr("p a d -> p (a d)"), AF.Identity, scale=ALPHA)
        stop_at("lnb")
        st = small[:, 64:72]
        for t in range(T):
            xt = X[:, t, :]
            rsum(st[:, 0:1], xt)
            stop_at("lnc")
            ts("dve", st[:, 1:2], st[:, 0:1], -1.0 / D, ALU.mult)
            act(junk, xt, AF.Square, bias=st[:, 1:2], accum=st[:, 2:3])
            stop_at("lnd")
            ts("dve", st[:, 3:4], st[:, 2:3], 1.0 / D, ALU.mult, EPS, ALU.add)
            act(st[:, 3:4], st[:, 3:4], AF.Sqrt)
            recip(st[:, 4:5], st[:, 3:4])
            ts("dve", xt, xt, st[:, 1:2], ALU.add, st[:, 4:5], ALU.mult)
            tt("dve", xt, xt, LNB[:, 0, :], ALU.mult)
            tt("dve", xt, xt, LNB[:, 1, :], ALU.add)

    wr_i = [0]

    def wslot(nring=3):
        s = WR[:, wr_i[0] % nring, :]
        wr_i[0] += 1
        return s

    WO = WR[:, 2, :]

    def xacc(t, psA, psB):
        tt("dve", X[:, t, 0:512], X[:, t, 0:512], psA, ALU.add)
        tt("dve", X[:, t, 512:1024], X[:, t, 512:1024], psB, ALU.add)

    def ffn(l, pass_id, T, cond):
        cr = ARENA_F[0:44, 0:128]
        for k in range(3):
            load("sp", cr, f_conv_d[l][k].rearrange("(c p) -> c p", p=128))
            ps = fw.psum()
            tr(ps[:, 0:44], cr, ident_f[0:44, 0:44])
            cp("act", convc[:, k * 44:(k + 1) * 44], ps[:, 0:44])
        stop_at("fa")
        gbcast(l, 40, cond)
        stop_at("fb")
        UTs = ARENA[:, 0:8224].rr("p (k n) -> p k n", k=8)
        actT = ARENA[:, 8224:8224 + 22528].rr("p (j n) -> p j n", j=NCH)
        o0 = 8224 + 22528
        hpre = [[ARENA[:, o0 + (2 * i + h) * 1032: o0 + (2 * i + h) * 1032 + 1028] for h in range(2)] for i in range(2)]
        o1 = (o0 + 4 * 1032 + 1) // 2 + 8
        accs = [[ARENA_F[:, o1 + (2 * i + h) * 1024: o1 + (2 * i + h + 1) * 1024] for h in range(2)] for i in range(2)]
        assert (o1 + 4096) <= 22016
        groups = [(0, 8, None)] if pass_id == 0 else [(0, 8, "R"), (8, 8, "L")]
        wup = f_w_up_d[l].rearrange("(k p) n -> p k n", p=128)
        wdn = f_w_down_d[l].rearrange("(c p) n -> p c n", p=128)
        for (t0, nt, hal) in groups:
            ntok = nt * 128
            halo = None
            if hal == "R":
                halo = (t0 + nt, 0, 1026)
            elif hal == "L":
                halo = (t0 - 1, 127, 1)
            make_ut(UTs, l, 1, cond, list(range(t0, t0 + nt)), 2, halo)
            stop_at("f0")
            for i in range(2):
                for h in range(2):
                    if hal != "L":
                        memset("pool", hpre[i][h][:, 0:2], 0.0)
                    if hal != "R":
                        memset("pool", hpre[i][h][:, 1026:1028], 0.0)
            segs = [(2, 512), (514, 512)]
            if hal == "R":
                segs.append((1026, 1))
            if hal == "L":
                segs.append((1, 1))
            jgs = [(j0, min(4, NCH - j0)) for j0 in range(0, NCH, 4)]
            for (j0, nj) in jgs:
                sa = wslot()[:, 0:8 * 128 * nj].rr("p (k n) -> p k n", k=8)
                sg = wslot()[:, 0:8 * 128 * nj].rr("p (k n) -> p k n", k=8)
                load("pool", sa, wup[:, :, j0 * 128:(j0 + nj) * 128])
                load("pool", sg, wup[:, :, DFF + j0 * 128:DFF + (j0 + nj) * 128])
                for jj in range(nj):
                    j = j0 + jj
                    hp = hpre[j % 2]
                    acc = accs[j % 2]
                    for h, sw in enumerate((sa, sg)):
                        cc = h * 22 + j
                        w0 = convc[:, cc:cc + 1]
                        w1 = convc[:, 44 + cc:44 + cc + 1]
                        w2 = convc[:, 88 + cc:88 + cc + 1]
                        for (c0, n) in segs:
                            ps = fw.psum()
                            for kc in range(8):
                                mm(ps[:, 0:n], sw[:, kc, jj * 128:(jj + 1) * 128], UTs[:, kc, c0:c0 + n],
                                   start=(kc == 0), stop=(kc == 7))
                            if n > 1:
                                cp("act", hp[h][:, c0:c0 + n], ps[:, 0:n])
                                act(acc[h][:, c0 - 2:c0 - 2 + n], ps[:, 0:n], AF.Identity, scale=w1)
                            else:
                                cp("act", hp[h][:, c0:c0 + n], ps[:, 0:n])
                        hh = hp[h]
                        if pass_id == 0:
                            a3 = acc[h].rr("p (s t) -> p s t", s=4)
                            h3 = hh[:, 2:1026].rr("p (s t) -> p s t", s=4)
                            stt("dve", a3[:, :, 1:256], h3[:, :, 0:255], w0, a3[:, :, 1:256], ALU.mult, ALU.add)
                            stt("dve", a3[:, :, 0:255], h3[:, :, 1:256], w2, a3[:, :, 0:255], ALU.mult, ALU.add)
                        else:
                            stt("dve", acc[h], hh[:, 1:1025], w0, acc[h], ALU.mult, ALU.add)
                            stt("dve", acc[h], hh[:, 3:1027], w2, acc[h], ALU.mult, ALU.add)
                    act(acc[1], acc[1], AF.Silu)
                    tt("dve", actT[:, j, :], acc[1], acc[0], ALU.mult)
                    stop_at("f0b")
            if l == 0 and t0 == 0:
                dbg("actT%d" % pass_id, actT[:, 0, :], [128, 1024])
            stop_at("f1")
            for q0 in range(0, nt, 4):
                for (j0, nj) in jgs:
                    sd = wslot()[:, 0:1024 * nj].rr("p (c n) -> p c n", c=nj)
                    load("pool", sd, wdn[:, j0:j0 + nj, :])
                    for ti in range(4):
                        for jj in range(nj):
                            j = j0 + jj
                            for hf in range(2):
                                mm(fw.banks[ti * 2 + hf], actT[:, j, (q0 + ti) * 128:(q0 + ti + 1) * 128],
                                   sd[:, jj, hf * 512:(hf + 1) * 512], start=(j == 0), stop=(j == NCH - 1))
                for ti in range(4):
                    t_ = t0 + q0 + ti
                    for hf in range(2):
                        tmpx = accs[ti % 2][hf][:, 0:512]
                        tt("dve", tmpx, fw.banks[ti * 2 + hf], GB[:, hf * 512:(hf + 1) * 512], ALU.mult)
                        tt("dve", X[:, t_, hf * 512:(hf + 1) * 512], X[:, t_, hf * 512:(hf + 1) * 512], tmpx, ALU.add)
                stop_at("f2")

    def gla_mixer(l, pass_id, T, seqs, cond):
        e_ = l // 2
        win = a_w_in_d[e_].rearrange("(k p) n -> p k n", p=128)
        wout = a_w_out_d[e_].rearrange("(c p) n -> p c n", p=128)
        UT = ARENA[:, 0:16384].rr("p (k n) -> p k n", k=8)
        TR = ARENA[:, 16384:39936]
        TRF = ARENA_F[:, 8192:19968]
        make_ut(UT, l, 0, cond, list(range(T)), 0)
        gbcast(l, 16, cond)
        memset("pool", wsm, 0.0)
        load("pool", wsm[:, :, 0:16], win[:, :, 1536:1552])
        load("pool", wsm[:, :, 32:48], win[:, :, 1552:1568])
        load("pool", wg[0:16, :], a_w_gate_d[e_, 0])
        load("pool", wg[32:48, :], a_w_gate_d[e_, 1])
        load("pool", bproj, b_proj_d[e_].rearrange("g c d -> c g d"))
        load("sp", normB, a_norm_d[e_:e_ + 1, :].to_broadcast([128, 128]))
        bg = TRF[0:8, 0:128]
        load("sp", bg[0:8, 0:64], a_b_gate_d[e_].rearrange("z (h d) -> (z h) d", d=64))
        ps = fw.psum()
        tr(ps[0:64, 0:8], bg[0:8, 0:64], ident_f[0:8, 0:8])
        ts("dve", small[0:64, 80:88], ps[0:64, 0:8], -1.0, ALU.mult)
        ps = fw.psum()
        bg2 = TRF[0:4, 128:256]
        load("sp", bg2, b_scale_d[e_].rearrange("(g d) -> g d", d=128))
        tr(ps[:, 0:4], bg2, ident_f[0:4, 0:4])
        cp("act", small[:, 96:100], ps[:, 0:4])
        wo_h = WO.rr("p (c n) -> p c n", c=4)
        load("pool", wo_h, wout[:, 0:4, :])
        for c in range(4):
            tt("dve", wo_h[:, c, :], wo_h[:, c, :], GB, ALU.mult)
        o_f = LNB.rr("p a d -> p (a d)").rr("p (t v) -> p t v", v=128)
        stop_at("g1")

        for si, (t0, nt) in enumerate(seqs):
            L = nt * 128
            c0 = t0 * 128
            qT = TR[0:64, 0:2048]
            kT = TR[0:64, 2048:4096]
            v_tok = TR[:, 4096:6144].rr("p (t v) -> p t v", v=128)
            r_tok = TR[:, 6144:8192].rr("p (t v) -> p t v", v=128)
            lrT = TR[0:64, 8192:10240]
            qe = TR[0:64, 10240:11264]
            ke = TR[0:64, 11264:12288]
            kd = TR[0:64, 12288:13312]
            kd_tok = TR[:, 13312:13824].rr("p (c d) -> p c d", d=64)
            ATb = [TR[:, 13824 + i * 128:13824 + (i + 1) * 128] for i in range(2)]
            ogb = TR[:, 14080:14208]
            oTb = TR[:, 14208:14336]
            fo = 14336 // 2
            cpos = TRF[0:64, fo:fo + 1024]
            tmp = TRF[0:64, fo + 1024:fo + 2048]
            dcol = TRF[0:64, fo + 2048:fo + 2056]
            osum = TRF[:, fo + 2056:fo + 2184]
            ost = TRF[:, fo + 2184:fo + 2192]
            totc = TRF[0:64, fo + 2192:fo + 2200]
            for (a, n) in seg512(L):
                ps = fw.psum()
                for kc in range(8):
                    mm(ps[0:64, 0:n], wsm[:, kc, 0:64], UT[:, kc, c0 + a:c0 + a + n], start=(kc == 0), stop=(kc == 7))
                cp("act", lrT[:, a:a + n], ps[0:64, 0:n])
            stop_at("g2")
            for h in range(4):
                sw = wslot(2)[:, 0:8 * 384].rr("p (k n) -> p k n", k=8)
                load("pool", sw[:, :, 0:64], win[:, :, h * 64:(h + 1) * 64])
                load("pool", sw[:, :, 64:128], win[:, :, 256 + h * 64:256 + (h + 1) * 64])
                load("pool", sw[:, :, 128:256], win[:, :, 512 + h * 128:512 + (h + 1) * 128])
                load("pool", sw[:, :, 256:384], win[:, :, 1024 + h * 128:1024 + (h + 1) * 128])
                stop_at("g2a")
                for (a, n) in seg512(L):
                    ps = fw.psum()
                    for kc in range(8):
                        mm(ps[0:64, 0:n], sw[:, kc, 0:64], UT[:, kc, c0 + a:c0 + a + n], start=(kc == 0), stop=(kc == 7))
                    act(qT[:, a:a + n], ps[0:64, 0:n], AF.Identity, scale=0.125)
                    stop_at("g2b")
                    ps = fw.psum()
                    for kc in range(8):
                        mm(ps[0:64, 0:n], sw[:, kc, 64:128], UT[:, kc, c0 + a:c0 + a + n], start=(kc == 0), stop=(kc == 7))
                    cp("dve", kT[:, a:a + n], ps[0:64, 0:n])
                    stop_at("g2c")
                for t in range(nt):
                    ps = fw.psum()
                    for kc in range(8):
                        mm(ps[:, 0:128], UT[:, kc, c0 + t * 128:c0 + (t + 1) * 128], sw[:, kc, 128:256], start=(kc == 0), stop=(kc == 7))
                    cp("dve", v_tok[:, t, :], ps[:, 0:128])
                    stop_at("g2d")
                    ps = fw.psum()
                    for kc in range(8):
                        mm(ps[:, 0:128], UT[:, kc, c0 + t * 128:c0 + (t + 1) * 128], sw[:, kc, 256:384], start=(kc == 0), stop=(kc == 7))
                    act(r_tok[:, t, :], ps[:, 0:128], AF.Silu)
                    stop_at("g2e")
                stop_at("g3")
                for z in range(2):
                    S = S_f[0:64, z, :]
                    Sb = S_b[0:64, z, :]
                    if pass_id == 0:
                        memset("pool", S, 0.0)
                    else:
                        load("sp", S, sgla_d[(e_ * 2 + z) * 4 + h])
                    cp("act", Sb, S)
                    blocks = [(b0, min(8, nt - b0)) for b0 in range(0, nt, 8)]
                    if z == 1:
                        blocks = blocks[::-1]
                    for (b0, nb) in blocks:
                        n = nb * 128
                        for (a, m) in seg512(n):
                            ps = fw.psum()
                            mm(ps[0:64, 0:m], wg[z * 32:z * 32 + 16, h * 64:(h + 1) * 64],
                               lrT[z * 32:z * 32 + 16, b0 * 128 + a:b0 * 128 + a + m])
                            act(tmp[:, a:a + m], ps[0:64, 0:m], AF.Exp, bias=small[0:64, 80 + z * 4 + h:81 + z * 4 + h], scale=-1.0)
                        act(tmp[:, 0:n], tmp[:, 0:n], AF.Ln, bias=1.0)
                        for c in range(nb):
                            cs = slice(c * 128, (c + 1) * 128)
                            scan(cpos[:, cs], ones_f[0:64, :], tmp[:, cs])
                        if z == 1:
                            tt("dve", tmp[:, 0:n], tmp[:, 0:n], cpos[:, 0:n], ALU.subtract)
                            cp("dve", totc[:, 0:nb], cpos[:, 0:n].rr("p (c i) -> p c i", i=128)[:, :, 127])
                            for c in range(nb):
                                cs = slice(c * 128, (c + 1) * 128)
                                ts("dve", cpos[:, cs], tmp[:, cs], totc[:, c:c + 1], ALU.add)
                        lastc = (lambda c: c * 128 + 127) if z == 0 else (lambda c: c * 128)
                        act(tmp[:, 0:n], cpos[:, 0:n], AF.Exp, scale=-1.0 / 16)
                        tt("dve", qe[:, 0:n], qT[:, b0 * 128:b0 * 128 + n], tmp[:, 0:n], ALU.mult)
                        tmp3 = tmp[:, 0:n].rr("p (c i) -> p c i", i=128)
                        cp("dve", dcol[:, 0:nb], tmp3[:, :, 127 if z == 0 else 0])
                        act(tmp[:, 0:n], cpos[:, 0:n], AF.Exp, scale=1.0 / 16)
                        tt("dve", ke[:, 0:n], kT[:, b0 * 128:b0 * 128 + n], tmp[:, 0:n], ALU.mult)
                        for c in range(nb):
                            cs = slice(c * 128, (c + 1) * 128)
                            ts("dve", tmp[:, cs], cpos[:, cs], cpos[:, lastc(c):lastc(c) + 1], ALU.subtract)
                        act(tmp[:, 0:n], tmp[:, 0:n], AF.Exp, scale=1.0 / 16)
                        tt("dve", kd[:, 0:n], kT[:, b0 * 128:b0 * 128 + n], tmp[:, 0:n], ALU.mult)
                        psk = fw.psum(dt=BF16)
                        for c in range(nb):
                            tr(psk[:, c * 64:(c + 1) * 64], kd[:, c * 128:(c + 1) * 128], ident_b[0:64, 0:64])
                        cp("act", kd_tok[:, 0:nb, :].rr("p c d -> p (c d)"), psk[:, 0:nb * 64])
                        stop_at("g4")
                        corder = range(nb) if z == 0 else range(nb - 1, -1, -1)
                        for c in corder:
                            t = b0 + c
                            cs = slice(c * 128, (c + 1) * 128)
                            ps = fw.psum()
                            mm(ps[:, 0:128], ke[:, cs], qe[:, cs])
                            AT = ATb[c % 2]
                            tt("dve", AT, ps[:, 0:128], MU if z == 0 else ML, ALU.mult)
                            ps2 = fw.psum()
                            mm(ps2[:, 0:128], AT, v_tok[:, t, :], start=True, stop=False)
                            mm(ps2[:, 0:128], qe[:, cs], Sb, start=False, stop=True)
                            ps3 = fw.psum()
                            mm(ps3[0:64, 0:128], kd_tok[:, c, :], v_tok[:, t, :])
                            stt("dve", S, S, dcol[:, c:c + 1], ps3[0:64, 0:128], ALU.mult, ALU.add)
                            cp("act", Sb, S)
                            if z == 0:
                                cp("act", o_f[:, t, :], ps2[:, 0:128])
                            else:
                                tt("dve", osum, ps2[:, 0:128], o_f[:, t, :], ALU.add)
                                act(junk[:, 0:128], osum, AF.Square, accum=ost[:, 0:1])
                                ts("dve", ost[:, 1:2], ost[:, 0:1], 1.0 / 128, ALU.mult, EPS, ALU.add)
                                act(ost[:, 1:2], ost[:, 1:2], AF.Sqrt)
                                recip(ost[:, 2:3], ost[:, 1:2])
                                stt("dve", osum, osum, ost[:, 2:3], normB, ALU.mult, ALU.mult)
                                tt("dve", ogb, osum, r_tok[:, t, :], ALU.mult)
                                pst = fw.psum(dt=BF16)
                                tr(pst[:, 0:128], ogb, ident_b)
                                cp("act", oTb, pst[:, 0:128])
                                psA = fw.psum()
                                psB = fw.psum()
                                mm(psA, oTb, wo_h[:, h, 0:512])
                                mm(psB, oTb, wo_h[:, h, 512:1024])
                                xacc(t0 + t, psA, psB)
                            stop_at("g5")
                        stop_at("g6")
                    if pass_id == 0:
                        s_glob = si
                        load("sp", ngla_d[((s_glob * 2 + e_) * 2 + z) * 4 + h], S)
        wo_p = WO.rr("p (c n) -> p c n", c=4)
        load("pool", wo_p, wout[:, 4:8, :])
        for c in range(4):
            tt("dve", wo_p[:, c, :], wo_p[:, c, :], GB, ALU.mult)
        swp = wslot(2).rr("p (k n) -> p k n", k=8)
        load("pool", swp, win[:, :, 1568:2080])
        for si, (t0, nt) in enumerate(seqs):
            L = nt * 128
            c0 = t0 * 128
            pm = TR[:, 0:8192].rr("p (g n) -> p g n", g=4)
            xpad = TRF[:, 4096:4096 + 2080]
            sA = TRF[:, 6176:6176 + 2080]
            sB = TRF[:, 8256:8256 + 2080]
            pooled = TR[:, 20672:20672 + 2048]
            assert 20672 + 2048 <= 23552 and (8256 + 2080) * 2 <= 20672
            for gi, w in enumerate(POOLW):
                lo = w // 2
                hi = w - 1 - lo
                memset("pool", xpad[:, 0:16], 0.0)
                memset("pool", xpad[:, 16 + L:32 + L], 0.0)
                for (a, n) in seg512(L):
                    ps = fw.psum()
                    for kc in range(8):
                        mm(ps[:, 0:n], swp[:, kc, gi * 128:(gi + 1) * 128], UT[:, kc, c0 + a:c0 + a + n], start=(kc == 0), stop=(kc == 7))
                    cp("act", xpad[:, 16 + a:16 + a + n], ps[:, 0:n])
                src = xpad
                k = 1
                bufs = [sA, sB]
                bi = 0
                while k < w:
                    dst = bufs[bi]
                    bi ^= 1
                    n = L + 32 - 2 * k
                    tt("dve", dst[:, 0:n], src[:, 0:n], src[:, k:k + n], ALU.add)
                    src = dst
                    k *= 2
                wsum = src[:, 16 - lo:16 - lo + L]
                tt("dve", wsum[:, 0:lo], wsum[:, 0:lo], efix[:, gi, 0:lo], ALU.mult)
                if hi > 0:
                    tt("dve", wsum[:, L - hi:L], wsum[:, L - hi:L], efix[:, gi, 8:8 + hi], ALU.mult)
                stt("dve", pooled[:, 0:L], wsum, 1.0 / w, xpad[:, 16:16 + L], ALU.mult, ALU.subtract)
                for (a, n) in seg512(L):
                    ps = fw.psum()
                    mm(ps[:, 0:n], bproj[:, gi, :], pooled[:, a:a + n])
                    act(pm[:, gi, a:a + n], ps[:, 0:n], AF.Identity, scale=small[:, 96 + gi:97 + gi])
            for t in range(nt):
                psA = fw.psum()
                psB = fw.psum()
                for gi in range(4):
                    mm(psA, pm[:, gi, t * 128:(t + 1) * 128], wo_p[:, gi, 0:512], start=(gi == 0), stop=(gi == 3))
                for gi in range(4):
                    mm(psB, pm[:, gi, t * 128:(t + 1) * 128], wo_p[:, gi, 512:1024], start=(gi == 0), stop=(gi == 3))
                xacc(t0 + t, psA, psB)

    def dn_mixer(l, pass_id, T, seqs, cond):
        o_ = l // 2
        win = c_w_in_d[o_].rearrange("(k p) n -> p k n", p=128)
        wout = c_w_out_d[o_].rearrange("(c p) n -> p c n", p=128)
        UT = ARENA[:, 0:16384].rr("p (k n) -> p k n", k=8)
        TR = ARENA[:, 16384:39936]
        TRF = ARENA_F[:, 8192:19968]
        make_ut(UT, l, 0, cond, list(range(T)), 0)
        gbcast(l, 16, cond)
        memset("pool", wsm, 0.0)
        for i, off in enumerate((0, 32, 64, 96)):
            load("pool", wsm[:, :, off:off + 8], win[:, :, 4096 + i * 8:4096 + (i + 1) * 8])
        load("sp", normB, c_norm_d[o_:o_ + 1, :].to_broadcast([128, 128]))
        memset("pool", small[0:40, 104:106], 0.0)
        for z in range(2):
            load("sp", small[z * 32:z * 32 + 8, 104:105], c_a_log_d[o_, z].rearrange("(h o) -> h o", o=1))
            load("sp", small[z * 32:z * 32 + 8, 105:106], c_dt_bias_d[o_, z].rearrange("(h o) -> h o", o=1))
        act(small[0:40, 104:105], small[0:40, 104:105], AF.Exp)
        cr = TRF[0:72, 0:128]
        load("sp", cr, c_conv_d[o_].rearrange("k (c p) -> (k c) p", p=128))
        ps = fw.psum()
        tr(ps[:, 0:72], cr, ident_f[0:72, 0:72])
        cp("act", convc[:, 0:72], ps[:, 0:72])
        o_f = LNB.rr("p a d -> p (a d)").rr("p (t v) -> p t v", v=128)
        wo = [None, None]

        for si, (t0, nt) in enumerate(seqs):
            L = nt * 128
            c0 = t0 * 128
            R1 = TRF[0:128, 0:L]
            R2 = TRF[0:64, L:2 * L]
            bo = 4 * L
            hpre = TR[:, bo:bo + L + 2]
            qT = TR[:, bo + L + 8:bo + 2 * L + 8]
            kT = TR[:, bo + 2 * L + 8:bo + 3 * L + 8]
            k_tok = TR[:, bo + 3 * L + 8:bo + 4 * L + 8].rr("p (t v) -> p t v", v=128)
            v_tok = TR[:, bo + 4 * L + 8:bo + 5 * L + 8].rr("p (t v) -> p t v", v=128)
            z_tok = TR[:, bo + 5 * L + 8:bo + 6 * L + 8].rr("p (t v) -> p t v", v=128)
            so = bo + 6 * L + 8
            NSLOT = 4 if L <= 1024 else 3
            slots = []
            for s_ in range(NSLOT):
                base = 16384 + so + s_ * 2176
                sbk = [ARENA[:, base + i * 128:base + (i + 1) * 128] for i in range(11)]
                fb = (base + 11 * 128) // 2
                fk = [ARENA_F[:, fb + i * 128:fb + (i + 1) * 128] for i in range(3)]
                assert base + 2176 <= 44032
                slots.append((sbk, fk, dncol[:, s_, 0:104], dncol[:, s_, 104:144], dncol[:, s_, 144:160],
                              (fw.banks[2 * s_], fw.banks[2 * s_ + 1])))
            totr = small[0:40, 296:312]
            memset("pool", R1[:, 0:L], 0.0)
            memset("pool", R2[:, 0:L], 0.0)
            for (a, n) in seg512(L):
                ps = fw.psum()
                for kc in range(8):
                    mm(ps[0:128, 0:n], wsm[:, kc, 0:128], UT[:, kc, c0 + a:c0 + a + n], start=(kc == 0), stop=(kc == 7))
                act(R1[0:40, a:a + n], ps[0:40, 0:n], AF.Exp, bias=small[0:40, 105:106])
                act(R1[0:40, a:a + n], R1[0:40, a:a + n], AF.Ln, bias=1.0)
                ts("dve", R2[0:40, a:a + n], R1[0:40, a:a + n], small[0:40, 104:105], ALU.mult)
                act(R1[64:104, a:a + n], ps[64:104, 0:n], AF.Sigmoid)
            for c in range(nt):
                cs = slice(c * 128, (c + 1) * 128)
                scan(R1[0:40, cs], ones_f[0:40, :], R2[0:40, cs])
            tt("dve", R2[32:40, 0:L], R2[32:40, 0:L], R1[32:40, 0:L], ALU.subtract)
            cp("dve", totr[32:40, 0:nt], R1[32:40, 0:L].rr("p (c i) -> p c i", i=128)[:, :, 127])
            for c in range(nt):
                cs = slice(c * 128, (c + 1) * 128)
                ts("dve", R1[32:40, cs], R2[32:40, cs], totr[32:40, c:c + 1], ALU.add)
            for c in range(nt):
                cs = slice(c * 128, (c + 1) * 128)
                ts("dve", R2[0:8, cs], R1[0:8, cs], R1[0:8, c * 128 + 127:c * 128 + 128], ALU.subtract)
                ts("dve", R2[32:40, cs], R1[32:40, cs], R1[32:40, c * 128:c * 128 + 1], ALU.subtract)
            if l == 1 and si == 0:
                dbg("R1_%d" % pass_id, R1[0:104, 0:256], [104, 256])
                dbg("R2_%d" % pass_id, R2[0:40, 0:256], [40, 256])

            for h in range(8):
                if h % 4 == 0:
                    wo_h = WO.rr("p (c n) -> p c n", c=4)
                    load("pool", wo_h, wout[:, h:h + 4, :])
                    for c in range(4):
                        tt("dve", wo_h[:, c, :], wo_h[:, c, :], GB, ALU.mult)
                sw = wslot(2).rr("p (k n) -> p k n", k=8)
                for i in range(4):
                    load("pool", sw[:, :, i * 128:(i + 1) * 128], win[:, :, i * 1024 + h * 128:i * 1024 + (h + 1) * 128])
                memset("pool", hpre[:, 0:1], 0.0)
                memset("pool", hpre[:, L + 1:L + 2], 0.0)
                for i, dstT in enumerate((qT, kT, None)):
                    for (a, n) in seg512(L):
                        ps = fw.psum()
                        for kc in range(8):
                            mm(ps[:, 0:n], sw[:, kc, i * 128:(i + 1) * 128], UT[:, kc, c0 + a:c0 + a + n], start=(kc == 0), stop=(kc == 7))
                        cp("act", hpre[:, 1 + a:1 + a + n], ps[:, 0:n])
                    cc = i * 8 + h
                    for (a, n) in seg512(L):
                        accv = junk.bitcast(F32)[:, 0:n]
                        act(accv, hpre[:, 1 + a:1 + a + n], AF.Identity, scale=convc[:, 24 + cc:25 + cc])
                        stt("dve", accv, hpre[:, a:a + n], convc[:, cc:cc + 1], accv, ALU.mult, ALU.add)
                        stt("dve", accv, hpre[:, 2 + a:2 + a + n], convc[:, 48 + cc:49 + cc], accv, ALU.mult, ALU.add)
                        if dstT is not None:
                            act(dstT[:, a:a + n], accv, AF.Silu)
                            sqv = TR[:, so:so + 512]
                            act(sqv[:, 0:n], dstT[:, a:a + n], AF.Square)
                            ps = fw.psum()
                            mm(ps[:, 0:n], ones_b, sqv[:, 0:n])
                            rn = junk.bitcast(F32)[:, 0:n]
                            ts("dve", rn, ps[:, 0:n], EPS, ALU.add)
                            act(rn, rn, AF.Sqrt)
                            recip(rn, rn)
                            if i == 0:
                                stt("dve", dstT[:, a:a + n], dstT[:, a:a + n], float(128 ** -0.5), rn, ALU.mult, ALU.mult)
                            else:
                                tt("dve", dstT[:, a:a + n], dstT[:, a:a + n], rn, ALU.mult)
                        else:
                            vT = TR[:, so:so + 512]
                            act(vT[:, 0:n], accv, AF.Silu)
                            for tq in range(n // 128):
                                pst = fw.psum(dt=BF16)
                                tr(pst[:, 0:128], vT[:, tq * 128:(tq + 1) * 128], ident_b)
                                cp("dve", v_tok[:, a // 128 + tq, :], pst[:, 0:128])
                for t in range(nt):
                    pst = fw.psum(dt=BF16)
                    tr(pst[:, 0:128], kT[:, t * 128:(t + 1) * 128], ident_b)
                    cp("act", k_tok[:, t, :], pst[:, 0:128])
                    ps = fw.psum()
                    for kc in range(8):
                        mm(ps[:, 0:128], UT[:, kc, c0 + t * 128:c0 + (t + 1) * 128], sw[:, kc, 384:512], start=(kc == 0), stop=(kc == 7))
                    act(z_tok[:, t, :], ps[:, 0:128], AF.Silu)
                if l == 1 and si == 0 and h == 0:
                    dbg("qT_%d" % pass_id, qT[:, 0:256], [128, 256])
                    dbg("kT_%d" % pass_id, kT[:, 0:256], [128, 256])
                    dbg("vtok_%d" % pass_id, v_tok[:, 0, :], [128, 128])

                for z in range(2):
                    S = S_f[:, z, :]
                    Sb = S_b[:, z, :]
                    if pass_id == 0:
                        memset("pool", S, 0.0)
                    else:
                        load("sp", S, sdn_d[(o_ * 2 + z) * 8 + h])
                    cp("act", Sb, S)
                    rb = z * 32 + h
                    fw.op("dve", lambda e, rb=rb, z=z: e.tensor_single_scalar(out=selc[:, z * 128:(z + 1) * 128].ap, in_=pidx.ap,
                                                                           scalar=float(rb), op=ALU.is_equal),
                          reads=[pidx], writes=[selc[:, z * 128:(z + 1) * 128]])
                rec_turn = [0, 0]
                stored = set()

                def chunk_task(z, c, zi, sl):
                    sbk, fk, cols, cold, ccol, bankpair = sl
                    nal = [0]

                    def palloc(dt=F32):
                        bnk = bankpair[nal[0] % 2]
                        nal[0] += 1
                        return bnk if dt == F32 else bnk.bitcast(dt)
                    S = S_f[:, z, :]
                    Sb = S_b[:, z, :]
                    rb = z * 32 + h
                    Msk_s = MLs if z == 0 else MUs
                    Msk_i = ML if z == 0 else MU
                    cs = slice(c * 128, (c + 1) * 128)
                    ps = palloc()
                    tr(ps[:, 0:128], R1[0:128, cs], ident_f)
                    cp("act", cols, ps[:, 0:104])
                    ps = palloc()
                    tr(ps[:, 0:64], R2[0:64, cs], ident_f[0:64, 0:64])
                    cp("dve", cold, ps[:, 0:40])
                    bcol = cols[:, rb:rb + 1]
                    betac = cols[:, 64 + rb:65 + rb]
                    act(ccol[:, 0:1], bcol, AF.Exp, scale=-1.0)
                    tt("dve", ccol[:, 1:2], ccol[:, 0:1], betac, ALU.mult)
                    ts("dve", ccol[:, 2:3], betac, -1.0, ALU.mult)
                    act(ccol[:, 3:4], cold[:, rb:rb + 1], AF.Exp)
                    psBGQ = palloc()
                    psB = psBGQ[:, 0:128]
                    psG = psBGQ[:, 128:256]
                    psQ = psBGQ[:, 256:384]
                    mm(psB[:, 0:128], selc[:, z * 128:(z + 1) * 128], R1[0:64, cs])
                    mm(psG[:, 0:128], kT[:, cs], kT[:, cs])
                    mm(psQ[:, 0:128], qT[:, cs], kT[:, cs])
                    yield
                    EB = fk[1]
                    act(EB, psB[:, 0:128], AF.Exp, scale=-1.0)
                    ld = fk[0]
                    fw.op("dve", lambda e, ld=ld, psB=psB, bcol=bcol: e.tensor_scalar(
                        out=ld.ap, in0=psB[:, 0:128].ap, scalar1=bcol.ap, scalar2=0.0, op0=ALU.subtract, op1=ALU.min),
                        reads=[psB[:, 0:128], bcol, EB], writes=[ld])
                    act(ld, ld, AF.Exp)
                    cp("act", ccol[:, 4:5], EB[:, 127:128] if z == 0 else EB[:, 0:1])
                    yield
                    t1 = fk[2]
                    tt("dve", t1, psG[:, 0:128], ld, ALU.mult)
                    P0 = sbk[0]
                    stt("dve", P0, t1, ccol[:, 2:3], Msk_s, ALU.mult, ALU.mult)
                    P = sbk[1]
                    stt("dve", P, t1, ccol[:, 2:3], MDS[:, z, :], ALU.mult, ALU.mult)
                    yield
                    pst = palloc(BF16)
                    tr(pst[:, 0:128], P, ident_b)
                    PT = sbk[2]
                    cp("act", PT, pst[:, 0:128])
                    TT = sbk[7]
                    tt("dve", TT, ident_b, PT, ALU.add)
                    tt("dve", t1, psQ[:, 0:128], ld, ALU.mult)
                    yield
                    for it in range(3):
                        Pn = sbk[3 + (it % 2) * 2]
                        PTn = sbk[4 + (it % 2) * 2]
                        ps1 = palloc()
                        mm(ps1[:, 0:128], PT, P)
                        if it < 2:
                            ps2 = palloc()
                            mm(ps2[:, 0:128], P, PT)
                        yield
                        cp("act", Pn, ps1[:, 0:128])
                        if it < 2:
                            cp("dve", PTn, ps2[:, 0:128])
                        yield
                        ps3 = palloc()
                        mm(ps3[:, 0:128], Pn, TT)
                        yield
                        TTn = sbk[8] if TT is sbk[7] else sbk[7]
                        tt("dve", TTn, ps3[:, 0:128], TT, ALU.add)
                        P, PT, TT = Pn, PTn, TTn
                        yield
                    for lv in range(3):
                        Bk = sbk[4]
                        tt("pool", Bk, P0, BMK[:, 1 + lv, :], ALU.mult)
                        pst = palloc(BF16)
                        tr(pst[:, 0:128], TT, ident_b)
                        psz = palloc()
                        mm(psz[:, 0:128], Bk, TT)
                        yield
                        Tn = sbk[3]
                        cp("act", Tn, pst[:, 0:128])
                        Zb = sbk[5]
                        cp("dve", Zb, psz[:, 0:128])
                        yield
                        psw = palloc()
                        mm(psw[:, 0:128], Tn, Zb)
                        yield
                        TTn = sbk[8] if TT is sbk[7] else sbk[7]
                        tt("dve", TTn, psw[:, 0:128], TT, ALU.add)
                        TT = TTn
                        yield
                    aq = sbk[2]
                    tt("dve", aq, t1, Msk_i, ALU.mult)
                    pst = palloc(BF16)
                    tr(pst[:, 0:128], aq, ident_b)
                    aqT = sbk[9]
                    cp("act", aqT, pst[:, 0:128])
                    qdT = sbk[10]
                    tt("dve", qdT, qT[:, cs], EB, ALU.mult)
                    kbe = sbk[3]
                    vb = sbk[4]
                    kdk = sbk[6]
                    ts("dve", kbe, k_tok[:, c, :], ccol[:, 1:2], ALU.mult)
                    act(vb, v_tok[:, c, :], AF.Identity, scale=betac)
                    act(kdk, k_tok[:, c, :], AF.Identity, scale=ccol[:, 3:4])
                    yield
                    psU = palloc()
                    mm(psU[:, 0:128], TT, vb)
                    psW = palloc()
                    mm(psW[:, 0:128], kbe, TT)
                    yield
                    u = fk[2]
                    cp("act", u, psU[:, 0:128])
                    wT = sbk[1]
                    cp("dve", wT, psW[:, 0:128])
                    yield
                    while rec_turn[z] != zi:
                        yield
                    psS = palloc()
                    mm(psS[:, 0:128], wT, Sb)
                    yield
                    vnew = sbk[2]
                    tt("dve", vnew, u, psS[:, 0:128], ALU.subtract)
                    yield
                    psO = palloc()
                    mm(psO[:, 0:128], qdT, Sb, start=True, stop=False)
                    mm(psO[:, 0:128], aqT, vnew, start=False, stop=True)
                    psN = palloc()
                    mm(psN[:, 0:128], kdk, vnew)
                    yield
                    stt("dve", S, S, ccol[:, 4:5], psN[:, 0:128], ALU.mult, ALU.add)
                    cp("act", Sb, S)
                    rec_turn[z] += 1
                    t = c
                    second = (z == 1 and 2 * t < nt) or (z == 0 and 2 * t >= nt)
                    if not second:
                        cp("act", o_f[:, t, :], psO[:, 0:128])
                        stored.add(t)
                    else:
                        while t not in stored:
                            yield
                        osum = fk[0]
                        ost = ccol[:, 8:16]
                        tt("dve", osum, psO[:, 0:128], o_f[:, t, :], ALU.add)
                        act(junk[:, 0:128], osum, AF.Square, accum=ost[:, 0:1])
                        ts("dve", ost[:, 1:2], ost[:, 0:1], 1.0 / 128, ALU.mult, EPS, ALU.add)
                        act(ost[:, 1:2], ost[:, 1:2], AF.Sqrt)
                        recip(ost[:, 2:3], ost[:, 1:2])
                        stt("dve", osum, osum, ost[:, 2:3], normB, ALU.mult, ALU.mult)
                        yield
                        ogb = sbk[9]
                        tt("dve", ogb, osum, z_tok[:, t, :], ALU.mult)
                        pst = palloc(BF16)
                        tr(pst[:, 0:128], ogb, ident_b)
                        yield
                        oTb = sbk[10]
                        cp("act", oTb, pst[:, 0:128])
                        psA = palloc()
                        psBk = palloc()
                        mm(psA, oTb, wo_h[:, h % 4, 0:512])
                        mm(psBk, oTb, wo_h[:, h % 4, 512:1024])
                        yield
                        xacc(t0 + t, psA, psBk)

                items = []
                for i in range(nt):
                    items.append((0, i, i))
                    items.append((1, nt - 1 - i, i))
                active = []
                free = list(range(len(slots)))
                qi = 0
                while qi < len(items) or active:
                    while free and qi < len(items):
                        z_, c_, zi_ = items[qi]
                        qi += 1
                        s_ = free.pop(0)
                        active.append([chunk_task(z_, c_, zi_, slots[s_]), s_])
                    for a_ in list(active):
                        try:
                            next(a_[0])
                        except StopIteration:
                            active.remove(a_)
                            free.append(a_[1])
                if pass_id == 0:
                    for z in range(2):
                        load("sp", ndn_d[((si * 2 + o_) * 2 + z) * 8 + h], S_f[:, z, :])

    passes = [(0, 8, [(0, 2), (2, 2), (4, 2), (6, 2)], 0), (1, 16, [(0, 16)], 1)]

    def _main():
        for (pass_id, T, seqs, cond) in passes:
            mark("P%d load" % pass_id)
            load_x(pass_id, T)
            if pass_id == 1:
                dbg("x0s", X[:, 0, :], [128, D])
            for l in range(NL):
                mark("P%d L%d mixer" % (pass_id, l))
                if l % 2 == 0:
                    gla_mixer(l, pass_id, T, seqs, cond)
                else:
                    dn_mixer(l, pass_id, T, seqs, cond)
                mark("P%d L%d ln1" % (pass_id, l))
                dbg("xmix%d_%d" % (l, pass_id), X[:, 0, :], [128, D])
                if stop == "mix%d_%d" % (l, pass_id):
                    raise _Stop()
                layernorm(l, 0, T, False)
                dbg("xln%d_%d" % (l, pass_id), X[:, 0, :], [128, D])
                stop_at("ln%d_%d" % (l, pass_id))
                mark("P%d L%d ffn" % (pass_id, l))
                ffn(l, pass_id, T, cond)
                mark("P%d L%d ln2" % (pass_id, l))
                dbg("xffn%d_%d" % (l, pass_id), X[:, 0, :], [128, D])
                stop_at("ffn%d_%d" % (l, pass_id))
                layernorm(l, 1, T, l == NL - 1)
                dbg("xl%d_%d" % (l, pass_id), X[:, 0, :], [128, D])
                if stop == "l%d_%d" % (l, pass_id):
                    raise _Stop()
            mark("P%d store" % pass_id)
            dst = yp_d if pass_id == 0 else ys_d
            for t in range(T):
                load("sp", dst[t * 128:(t + 1) * 128, :], X[:, t, :])

    try:
        _main()
    except _Stop:
        pass
    fw.emit()
    return nc, fw, dbg_d


_CACHE = {}


def _inputs_per_core(inp, core):
    b = core % 2
    m = {}
    m["xp"] = np.ascontiguousarray(inp["x_prompt"][core * 4:(core + 1) * 4].reshape(1024, D))
    m["xs"] = np.ascontiguousarray(inp["x_sample"][b])
    m["cond2"] = np.ascontiguousarray(np.stack([inp["c_ctx"], inp["c"][b]], 0))
    m["sgla"] = np.ascontiguousarray(inp["state_gla"][b].reshape(16, 64, 128))
    m["sdn"] = np.ascontiguousarray(inp["state_dn"][b].reshape(32, 128, 128))
    for k in ("w_mod", "b_mod", "ln1_g", "ln1_b", "ln2_g", "ln2_b", "a_w_in", "a_w_gate", "a_b_gate", "a_norm",
              "b_proj", "b_scale", "a_w_out", "c_w_in", "c_conv", "c_a_log", "c_dt_bias", "c_norm", "c_w_out",
              "f_w_up", "f_conv", "f_w_down"):
        m[k] = np.ascontiguousarray(inp[k])
    return m


def kernel(**inputs):
    inp = {k: np.asarray(v, dtype=np.float32) for k, v in inputs.items()}
    if "nc" not in _CACHE:
        _CACHE["nc"] = build_program()[0]
    nc = _CACHE["nc"]
    n = 8
    in_maps = [_inputs_per_core(inp, c) for c in range(n)]
    res = run_bass_kernel_spmd(nc, in_maps, core_ids=list(range(n)))
    R = res.results
    y_prompt = np.concatenate([R[c]["yp"].reshape(4, 256, D) for c in range(n)], 0)
    y_sample = np.stack([R[0]["ys"], R[1]["ys"]], 0)
    ngla = np.concatenate([R[c]["ngla"].reshape(4, 2, 2, 4, 64, 128) for c in range(n)], 0)
    ndn = np.concatenate([R[c]["ndn"].reshape(4, 2, 2, 8, 128, 128) for c in range(n)], 0)
    return (y_prompt.astype(np.float32), y_sample.astype(np.float32), ngla.astype(np.float32), ndn.astype(np.float32))
```

```python
import math
import numpy as np
from concourse.bass_utils import run_bass_kernel_spmd
import concourse.bass as bass
import concourse.mybir as mybir

F32 = mybir.dt.float32
BF16 = mybir.dt.bfloat16
I32 = mybir.dt.int32
AF = mybir.ActivationFunctionType
ALU = mybir.AluOpType
AX = mybir.AxisListType

CELL = 256
_DT_SIZE = {F32: 4, BF16: 2, I32: 4}


class Region:
    def __init__(self, fw, name, handle, nbytes, cell=CELL):
        self.fw = fw
        self.name = name
        self.h = handle
        self.cell = cell
        self.ncell = (nbytes + cell - 1) // cell
        self.w = [None] * self.ncell
        self.r = [dict() for _ in range(self.ncell)]


class V:
    def __init__(self, region, ap):
        self.region = region
        self.ap = ap
        self._cells = None

    def __getitem__(self, key):
        return V(self.region, self.ap[key])

    def rr(self, pattern_, **kw):
        return V(self.region, self.ap.rearrange(pattern_, **kw))

    def bitcast(self, dt):
        return V(self.region, self.ap.bitcast(dt))

    def bc(self, shape):
        return V(self.region, self.ap.to_broadcast(shape))

    @property
    def shape(self):
        return self.ap.shape

    def cells(self):
        if self._cells is None:
            ap = self.ap
            esz = _DT_SIZE[ap.dtype]
            dims = list(ap.ap)[1:]
            base = int(ap.offset) if not isinstance(ap.offset, int) else ap.offset
            pstep = list(ap.ap)[0][0]
            if pstep > 0:
                base = base % pstep
            base_b = base * esz
            cs = set()
            CELL = self.region.cell
            dims = [(s, n) for (s, n) in dims if n > 1 or True]
            if not dims:
                dims = [(1, 1)]
            *outer, (ls, ln) = dims
            if ls in (0, 1):
                run = (esz * (ln if ls == 1 else 1))
                inner_iter = [0]
            else:
                run = esz
                inner_iter = [i * ls * esz for i in range(ln)]
            offs = [0]
            for (s, n) in outer:
                if s == 0:
                    continue
                offs = [o + i * s * esz for o in offs for i in range(n)]
            for o in offs:
                for ii in inner_iter:
                    a = base_b + o + ii
                    for c in range(a // CELL, (a + run - 1) // CELL + 1):
                        cs.add(c)
            self._cells = sorted(cs)
            assert self._cells[-1] < self.region.ncell, (self.region.name, self._cells[-1], self.region.ncell, ap)
        return self._cells


class Op:
    __slots__ = ("eng", "fn", "waits", "signal", "semval", "dma_sem", "is_dma")

    def __init__(self, eng, fn):
        self.eng = eng
        self.fn = fn
        self.waits = []
        self.signal = False
        self.semval = None
        self.dma_sem = None
        self.is_dma = False


ENGS = ("pe", "dve", "act", "pool", "sp")
N_DMA_SEMS = 12


class FW:
    def __init__(self, nc):
        self.nc = nc
        self.ops = {e: [] for e in ENGS}
        self.regions = []
        self.dma_rr = {"sp": 0, "pool": 0, "act": 0}
        self.dma_last = {}
        self._ctx = []
        self.psum_ptr = 0
        self.nops = 0

    def sbuf(self, name, shape, dt):
        g = self.nc.sbuf_tensor(name, list(shape), dt)
        h = g.__enter__()
        self._ctx.append(g)
        nb = int(np.prod(shape[1:])) * _DT_SIZE[dt]
        reg = Region(self, name, h, nb)
        self.regions.append(reg)
        return V(reg, h[:] if hasattr(h, "__getitem__") else h.ap())

    def psum_banks(self):
        self.banks = []
        for i in range(8):
            g = self.nc.psum_tensor(f"psb{i}", [128, 512], F32)
            h = g.__enter__()
            self._ctx.append(g)
            reg = Region(self, f"psb{i}", h, 2048, cell=2048)
            self.banks.append(V(reg, h[:]))

    def psum(self, ncols=512, parts=128, dt=F32):
        b = self.psum_ptr
        self.psum_ptr = (b + 1) % 8
        bank = self.banks[b] if dt == F32 else self.banks[b].bitcast(dt)
        return bank[0:parts, 0:ncols]

    def _deps(self, op, reads, writes):
        deps = {}
        for v in reads:
            reg = v.region
            for c in v.cells():
                w = reg.w[c]
                if w is not None:
                    deps[id(w)] = w
        for v in writes:
            reg = v.region
            for c in v.cells():
                w = reg.w[c]
                if w is not None:
                    deps[id(w)] = w
                for t in reg.r[c].values():
                    deps[id(t)] = t
        for t in deps.values():
            if t is op:
                continue
            if t.eng == "pe" and op.eng == "pe" and not t.is_dma and not op.is_dma:
                continue
            op.waits.append(t)
            t.signal = True
        for v in reads:
            reg = v.region
            key = op.dma_sem if op.is_dma else op.eng
            for c in v.cells():
                reg.r[c][key] = op
        for v in writes:
            reg = v.region
            for c in v.cells():
                reg.w[c] = op
                reg.r[c] = {}

    def op(self, eng, fn, reads=(), writes=()):
        if getattr(self, "halted", False):
            return None
        o = Op(eng, fn)
        self._deps(o, reads, writes)
        self.ops[eng].append(o)
        self.nops += 1
        return o

    def dma(self, queue, out, in_, reads=(), writes=(), **kw):
        if getattr(self, "halted", False):
            return None
        o = Op(queue, None)
        o.is_dma = True
        o.signal = True
        oap = out.ap if isinstance(out, V) else out
        iap = in_.ap if isinstance(in_, V) else in_
        rd = list(reads) + ([in_] if isinstance(in_, V) else [])
        wr = list(writes) + ([out] if isinstance(out, V) else [])
        k = self.dma_rr[queue]
        self.dma_rr[queue] = (k + 1) % N_DMA_SEMS
        o.dma_sem = (queue, k)
        prev = self.dma_last.get((queue, k))
        self._deps(o, rd, wr)
        if prev is not None:
            o.waits.append(prev)
        self.dma_last[(queue, k)] = o
        o.fn = lambda e: e.dma_start(out=oap, in_=iap, **kw)
        self.ops[queue].append(o)
        self.nops += 1
        return o

    def emit(self):
        nc = self.nc
        sems = {}
        semctx = []
        for e in ENGS:
            g = nc.semaphore(f"s_{e}")
            sems[e] = g.__enter__()
            semctx.append(g)
        dsems = {}
        for q in ("sp", "pool"):
            for k in range(N_DMA_SEMS):
                g = nc.semaphore(f"d_{q}{k}")
                dsems[(q, k)] = g.__enter__()
                semctx.append(g)
        for e in ENGS:
            cnt = 0
            for o in self.ops[e]:
                if o.is_dma:
                    continue
                if o.signal:
                    cnt += 1
                    o.semval = cnt
            self.maxsem = max(getattr(self, "maxsem", 0), cnt)
        dcnt = {}
        for e in ENGS:
            for o in self.ops[e]:
                if o.is_dma:
                    dcnt[o.dma_sem] = dcnt.get(o.dma_sem, 0) + 16
                    o.semval = dcnt[o.dma_sem]

        def semof(t):
            return dsems[t.dma_sem] if t.is_dma else sems[t.eng]

        def run(engname, engobj):
            known = {}
            for o in self.ops[engname]:
                need = {}
                for t in o.waits:
                    s = t.dma_sem if t.is_dma else t.eng
                    if t.semval > need.get(s, (0, None))[0]:
                        need[s] = (t.semval, t)
                for s, (val, t) in need.items():
                    if known.get(s, 0) >= val:
                        continue
                    engobj.wait_ge(semof(t), val)
                    known[s] = val
                ins = o.fn(engobj)
                if o.is_dma:
                    ins.then_inc(dsems[o.dma_sem], 16)
                elif o.signal:
                    ins.then_inc(sems[engname], 1)
            if engname in ("sp", "pool"):
                for k in range(N_DMA_SEMS):
                    if dcnt.get((engname, k), 0) > 0:
                        engobj.wait_ge(dsems[(engname, k)], dcnt[(engname, k)])

        with nc.Block() as block:
            @block.tensor
            def _(e):
                run("pe", e)

            @block.vector
            def _(e):
                run("dve", e)

            @block.scalar
            def _(e):
                run("act", e)

            @block.gpsimd
            def _(e):
                run("pool", e)

            @block.sync
            def _(e):
                run("sp", e)
        for g in reversed(semctx):
            g.__exit__(None, None, None)
        for g in reversed(self._ctx):
            g.__exit__(None, None, None)

D = 1024
NL = 4
DFF = 2816
NCH = DFF // 128
ALPHA = float(8 ** 0.25)
EPS = 1e-6
KC = 8
A_IN = 2080
C_IN = 4128
PI = float(np.pi)

DBG_SPECS = {}


class _Stop(Exception):
    pass


def build_program(debug=(), stop=None):
    nc = bass.Bass("TRN2", target_bir_lowering=False)
    fw = FW(nc)

    def din(name, shape):
        return nc.dram_tensor(name, list(shape), F32, kind="ExternalInput").ap()

    def dout(name, shape):
        return nc.dram_tensor(name, list(shape), F32, kind="ExternalOutput").ap()

    xp_d = din("xp", [1024, D])
    xs_d = din("xs", [2048, D])
    cond_d = din("cond2", [2, D])
    sgla_d = din("sgla", [16, 64, 128])
    sdn_d = din("sdn", [32, 128, 128])
    w_mod_d = din("w_mod", [NL, D, 6 * D])
    b_mod_d = din("b_mod", [NL, 6 * D])
    ln_d = {k: din(k, [NL, D]) for k in ("ln1_g", "ln1_b", "ln2_g", "ln2_b")}
    a_w_in_d = din("a_w_in", [2, D, A_IN])
    a_w_gate_d = din("a_w_gate", [2, 2, 16, 256])
    a_b_gate_d = din("a_b_gate", [2, 2, 256])
    a_norm_d = din("a_norm", [2, 128])
    b_proj_d = din("b_proj", [2, 4, 128, 128])
    b_scale_d = din("b_scale", [2, 512])
    a_w_out_d = din("a_w_out", [2, D, D])
    c_w_in_d = din("c_w_in", [2, D, C_IN])
    c_conv_d = din("c_conv", [2, 3, 3072])
    c_a_log_d = din("c_a_log", [2, 2, 8])
    c_dt_bias_d = din("c_dt_bias", [2, 2, 8])
    c_norm_d = din("c_norm", [2, 128])
    c_w_out_d = din("c_w_out", [2, D, D])
    f_w_up_d = din("f_w_up", [NL, D, 2 * DFF])
    f_conv_d = din("f_conv", [NL, 3, 2 * DFF])
    f_w_down_d = din("f_w_down", [NL, DFF, D])

    yp_d = dout("yp", [1024, D])
    ys_d = dout("ys", [2048, D])
    ngla_d = dout("ngla", [64, 64, 128])
    ndn_d = dout("ndn", [128, 128, 128])
    dbg_d = {}

    def A(v):
        return v.ap if isinstance(v, V) else v

    def rds(*xs):
        return [x for x in xs if isinstance(x, V)]

    def mm(out, lhsT, rhs, start=True, stop=True):
        fw.op("pe", lambda e: e.matmul(out.ap, lhsT=lhsT.ap, rhs=rhs.ap, start=start, stop=stop),
              reads=[lhsT, rhs], writes=[out])

    def tr(out, in_, idn):
        fw.op("pe", lambda e: e.transpose(out=out.ap, in_=in_.ap, identity=idn.ap),
              reads=[in_, idn], writes=[out])

    def act(out, in_, func, bias=None, scale=None, accum=None):
        kw = {}
        if bias is not None:
            kw["bias"] = A(bias)
        if scale is not None:
            kw["scale"] = A(scale)
        if accum is not None:
            kw["accum_out"] = accum.ap
        fw.op("act", lambda e: e.activation(out=out.ap, in_=in_.ap, func=func, **kw),
              reads=rds(in_, bias, scale), writes=rds(out, accum))

    def ts(eng, out, in0, s1, op0, s2=None, op1=None):
        kw = {"op1": op1} if op1 is not None else {}
        fw.op(eng, lambda e: e.tensor_scalar(out=out.ap, in0=in0.ap, scalar1=A(s1),
                                             scalar2=(A(s2) if s2 is not None else None), op0=op0, **kw),
              reads=rds(in0, s1, s2), writes=[out])

    def tt(eng, out, in0, in1, op):
        fw.op(eng, lambda e: e.tensor_tensor(out=out.ap, in0=in0.ap, in1=in1.ap, op=op),
              reads=[in0, in1], writes=[out])

    def stt(eng, out, in0, scalar, in1, op0, op1):
        fw.op(eng, lambda e: e.scalar_tensor_tensor(out=out.ap, in0=in0.ap, scalar=A(scalar), in1=in1.ap,
                                                    op0=op0, op1=op1),
              reads=rds(in0, scalar, in1), writes=[out])

    def cp(eng, out, in_):
        if eng == "act":
            fw.op("act", lambda e: e.copy(out=out.ap, in_=in_.ap), reads=[in_], writes=[out])
        else:
            fw.op(eng, lambda e: e.tensor_copy(out=out.ap, in_=in_.ap), reads=[in_], writes=[out])

    def memset(eng, out, val):
        fw.op(eng, lambda e: e.memset(out.ap, val), writes=[out])

    def scan(out, d0, d1, init=0.0):
        fw.op("dve", lambda e: e.tensor_tensor_scan(out=out.ap, data0=d0.ap, data1=d1.ap, initial=init,
                                                    op0=ALU.mult, op1=ALU.add),
              reads=[d0, d1], writes=[out])

    def recip(out, in_):
        fw.op("dve", lambda e: e.reciprocal(out=out.ap, in_=in_.ap), reads=[in_], writes=[out])

    def rsum(out, in_):
        fw.op("dve", lambda e: e.reduce_sum(out=out.ap, in_=in_.ap, axis=AX.X), reads=[in_], writes=[out])

    def load(q, out, src, **kw):
        fw.dma(q, out, src, **kw)

    def dbg(name, v, shape):
        if name in debug:
            d = dout("dbg_" + name, list(shape))
            dbg_d[name] = d
            fw.dma("sp" if v.ap.dtype == F32 else "pool", d, v)

    def stop_at(name):
        if stop == name:
            fw.halted = True

    fw.marks = []

    def mark(name):
        fw.marks.append((name, len(fw.ops["dve"]), len(fw.ops["pe"])))

    rr = [0]

    def evac_eng():
        rr[0] ^= 1
        return "act" if rr[0] else "dve"

    fw.psum_banks()
    X = fw.sbuf("X", [128, 16, D], F32)
    ARENA = fw.sbuf("ARENA", [128, 44032], BF16)
    ARENA_F = ARENA.bitcast(F32)
    ARENA_I = ARENA.bitcast(I32)
    WR = fw.sbuf("WR", [128, 3, 4096], BF16)
    LNB = fw.sbuf("LNB", [128, 2, D], F32)
    GB = fw.sbuf("GB", [128, D], F32)
    ones_f = fw.sbuf("ones_f", [128, 128], F32)
    ident_f = fw.sbuf("ident_f", [128, 128], F32)
    ident_b = fw.sbuf("ident_b", [128, 128], BF16)
    ones_b = fw.sbuf("ones_b", [128, 128], BF16)
    MU = fw.sbuf("MU", [128, 128], F32)
    MUs = fw.sbuf("MUs", [128, 128], F32)
    ML = fw.sbuf("ML", [128, 128], F32)
    MLs = fw.sbuf("MLs", [128, 128], F32)
    pidx = fw.sbuf("pidx", [64, 128], F32)
    BMK = fw.sbuf("BMK", [128, 4, 128], BF16)
    MDS = fw.sbuf("MDS", [128, 2, 128], BF16)
    selc = fw.sbuf("selc", [64, 256], F32)
    dncol = fw.sbuf("dncol", [128, 4, 160], F32)
    modcol = fw.sbuf("modcol", [128, NL, 48, 2], F32)
    mscale = fw.sbuf("mscale", [128, NL, 2, 8, 2], F32)
    scT = fw.sbuf("scT", [128, 8, 2], F32)
    small = fw.sbuf("small", [128, 320], F32)
    S_f = fw.sbuf("S_f", [128, 2, 128], F32)
    S_b = fw.sbuf("S_b", [128, 2, 128], BF16)
    wsm = fw.sbuf("wsm", [128, 8, 128], BF16)
    wg = fw.sbuf("wg", [48, 256], BF16)
    bproj = fw.sbuf("bproj", [128, 4, 128], BF16)
    convc = fw.sbuf("convc", [128, 160], F32)
    normB = fw.sbuf("normB", [128, 128], F32)
    junk = fw.sbuf("junk", [128, D], BF16)
    pe_c = ARENA_F[:, 0:512]
    freq = ARENA_F[:, 512:768]
    ptmp = ARENA_F[:, 768:1536].rr("p (a n) -> p a n", a=3)
    ptmpi = ARENA_I[:, 1536:1792]
    pcol = fw.sbuf("pcol", [128, 8], F32)
    dgs = fw.sbuf("dgs", [128, 128], F32)
    efix = fw.sbuf("efix", [128, 4, 16], F32)

    fw.sbuf_free = nc.sbuf_bytes_remaining
    memset("pool", ones_f, 1.0)
    memset("pool", ones_b, 1.0)

    def asel(out, in_, pattern, cmp, cm, base=0):
        fw.op("pool", lambda e: e.affine_select(out=out.ap, in_=in_.ap, pattern=pattern, compare_op=cmp, fill=0.0,
                                                base=base, channel_multiplier=cm), reads=[in_], writes=[out])

    asel(ident_f, ones_f, [[-1, 128]], ALU.is_equal, 1)
    cp("dve", ident_b, ident_f)
    asel(MU, ones_f, [[1, 128]], ALU.is_ge, -1)
    asel(MUs, ones_f, [[1, 128]], ALU.is_gt, -1)
    asel(ML, ones_f, [[-1, 128]], ALU.is_ge, 1)
    asel(MLs, ones_f, [[-1, 128]], ALU.is_gt, 1)
    fw.op("pool", lambda e: e.iota(pidx.ap, [[0, 128]], base=0, channel_multiplier=1,
                                   allow_small_or_imprecise_dtypes=True), writes=[pidx])
    mdt = ptmp.rr("p a n -> p (a n)")
    for bi_, bsz in enumerate((16, 32, 64)):
        nb_ = 128 // bsz
        Eb = ptmp[0:8, 0, 0:128]
        asel(Eb, ones_f[0:8, :], [[1, 128]], ALU.is_ge, -bsz, base=0)
        asel(Eb, Eb, [[-1, 128]], ALU.is_ge, bsz, base=bsz - 1)
        ps = fw.psum()
        mm(ps[:, 0:128], Eb[0:nb_, :], Eb[0:nb_, :])
        cp("dve", mdt[:, 256 + bi_ * 128:256 + (bi_ + 1) * 128], ps[:, 0:128])
    md16, md32, md64 = (mdt[:, 256 + i * 128:256 + (i + 1) * 128] for i in range(3))
    cp("dve", BMK[:, 0, :], md16)
    tt("dve", BMK[:, 1, :], md32, md16, ALU.subtract)
    tt("dve", BMK[:, 2, :], md64, md32, ALU.subtract)
    ts("dve", BMK[:, 3, :], md64, -1.0, ALU.mult, 1.0, ALU.add)
    tt("dve", MDS[:, 0, :], md16, MLs, ALU.mult)
    tt("dve", MDS[:, 1, :], md16, MUs, ALU.mult)

    stop_at("c1")
    condt = ARENA_F[0:2, 0:1024]
    load("sp", condt, cond_d)
    act(condt, condt, AF.Silu)
    ps = fw.psum()
    for kc in range(8):
        tr(ps[:, kc * 2:(kc + 1) * 2], condt[:, kc * 128:(kc + 1) * 128], ident_f[0:2, 0:2])
    cp("dve", scT.rr("p k c -> p (k c)"), ps[:, 0:16])

    stop_at("c2")
    bmr = ARENA_F[0:48, 1024:1152]
    bmT = small[:, 0:48]
    wm_slots = [ARENA_F[:, 2048 + i * 4096: 2048 + (i + 1) * 4096].rr("p (k n) -> p k n", k=8) for i in range(2)]
    for l in range(NL):
        load("sp", bmr, b_mod_d[l].rearrange("(b p) -> b p", p=128))
        ps = fw.psum()
        tr(ps[:, 0:48], bmr, ident_f[0:48, 0:48])
        cp("act", bmT, ps[:, 0:48])
        for g in range(12):
            slot = wm_slots[g % 2]
            load("sp", slot, w_mod_d[l].rearrange("(k p) n -> p k n", p=128)[:, :, g * 512:(g + 1) * 512])
            ps = fw.psum()
            for b4 in range(4):
                for kc in range(8):
                    mm(ps[:, b4 * 2:(b4 + 1) * 2], slot[:, kc, b4 * 128:(b4 + 1) * 128], scT[:, kc, :],
                       start=(kc == 0), stop=(kc == 7))
            ps3 = ps[:, 0:8].rr("p (b c) -> p b c", c=2)
            for c in range(2):
                tt("dve", modcol[:, l, 4 * g:4 * g + 4, c], ps3[:, :, c], bmT[:, 4 * g:4 * g + 4], ALU.add)
        for w, blk0 in enumerate((8, 32)):
            ts("dve", mscale[:, l, w], modcol[:, l, blk0:blk0 + 8, :], 1.0, ALU.add, 1.0 / ALPHA, ALU.mult)
    dbg("modcol", modcol.rr("p l b c -> p (l b c)"), [128, NL * 96])

    stop_at("c3")
    POOLW = (2, 4, 8, 16)
    for gi, w in enumerate(POOLW):
        lo = w // 2
        hi = w - 1 - lo
        fw.op("pool", lambda e, gi=gi, lo=lo, hi=hi: e.iota(efix[:, gi, 0:lo].ap, [[1, lo]], base=hi + 1, channel_multiplier=0,
                                                           allow_small_or_imprecise_dtypes=True), writes=[efix[:, gi, 0:lo]])
        if hi > 0:
            fw.op("pool", lambda e, gi=gi, hi=hi, w=w: e.iota(efix[:, gi, 8:8 + hi].ap, [[-1, hi]], base=w - 1, channel_multiplier=0,
                                                              allow_small_or_imprecise_dtypes=True), writes=[efix[:, gi, 8:8 + hi]])
        for (a, n) in ((0, lo), (8, hi)):
            if n > 0:
                recip(efix[:, gi, a:a + n], efix[:, gi, a:a + n])
                ts("dve", efix[:, gi, a:a + n], efix[:, gi, a:a + n], float(w), ALU.mult)

    def sincos(out_sin, out_cos, theta):
        for out, shift in ((out_sin, 0.0), (out_cos, 0.25)):
            t = ptmp[:, 1, :]
            gq = ptmp[:, 2, :]
            ts("dve", t, theta, 1.0 / (2 * PI), ALU.mult, shift, ALU.add)
            cp("dve", ptmpi, t)
            tt("dve", t, t, ptmpi, ALU.subtract)
            fw.op("dve", lambda e, t=t, gq=gq: e.tensor_single_scalar(out=gq.ap, in_=t.ap, scalar=0.5, op=ALU.is_ge),
                  reads=[t], writes=[gq])
            tt("dve", t, t, gq, ALU.subtract)
            fw.op("dve", lambda e, t=t, gq=gq: e.tensor_single_scalar(out=gq.ap, in_=t.ap, scalar=-0.5, op=ALU.is_lt),
                  reads=[t], writes=[gq])
            tt("dve", t, t, gq, ALU.add)
            act(out, t, AF.Sin, scale=2 * PI)

    def pe_consts():
        fw.op("pool", lambda e: e.iota(freq.ap, [[1, 256]], base=0, channel_multiplier=0, allow_small_or_imprecise_dtypes=True),
              writes=[freq])
        act(freq, freq, AF.Exp, scale=-math.log(10000.0) / 256.0)
        fw.op("pool", lambda e: e.iota(pcol[:, 0:1].ap, [[0, 1]], base=0, channel_multiplier=1, allow_small_or_imprecise_dtypes=True),
              writes=[pcol[:, 0:1]])
        fw.op("dve", lambda e: e.tensor_single_scalar(out=pcol[:, 1:2].ap, in_=pcol[:, 0:1].ap, scalar=64.0, op=ALU.is_ge),
              reads=[pcol[:, 0:1]], writes=[pcol[:, 1:2]])
        stt("dve", pcol[:, 2:3], pcol[:, 1:2], -64.0, pcol[:, 0:1], ALU.mult, ALU.add)

        ts("dve", ptmp[:, 0, :], freq, pcol[:, 2:3], ALU.mult)
        sincos(pe_c[:, 0:256], pe_c[:, 256:512], ptmp[:, 0, :])


    stop_at("c5")
    def seg512(n):
        out = []
        a = 0
        while a < n:
            out.append((a, min(512, n - a)))
            a += 512
        return out

    def load_x(pass_id, T):
        src = xp_d if pass_id == 0 else xs_d
        if pass_id == 1:
            pe_consts()
        for t in range(T):
            load("sp", X[:, t, :], src[t * 128:(t + 1) * 128, :])
            if pass_id == 1:
                ts("dve", pcol[:, 3:4], pcol[:, 1:2], float(2 * t), ALU.add)
                ts("dve", ptmp[:, 0, :], freq, pcol[:, 3:4], ALU.mult)
                pe_r = junk.bitcast(F32)
                sincos(pe_r[:, 0:256], pe_r[:, 256:512], ptmp[:, 0, :])
                tt("dve", X[:, t, 0:512], X[:, t, 0:512], pe_r, ALU.add)
                tt("dve", X[:, t, 512:1024], X[:, t, 512:1024], pe_c, ALU.add)
            act(X[:, t, :], X[:, t, :], AF.Identity, scale=ALPHA)
        stop_at("c6")

    def make_ut(UT, l, which, cond, tiles, col0, halo=None):
        shb = 0 if which == 0 else 24
        jobs = [(t, col0 + i * 128, None) for i, t in enumerate(tiles)]
        if halo is not None:
            jobs.append((halo[0], halo[2], halo[1]))
        for (t, c0, hc) in jobs:
            for half in range(2):
                ps = fw.psum()
                for q in range(4):
                    kc = half * 4 + q
                    tr(ps[:, q * 128:(q + 1) * 128], X[:, t, kc * 128:(kc + 1) * 128], ident_f)
                if which == 1:
                    stop_at("u1")
                eng_b = evac_eng()
                for q in range(4):
                    kc = half * 4 + q
                    sc = mscale[:, l, which, kc, cond:cond + 1]
                    sh = modcol[:, l, shb + kc, cond:cond + 1]
                    if hc is None:
                        src = ps[:, q * 128:(q + 1) * 128]
                        dst = UT[:, kc, c0:c0 + 128]
                    else:
                        src = ps[:, q * 128 + hc:q * 128 + hc + 1]
                        dst = UT[:, kc, c0:c0 + 1]
                    if eng_b == "act":
                        act(dst, src, AF.Identity, bias=sh, scale=sc)
                    else:
                        ts("dve", dst, src, sc, ALU.mult, sh, ALU.add)
                    if which == 1:
                        stop_at("u2")
                if which == 1:
                    stop_at("u3")
            if which == 1:
                stop_at("u4")
        if stop == "ut":
            dbg("ut", UT[:, 0, 0:1024], [128, 1024])
            for kc_ in range(8):
                dbg("ut%d" % kc_, UT[:, kc_, 0:256], [128, 256])
            dbg("x0", X[:, 0, :], [128, D])
            dbg("mscale", mscale.rr("p l w k c -> p (l w k c)"), [128, NL * 32])
            raise _Stop()

    def gbcast(l, blk0, cond):
        for half in range(2):
            ps = fw.psum()
            for q in range(4):
                kc = half * 4 + q
                dg = dgs
                ts("dve", dg, ident_f, modcol[:, l, blk0 + kc, cond:cond + 1], ALU.mult)
                mm(ps[:, q * 128:(q + 1) * 128], ones_f, dg)
            cp("act", GB[:, half * 512:(half + 1) * 512], ps)

    def layernorm(l, which, T, last):
        gname, bname = ("ln1_g", "ln1_b") if which == 0 else ("ln2_g", "ln2_b")
        load("sp", LNB[:, 0, :], ln_d[gname][l:l + 1, :].to_broadcast([128, D]))
        load("sp", LNB[:, 1, :], ln_d[bname][l:l + 1, :].to_broadcast([128, D]))
        stop_at("lna")
        if not last:
            act(LNB.rr("p a d -> p (a d)"), LNB.rr("p a d -> p (a d)"), AF.Identity, scale=ALPHA)
        stop_at("lnb")
        st = small[:, 64:72]
        for t in range(T):
            xt = X[:, t, :]
            rsum(st[:, 0:1], xt)
            stop_at("lnc")
            ts("dve", st[:, 1:2], st[:, 0:1], -1.0 / D, ALU.mult)
            act(junk, xt, AF.Square, bias=st[:, 1:2], accum=st[:, 2:3])
            stop_at("lnd")
            act(st[:, 3:4], st[:, 2:3], AF.Ln, bias=EPS, scale=1.0 / D)
            act(st[:, 4:5], st[:, 3:4], AF.Exp, scale=-0.5)
            ts("dve", xt, xt, st[:, 1:2], ALU.add, st[:, 4:5], ALU.mult)
            tt("dve", xt, xt, LNB[:, 0, :], ALU.mult)
            tt("dve", xt, xt, LNB[:, 1, :], ALU.add)

    wr_i = [0]

    def wslot(nring=3):
        s = WR[:, wr_i[0] % nring, :]
        wr_i[0] += 1
        return s

    WO = WR[:, 2, :]

    def xacc(t, psA, psB):
        tt("dve", X[:, t, 0:512], X[:, t, 0:512], psA, ALU.add)
        tt("dve", X[:, t, 512:1024], X[:, t, 512:1024], psB, ALU.add)

    def ffn(l, pass_id, T, cond):
        cr = ARENA_F[0:44, 0:128]
        for k in range(3):
            load("sp", cr, f_conv_d[l][k].rearrange("(c p) -> c p", p=128))
            ps = fw.psum()
            tr(ps[:, 0:44], cr, ident_f[0:44, 0:44])
            cp("act", convc[:, k * 44:(k + 1) * 44], ps[:, 0:44])
        stop_at("fa")
        gbcast(l, 40, cond)
        stop_at("fb")
        UTs = ARENA[:, 0:8224].rr("p (k n) -> p k n", k=8)
        actT = ARENA[:, 8224:8224 + 22528].rr("p (j n) -> p j n", j=NCH)
        o0 = 8224 + 22528
        hpre = [[ARENA[:, o0 + (2 * i + h) * 1032: o0 + (2 * i + h) * 1032 + 1028] for h in range(2)] for i in range(2)]
        o1 = (o0 + 4 * 1032 + 1) // 2 + 8
        accs = [[ARENA_F[:, o1 + (2 * i + h) * 1024: o1 + (2 * i + h + 1) * 1024] for h in range(2)] for i in range(2)]
        assert (o1 + 4096) <= 22016
        groups = [(0, 8, None)] if pass_id == 0 else [(0, 8, "R"), (8, 8, "L")]
        wup = f_w_up_d[l].rearrange("(k p) n -> p k n", p=128)
        wdn = f_w_down_d[l].rearrange("(c p) n -> p c n", p=128)
        for (t0, nt, hal) in groups:
            ntok = nt * 128
            halo = None
            if hal == "R":
                halo = (t0 + nt, 0, 1026)
            elif hal == "L":
                halo = (t0 - 1, 127, 1)
            make_ut(UTs, l, 1, cond, list(range(t0, t0 + nt)), 2, halo)
            stop_at("f0")
            for i in range(2):
                for h in range(2):
                    if hal != "L":
                        memset("pool", hpre[i][h][:, 0:2], 0.0)
                    if hal != "R":
                        memset("pool", hpre[i][h][:, 1026:1028], 0.0)
            segs = [(2, 512), (514, 512)]
            if hal == "R":
                segs.append((1026, 1))
            if hal == "L":
                segs.append((1, 1))
            jgs = [(j0, min(4, NCH - j0)) for j0 in range(0, NCH, 4)]
            for (j0, nj) in jgs:
                sa = wslot()[:, 0:8 * 128 * nj].rr("p (k n) -> p k n", k=8)
                sg = wslot()[:, 0:8 * 128 * nj].rr("p (k n) -> p k n", k=8)
                load("pool", sa, wup[:, :, j0 * 128:(j0 + nj) * 128])
                load("pool", sg, wup[:, :, DFF + j0 * 128:DFF + (j0 + nj) * 128])
                for jj in range(nj):
                    j = j0 + jj
                    hp = hpre[j % 2]
                    acc = accs[j % 2]
                    for h, sw in enumerate((sa, sg)):
                        cc = h * 22 + j
                        w0 = convc[:, cc:cc + 1]
                        w1 = convc[:, 44 + cc:44 + cc + 1]
                        w2 = convc[:, 88 + cc:88 + cc + 1]
                        for (c0, n) in segs:
                            ps = fw.psum()
                            for kc in range(8):
                                mm(ps[:, 0:n], sw[:, kc, jj * 128:(jj + 1) * 128], UTs[:, kc, c0:c0 + n],
                                   start=(kc == 0), stop=(kc == 7))
                            if n > 1:
                                cp("act", hp[h][:, c0:c0 + n], ps[:, 0:n])
                                act(acc[h][:, c0 - 2:c0 - 2 + n], ps[:, 0:n], AF.Identity, scale=w1)
                            else:
                                cp("act", hp[h][:, c0:c0 + n], ps[:, 0:n])
                        hh = hp[h]
                        if pass_id == 0:
                            a3 = acc[h].rr("p (s t) -> p s t", s=4)
                            h3 = hh[:, 2:1026].rr("p (s t) -> p s t", s=4)
                            stt("dve", a3[:, :, 1:256], h3[:, :, 0:255], w0, a3[:, :, 1:256], ALU.mult, ALU.add)
                            stt("dve", a3[:, :, 0:255], h3[:, :, 1:256], w2, a3[:, :, 0:255], ALU.mult, ALU.add)
                        else:
                            stt("dve", acc[h], hh[:, 1:1025], w0, acc[h], ALU.mult, ALU.add)
                            stt("dve", acc[h], hh[:, 3:1027], w2, acc[h], ALU.mult, ALU.add)
                    act(acc[1], acc[1], AF.Silu)
                    tt("dve", actT[:, j, :], acc[1], acc[0], ALU.mult)
                    stop_at("f0b")
            if l == 0 and t0 == 0:
                dbg("actT%d" % pass_id, actT[:, 0, :], [128, 1024])
            stop_at("f1")
            for q0 in range(0, nt, 4):
                for (j0, nj) in jgs:
                    sd = wslot()[:, 0:1024 * nj].rr("p (c n) -> p c n", c=nj)
                    load("pool", sd, wdn[:, j0:j0 + nj, :])
                    for ti in range(4):
                        for jj in range(nj):
                            j = j0 + jj
                            for hf in range(2):
                                mm(fw.banks[ti * 2 + hf], actT[:, j, (q0 + ti) * 128:(q0 + ti + 1) * 128],
                                   sd[:, jj, hf * 512:(hf + 1) * 512], start=(j == 0), stop=(j == NCH - 1))
                for ti in range(4):
                    t_ = t0 + q0 + ti
                    for hf in range(2):
                        tmpx = accs[ti % 2][hf][:, 0:512]
                        tt("dve", tmpx, fw.banks[ti * 2 + hf], GB[:, hf * 512:(hf + 1) * 512], ALU.mult)
                        tt("dve", X[:, t_, hf * 512:(hf + 1) * 512], X[:, t_, hf * 512:(hf + 1) * 512], tmpx, ALU.add)
                stop_at("f2")

    def gla_mixer(l, pass_id, T, seqs, cond):
        e_ = l // 2
        win = a_w_in_d[e_].rearrange("(k p) n -> p k n", p=128)
        wout = a_w_out_d[e_].rearrange("(c p) n -> p c n", p=128)
        UT = ARENA[:, 0:16384].rr("p (k n) -> p k n", k=8)
        TR = ARENA[:, 16384:39936]
        TRF = ARENA_F[:, 8192:19968]
        make_ut(UT, l, 0, cond, list(range(T)), 0)
        gbcast(l, 16, cond)
        memset("pool", wsm, 0.0)
        load("pool", wsm[:, :, 0:16], win[:, :, 1536:1552])
        load("pool", wsm[:, :, 32:48], win[:, :, 1552:1568])
        load("pool", wg[0:16, :], a_w_gate_d[e_, 0])
        load("pool", wg[32:48, :], a_w_gate_d[e_, 1])
        load("pool", bproj, b_proj_d[e_].rearrange("g c d -> c g d"))
        load("sp", normB, a_norm_d[e_:e_ + 1, :].to_broadcast([128, 128]))
        bg = TRF[0:8, 0:128]
        load("sp", bg[0:8, 0:64], a_b_gate_d[e_].rearrange("z (h d) -> (z h) d", d=64))
        ps = fw.psum()
        tr(ps[0:64, 0:8], bg[0:8, 0:64], ident_f[0:8, 0:8])
        ts("dve", small[0:64, 80:88], ps[0:64, 0:8], -1.0, ALU.mult)
        ps = fw.psum()
        bg2 = TRF[0:4, 128:256]
        load("sp", bg2, b_scale_d[e_].rearrange("(g d) -> g d", d=128))
        tr(ps[:, 0:4], bg2, ident_f[0:4, 0:4])
        cp("act", small[:, 96:100], ps[:, 0:4])
        wo_h = WO.rr("p (c n) -> p c n", c=4)
        load("pool", wo_h, wout[:, 0:4, :])
        for c in range(4):
            tt("dve", wo_h[:, c, :], wo_h[:, c, :], GB, ALU.mult)
        o_f = LNB.rr("p a d -> p (a d)").rr("p (t v) -> p t v", v=128)
        stop_at("g1")

        for si, (t0, nt) in enumerate(seqs):
            L = nt * 128
            c0 = t0 * 128
            qT = TR[0:64, 0:2048]
            kT = TR[0:64, 2048:4096]
            v_tok = TR[:, 4096:6144].rr("p (t v) -> p t v", v=128)
            r_tok = TR[:, 6144:8192].rr("p (t v) -> p t v", v=128)
            lrT = TR[0:64, 8192:10240]
            qe = TR[0:64, 10240:11264]
            ke = TR[0:64, 11264:12288]
            kd = TR[0:64, 12288:13312]
            kd_tok = TR[:, 13312:13824].rr("p (c d) -> p c d", d=64)
            ATb = [TR[:, 13824 + i * 128:13824 + (i + 1) * 128] for i in range(2)]
            ogb = TR[:, 14080:14208]
            oTb = TR[:, 14208:14336]
            fo = 14336 // 2
            cpos = TRF[0:64, fo:fo + 1024]
            tmp = TRF[0:64, fo + 1024:fo + 2048]
            dcol = TRF[0:64, fo + 2048:fo + 2056]
            osum = TRF[:, fo + 2056:fo + 2184]
            ost = TRF[:, fo + 2184:fo + 2192]
            totc = TRF[0:64, fo + 2192:fo + 2200]
            for (a, n) in seg512(L):
                ps = fw.psum()
                for kc in range(8):
                    mm(ps[0:64, 0:n], wsm[:, kc, 0:64], UT[:, kc, c0 + a:c0 + a + n], start=(kc == 0), stop=(kc == 7))
                cp("act", lrT[:, a:a + n], ps[0:64, 0:n])
            stop_at("g2")
            for h in range(4):
                sw = wslot(2)[:, 0:8 * 384].rr("p (k n) -> p k n", k=8)
                load("pool", sw[:, :, 0:64], win[:, :, h * 64:(h + 1) * 64])
                load("pool", sw[:, :, 64:128], win[:, :, 256 + h * 64:256 + (h + 1) * 64])
                load("pool", sw[:, :, 128:256], win[:, :, 512 + h * 128:512 + (h + 1) * 128])
                load("pool", sw[:, :, 256:384], win[:, :, 1024 + h * 128:1024 + (h + 1) * 128])
                stop_at("g2a")
                for (a, n) in seg512(L):
                    ps = fw.psum()
                    for kc in range(8):
                        mm(ps[0:64, 0:n], sw[:, kc, 0:64], UT[:, kc, c0 + a:c0 + a + n], start=(kc == 0), stop=(kc == 7))
                    act(qT[:, a:a + n], ps[0:64, 0:n], AF.Identity, scale=0.125)
                    stop_at("g2b")
                    ps = fw.psum()
                    for kc in range(8):
                        mm(ps[0:64, 0:n], sw[:, kc, 64:128], UT[:, kc, c0 + a:c0 + a + n], start=(kc == 0), stop=(kc == 7))
                    cp("dve", kT[:, a:a + n], ps[0:64, 0:n])
                    stop_at("g2c")
                for t in range(nt):
                    ps = fw.psum()
                    for kc in range(8):
                        mm(ps[:, 0:128], UT[:, kc, c0 + t * 128:c0 + (t + 1) * 128], sw[:, kc, 128:256], start=(kc == 0), stop=(kc == 7))
                    cp("dve", v_tok[:, t, :], ps[:, 0:128])
                    stop_at("g2d")
                    ps = fw.psum()
                    for kc in range(8):
                        mm(ps[:, 0:128], UT[:, kc, c0 + t * 128:c0 + (t + 1) * 128], sw[:, kc, 256:384], start=(kc == 0), stop=(kc == 7))
                    act(r_tok[:, t, :], ps[:, 0:128], AF.Silu)
                    stop_at("g2e")
                stop_at("g3")
                for z in range(2):
                    S = S_f[0:64, z, :]
                    Sb = S_b[0:64, z, :]
                    if pass_id == 0:
                        memset("pool", S, 0.0)
                    else:
                        load("sp", S, sgla_d[(e_ * 2 + z) * 4 + h])
                    cp("act", Sb, S)
                    blocks = [(b0, min(8, nt - b0)) for b0 in range(0, nt, 8)]
                    if z == 1:
                        blocks = blocks[::-1]
                    for (b0, nb) in blocks:
                        n = nb * 128
                        for (a, m) in seg512(n):
                            ps = fw.psum()
                            mm(ps[0:64, 0:m], wg[z * 32:z * 32 + 16, h * 64:(h + 1) * 64],
                               lrT[z * 32:z * 32 + 16, b0 * 128 + a:b0 * 128 + a + m])
                            act(tmp[:, a:a + m], ps[0:64, 0:m], AF.Exp, bias=small[0:64, 80 + z * 4 + h:81 + z * 4 + h], scale=-1.0)
                        act(tmp[:, 0:n], tmp[:, 0:n], AF.Ln, bias=1.0)
                        for c in range(nb):
                            cs = slice(c * 128, (c + 1) * 128)
                            scan(cpos[:, cs], ones_f[0:64, :], tmp[:, cs])
                        if z == 1:
                            tt("dve", tmp[:, 0:n], tmp[:, 0:n], cpos[:, 0:n], ALU.subtract)
                            cp("dve", totc[:, 0:nb], cpos[:, 0:n].rr("p (c i) -> p c i", i=128)[:, :, 127])
                            for c in range(nb):
                                cs = slice(c * 128, (c + 1) * 128)
                                ts("dve", cpos[:, cs], tmp[:, cs], totc[:, c:c + 1], ALU.add)
                        lastc = (lambda c: c * 128 + 127) if z == 0 else (lambda c: c * 128)
                        act(tmp[:, 0:n], cpos[:, 0:n], AF.Exp, scale=-1.0 / 16)
                        tt("dve", qe[:, 0:n], qT[:, b0 * 128:b0 * 128 + n], tmp[:, 0:n], ALU.mult)
                        tmp3 = tmp[:, 0:n].rr("p (c i) -> p c i", i=128)
                        cp("dve", dcol[:, 0:nb], tmp3[:, :, 127 if z == 0 else 0])
                        act(tmp[:, 0:n], cpos[:, 0:n], AF.Exp, scale=1.0 / 16)
                        tt("dve", ke[:, 0:n], kT[:, b0 * 128:b0 * 128 + n], tmp[:, 0:n], ALU.mult)
                        for c in range(nb):
                            cs = slice(c * 128, (c + 1) * 128)
                            ts("dve", tmp[:, cs], cpos[:, cs], cpos[:, lastc(c):lastc(c) + 1], ALU.subtract)
                        act(tmp[:, 0:n], tmp[:, 0:n], AF.Exp, scale=1.0 / 16)
                        tt("dve", kd[:, 0:n], kT[:, b0 * 128:b0 * 128 + n], tmp[:, 0:n], ALU.mult)
                        psk = fw.psum(dt=BF16)
                        for c in range(nb):
                            tr(psk[:, c * 64:(c + 1) * 64], kd[:, c * 128:(c + 1) * 128], ident_b[0:64, 0:64])
                        cp("act", kd_tok[:, 0:nb, :].rr("p c d -> p (c d)"), psk[:, 0:nb * 64])
                        stop_at("g4")
                        corder = range(nb) if z == 0 else range(nb - 1, -1, -1)
                        for c in corder:
                            t = b0 + c
                            cs = slice(c * 128, (c + 1) * 128)
                            ps = fw.psum()
                            mm(ps[:, 0:128], ke[:, cs], qe[:, cs])
                            AT = ATb[c % 2]
                            tt("dve", AT, ps[:, 0:128], MU if z == 0 else ML, ALU.mult)
                            ps2 = fw.psum()
                            mm(ps2[:, 0:128], AT, v_tok[:, t, :], start=True, stop=False)
                            mm(ps2[:, 0:128], qe[:, cs], Sb, start=False, stop=True)
                            ps3 = fw.psum()
                            mm(ps3[0:64, 0:128], kd_tok[:, c, :], v_tok[:, t, :])
                            stt("dve", S, S, dcol[:, c:c + 1], ps3[0:64, 0:128], ALU.mult, ALU.add)
                            cp("act", Sb, S)
                            if z == 0:
                                cp("act", o_f[:, t, :], ps2[:, 0:128])
                            else:
                                tt("dve", osum, ps2[:, 0:128], o_f[:, t, :], ALU.add)
                                act(junk[:, 0:128], osum, AF.Square, accum=ost[:, 0:1])
                                act(ost[:, 1:2], ost[:, 0:1], AF.Ln, bias=EPS, scale=1.0 / 128)
                                act(ost[:, 2:3], ost[:, 1:2], AF.Exp, scale=-0.5)
                                stt("dve", osum, osum, ost[:, 2:3], normB, ALU.mult, ALU.mult)
                                tt("dve", ogb, osum, r_tok[:, t, :], ALU.mult)
                                pst = fw.psum(dt=BF16)
                                tr(pst[:, 0:128], ogb, ident_b)
                                cp("act", oTb, pst[:, 0:128])
                                psA = fw.psum()
                                psB = fw.psum()
                                mm(psA, oTb, wo_h[:, h, 0:512])
                                mm(psB, oTb, wo_h[:, h, 512:1024])
                                xacc(t0 + t, psA, psB)
                            stop_at("g5")
                        stop_at("g6")
                    if pass_id == 0:
                        s_glob = si
                        load("sp", ngla_d[((s_glob * 2 + e_) * 2 + z) * 4 + h], S)
        wo_p = WO.rr("p (c n) -> p c n", c=4)
        load("pool", wo_p, wout[:, 4:8, :])
        for c in range(4):
            tt("dve", wo_p[:, c, :], wo_p[:, c, :], GB, ALU.mult)
        swp = wslot(2).rr("p (k n) -> p k n", k=8)
        load("pool", swp, win[:, :, 1568:2080])
        for si, (t0, nt) in enumerate(seqs):
            L = nt * 128
            c0 = t0 * 128
            pm = TR[:, 0:8192].rr("p (g n) -> p g n", g=4)
            xpad = TRF[:, 4096:4096 + 2080]
            sA = TRF[:, 6176:6176 + 2080]
            sB = TRF[:, 8256:8256 + 2080]
            pooled = TR[:, 20672:20672 + 2048]
            assert 20672 + 2048 <= 23552 and (8256 + 2080) * 2 <= 20672
            for gi, w in enumerate(POOLW):
                lo = w // 2
                hi = w - 1 - lo
                memset("pool", xpad[:, 0:16], 0.0)
                memset("pool", xpad[:, 16 + L:32 + L], 0.0)
                for (a, n) in seg512(L):
                    ps = fw.psum()
                    for kc in range(8):
                        mm(ps[:, 0:n], swp[:, kc, gi * 128:(gi + 1) * 128], UT[:, kc, c0 + a:c0 + a + n], start=(kc == 0), stop=(kc == 7))
                    cp("act", xpad[:, 16 + a:16 + a + n], ps[:, 0:n])
                src = xpad
                k = 1
                bufs = [sA, sB]
                bi = 0
                while k < w:
                    dst = bufs[bi]
                    bi ^= 1
                    n = L + 32 - 2 * k
                    tt("dve", dst[:, 0:n], src[:, 0:n], src[:, k:k + n], ALU.add)
                    src = dst
                    k *= 2
                wsum = src[:, 16 - lo:16 - lo + L]
                tt("dve", wsum[:, 0:lo], wsum[:, 0:lo], efix[:, gi, 0:lo], ALU.mult)
                if hi > 0:
                    tt("dve", wsum[:, L - hi:L], wsum[:, L - hi:L], efix[:, gi, 8:8 + hi], ALU.mult)
                stt("dve", pooled[:, 0:L], wsum, 1.0 / w, xpad[:, 16:16 + L], ALU.mult, ALU.subtract)
                for (a, n) in seg512(L):
                    ps = fw.psum()
                    mm(ps[:, 0:n], bproj[:, gi, :], pooled[:, a:a + n])
                    act(pm[:, gi, a:a + n], ps[:, 0:n], AF.Identity, scale=small[:, 96 + gi:97 + gi])
            for t in range(nt):
                psA = fw.psum()
                psB = fw.psum()
                for gi in range(4):
                    mm(psA, pm[:, gi, t * 128:(t + 1) * 128], wo_p[:, gi, 0:512], start=(gi == 0), stop=(gi == 3))
                for gi in range(4):
                    mm(psB, pm[:, gi, t * 128:(t + 1) * 128], wo_p[:, gi, 512:1024], start=(gi == 0), stop=(gi == 3))
                xacc(t0 + t, psA, psB)

    def dn_mixer(l, pass_id, T, seqs, cond):
        o_ = l // 2
        win = c_w_in_d[o_].rearrange("(k p) n -> p k n", p=128)
        wout = c_w_out_d[o_].rearrange("(c p) n -> p c n", p=128)
        UT = ARENA[:, 0:16384].rr("p (k n) -> p k n", k=8)
        TR = ARENA[:, 16384:39936]
        TRF = ARENA_F[:, 8192:19968]
        make_ut(UT, l, 0, cond, list(range(T)), 0)
        gbcast(l, 16, cond)
        memset("pool", wsm, 0.0)
        for i, off in enumerate((0, 32, 64, 96)):
            load("pool", wsm[:, :, off:off + 8], win[:, :, 4096 + i * 8:4096 + (i + 1) * 8])
        load("sp", normB, c_norm_d[o_:o_ + 1, :].to_broadcast([128, 128]))
        memset("pool", small[0:40, 104:106], 0.0)
        for z in range(2):
            load("sp", small[z * 32:z * 32 + 8, 104:105], c_a_log_d[o_, z].rearrange("(h o) -> h o", o=1))
            load("sp", small[z * 32:z * 32 + 8, 105:106], c_dt_bias_d[o_, z].rearrange("(h o) -> h o", o=1))
        act(small[0:40, 104:105], small[0:40, 104:105], AF.Exp)
        cr = TRF[0:72, 0:128]
        load("sp", cr, c_conv_d[o_].rearrange("k (c p) -> (k c) p", p=128))
        ps = fw.psum()
        tr(ps[:, 0:72], cr, ident_f[0:72, 0:72])
        cp("act", convc[:, 0:72], ps[:, 0:72])
        o_f = LNB.rr("p a d -> p (a d)").rr("p (t v) -> p t v", v=128)
        wo = [None, None]

        for si, (t0, nt) in enumerate(seqs):
            L = nt * 128
            c0 = t0 * 128
            R1 = TRF[0:128, 0:L]
            R2 = TRF[0:64, L:2 * L]
            bo = 4 * L
            hpre = TR[:, bo:bo + L + 2]
            qT = TR[:, bo + L + 8:bo + 2 * L + 8]
            kT = TR[:, bo + 2 * L + 8:bo + 3 * L + 8]
            k_tok = TR[:, bo + 3 * L + 8:bo + 4 * L + 8].rr("p (t v) -> p t v", v=128)
            v_tok = TR[:, bo + 4 * L + 8:bo + 5 * L + 8].rr("p (t v) -> p t v", v=128)
            z_tok = TR[:, bo + 5 * L + 8:bo + 6 * L + 8].rr("p (t v) -> p t v", v=128)
            so = bo + 6 * L + 8
            NSLOT = 4
            slots = []
            for s_ in range(NSLOT):
                base = 16384 + so + s_ * 2176
                if base + 2176 <= 44032:
                    sbk = [ARENA[:, base + i * 128:base + (i + 1) * 128] for i in range(11)]
                    fb = (base + 11 * 128) // 2
                    fk = [ARENA_F[:, fb + i * 128:fb + (i + 1) * 128] for i in range(3)]
                else:
                    assert s_ == 3 and L + 2 >= 11 * 128
                    hb = 16384 + bo
                    sbk = [ARENA[:, hb + i * 128:hb + (i + 1) * 128] for i in range(11)]
                    jf = junk.bitcast(F32)
                    fk = [jf[:, 128 + i * 128:128 + (i + 1) * 128] for i in range(3)]
                slots.append((sbk, fk, dncol[:, s_, 0:104], dncol[:, s_, 104:144], dncol[:, s_, 144:160],
                              (fw.banks[2 * s_], fw.banks[2 * s_ + 1])))
            totr = small[0:40, 296:312]
            memset("pool", R1[:, 0:L], 0.0)
            memset("pool", R2[:, 0:L], 0.0)
            for (a, n) in seg512(L):
                ps = fw.psum()
                for kc in range(8):
                    mm(ps[0:128, 0:n], wsm[:, kc, 0:128], UT[:, kc, c0 + a:c0 + a + n], start=(kc == 0), stop=(kc == 7))
                act(R1[0:40, a:a + n], ps[0:40, 0:n], AF.Exp, bias=small[0:40, 105:106])
                act(R1[0:40, a:a + n], R1[0:40, a:a + n], AF.Ln, bias=1.0)
                ts("dve", R2[0:40, a:a + n], R1[0:40, a:a + n], small[0:40, 104:105], ALU.mult)
                act(R1[64:104, a:a + n], ps[64:104, 0:n], AF.Sigmoid)
            for c in range(nt):
                cs = slice(c * 128, (c + 1) * 128)
                scan(R1[0:40, cs], ones_f[0:40, :], R2[0:40, cs])
            tt("dve", R2[32:40, 0:L], R2[32:40, 0:L], R1[32:40, 0:L], ALU.subtract)
            cp("dve", totr[32:40, 0:nt], R1[32:40, 0:L].rr("p (c i) -> p c i", i=128)[:, :, 127])
            for c in range(nt):
                cs = slice(c * 128, (c + 1) * 128)
                ts("dve", R1[32:40, cs], R2[32:40, cs], totr[32:40, c:c + 1], ALU.add)
            for c in range(nt):
                cs = slice(c * 128, (c + 1) * 128)
                ts("dve", R2[0:8, cs], R1[0:8, cs], R1[0:8, c * 128 + 127:c * 128 + 128], ALU.subtract)
                ts("dve", R2[32:40, cs], R1[32:40, cs], R1[32:40, c * 128:c * 128 + 1], ALU.subtract)
            if l == 1 and si == 0:
                dbg("R1_%d" % pass_id, R1[0:104, 0:256], [104, 256])
                dbg("R2_%d" % pass_id, R2[0:40, 0:256], [40, 256])

            for h in range(8):
                if h % 4 == 0:
                    wo_h = WO.rr("p (c n) -> p c n", c=4)
                    load("pool", wo_h, wout[:, h:h + 4, :])
                    for c in range(4):
                        tt("dve", wo_h[:, c, :], wo_h[:, c, :], GB, ALU.mult)
                sw = wslot(2).rr("p (k n) -> p k n", k=8)
                for i in range(4):
                    load("pool", sw[:, :, i * 128:(i + 1) * 128], win[:, :, i * 1024 + h * 128:i * 1024 + (h + 1) * 128])
                memset("pool", hpre[:, 0:1], 0.0)
                memset("pool", hpre[:, L + 1:L + 2], 0.0)
                for i, dstT in enumerate((qT, kT, None)):
                    for (a, n) in seg512(L):
                        ps = fw.psum()
                        for kc in range(8):
                            mm(ps[:, 0:n], sw[:, kc, i * 128:(i + 1) * 128], UT[:, kc, c0 + a:c0 + a + n], start=(kc == 0), stop=(kc == 7))
                        cp("act", hpre[:, 1 + a:1 + a + n], ps[:, 0:n])
                    cc = i * 8 + h
                    for (a, n) in seg512(L):
                        accv = junk.bitcast(F32)[:, 0:n]
                        act(accv, hpre[:, 1 + a:1 + a + n], AF.Identity, scale=convc[:, 24 + cc:25 + cc])
                        stt("dve", accv, hpre[:, a:a + n], convc[:, cc:cc + 1], accv, ALU.mult, ALU.add)
                        stt("dve", accv, hpre[:, 2 + a:2 + a + n], convc[:, 48 + cc:49 + cc], accv, ALU.mult, ALU.add)
                        if dstT is not None:
                            act(dstT[:, a:a + n], accv, AF.Silu)
                            sqv = TR[:, so:so + 512]
                            act(sqv[:, 0:n], dstT[:, a:a + n], AF.Square)
                            ps = fw.psum()
                            mm(ps[:, 0:n], ones_b, sqv[:, 0:n])
                            rn = junk.bitcast(F32)[:, 0:n]
                            act(rn, ps[:, 0:n], AF.Ln, bias=EPS)
                            act(rn, rn, AF.Exp, scale=-0.5)
                            if i == 0:
                                stt("dve", dstT[:, a:a + n], dstT[:, a:a + n], float(128 ** -0.5), rn, ALU.mult, ALU.mult)
                            else:
                                tt("dve", dstT[:, a:a + n], dstT[:, a:a + n], rn, ALU.mult)
                        else:
                            vT = TR[:, so:so + 512]
                            act(vT[:, 0:n], accv, AF.Silu)
                            for tq in range(n // 128):
                                pst = fw.psum(dt=BF16)
                                tr(pst[:, 0:128], vT[:, tq * 128:(tq + 1) * 128], ident_b)
                                cp("dve", v_tok[:, a // 128 + tq, :], pst[:, 0:128])
                for t in range(nt):
                    pst = fw.psum(dt=BF16)
                    tr(pst[:, 0:128], kT[:, t * 128:(t + 1) * 128], ident_b)
                    cp("act", k_tok[:, t, :], pst[:, 0:128])
                    ps = fw.psum()
                    for kc in range(8):
                        mm(ps[:, 0:128], UT[:, kc, c0 + t * 128:c0 + (t + 1) * 128], sw[:, kc, 384:512], start=(kc == 0), stop=(kc == 7))
                    act(z_tok[:, t, :], ps[:, 0:128], AF.Silu)
                if l == 1 and si == 0 and h == 0:
                    dbg("qT_%d" % pass_id, qT[:, 0:256], [128, 256])
                    dbg("kT_%d" % pass_id, kT[:, 0:256], [128, 256])
                    dbg("vtok_%d" % pass_id, v_tok[:, 0, :], [128, 128])

                for z in range(2):
                    S = S_f[:, z, :]
                    Sb = S_b[:, z, :]
                    if pass_id == 0:
                        memset("pool", S, 0.0)
                    else:
                        load("sp", S, sdn_d[(o_ * 2 + z) * 8 + h])
                    cp("act", Sb, S)
                    rb = z * 32 + h
                    fw.op("dve", lambda e, rb=rb, z=z: e.tensor_single_scalar(out=selc[:, z * 128:(z + 1) * 128].ap, in_=pidx.ap,
                                                                           scalar=float(rb), op=ALU.is_equal),
                          reads=[pidx], writes=[selc[:, z * 128:(z + 1) * 128]])
                rec_turn = [0, 0]
                stored = set()

                def chunk_task(z, c, zi, sl):
                    sbk, fk, cols, cold, ccol, bankpair = sl
                    nal = [0]

                    def palloc(dt=F32):
                        bnk = bankpair[nal[0] % 2]
                        nal[0] += 1
                        return bnk if dt == F32 else bnk.bitcast(dt)
                    S = S_f[:, z, :]
                    Sb = S_b[:, z, :]
                    rb = z * 32 + h
                    Msk_s = MLs if z == 0 else MUs
                    Msk_i = ML if z == 0 else MU
                    cs = slice(c * 128, (c + 1) * 128)
                    ps = palloc()
                    tr(ps[:, 0:128], R1[0:128, cs], ident_f)
                    cp("act", cols, ps[:, 0:104])
                    ps = palloc()
                    tr(ps[:, 0:64], R2[0:64, cs], ident_f[0:64, 0:64])
                    cp("dve", cold, ps[:, 0:40])
                    bcol = cols[:, rb:rb + 1]
                    betac = cols[:, 64 + rb:65 + rb]
                    act(ccol[:, 0:1], bcol, AF.Exp, scale=-1.0)
                    tt("dve", ccol[:, 1:2], ccol[:, 0:1], betac, ALU.mult)
                    ts("dve", ccol[:, 2:3], betac, -1.0, ALU.mult)
                    act(ccol[:, 3:4], cold[:, rb:rb + 1], AF.Exp)
                    psBGQ = palloc()
                    psB = psBGQ[:, 0:128]
                    psG = psBGQ[:, 128:256]
                    psQ = psBGQ[:, 256:384]
                    mm(psB[:, 0:128], selc[:, z * 128:(z + 1) * 128], R1[0:64, cs])
                    mm(psG[:, 0:128], kT[:, cs], kT[:, cs])
                    mm(psQ[:, 0:128], qT[:, cs], kT[:, cs])
                    yield
                    EB = fk[1]
                    act(EB, psB[:, 0:128], AF.Exp, scale=-1.0)
                    ld = fk[0]
                    fw.op("dve", lambda e, ld=ld, psB=psB, bcol=bcol: e.tensor_scalar(
                        out=ld.ap, in0=psB[:, 0:128].ap, scalar1=bcol.ap, scalar2=0.0, op0=ALU.subtract, op1=ALU.min),
                        reads=[psB[:, 0:128], bcol, EB], writes=[ld])
                    act(ld, ld, AF.Exp)
                    cp("act", ccol[:, 4:5], EB[:, 127:128] if z == 0 else EB[:, 0:1])
                    yield
                    t1 = fk[2]
                    tt("dve", t1, psG[:, 0:128], ld, ALU.mult)
                    P0 = sbk[0]
                    stt("dve", P0, t1, ccol[:, 2:3], Msk_s, ALU.mult, ALU.mult)
                    P = sbk[1]
                    stt("dve", P, t1, ccol[:, 2:3], MDS[:, z, :], ALU.mult, ALU.mult)
                    yield
                    pst = palloc(BF16)
                    tr(pst[:, 0:128], P, ident_b)
                    PT = sbk[2]
                    cp("act", PT, pst[:, 0:128])
                    TT = sbk[7]
                    tt("dve", TT, ident_b, PT, ALU.add)
                    tt("dve", t1, psQ[:, 0:128], ld, ALU.mult)
                    yield
                    for it in range(3):
                        Pn = sbk[3 + (it % 2) * 2]
                        PTn = sbk[4 + (it % 2) * 2]
                        ps1 = palloc()
                        mm(ps1[:, 0:128], PT, P)
                        if it < 2:
                            ps2 = palloc()
                            mm(ps2[:, 0:128], P, PT)
                        yield
                        cp("act", Pn, ps1[:, 0:128])
                        if it < 2:
                            cp("dve", PTn, ps2[:, 0:128])
                        yield
                        ps3 = palloc()
                        mm(ps3[:, 0:128], Pn, TT)
                        yield
                        TTn = sbk[8] if TT is sbk[7] else sbk[7]
                        tt("dve", TTn, ps3[:, 0:128], TT, ALU.add)
                        P, PT, TT = Pn, PTn, TTn
                        yield
                    for lv in range(3):
                        Bk = sbk[4]
                        tt("pool", Bk, P0, BMK[:, 1 + lv, :], ALU.mult)
                        pst = palloc(BF16)
                        tr(pst[:, 0:128], TT, ident_b)
                        psz = palloc()
                        mm(psz[:, 0:128], Bk, TT)
                        yield
                        Tn = sbk[3]
                        cp("act", Tn, pst[:, 0:128])
                        Zb = sbk[5]
                        cp("dve", Zb, psz[:, 0:128])
                        yield
                        psw = palloc()
                        mm(psw[:, 0:128], Tn, Zb)
                        yield
                        TTn = sbk[8] if TT is sbk[7] else sbk[7]
                        tt("dve", TTn, psw[:, 0:128], TT, ALU.add)
                        TT = TTn
                        yield
                    aq = sbk[2]
                    tt("dve", aq, t1, Msk_i, ALU.mult)
                    pst = palloc(BF16)
                    tr(pst[:, 0:128], aq, ident_b)
                    aqT = sbk[9]
                    cp("act", aqT, pst[:, 0:128])
                    qdT = sbk[10]
                    tt("dve", qdT, qT[:, cs], EB, ALU.mult)
                    kbe = sbk[3]
                    vb = sbk[4]
                    kdk = sbk[6]
                    ts("dve", kbe, k_tok[:, c, :], ccol[:, 1:2], ALU.mult)
                    act(vb, v_tok[:, c, :], AF.Identity, scale=betac)
                    act(kdk, k_tok[:, c, :], AF.Identity, scale=ccol[:, 3:4])
                    yield
                    psU = palloc()
                    mm(psU[:, 0:128], TT, vb)
                    psW = palloc()
                    mm(psW[:, 0:128], kbe, TT)
                    yield
                    u = fk[2]
                    cp("act", u, psU[:, 0:128])
                    wT = sbk[1]
                    cp("dve", wT, psW[:, 0:128])
                    yield
                    while rec_turn[z] != zi:
                        yield
                    psS = palloc()
                    mm(psS[:, 0:128], wT, Sb)
                    yield
                    vnew = sbk[2]
                    tt("dve", vnew, u, psS[:, 0:128], ALU.subtract)
                    yield
                    psO = palloc()
                    mm(psO[:, 0:128], qdT, Sb, start=True, stop=False)
                    mm(psO[:, 0:128], aqT, vnew, start=False, stop=True)
                    psN = palloc()
                    mm(psN[:, 0:128], kdk, vnew)
                    yield
                    stt("dve", S, S, ccol[:, 4:5], psN[:, 0:128], ALU.mult, ALU.add)
                    cp("act", Sb, S)
                    rec_turn[z] += 1
                    t = c
                    second = (z == 1 and 2 * t < nt) or (z == 0 and 2 * t >= nt)
                    if not second:
                        cp("act", o_f[:, t, :], psO[:, 0:128])
                        stored.add(t)
                    else:
                        while t not in stored:
                            yield
                        osum = fk[0]
                        ost = ccol[:, 8:16]
                        tt("dve", osum, psO[:, 0:128], o_f[:, t, :], ALU.add)
                        act(junk[:, 0:128], osum, AF.Square, accum=ost[:, 0:1])
                        act(ost[:, 1:2], ost[:, 0:1], AF.Ln, bias=EPS, scale=1.0 / 128)
                        act(ost[:, 2:3], ost[:, 1:2], AF.Exp, scale=-0.5)
                        stt("dve", osum, osum, ost[:, 2:3], normB, ALU.mult, ALU.mult)
                        yield
                        ogb = sbk[9]
                        tt("dve", ogb, osum, z_tok[:, t, :], ALU.mult)
                        pst = palloc(BF16)
                        tr(pst[:, 0:128], ogb, ident_b)
                        yield
                        oTb = sbk[10]
                        cp("act", oTb, pst[:, 0:128])
                        psA = palloc()
                        psBk = palloc()
                        mm(psA, oTb, wo_h[:, h % 4, 0:512])
                        mm(psBk, oTb, wo_h[:, h % 4, 512:1024])
                        yield
                        xacc(t0 + t, psA, psBk)

                items = []
                for i in range(nt):
                    items.append((0, i, i))
                    items.append((1, nt - 1 - i, i))
                active = []
                free = list(range(len(slots)))
                qi = 0
                while qi < len(items) or active:
                    while free and qi < len(items):
                        z_, c_, zi_ = items[qi]
                        qi += 1
                        s_ = free.pop(0)
                        active.append([chunk_task(z_, c_, zi_, slots[s_]), s_])
                    for a_ in list(active):
                        try:
                            next(a_[0])
                        except StopIteration:
                            active.remove(a_)
                            free.append(a_[1])
                if pass_id == 0:
                    for z in range(2):
                        load("sp", ndn_d[((si * 2 + o_) * 2 + z) * 8 + h], S_f[:, z, :])

    passes = [(0, 8, [(0, 2), (2, 2), (4, 2), (6, 2)], 0), (1, 16, [(0, 16)], 1)]

    def _main():
        for (pass_id, T, seqs, cond) in passes:
            mark("P%d load" % pass_id)
            load_x(pass_id, T)
            if pass_id == 1:
                dbg("x0s", X[:, 0, :], [128, D])
            for l in range(NL):
                mark("P%d L%d mixer" % (pass_id, l))
                if l % 2 == 0:
                    gla_mixer(l, pass_id, T, seqs, cond)
                else:
                    dn_mixer(l, pass_id, T, seqs, cond)
                mark("P%d L%d ln1" % (pass_id, l))
                dbg("xmix%d_%d" % (l, pass_id), X[:, 0, :], [128, D])
                if stop == "mix%d_%d" % (l, pass_id):
                    raise _Stop()
                layernorm(l, 0, T, False)
                dbg("xln%d_%d" % (l, pass_id), X[:, 0, :], [128, D])
                stop_at("ln%d_%d" % (l, pass_id))
                mark("P%d L%d ffn" % (pass_id, l))
                ffn(l, pass_id, T, cond)
                mark("P%d L%d ln2" % (pass_id, l))
                dbg("xffn%d_%d" % (l, pass_id), X[:, 0, :], [128, D])
                stop_at("ffn%d_%d" % (l, pass_id))
                layernorm(l, 1, T, l == NL - 1)
                dbg("xl%d_%d" % (l, pass_id), X[:, 0, :], [128, D])
                if stop == "l%d_%d" % (l, pass_id):
                    raise _Stop()
            mark("P%d store" % pass_id)
            dst = yp_d if pass_id == 0 else ys_d
            for t in range(T):
                load("sp", dst[t * 128:(t + 1) * 128, :], X[:, t, :])

    try:
        _main()
    except _Stop:
        pass
    fw.emit()
    return nc, fw, dbg_d


_CACHE = {}


def _inputs_per_core(inp, core):
    b = core % 2
    m = {}
    m["xp"] = np.ascontiguousarray(inp["x_prompt"][core * 4:(core + 1) * 4].reshape(1024, D))
    m["xs"] = np.ascontiguousarray(inp["x_sample"][b])
    m["cond2"] = np.ascontiguousarray(np.stack([inp["c_ctx"], inp["c"][b]], 0))
    m["sgla"] = np.ascontiguousarray(inp["state_gla"][b].reshape(16, 64, 128))
    m["sdn"] = np.ascontiguousarray(inp["state_dn"][b].reshape(32, 128, 128))
    for k in ("w_mod", "b_mod", "ln1_g", "ln1_b", "ln2_g", "ln2_b", "a_w_in", "a_w_gate", "a_b_gate", "a_norm",
              "b_proj", "b_scale", "a_w_out", "c_w_in", "c_conv", "c_a_log", "c_dt_bias", "c_norm", "c_w_out",
              "f_w_up", "f_conv", "f_w_down"):
        m[k] = np.ascontiguousarray(inp[k])
    return m


def kernel(**inputs):
    inp = {k: np.asarray(v, dtype=np.float32) for k, v in inputs.items()}
    if "nc" not in _CACHE:
        _CACHE["nc"] = build_program()[0]
    nc = _CACHE["nc"]
    n = 8
    in_maps = [_inputs_per_core(inp, c) for c in range(n)]
    res = run_bass_kernel_spmd(nc, in_maps, core_ids=list(range(n)))
    R = res.results
    y_prompt = np.concatenate([R[c]["yp"].reshape(4, 256, D) for c in range(n)], 0)
    y_sample = np.stack([R[0]["ys"], R[1]["ys"]], 0)
    ngla = np.concatenate([R[c]["ngla"].reshape(4, 2, 2, 4, 64, 128) for c in range(n)], 0)
    ndn = np.concatenate([R[c]["ndn"].reshape(4, 2, 2, 8, 128, 128) for c in range(n)], 0)
    return (y_prompt.astype(np.float32), y_sample.astype(np.float32), ngla.astype(np.float32), ndn.astype(np.float32))
```

```python
import math
import numpy as np
from concourse.bass_utils import run_bass_kernel_spmd
import concourse.bass as bass
import concourse.mybir as mybir

F32 = mybir.dt.float32
BF16 = mybir.dt.bfloat16
I32 = mybir.dt.int32
AF = mybir.ActivationFunctionType
ALU = mybir.AluOpType
AX = mybir.AxisListType

CELL = 256
_DT_SIZE = {F32: 4, BF16: 2, I32: 4}


class Region:
    def __init__(self, fw, name, handle, nbytes, cell=CELL):
        self.fw = fw
        self.name = name
        self.h = handle
        self.cell = cell
        self.ncell = (nbytes + cell - 1) // cell
        self.w = [None] * self.ncell
        self.r = [dict() for _ in range(self.ncell)]


class V:
    def __init__(self, region, ap):
        self.region = region
        self.ap = ap
        self._cells = None

    def __getitem__(self, key):
        return V(self.region, self.ap[key])

    def rr(self, pattern_, **kw):
        return V(self.region, self.ap.rearrange(pattern_, **kw))

    def bitcast(self, dt):
        return V(self.region, self.ap.bitcast(dt))

    def bc(self, shape):
        return V(self.region, self.ap.to_broadcast(shape))

    @property
    def shape(self):
        return self.ap.shape

    def cells(self):
        if self._cells is None:
            ap = self.ap
            esz = _DT_SIZE[ap.dtype]
            dims = list(ap.ap)[1:]
            base = int(ap.offset) if not isinstance(ap.offset, int) else ap.offset
            pstep = list(ap.ap)[0][0]
            if pstep > 0:
                base = base % pstep
            base_b = base * esz
            cs = set()
            CELL = self.region.cell
            dims = [(s, n) for (s, n) in dims if n > 1 or True]
            if not dims:
                dims = [(1, 1)]
            *outer, (ls, ln) = dims
            if ls in (0, 1):
                run = (esz * (ln if ls == 1 else 1))
                inner_iter = [0]
            else:
                run = esz
                inner_iter = [i * ls * esz for i in range(ln)]
            offs = [0]
            for (s, n) in outer:
                if s == 0:
                    continue
                offs = [o + i * s * esz for o in offs for i in range(n)]
            for o in offs:
                for ii in inner_iter:
                    a = base_b + o + ii
                    for c in range(a // CELL, (a + run - 1) // CELL + 1):
                        cs.add(c)
            self._cells = sorted(cs)
            assert self._cells[-1] < self.region.ncell, (self.region.name, self._cells[-1], self.region.ncell, ap)
        return self._cells


class Op:
    __slots__ = ("eng", "fn", "waits", "signal", "semval", "dma_sem", "is_dma")

    def __init__(self, eng, fn):
        self.eng = eng
        self.fn = fn
        self.waits = []
        self.signal = False
        self.semval = None
        self.dma_sem = None
        self.is_dma = False


ENGS = ("pe", "dve", "act", "pool", "sp")
N_DMA_SEMS = 12


class FW:
    def __init__(self, nc):
        self.nc = nc
        self.ops = {e: [] for e in ENGS}
        self.regions = []
        self.dma_rr = {"sp": 0, "pool": 0, "act": 0}
        self.dma_last = {}
        self._ctx = []
        self.psum_ptr = 0
        self.nops = 0

    def sbuf(self, name, shape, dt):
        g = self.nc.sbuf_tensor(name, list(shape), dt)
        h = g.__enter__()
        self._ctx.append(g)
        nb = int(np.prod(shape[1:])) * _DT_SIZE[dt]
        reg = Region(self, name, h, nb)
        self.regions.append(reg)
        return V(reg, h[:] if hasattr(h, "__getitem__") else h.ap())

    def psum_banks(self):
        self.banks = []
        for i in range(8):
            g = self.nc.psum_tensor(f"psb{i}", [128, 512], F32)
            h = g.__enter__()
            self._ctx.append(g)
            reg = Region(self, f"psb{i}", h, 2048, cell=2048)
            self.banks.append(V(reg, h[:]))

    def psum(self, ncols=512, parts=128, dt=F32):
        b = self.psum_ptr
        self.psum_ptr = (b + 1) % 8
        bank = self.banks[b] if dt == F32 else self.banks[b].bitcast(dt)
        return bank[0:parts, 0:ncols]

    def _deps(self, op, reads, writes):
        deps = {}
        for v in reads:
            reg = v.region
            for c in v.cells():
                w = reg.w[c]
                if w is not None:
                    deps[id(w)] = w
        for v in writes:
            reg = v.region
            for c in v.cells():
                w = reg.w[c]
                if w is not None:
                    deps[id(w)] = w
                for t in reg.r[c].values():
                    deps[id(t)] = t
        for t in deps.values():
            if t is op:
                continue
            if t.eng == "pe" and op.eng == "pe" and not t.is_dma and not op.is_dma:
                continue
            op.waits.append(t)
            t.signal = True
        for v in reads:
            reg = v.region
            key = op.dma_sem if op.is_dma else op.eng
            for c in v.cells():
                reg.r[c][key] = op
        for v in writes:
            reg = v.region
            for c in v.cells():
                reg.w[c] = op
                reg.r[c] = {}

    def op(self, eng, fn, reads=(), writes=()):
        if getattr(self, "halted", False):
            return None
        o = Op(eng, fn)
        self._deps(o, reads, writes)
        self.ops[eng].append(o)
        self.nops += 1
        return o

    def dma(self, queue, out, in_, reads=(), writes=(), **kw):
        if getattr(self, "halted", False):
            return None
        o = Op(queue, None)
        o.is_dma = True
        o.signal = True
        oap = out.ap if isinstance(out, V) else out
        iap = in_.ap if isinstance(in_, V) else in_
        rd = list(reads) + ([in_] if isinstance(in_, V) else [])
        wr = list(writes) + ([out] if isinstance(out, V) else [])
        k = self.dma_rr[queue]
        self.dma_rr[queue] = (k + 1) % N_DMA_SEMS
        o.dma_sem = (queue, k)
        prev = self.dma_last.get((queue, k))
        self._deps(o, rd, wr)
        if prev is not None:
            o.waits.append(prev)
        self.dma_last[(queue, k)] = o
        o.fn = lambda e: e.dma_start(out=oap, in_=iap, **kw)
        self.ops[queue].append(o)
        self.nops += 1
        return o

    def emit(self):
        nc = self.nc
        sems = {}
        semctx = []
        for e in ENGS:
            g = nc.semaphore(f"s_{e}")
            sems[e] = g.__enter__()
            semctx.append(g)
        dsems = {}
        for q in ("sp", "pool"):
            for k in range(N_DMA_SEMS):
                g = nc.semaphore(f"d_{q}{k}")
                dsems[(q, k)] = g.__enter__()
                semctx.append(g)
        for e in ENGS:
            cnt = 0
            for o in self.ops[e]:
                if o.is_dma:
                    continue
                if o.signal:
                    cnt += 1
                    o.semval = cnt
            self.maxsem = max(getattr(self, "maxsem", 0), cnt)
        dcnt = {}
        for e in ENGS:
            for o in self.ops[e]:
                if o.is_dma:
                    dcnt[o.dma_sem] = dcnt.get(o.dma_sem, 0) + 16
                    o.semval = dcnt[o.dma_sem]

        def semof(t):
            return dsems[t.dma_sem] if t.is_dma else sems[t.eng]

        def run(engname, engobj):
            known = {}
            for o in self.ops[engname]:
                need = {}
                for t in o.waits:
                    s = t.dma_sem if t.is_dma else t.eng
                    if t.semval > need.get(s, (0, None))[0]:
                        need[s] = (t.semval, t)
                for s, (val, t) in need.items():
                    if known.get(s, 0) >= val:
                        continue
                    engobj.wait_ge(semof(t), val)
                    known[s] = val
                ins = o.fn(engobj)
                if o.is_dma:
                    ins.then_inc(dsems[o.dma_sem], 16)
                elif o.signal:
                    ins.then_inc(sems[engname], 1)
            if engname in ("sp", "pool"):
                for k in range(N_DMA_SEMS):
                    if dcnt.get((engname, k), 0) > 0:
                        engobj.wait_ge(dsems[(engname, k)], dcnt[(engname, k)])

        with nc.Block() as block:
            @block.tensor
            def _(e):
                run("pe", e)

            @block.vector
            def _(e):
                run("dve", e)

            @block.scalar
            def _(e):
                run("act", e)

            @block.gpsimd
            def _(e):
                run("pool", e)

            @block.sync
            def _(e):
                run("sp", e)
        for g in reversed(semctx):
            g.__exit__(None, None, None)
        for g in reversed(self._ctx):
            g.__exit__(None, None, None)

D = 1024
NL = 4
DFF = 2816
NCH = DFF // 128
ALPHA = float(8 ** 0.25)
EPS = 1e-6
KC = 8
A_IN = 2080
C_IN = 4128
PI = float(np.pi)

DBG_SPECS = {}


class _Stop(Exception):
    pass


def build_program(debug=(), stop=None):
    nc = bass.Bass("TRN2", target_bir_lowering=False)
    fw = FW(nc)

    def din(name, shape):
        return nc.dram_tensor(name, list(shape), F32, kind="ExternalInput").ap()

    def dout(name, shape):
        return nc.dram_tensor(name, list(shape), F32, kind="ExternalOutput").ap()

    xp_d = din("xp", [1024, D])
    xs_d = din("xs", [2048, D])
    cond_d = din("cond2", [2, D])
    sgla_d = din("sgla", [16, 64, 128])
    sdn_d = din("sdn", [32, 128, 128])
    w_mod_d = din("w_mod", [NL, D, 6 * D])
    b_mod_d = din("b_mod", [NL, 6 * D])
    ln_d = {k: din(k, [NL, D]) for k in ("ln1_g", "ln1_b", "ln2_g", "ln2_b")}
    a_w_in_d = din("a_w_in", [2, D, A_IN])
    a_w_gate_d = din("a_w_gate", [2, 2, 16, 256])
    a_b_gate_d = din("a_b_gate", [2, 2, 256])
    a_norm_d = din("a_norm", [2, 128])
    b_proj_d = din("b_proj", [2, 4, 128, 128])
    b_scale_d = din("b_scale", [2, 512])
    a_w_out_d = din("a_w_out", [2, D, D])
    c_w_in_d = din("c_w_in", [2, D, C_IN])
    c_conv_d = din("c_conv", [2, 3, 3072])
    c_a_log_d = din("c_a_log", [2, 2, 8])
    c_dt_bias_d = din("c_dt_bias", [2, 2, 8])
    c_norm_d = din("c_norm", [2, 128])
    c_w_out_d = din("c_w_out", [2, D, D])
    f_w_up_d = din("f_w_up", [NL, D, 2 * DFF])
    f_conv_d = din("f_conv", [NL, 3, 2 * DFF])
    f_w_down_d = din("f_w_down", [NL, DFF, D])

    yp_d = dout("yp", [1024, D])
    ys_d = dout("ys", [2048, D])
    ngla_d = dout("ngla", [64, 64, 128])
    ndn_d = dout("ndn", [128, 128, 128])
    dbg_d = {}

    def A(v):
        return v.ap if isinstance(v, V) else v

    def rds(*xs):
        return [x for x in xs if isinstance(x, V)]

    def mm(out, lhsT, rhs, start=True, stop=True):
        fw.op("pe", lambda e: e.matmul(out.ap, lhsT=lhsT.ap, rhs=rhs.ap, start=start, stop=stop),
              reads=[lhsT, rhs], writes=[out])

    def tr(out, in_, idn):
        fw.op("pe", lambda e: e.transpose(out=out.ap, in_=in_.ap, identity=idn.ap),
              reads=[in_, idn], writes=[out])

    def act(out, in_, func, bias=None, scale=None, accum=None):
        kw = {}
        if bias is not None:
            kw["bias"] = A(bias)
        if scale is not None:
            kw["scale"] = A(scale)
        if accum is not None:
            kw["accum_out"] = accum.ap
        fw.op("act", lambda e: e.activation(out=out.ap, in_=in_.ap, func=func, **kw),
              reads=rds(in_, bias, scale), writes=rds(out, accum))

    def ts(eng, out, in0, s1, op0, s2=None, op1=None):
        kw = {"op1": op1} if op1 is not None else {}
        fw.op(eng, lambda e: e.tensor_scalar(out=out.ap, in0=in0.ap, scalar1=A(s1),
                                             scalar2=(A(s2) if s2 is not None else None), op0=op0, **kw),
              reads=rds(in0, s1, s2), writes=[out])

    def tt(eng, out, in0, in1, op):
        fw.op(eng, lambda e: e.tensor_tensor(out=out.ap, in0=in0.ap, in1=in1.ap, op=op),
              reads=[in0, in1], writes=[out])

    def stt(eng, out, in0, scalar, in1, op0, op1):
        fw.op(eng, lambda e: e.scalar_tensor_tensor(out=out.ap, in0=in0.ap, scalar=A(scalar), in1=in1.ap,
                                                    op0=op0, op1=op1),
              reads=rds(in0, scalar, in1), writes=[out])

    def cp(eng, out, in_):
        if eng == "act":
            fw.op("act", lambda e: e.copy(out=out.ap, in_=in_.ap), reads=[in_], writes=[out])
        else:
            fw.op(eng, lambda e: e.tensor_copy(out=out.ap, in_=in_.ap), reads=[in_], writes=[out])

    def memset(eng, out, val):
        fw.op(eng, lambda e: e.memset(out.ap, val), writes=[out])

    def scan(out, d0, d1, init=0.0):
        fw.op("dve", lambda e: e.tensor_tensor_scan(out=out.ap, data0=d0.ap, data1=d1.ap, initial=init,
                                                    op0=ALU.mult, op1=ALU.add),
              reads=[d0, d1], writes=[out])

    def recip(out, in_):
        fw.op("dve", lambda e: e.reciprocal(out=out.ap, in_=in_.ap), reads=[in_], writes=[out])

    def rsum(out, in_):
        fw.op("dve", lambda e: e.reduce_sum(out=out.ap, in_=in_.ap, axis=AX.X), reads=[in_], writes=[out])

    def load(q, out, src, **kw):
        fw.dma(q, out, src, **kw)

    def dbg(name, v, shape):
        if name in debug:
            d = dout("dbg_" + name, list(shape))
            dbg_d[name] = d
            fw.dma("sp" if v.ap.dtype == F32 else "pool", d, v)

    def stop_at(name):
        if stop == name:
            fw.halted = True

    fw.marks = []

    def mark(name):
        fw.marks.append((name, len(fw.ops["dve"]), len(fw.ops["pe"])))

    rr = [0]

    def evac_eng():
        rr[0] ^= 1
        return "act" if rr[0] else "dve"

    fw.psum_banks()
    X = fw.sbuf("X", [128, 16, D], F32)
    ARENA = fw.sbuf("ARENA", [128, 44032], BF16)
    ARENA_F = ARENA.bitcast(F32)
    ARENA_I = ARENA.bitcast(I32)
    WR = fw.sbuf("WR", [128, 3, 4096], BF16)
    LNB = fw.sbuf("LNB", [128, 2, D], F32)
    GB = fw.sbuf("GB", [128, D], F32)
    ones_f = fw.sbuf("ones_f", [128, 128], F32)
    ident_f = fw.sbuf("ident_f", [128, 128], F32)
    ident_b = fw.sbuf("ident_b", [128, 128], BF16)
    ones_b = fw.sbuf("ones_b", [128, 128], BF16)
    MU = fw.sbuf("MU", [128, 128], F32)
    MUs = fw.sbuf("MUs", [128, 128], F32)
    ML = fw.sbuf("ML", [128, 128], F32)
    MLs = fw.sbuf("MLs", [128, 128], F32)
    pidx = fw.sbuf("pidx", [64, 128], F32)
    BMK = fw.sbuf("BMK", [128, 4, 128], BF16)
    MDS = fw.sbuf("MDS", [128, 2, 128], BF16)
    selc = fw.sbuf("selc", [64, 256], F32)
    dncol = fw.sbuf("dncol", [128, 4, 160], F32)
    modcol = fw.sbuf("modcol", [128, NL, 48, 2], F32)
    mscale = fw.sbuf("mscale", [128, NL, 2, 8, 2], F32)
    scT = fw.sbuf("scT", [128, 8, 2], F32)
    small = fw.sbuf("small", [128, 320], F32)
    S_f = fw.sbuf("S_f", [128, 2, 128], F32)
    S_b = fw.sbuf("S_b", [128, 2, 128], BF16)
    wsm = fw.sbuf("wsm", [128, 8, 128], BF16)
    wg = fw.sbuf("wg", [48, 256], BF16)
    bproj = fw.sbuf("bproj", [128, 4, 128], BF16)
    convc = fw.sbuf("convc", [128, 160], F32)
    normB = fw.sbuf("normB", [128, 128], F32)
    junk = fw.sbuf("junk", [128, D], BF16)
    pe_c = ARENA_F[:, 0:512]
    freq = ARENA_F[:, 512:768]
    ptmp = ARENA_F[:, 768:1536].rr("p (a n) -> p a n", a=3)
    ptmpi = ARENA_I[:, 1536:1792]
    pcol = fw.sbuf("pcol", [128, 8], F32)
    dgs = fw.sbuf("dgs", [128, 128], F32)
    efix = fw.sbuf("efix", [128, 4, 16], F32)

    fw.sbuf_free = nc.sbuf_bytes_remaining
    memset("pool", ones_f, 1.0)
    memset("pool", ones_b, 1.0)

    def asel(out, in_, pattern, cmp, cm, base=0):
        fw.op("pool", lambda e: e.affine_select(out=out.ap, in_=in_.ap, pattern=pattern, compare_op=cmp, fill=0.0,
                                                base=base, channel_multiplier=cm), reads=[in_], writes=[out])

    asel(ident_f, ones_f, [[-1, 128]], ALU.is_equal, 1)
    cp("dve", ident_b, ident_f)
    asel(MU, ones_f, [[1, 128]], ALU.is_ge, -1)
    asel(MUs, ones_f, [[1, 128]], ALU.is_gt, -1)
    asel(ML, ones_f, [[-1, 128]], ALU.is_ge, 1)
    asel(MLs, ones_f, [[-1, 128]], ALU.is_gt, 1)
    fw.op("pool", lambda e: e.iota(pidx.ap, [[0, 128]], base=0, channel_multiplier=1,
                                   allow_small_or_imprecise_dtypes=True), writes=[pidx])
    mdt = ptmp.rr("p a n -> p (a n)")
    for bi_, bsz in enumerate((16, 32, 64)):
        nb_ = 128 // bsz
        Eb = ptmp[0:8, 0, 0:128]
        asel(Eb, ones_f[0:8, :], [[1, 128]], ALU.is_ge, -bsz, base=0)
        asel(Eb, Eb, [[-1, 128]], ALU.is_ge, bsz, base=bsz - 1)
        ps = fw.psum()
        mm(ps[:, 0:128], Eb[0:nb_, :], Eb[0:nb_, :])
        cp("dve", mdt[:, 256 + bi_ * 128:256 + (bi_ + 1) * 128], ps[:, 0:128])
    md16, md32, md64 = (mdt[:, 256 + i * 128:256 + (i + 1) * 128] for i in range(3))
    cp("dve", BMK[:, 0, :], md16)
    tt("dve", BMK[:, 1, :], md32, md16, ALU.subtract)
    tt("dve", BMK[:, 2, :], md64, md32, ALU.subtract)
    ts("dve", BMK[:, 3, :], md64, -1.0, ALU.mult, 1.0, ALU.add)
    tt("dve", MDS[:, 0, :], md16, MLs, ALU.mult)
    tt("dve", MDS[:, 1, :], md16, MUs, ALU.mult)

    stop_at("c1")
    condt = ARENA_F[0:2, 0:1024]
    load("sp", condt, cond_d)
    act(condt, condt, AF.Silu)
    ps = fw.psum()
    for kc in range(8):
        tr(ps[:, kc * 2:(kc + 1) * 2], condt[:, kc * 128:(kc + 1) * 128], ident_f[0:2, 0:2])
    cp("dve", scT.rr("p k c -> p (k c)"), ps[:, 0:16])

    stop_at("c2")
    bmr = ARENA_F[0:48, 1024:1152]
    bmT = small[:, 0:48]
    wm_slots = [ARENA_F[:, 2048 + i * 4096: 2048 + (i + 1) * 4096].rr("p (k n) -> p k n", k=8) for i in range(2)]
    for l in range(NL):
        load("sp", bmr, b_mod_d[l].rearrange("(b p) -> b p", p=128))
        ps = fw.psum()
        tr(ps[:, 0:48], bmr, ident_f[0:48, 0:48])
        cp("act", bmT, ps[:, 0:48])
        for g in range(12):
            slot = wm_slots[g % 2]
            load("sp", slot, w_mod_d[l].rearrange("(k p) n -> p k n", p=128)[:, :, g * 512:(g + 1) * 512])
            ps = fw.psum()
            for b4 in range(4):
                for kc in range(8):
                    mm(ps[:, b4 * 2:(b4 + 1) * 2], slot[:, kc, b4 * 128:(b4 + 1) * 128], scT[:, kc, :],
                       start=(kc == 0), stop=(kc == 7))
            ps3 = ps[:, 0:8].rr("p (b c) -> p b c", c=2)
            for c in range(2):
                tt("dve", modcol[:, l, 4 * g:4 * g + 4, c], ps3[:, :, c], bmT[:, 4 * g:4 * g + 4], ALU.add)
        for w, blk0 in enumerate((8, 32)):
            ts("dve", mscale[:, l, w], modcol[:, l, blk0:blk0 + 8, :], 1.0, ALU.add, 1.0 / ALPHA, ALU.mult)
    dbg("modcol", modcol.rr("p l b c -> p (l b c)"), [128, NL * 96])

    stop_at("c3")
    POOLW = (2, 4, 8, 16)
    for gi, w in enumerate(POOLW):
        lo = w // 2
        hi = w - 1 - lo
        fw.op("pool", lambda e, gi=gi, lo=lo, hi=hi: e.iota(efix[:, gi, 0:lo].ap, [[1, lo]], base=hi + 1, channel_multiplier=0,
                                                           allow_small_or_imprecise_dtypes=True), writes=[efix[:, gi, 0:lo]])
        if hi > 0:
            fw.op("pool", lambda e, gi=gi, hi=hi, w=w: e.iota(efix[:, gi, 8:8 + hi].ap, [[-1, hi]], base=w - 1, channel_multiplier=0,
                                                              allow_small_or_imprecise_dtypes=True), writes=[efix[:, gi, 8:8 + hi]])
        for (a, n) in ((0, lo), (8, hi)):
            if n > 0:
                recip(efix[:, gi, a:a + n], efix[:, gi, a:a + n])
                ts("dve", efix[:, gi, a:a + n], efix[:, gi, a:a + n], float(w), ALU.mult)

    def sincos(out_sin, out_cos, theta):
        for out, shift in ((out_sin, 0.0), (out_cos, 0.25)):
            t = ptmp[:, 1, :]
            gq = ptmp[:, 2, :]
            ts("dve", t, theta, 1.0 / (2 * PI), ALU.mult, shift, ALU.add)
            cp("dve", ptmpi, t)
            tt("dve", t, t, ptmpi, ALU.subtract)
            fw.op("dve", lambda e, t=t, gq=gq: e.tensor_single_scalar(out=gq.ap, in_=t.ap, scalar=0.5, op=ALU.is_ge),
                  reads=[t], writes=[gq])
            tt("dve", t, t, gq, ALU.subtract)
            fw.op("dve", lambda e, t=t, gq=gq: e.tensor_single_scalar(out=gq.ap, in_=t.ap, scalar=-0.5, op=ALU.is_lt),
                  reads=[t], writes=[gq])
            tt("dve", t, t, gq, ALU.add)
            act(out, t, AF.Sin, scale=2 * PI)

    def pe_consts():
        fw.op("pool", lambda e: e.iota(freq.ap, [[1, 256]], base=0, channel_multiplier=0, allow_small_or_imprecise_dtypes=True),
              writes=[freq])
        act(freq, freq, AF.Exp, scale=-math.log(10000.0) / 256.0)
        fw.op("pool", lambda e: e.iota(pcol[:, 0:1].ap, [[0, 1]], base=0, channel_multiplier=1, allow_small_or_imprecise_dtypes=True),
              writes=[pcol[:, 0:1]])
        fw.op("dve", lambda e: e.tensor_single_scalar(out=pcol[:, 1:2].ap, in_=pcol[:, 0:1].ap, scalar=64.0, op=ALU.is_ge),
              reads=[pcol[:, 0:1]], writes=[pcol[:, 1:2]])
        stt("dve", pcol[:, 2:3], pcol[:, 1:2], -64.0, pcol[:, 0:1], ALU.mult, ALU.add)

        ts("dve", ptmp[:, 0, :], freq, pcol[:, 2:3], ALU.mult)
        sincos(pe_c[:, 0:256], pe_c[:, 256:512], ptmp[:, 0, :])


    stop_at("c5")
    def seg512(n):
        out = []
        a = 0
        while a < n:
            out.append((a, min(512, n - a)))
            a += 512
        return out

    def load_x(pass_id, T):
        src = xp_d if pass_id == 0 else xs_d
        if pass_id == 1:
            pe_consts()
        for t in range(T):
            load("sp", X[:, t, :], src[t * 128:(t + 1) * 128, :])
            if pass_id == 1:
                ts("dve", pcol[:, 3:4], pcol[:, 1:2], float(2 * t), ALU.add)
                ts("dve", ptmp[:, 0, :], freq, pcol[:, 3:4], ALU.mult)
                pe_r = junk.bitcast(F32)
                sincos(pe_r[:, 0:256], pe_r[:, 256:512], ptmp[:, 0, :])
                tt("dve", X[:, t, 0:512], X[:, t, 0:512], pe_r, ALU.add)
                tt("dve", X[:, t, 512:1024], X[:, t, 512:1024], pe_c, ALU.add)
            act(X[:, t, :], X[:, t, :], AF.Identity, scale=ALPHA)
        stop_at("c6")

    def make_ut(UT, l, which, cond, tiles, col0, halo=None):
        shb = 0 if which == 0 else 24
        jobs = [(t, col0 + i * 128, None) for i, t in enumerate(tiles)]
        if halo is not None:
            jobs.append((halo[0], halo[2], halo[1]))
        for (t, c0, hc) in jobs:
            for half in range(2):
                ps = fw.psum()
                for q in range(4):
                    kc = half * 4 + q
                    tr(ps[:, q * 128:(q + 1) * 128], X[:, t, kc * 128:(kc + 1) * 128], ident_f)
                if which == 1:
                    stop_at("u1")
                eng_b = evac_eng()
                for q in range(4):
                    kc = half * 4 + q
                    sc = mscale[:, l, which, kc, cond:cond + 1]
                    sh = modcol[:, l, shb + kc, cond:cond + 1]
                    if hc is None:
                        src = ps[:, q * 128:(q + 1) * 128]
                        dst = UT[:, kc, c0:c0 + 128]
                    else:
                        src = ps[:, q * 128 + hc:q * 128 + hc + 1]
                        dst = UT[:, kc, c0:c0 + 1]
                    if eng_b == "act":
                        act(dst, src, AF.Identity, bias=sh, scale=sc)
                    else:
                        ts("dve", dst, src, sc, ALU.mult, sh, ALU.add)
                    if which == 1:
                        stop_at("u2")
                if which == 1:
                    stop_at("u3")
            if which == 1:
                stop_at("u4")
        if stop == "ut":
            dbg("ut", UT[:, 0, 0:1024], [128, 1024])
            for kc_ in range(8):
                dbg("ut%d" % kc_, UT[:, kc_, 0:256], [128, 256])
            dbg("x0", X[:, 0, :], [128, D])
            dbg("mscale", mscale.rr("p l w k c -> p (l w k c)"), [128, NL * 32])
            raise _Stop()

    def gbcast(l, blk0, cond):
        for half in range(2):
            ps = fw.psum()
            for q in range(4):
                kc = half * 4 + q
                dg = dgs
                ts("dve", dg, ident_f, modcol[:, l, blk0 + kc, cond:cond + 1], ALU.mult)
                mm(ps[:, q * 128:(q + 1) * 128], ones_f, dg)
            cp("act", GB[:, half * 512:(half + 1) * 512], ps)

    def layernorm(l, which, T, last):
        gname, bname = ("ln1_g", "ln1_b") if which == 0 else ("ln2_g", "ln2_b")
        load("sp", LNB[:, 0, :], ln_d[gname][l:l + 1, :].to_broadcast([128, D]))
        load("sp", LNB[:, 1, :], ln_d[bname][l:l + 1, :].to_broadcast([128, D]))
        stop_at("lna")
        if not last:
            act(LNB.rr("p a d -> p (a d)"), LNB.rr("p a d -> p (a d)"), AF.Identity, scale=ALPHA)
        stop_at("lnb")
        st = small[:, 64:72]
        for t in range(T):
            xt = X[:, t, :]
            rsum(st[:, 0:1], xt)
            stop_at("lnc")
            ts("dve", st[:, 1:2], st[:, 0:1], -1.0 / D, ALU.mult)
            act(junk, xt, AF.Square, bias=st[:, 1:2], accum=st[:, 2:3])
            stop_at("lnd")
            act(st[:, 3:4], st[:, 2:3], AF.Ln, bias=EPS, scale=1.0 / D)
            act(st[:, 4:5], st[:, 3:4], AF.Exp, scale=-0.5)
            ts("dve", xt, xt, st[:, 1:2], ALU.add, st[:, 4:5], ALU.mult)
            tt("dve", xt, xt, LNB[:, 0, :], ALU.mult)
            tt("dve", xt, xt, LNB[:, 1, :], ALU.add)

    wr_i = [0]
    hs_i = [0]

    def wslot(nring=3):
        s = WR[:, wr_i[0] % nring, :]
        wr_i[0] += 1
        return s

    WO = WR[:, 2, :]

    def xacc(t, psA, psB):
        tt("dve", X[:, t, 0:512], X[:, t, 0:512], psA, ALU.add)
        tt("dve", X[:, t, 512:1024], X[:, t, 512:1024], psB, ALU.add)

    def ffn(l, pass_id, T, cond):
        cr = ARENA_F[0:44, 0:128]
        for k in range(3):
            load("sp", cr, f_conv_d[l][k].rearrange("(c p) -> c p", p=128))
            ps = fw.psum()
            tr(ps[:, 0:44], cr, ident_f[0:44, 0:44])
            cp("act", convc[:, k * 44:(k + 1) * 44], ps[:, 0:44])
        stop_at("fa")
        gbcast(l, 40, cond)
        stop_at("fb")
        UTs = ARENA[:, 0:8224].rr("p (k n) -> p k n", k=8)
        actT = ARENA[:, 8224:8224 + 22528].rr("p (j n) -> p j n", j=NCH)
        o0 = 8224 + 22528
        hpre = [[ARENA[:, o0 + (2 * i + h) * 1032: o0 + (2 * i + h) * 1032 + 1028] for h in range(2)] for i in range(2)]
        o1 = (o0 + 4 * 1032 + 1) // 2 + 8
        accs = [[ARENA_F[:, o1 + (2 * i + h) * 1024: o1 + (2 * i + h + 1) * 1024] for h in range(2)] for i in range(2)]
        assert (o1 + 4096) <= 22016
        groups = [(0, 8, None)] if pass_id == 0 else [(0, 8, "R"), (8, 8, "L")]
        wup = f_w_up_d[l].rearrange("(k p) n -> p k n", p=128)
        wdn = f_w_down_d[l].rearrange("(c p) n -> p c n", p=128)
        for (t0, nt, hal) in groups:
            ntok = nt * 128
            halo = None
            if hal == "R":
                halo = (t0 + nt, 0, 1026)
            elif hal == "L":
                halo = (t0 - 1, 127, 1)
            make_ut(UTs, l, 1, cond, list(range(t0, t0 + nt)), 2, halo)
            stop_at("f0")
            for i in range(2):
                for h in range(2):
                    if hal != "L":
                        memset("pool", hpre[i][h][:, 0:2], 0.0)
                    if hal != "R":
                        memset("pool", hpre[i][h][:, 1026:1028], 0.0)
            segs = [(2, 512), (514, 512)]
            if hal == "R":
                segs.append((1026, 1))
            if hal == "L":
                segs.append((1, 1))
            jgs = [(j0, min(4, NCH - j0)) for j0 in range(0, NCH, 4)]
            jg2 = [(j0, 2) for j0 in range(0, NCH, 2)]
            WRH = WR.rr("p s n -> p (s n)").rr("p (s n) -> p s n", s=6)

            def up_load(gi_):
                j0_, nj_ = jg2[gi_]
                sa_ = WRH[:, hs_i[0] % 6, :].rr("p (k n) -> p k n", k=8)
                sg_ = WRH[:, (hs_i[0] + 1) % 6, :].rr("p (k n) -> p k n", k=8)
                hs_i[0] += 2
                load("pool", sa_, wup[:, :, j0_ * 128:(j0_ + nj_) * 128])
                load("pool", sg_, wup[:, :, DFF + j0_ * 128:DFF + (j0_ + nj_) * 128])
                return sa_, sg_

            nxt = up_load(0)
            for gi_, (j0, nj) in enumerate(jg2):
                sa, sg = nxt
                if gi_ + 1 < len(jg2):
                    nxt = up_load(gi_ + 1)
                for jj in range(nj):
                    j = j0 + jj
                    hp = hpre[j % 2]
                    acc = accs[j % 2]
                    for h, sw in enumerate((sa, sg)):
                        cc = h * 22 + j
                        w0 = convc[:, cc:cc + 1]
                        w1 = convc[:, 44 + cc:44 + cc + 1]
                        w2 = convc[:, 88 + cc:88 + cc + 1]
                        for (c0, n) in segs:
                            ps = fw.psum()
                            for kc in range(8):
                                mm(ps[:, 0:n], sw[:, kc, jj * 128:(jj + 1) * 128], UTs[:, kc, c0:c0 + n],
                                   start=(kc == 0), stop=(kc == 7))
                            if n > 1:
                                cp("act", hp[h][:, c0:c0 + n], ps[:, 0:n])
                                act(acc[h][:, c0 - 2:c0 - 2 + n], ps[:, 0:n], AF.Identity, scale=w1)
                            else:
                                cp("act", hp[h][:, c0:c0 + n], ps[:, 0:n])
                        hh = hp[h]
                        if pass_id == 0:
                            a3 = acc[h].rr("p (s t) -> p s t", s=4)
                            h3 = hh[:, 2:1026].rr("p (s t) -> p s t", s=4)
                            stt("dve", a3[:, :, 1:256], h3[:, :, 0:255], w0, a3[:, :, 1:256], ALU.mult, ALU.add)
                            stt("dve", a3[:, :, 0:255], h3[:, :, 1:256], w2, a3[:, :, 0:255], ALU.mult, ALU.add)
                        else:
                            stt("dve", acc[h], hh[:, 1:1025], w0, acc[h], ALU.mult, ALU.add)
                            stt("dve", acc[h], hh[:, 3:1027], w2, acc[h], ALU.mult, ALU.add)
                    act(acc[1], acc[1], AF.Silu)
                    tt("dve", actT[:, j, :], acc[1], acc[0], ALU.mult)
                    stop_at("f0b")
            if l == 0 and t0 == 0:
                dbg("actT%d" % pass_id, actT[:, 0, :], [128, 1024])
            stop_at("f1")
            def dn_load_blk(bi_):
                j0_, nj_ = jgs[bi_ % len(jgs)]
                sd_ = wslot()[:, 0:1024 * nj_].rr("p (c n) -> p c n", c=nj_)
                load("pool", sd_, wdn[:, j0_:j0_ + nj_, :])
                return sd_

            nblk = (nt // 4) * len(jgs)
            nxtd = dn_load_blk(0)
            bcnt = 0
            for q0 in range(0, nt, 4):
                for (j0, nj) in jgs:
                    sd = nxtd
                    bcnt += 1
                    if bcnt < nblk:
                        nxtd = dn_load_blk(bcnt)
                    for ti in range(4):
                        for jj in range(nj):
                            j = j0 + jj
                            for hf in range(2):
                                mm(fw.banks[ti * 2 + hf], actT[:, j, (q0 + ti) * 128:(q0 + ti + 1) * 128],
                                   sd[:, jj, hf * 512:(hf + 1) * 512], start=(j == 0), stop=(j == NCH - 1))
                for ti in range(4):
                    t_ = t0 + q0 + ti
                    for hf in range(2):
                        tmpx = accs[ti % 2][hf][:, 0:512]
                        tt("dve", tmpx, fw.banks[ti * 2 + hf], GB[:, hf * 512:(hf + 1) * 512], ALU.mult)
                        tt("dve", X[:, t_, hf * 512:(hf + 1) * 512], X[:, t_, hf * 512:(hf + 1) * 512], tmpx, ALU.add)
                stop_at("f2")

    def gla_mixer(l, pass_id, T, seqs, cond):
        e_ = l // 2
        win = a_w_in_d[e_].rearrange("(k p) n -> p k n", p=128)
        wout = a_w_out_d[e_].rearrange("(c p) n -> p c n", p=128)
        UT = ARENA[:, 0:16384].rr("p (k n) -> p k n", k=8)
        TR = ARENA[:, 16384:39936]
        TRF = ARENA_F[:, 8192:19968]
        make_ut(UT, l, 0, cond, list(range(T)), 0)
        gbcast(l, 16, cond)
        memset("pool", wsm, 0.0)
        load("pool", wsm[:, :, 0:16], win[:, :, 1536:1552])
        load("pool", wsm[:, :, 32:48], win[:, :, 1552:1568])
        load("pool", wg[0:16, :], a_w_gate_d[e_, 0])
        load("pool", wg[32:48, :], a_w_gate_d[e_, 1])
        load("pool", bproj, b_proj_d[e_].rearrange("g c d -> c g d"))
        load("sp", normB, a_norm_d[e_:e_ + 1, :].to_broadcast([128, 128]))
        bg = TRF[0:8, 0:128]
        load("sp", bg[0:8, 0:64], a_b_gate_d[e_].rearrange("z (h d) -> (z h) d", d=64))
        ps = fw.psum()
        tr(ps[0:64, 0:8], bg[0:8, 0:64], ident_f[0:8, 0:8])
        ts("dve", small[0:64, 80:88], ps[0:64, 0:8], -1.0, ALU.mult)
        ps = fw.psum()
        bg2 = TRF[0:4, 128:256]
        load("sp", bg2, b_scale_d[e_].rearrange("(g d) -> g d", d=128))
        tr(ps[:, 0:4], bg2, ident_f[0:4, 0:4])
        cp("act", small[:, 96:100], ps[:, 0:4])
        wo_h = WO.rr("p (c n) -> p c n", c=4)
        load("pool", wo_h, wout[:, 0:4, :])
        for c in range(4):
            tt("dve", wo_h[:, c, :], wo_h[:, c, :], GB, ALU.mult)
        o_f = LNB.rr("p a d -> p (a d)").rr("p (t v) -> p t v", v=128)
        gla_next = [None]
        stop_at("g1")

        for si, (t0, nt) in enumerate(seqs):
            L = nt * 128
            c0 = t0 * 128
            qT = TR[0:64, 0:2048]
            kT = TR[0:64, 2048:4096]
            v_tok = TR[:, 4096:6144].rr("p (t v) -> p t v", v=128)
            r_tok = TR[:, 6144:8192].rr("p (t v) -> p t v", v=128)
            lrT = TR[0:64, 8192:10240]
            qe = TR[0:64, 10240:11264]
            ke = TR[0:64, 11264:12288]
            kd = TR[0:64, 12288:13312]
            kd_tok = TR[:, 13312:13824].rr("p (c d) -> p c d", d=64)
            ATb = [TR[:, 13824 + i * 128:13824 + (i + 1) * 128] for i in range(2)]
            ogb = TR[:, 14080:14208]
            oTb = TR[:, 14208:14336]
            fo = 14336 // 2
            cpos = TRF[0:64, fo:fo + 1024]
            tmp = TRF[0:64, fo + 1024:fo + 2048]
            dcol = TRF[0:64, fo + 2048:fo + 2056]
            osum = TRF[:, fo + 2056:fo + 2184]
            ost = TRF[:, fo + 2184:fo + 2192]
            totc = TRF[0:64, fo + 2192:fo + 2200]
            for (a, n) in seg512(L):
                ps = fw.psum()
                for kc in range(8):
                    mm(ps[0:64, 0:n], wsm[:, kc, 0:64], UT[:, kc, c0 + a:c0 + a + n], start=(kc == 0), stop=(kc == 7))
                cp("act", lrT[:, a:a + n], ps[0:64, 0:n])
            stop_at("g2")
            def gla_load(h_):
                sw_ = wslot(2)[:, 0:8 * 384].rr("p (k n) -> p k n", k=8)
                load("pool", sw_[:, :, 0:64], win[:, :, h_ * 64:(h_ + 1) * 64])
                load("pool", sw_[:, :, 64:128], win[:, :, 256 + h_ * 64:256 + (h_ + 1) * 64])
                load("pool", sw_[:, :, 128:256], win[:, :, 512 + h_ * 128:512 + (h_ + 1) * 128])
                load("pool", sw_[:, :, 256:384], win[:, :, 1024 + h_ * 128:1024 + (h_ + 1) * 128])
                return sw_

            if si == 0:
                gla_next[0] = gla_load(0)
            for h in range(4):
                sw = gla_next[0]
                stop_at("g2a")
                for (a, n) in seg512(L):
                    ps = fw.psum()
                    for kc in range(8):
                        mm(ps[0:64, 0:n], sw[:, kc, 0:64], UT[:, kc, c0 + a:c0 + a + n], start=(kc == 0), stop=(kc == 7))
                    act(qT[:, a:a + n], ps[0:64, 0:n], AF.Identity, scale=0.125)
                    stop_at("g2b")
                    ps = fw.psum()
                    for kc in range(8):
                        mm(ps[0:64, 0:n], sw[:, kc, 64:128], UT[:, kc, c0 + a:c0 + a + n], start=(kc == 0), stop=(kc == 7))
                    cp("dve", kT[:, a:a + n], ps[0:64, 0:n])
                    stop_at("g2c")
                for t in range(nt):
                    ps = fw.psum()
                    for kc in range(8):
                        mm(ps[:, 0:128], UT[:, kc, c0 + t * 128:c0 + (t + 1) * 128], sw[:, kc, 128:256], start=(kc == 0), stop=(kc == 7))
                    cp("dve", v_tok[:, t, :], ps[:, 0:128])
                    stop_at("g2d")
                    ps = fw.psum()
                    for kc in range(8):
                        mm(ps[:, 0:128], UT[:, kc, c0 + t * 128:c0 + (t + 1) * 128], sw[:, kc, 256:384], start=(kc == 0), stop=(kc == 7))
                    act(r_tok[:, t, :], ps[:, 0:128], AF.Silu)
                    stop_at("g2e")
                stop_at("g3")
                if h + 1 < 4:
                    gla_next[0] = gla_load(h + 1)
                elif si + 1 < len(seqs):
                    gla_next[0] = gla_load(0)
                for z in range(2):
                    S = S_f[0:64, z, :]
                    Sb = S_b[0:64, z, :]
                    if pass_id == 0:
                        memset("pool", S, 0.0)
                    else:
                        load("sp", S, sgla_d[(e_ * 2 + z) * 4 + h])
                    cp("act", Sb, S)
                    blocks = [(b0, min(8, nt - b0)) for b0 in range(0, nt, 8)]
                    if z == 1:
                        blocks = blocks[::-1]
                    for (b0, nb) in blocks:
                        n = nb * 128
                        for (a, m) in seg512(n):
                            ps = fw.psum()
                            mm(ps[0:64, 0:m], wg[z * 32:z * 32 + 16, h * 64:(h + 1) * 64],
                               lrT[z * 32:z * 32 + 16, b0 * 128 + a:b0 * 128 + a + m])
                            act(tmp[:, a:a + m], ps[0:64, 0:m], AF.Exp, bias=small[0:64, 80 + z * 4 + h:81 + z * 4 + h], scale=-1.0)
                        act(tmp[:, 0:n], tmp[:, 0:n], AF.Ln, bias=1.0)
                        for c in range(nb):
                            cs = slice(c * 128, (c + 1) * 128)
                            scan(cpos[:, cs], ones_f[0:64, :], tmp[:, cs])
                        if z == 1:
                            tt("dve", tmp[:, 0:n], tmp[:, 0:n], cpos[:, 0:n], ALU.subtract)
                            cp("dve", totc[:, 0:nb], cpos[:, 0:n].rr("p (c i) -> p c i", i=128)[:, :, 127])
                            for c in range(nb):
                                cs = slice(c * 128, (c + 1) * 128)
                                ts("dve", cpos[:, cs], tmp[:, cs], totc[:, c:c + 1], ALU.add)
                        lastc = (lambda c: c * 128 + 127) if z == 0 else (lambda c: c * 128)
                        act(tmp[:, 0:n], cpos[:, 0:n], AF.Exp, scale=-1.0 / 16)
                        tt("dve", qe[:, 0:n], qT[:, b0 * 128:b0 * 128 + n], tmp[:, 0:n], ALU.mult)
                        tmp3 = tmp[:, 0:n].rr("p (c i) -> p c i", i=128)
                        cp("dve", dcol[:, 0:nb], tmp3[:, :, 127 if z == 0 else 0])
                        act(tmp[:, 0:n], cpos[:, 0:n], AF.Exp, scale=1.0 / 16)
                        tt("dve", ke[:, 0:n], kT[:, b0 * 128:b0 * 128 + n], tmp[:, 0:n], ALU.mult)
                        for c in range(nb):
                            cs = slice(c * 128, (c + 1) * 128)
                            ts("dve", tmp[:, cs], cpos[:, cs], cpos[:, lastc(c):lastc(c) + 1], ALU.subtract)
                        act(tmp[:, 0:n], tmp[:, 0:n], AF.Exp, scale=1.0 / 16)
                        tt("dve", kd[:, 0:n], kT[:, b0 * 128:b0 * 128 + n], tmp[:, 0:n], ALU.mult)
                        psk = fw.psum(dt=BF16)
                        for c in range(nb):
                            tr(psk[:, c * 64:(c + 1) * 64], kd[:, c * 128:(c + 1) * 128], ident_b[0:64, 0:64])
                        cp("act", kd_tok[:, 0:nb, :].rr("p c d -> p (c d)"), psk[:, 0:nb * 64])
                        stop_at("g4")
                        corder = range(nb) if z == 0 else range(nb - 1, -1, -1)
                        for c in corder:
                            t = b0 + c
                            cs = slice(c * 128, (c + 1) * 128)
                            ps = fw.psum()
                            mm(ps[:, 0:128], ke[:, cs], qe[:, cs])
                            AT = ATb[c % 2]
                            tt("dve", AT, ps[:, 0:128], MU if z == 0 else ML, ALU.mult)
                            ps2 = fw.psum()
                            mm(ps2[:, 0:128], AT, v_tok[:, t, :], start=True, stop=False)
                            mm(ps2[:, 0:128], qe[:, cs], Sb, start=False, stop=True)
                            ps3 = fw.psum()
                            mm(ps3[0:64, 0:128], kd_tok[:, c, :], v_tok[:, t, :])
                            stt("dve", S, S, dcol[:, c:c + 1], ps3[0:64, 0:128], ALU.mult, ALU.add)
                            cp("act", Sb, S)
                            if z == 0:
                                cp("act", o_f[:, t, :], ps2[:, 0:128])
                            else:
                                tt("dve", osum, ps2[:, 0:128], o_f[:, t, :], ALU.add)
                                act(junk[:, 0:128], osum, AF.Square, accum=ost[:, 0:1])
                                act(ost[:, 1:2], ost[:, 0:1], AF.Ln, bias=EPS, scale=1.0 / 128)
                                act(ost[:, 2:3], ost[:, 1:2], AF.Exp, scale=-0.5)
                                stt("dve", osum, osum, ost[:, 2:3], normB, ALU.mult, ALU.mult)
                                tt("dve", ogb, osum, r_tok[:, t, :], ALU.mult)
                                pst = fw.psum(dt=BF16)
                                tr(pst[:, 0:128], ogb, ident_b)
                                cp("act", oTb, pst[:, 0:128])
                                psA = fw.psum()
                                psB = fw.psum()
                                mm(psA, oTb, wo_h[:, h, 0:512])
                                mm(psB, oTb, wo_h[:, h, 512:1024])
                                xacc(t0 + t, psA, psB)
                            stop_at("g5")
                        stop_at("g6")
                    if pass_id == 0:
                        s_glob = si
                        load("sp", ngla_d[((s_glob * 2 + e_) * 2 + z) * 4 + h], S)
        wo_p = WO.rr("p (c n) -> p c n", c=4)
        load("pool", wo_p, wout[:, 4:8, :])
        for c in range(4):
            tt("dve", wo_p[:, c, :], wo_p[:, c, :], GB, ALU.mult)
        swp = wslot(2).rr("p (k n) -> p k n", k=8)
        load("pool", swp, win[:, :, 1568:2080])
        for si, (t0, nt) in enumerate(seqs):
            L = nt * 128
            c0 = t0 * 128
            pm = TR[:, 0:8192].rr("p (g n) -> p g n", g=4)
            xpad = TRF[:, 4096:4096 + 2080]
            sA = TRF[:, 6176:6176 + 2080]
            sB = TRF[:, 8256:8256 + 2080]
            pooled = TR[:, 20672:20672 + 2048]
            assert 20672 + 2048 <= 23552 and (8256 + 2080) * 2 <= 20672
            for gi, w in enumerate(POOLW):
                lo = w // 2
                hi = w - 1 - lo
                memset("pool", xpad[:, 0:16], 0.0)
                memset("pool", xpad[:, 16 + L:32 + L], 0.0)
                for (a, n) in seg512(L):
                    ps = fw.psum()
                    for kc in range(8):
                        mm(ps[:, 0:n], swp[:, kc, gi * 128:(gi + 1) * 128], UT[:, kc, c0 + a:c0 + a + n], start=(kc == 0), stop=(kc == 7))
                    cp("act", xpad[:, 16 + a:16 + a + n], ps[:, 0:n])
                src = xpad
                k = 1
                bufs = [sA, sB]
                bi = 0
                while k < w:
                    dst = bufs[bi]
                    bi ^= 1
                    n = L + 32 - 2 * k
                    tt("dve", dst[:, 0:n], src[:, 0:n], src[:, k:k + n], ALU.add)
                    src = dst
                    k *= 2
                wsum = src[:, 16 - lo:16 - lo + L]
                tt("dve", wsum[:, 0:lo], wsum[:, 0:lo], efix[:, gi, 0:lo], ALU.mult)
                if hi > 0:
                    tt("dve", wsum[:, L - hi:L], wsum[:, L - hi:L], efix[:, gi, 8:8 + hi], ALU.mult)
                stt("dve", pooled[:, 0:L], wsum, 1.0 / w, xpad[:, 16:16 + L], ALU.mult, ALU.subtract)
                for (a, n) in seg512(L):
                    ps = fw.psum()
                    mm(ps[:, 0:n], bproj[:, gi, :], pooled[:, a:a + n])
                    act(pm[:, gi, a:a + n], ps[:, 0:n], AF.Identity, scale=small[:, 96 + gi:97 + gi])
            for t in range(nt):
                psA = fw.psum()
                psB = fw.psum()
                for gi in range(4):
                    mm(psA, pm[:, gi, t * 128:(t + 1) * 128], wo_p[:, gi, 0:512], start=(gi == 0), stop=(gi == 3))
                for gi in range(4):
                    mm(psB, pm[:, gi, t * 128:(t + 1) * 128], wo_p[:, gi, 512:1024], start=(gi == 0), stop=(gi == 3))
                xacc(t0 + t, psA, psB)

    def dn_mixer(l, pass_id, T, seqs, cond):
        o_ = l // 2
        win = c_w_in_d[o_].rearrange("(k p) n -> p k n", p=128)
        wout = c_w_out_d[o_].rearrange("(c p) n -> p c n", p=128)
        UT = ARENA[:, 0:16384].rr("p (k n) -> p k n", k=8)
        TR = ARENA[:, 16384:39936]
        TRF = ARENA_F[:, 8192:19968]
        make_ut(UT, l, 0, cond, list(range(T)), 0)
        gbcast(l, 16, cond)
        memset("pool", wsm, 0.0)
        for i, off in enumerate((0, 32, 64, 96)):
            load("pool", wsm[:, :, off:off + 8], win[:, :, 4096 + i * 8:4096 + (i + 1) * 8])
        load("sp", normB, c_norm_d[o_:o_ + 1, :].to_broadcast([128, 128]))
        memset("pool", small[0:40, 104:106], 0.0)
        for z in range(2):
            load("sp", small[z * 32:z * 32 + 8, 104:105], c_a_log_d[o_, z].rearrange("(h o) -> h o", o=1))
            load("sp", small[z * 32:z * 32 + 8, 105:106], c_dt_bias_d[o_, z].rearrange("(h o) -> h o", o=1))
        act(small[0:40, 104:105], small[0:40, 104:105], AF.Exp)
        cr = TRF[0:72, 0:128]
        load("sp", cr, c_conv_d[o_].rearrange("k (c p) -> (k c) p", p=128))
        ps = fw.psum()
        tr(ps[:, 0:72], cr, ident_f[0:72, 0:72])
        cp("act", convc[:, 0:72], ps[:, 0:72])
        o_f = LNB.rr("p a d -> p (a d)").rr("p (t v) -> p t v", v=128)
        dn_next = [None]

        for si, (t0, nt) in enumerate(seqs):
            L = nt * 128
            c0 = t0 * 128
            R1 = TRF[0:128, 0:L]
            R2 = TRF[0:64, L:2 * L]
            bo = 4 * L
            hpre = TR[:, bo:bo + L + 2]
            qT = TR[:, bo + L + 8:bo + 2 * L + 8]
            kT = TR[:, bo + 2 * L + 8:bo + 3 * L + 8]
            k_tok = TR[:, bo + 3 * L + 8:bo + 4 * L + 8].rr("p (t v) -> p t v", v=128)
            v_tok = TR[:, bo + 4 * L + 8:bo + 5 * L + 8].rr("p (t v) -> p t v", v=128)
            z_tok = TR[:, bo + 5 * L + 8:bo + 6 * L + 8].rr("p (t v) -> p t v", v=128)
            so = bo + 6 * L + 8
            NSLOT = 4
            slots = []
            for s_ in range(NSLOT):
                base = 16384 + so + s_ * 2176
                if base + 2176 <= 44032:
                    sbk = [ARENA[:, base + i * 128:base + (i + 1) * 128] for i in range(11)]
                    fb = (base + 11 * 128) // 2
                    fk = [ARENA_F[:, fb + i * 128:fb + (i + 1) * 128] for i in range(3)]
                else:
                    assert s_ == 3 and L + 2 >= 11 * 128
                    hb = 16384 + bo
                    sbk = [ARENA[:, hb + i * 128:hb + (i + 1) * 128] for i in range(11)]
                    jf = junk.bitcast(F32)
                    fk = [jf[:, 128 + i * 128:128 + (i + 1) * 128] for i in range(3)]
                slots.append((sbk, fk, dncol[:, s_, 0:104], dncol[:, s_, 104:144], dncol[:, s_, 144:160],
                              (fw.banks[2 * s_], fw.banks[2 * s_ + 1])))
            totr = small[0:40, 296:312]
            memset("pool", R1[:, 0:L], 0.0)
            memset("pool", R2[:, 0:L], 0.0)
            for (a, n) in seg512(L):
                ps = fw.psum()
                for kc in range(8):
                    mm(ps[0:128, 0:n], wsm[:, kc, 0:128], UT[:, kc, c0 + a:c0 + a + n], start=(kc == 0), stop=(kc == 7))
                act(R1[0:40, a:a + n], ps[0:40, 0:n], AF.Exp, bias=small[0:40, 105:106])
                act(R1[0:40, a:a + n], R1[0:40, a:a + n], AF.Ln, bias=1.0)
                ts("dve", R2[0:40, a:a + n], R1[0:40, a:a + n], small[0:40, 104:105], ALU.mult)
                act(R1[64:104, a:a + n], ps[64:104, 0:n], AF.Sigmoid)
            for c in range(nt):
                cs = slice(c * 128, (c + 1) * 128)
                scan(R1[0:40, cs], ones_f[0:40, :], R2[0:40, cs])
            tt("dve", R2[32:40, 0:L], R2[32:40, 0:L], R1[32:40, 0:L], ALU.subtract)
            cp("dve", totr[32:40, 0:nt], R1[32:40, 0:L].rr("p (c i) -> p c i", i=128)[:, :, 127])
            for c in range(nt):
                cs = slice(c * 128, (c + 1) * 128)
                ts("dve", R1[32:40, cs], R2[32:40, cs], totr[32:40, c:c + 1], ALU.add)
            for c in range(nt):
                cs = slice(c * 128, (c + 1) * 128)
                ts("dve", R2[0:8, cs], R1[0:8, cs], R1[0:8, c * 128 + 127:c * 128 + 128], ALU.subtract)
                ts("dve", R2[32:40, cs], R1[32:40, cs], R1[32:40, c * 128:c * 128 + 1], ALU.subtract)
            if l == 1 and si == 0:
                dbg("R1_%d" % pass_id, R1[0:104, 0:256], [104, 256])
                dbg("R2_%d" % pass_id, R2[0:40, 0:256], [40, 256])

            def dn_load(h_):
                sw_ = wslot(2).rr("p (k n) -> p k n", k=8)
                for i_ in range(4):
                    load("pool", sw_[:, :, i_ * 128:(i_ + 1) * 128], win[:, :, i_ * 1024 + h_ * 128:i_ * 1024 + (h_ + 1) * 128])
                return sw_

            if si == 0:
                dn_next[0] = dn_load(0)
            for h in range(8):
                if h % 4 == 0:
                    wo_h = WO.rr("p (c n) -> p c n", c=4)
                    load("pool", wo_h, wout[:, h:h + 4, :])
                sw = dn_next[0]
                memset("pool", hpre[:, 0:1], 0.0)
                memset("pool", hpre[:, L + 1:L + 2], 0.0)
                for i, dstT in enumerate((qT, kT, None)):
                    for (a, n) in seg512(L):
                        ps = fw.psum()
                        for kc in range(8):
                            mm(ps[:, 0:n], sw[:, kc, i * 128:(i + 1) * 128], UT[:, kc, c0 + a:c0 + a + n], start=(kc == 0), stop=(kc == 7))
                        cp("act", hpre[:, 1 + a:1 + a + n], ps[:, 0:n])
                    cc = i * 8 + h
                    for (a, n) in seg512(L):
                        accv = junk.bitcast(F32)[:, 0:n]
                        act(accv, hpre[:, 1 + a:1 + a + n], AF.Identity, scale=convc[:, 24 + cc:25 + cc])
                        stt("dve", accv, hpre[:, a:a + n], convc[:, cc:cc + 1], accv, ALU.mult, ALU.add)
                        stt("dve", accv, hpre[:, 2 + a:2 + a + n], convc[:, 48 + cc:49 + cc], accv, ALU.mult, ALU.add)
                        if dstT is not None:
                            act(dstT[:, a:a + n], accv, AF.Silu)
                            sqv = TR[:, so:so + 512]
                            act(sqv[:, 0:n], dstT[:, a:a + n], AF.Square)
                            ps = fw.psum()
                            mm(ps[:, 0:n], ones_b, sqv[:, 0:n])
                            rn = junk.bitcast(F32)[:, 0:n]
                            act(rn, ps[:, 0:n], AF.Ln, bias=EPS)
                            act(rn, rn, AF.Exp, scale=-0.5)
                            if i == 0:
                                stt("dve", dstT[:, a:a + n], dstT[:, a:a + n], float(128 ** -0.5), rn, ALU.mult, ALU.mult)
                            else:
                                tt("dve", dstT[:, a:a + n], dstT[:, a:a + n], rn, ALU.mult)
                        else:
                            vT = TR[:, so:so + 512]
                            act(vT[:, 0:n], accv, AF.Silu)
                            for tq in range(n // 128):
                                pst = fw.psum(dt=BF16)
                                tr(pst[:, 0:128], vT[:, tq * 128:(tq + 1) * 128], ident_b)
                                cp("dve", v_tok[:, a // 128 + tq, :], pst[:, 0:128])
                for t in range(nt):
                    pst = fw.psum(dt=BF16)
                    tr(pst[:, 0:128], kT[:, t * 128:(t + 1) * 128], ident_b)
                    cp("act", k_tok[:, t, :], pst[:, 0:128])
                    ps = fw.psum()
                    for kc in range(8):
                        mm(ps[:, 0:128], UT[:, kc, c0 + t * 128:c0 + (t + 1) * 128], sw[:, kc, 384:512], start=(kc == 0), stop=(kc == 7))
                    act(z_tok[:, t, :], ps[:, 0:128], AF.Silu)
                if l == 1 and si == 0 and h == 0:
                    dbg("qT_%d" % pass_id, qT[:, 0:256], [128, 256])
                    dbg("kT_%d" % pass_id, kT[:, 0:256], [128, 256])
                    dbg("vtok_%d" % pass_id, v_tok[:, 0, :], [128, 128])

                if h + 1 < 8:
                    dn_next[0] = dn_load(h + 1)
                elif si + 1 < len(seqs):
                    dn_next[0] = dn_load(0)
                if h % 4 == 0:
                    for c in range(4):
                        tt("dve", wo_h[:, c, :], wo_h[:, c, :], GB, ALU.mult)
                for z in range(2):
                    S = S_f[:, z, :]
                    Sb = S_b[:, z, :]
                    if pass_id == 0:
                        memset("pool", S, 0.0)
                    else:
                        load("sp", S, sdn_d[(o_ * 2 + z) * 8 + h])
                    cp("act", Sb, S)
                    rb = z * 32 + h
                    fw.op("dve", lambda e, rb=rb, z=z: e.tensor_single_scalar(out=selc[:, z * 128:(z + 1) * 128].ap, in_=pidx.ap,
                                                                           scalar=float(rb), op=ALU.is_equal),
                          reads=[pidx], writes=[selc[:, z * 128:(z + 1) * 128]])
                rec_turn = [0, 0]
                stored = set()

                def chunk_task(z, c, zi, sl):
                    sbk, fk, cols, cold, ccol, bankpair = sl
                    nal = [0]

                    def palloc(dt=F32):
                        bnk = bankpair[nal[0] % 2]
                        nal[0] += 1
                        return bnk if dt == F32 else bnk.bitcast(dt)
                    S = S_f[:, z, :]
                    Sb = S_b[:, z, :]
                    rb = z * 32 + h
                    Msk_s = MLs if z == 0 else MUs
                    Msk_i = ML if z == 0 else MU
                    cs = slice(c * 128, (c + 1) * 128)
                    ps = palloc()
                    tr(ps[:, 0:128], R1[0:128, cs], ident_f)
                    cp("act", cols, ps[:, 0:104])
                    ps = palloc()
                    tr(ps[:, 0:64], R2[0:64, cs], ident_f[0:64, 0:64])
                    cp("dve", cold, ps[:, 0:40])
                    bcol = cols[:, rb:rb + 1]
                    betac = cols[:, 64 + rb:65 + rb]
                    act(ccol[:, 0:1], bcol, AF.Exp, scale=-1.0)
                    tt("dve", ccol[:, 1:2], ccol[:, 0:1], betac, ALU.mult)
                    ts("dve", ccol[:, 2:3], betac, -1.0, ALU.mult)
                    act(ccol[:, 3:4], cold[:, rb:rb + 1], AF.Exp)
                    psBGQ = palloc()
                    psB = psBGQ[:, 0:128]
                    psG = psBGQ[:, 128:256]
                    psQ = psBGQ[:, 256:384]
                    mm(psB[:, 0:128], selc[:, z * 128:(z + 1) * 128], R1[0:64, cs])
                    mm(psG[:, 0:128], kT[:, cs], kT[:, cs])
                    mm(psQ[:, 0:128], qT[:, cs], kT[:, cs])
                    yield
                    EB = fk[1]
                    act(EB, psB[:, 0:128], AF.Exp, scale=-1.0)
                    ld = fk[0]
                    fw.op("dve", lambda e, ld=ld, psB=psB, bcol=bcol: e.tensor_scalar(
                        out=ld.ap, in0=psB[:, 0:128].ap, scalar1=bcol.ap, scalar2=0.0, op0=ALU.subtract, op1=ALU.min),
                        reads=[psB[:, 0:128], bcol, EB], writes=[ld])
                    act(ld, ld, AF.Exp)
                    cp("act", ccol[:, 4:5], EB[:, 127:128] if z == 0 else EB[:, 0:1])
                    yield
                    t1 = fk[2]
                    tt("dve", t1, psG[:, 0:128], ld, ALU.mult)
                    P0 = sbk[0]
                    stt("dve", P0, t1, ccol[:, 2:3], Msk_s, ALU.mult, ALU.mult)
                    P = sbk[1]
                    stt("dve", P, t1, ccol[:, 2:3], MDS[:, z, :], ALU.mult, ALU.mult)
                    yield
                    pst = palloc(BF16)
                    tr(pst[:, 0:128], P, ident_b)
                    PT = sbk[2]
                    cp("act", PT, pst[:, 0:128])
                    TT = sbk[7]
                    tt("dve", TT, ident_b, PT, ALU.add)
                    tt("dve", t1, psQ[:, 0:128], ld, ALU.mult)
                    yield
                    for it in range(3):
                        Pn = sbk[3 + (it % 2) * 2]
                        PTn = sbk[4 + (it % 2) * 2]
                        ps1 = palloc()
                        mm(ps1[:, 0:128], PT, P)
                        if it < 2:
                            ps2 = palloc()
                            mm(ps2[:, 0:128], P, PT)
                        yield
                        cp("act", Pn, ps1[:, 0:128])
                        if it < 2:
                            cp("dve", PTn, ps2[:, 0:128])
                        yield
                        ps3 = palloc()
                        mm(ps3[:, 0:128], Pn, TT)
                        yield
                        TTn = sbk[8] if TT is sbk[7] else sbk[7]
                        tt("dve", TTn, ps3[:, 0:128], TT, ALU.add)
                        P, PT, TT = Pn, PTn, TTn
                        yield
                    for lv in range(3):
                        Bk = sbk[4]
                        tt("pool", Bk, P0, BMK[:, 1 + lv, :], ALU.mult)
                        pst = palloc(BF16)
                        tr(pst[:, 0:128], TT, ident_b)
                        psz = palloc()
                        mm(psz[:, 0:128], Bk, TT)
                        yield
                        Tn = sbk[3]
                        cp("act", Tn, pst[:, 0:128])
                        Zb = sbk[5]
                        cp("dve", Zb, psz[:, 0:128])
                        yield
                        psw = palloc()
                        mm(psw[:, 0:128], Tn, Zb)
                        yield
                        TTn = sbk[8] if TT is sbk[7] else sbk[7]
                        tt("dve", TTn, psw[:, 0:128], TT, ALU.add)
                        TT = TTn
                        yield
                    aq = sbk[2]
                    tt("dve", aq, t1, Msk_i, ALU.mult)
                    pst = palloc(BF16)
                    tr(pst[:, 0:128], aq, ident_b)
                    aqT = sbk[9]
                    cp("act", aqT, pst[:, 0:128])
                    qdT = sbk[10]
                    tt("dve", qdT, qT[:, cs], EB, ALU.mult)
                    kbe = sbk[3]
                    vb = sbk[4]
                    kdk = sbk[6]
                    ts("dve", kbe, k_tok[:, c, :], ccol[:, 1:2], ALU.mult)
                    act(vb, v_tok[:, c, :], AF.Identity, scale=betac)
                    act(kdk, k_tok[:, c, :], AF.Identity, scale=ccol[:, 3:4])
                    yield
                    psU = palloc()
                    mm(psU[:, 0:128], TT, vb)
                    psW = palloc()
                    mm(psW[:, 0:128], kbe, TT)
                    yield
                    u = fk[2]
                    cp("act", u, psU[:, 0:128])
                    wT = sbk[1]
                    cp("dve", wT, psW[:, 0:128])
                    yield
                    while rec_turn[z] != zi:
                        yield
                    psS = palloc()
                    mm(psS[:, 0:128], wT, Sb)
                    yield
                    vnew = sbk[2]
                    tt("dve", vnew, u, psS[:, 0:128], ALU.subtract)
                    yield
                    psO = palloc()
                    mm(psO[:, 0:128], qdT, Sb, start=True, stop=False)
                    mm(psO[:, 0:128], aqT, vnew, start=False, stop=True)
                    psN = palloc()
                    mm(psN[:, 0:128], kdk, vnew)
                    yield
                    stt("dve", S, S, ccol[:, 4:5], psN[:, 0:128], ALU.mult, ALU.add)
                    cp("act", Sb, S)
                    rec_turn[z] += 1
                    t = c
                    second = (z == 1 and 2 * t < nt) or (z == 0 and 2 * t >= nt)
                    if not second:
                        cp("act", o_f[:, t, :], psO[:, 0:128])
                        stored.add(t)
                    else:
                        while t not in stored:
                            yield
                        osum = fk[0]
                        ost = ccol[:, 8:16]
                        tt("dve", osum, psO[:, 0:128], o_f[:, t, :], ALU.add)
                        act(junk[:, 0:128], osum, AF.Square, accum=ost[:, 0:1])
                        act(ost[:, 1:2], ost[:, 0:1], AF.Ln, bias=EPS, scale=1.0 / 128)
                        act(ost[:, 2:3], ost[:, 1:2], AF.Exp, scale=-0.5)
                        stt("dve", osum, osum, ost[:, 2:3], normB, ALU.mult, ALU.mult)
                        yield
                        ogb = sbk[9]
                        tt("dve", ogb, osum, z_tok[:, t, :], ALU.mult)
                        pst = palloc(BF16)
                        tr(pst[:, 0:128], ogb, ident_b)
                        yield
                        oTb = sbk[10]
                        cp("act", oTb, pst[:, 0:128])
                        psA = palloc()
                        psBk = palloc()
                        mm(psA, oTb, wo_h[:, h % 4, 0:512])
                        mm(psBk, oTb, wo_h[:, h % 4, 512:1024])
                        yield
                        xacc(t0 + t, psA, psBk)

                items = []
                for i in range(nt):
                    items.append((0, i, i))
                    items.append((1, nt - 1 - i, i))
                active = []
                free = list(range(len(slots)))
                qi = 0
                while qi < len(items) or active:
                    while free and qi < len(items):
                        z_, c_, zi_ = items[qi]
                        qi += 1
                        s_ = free.pop(0)
                        active.append([chunk_task(z_, c_, zi_, slots[s_]), s_])
                    for a_ in list(active):
                        try:
                            next(a_[0])
                        except StopIteration:
                            active.remove(a_)
                            free.append(a_[1])
                if pass_id == 0:
                    for z in range(2):
                        load("sp", ndn_d[((si * 2 + o_) * 2 + z) * 8 + h], S_f[:, z, :])

    passes = [(0, 8, [(0, 2), (2, 2), (4, 2), (6, 2)], 0), (1, 16, [(0, 16)], 1)]

    def _main():
        for (pass_id, T, seqs, cond) in passes:
            mark("P%d load" % pass_id)
            load_x(pass_id, T)
            if pass_id == 1:
                dbg("x0s", X[:, 0, :], [128, D])
            for l in range(NL):
                mark("P%d L%d mixer" % (pass_id, l))
                if l % 2 == 0:
                    gla_mixer(l, pass_id, T, seqs, cond)
                else:
                    dn_mixer(l, pass_id, T, seqs, cond)
                mark("P%d L%d ln1" % (pass_id, l))
                dbg("xmix%d_%d" % (l, pass_id), X[:, 0, :], [128, D])
                if stop == "mix%d_%d" % (l, pass_id):
                    raise _Stop()
                layernorm(l, 0, T, False)
                dbg("xln%d_%d" % (l, pass_id), X[:, 0, :], [128, D])
                stop_at("ln%d_%d" % (l, pass_id))
                mark("P%d L%d ffn" % (pass_id, l))
                ffn(l, pass_id, T, cond)
                mark("P%d L%d ln2" % (pass_id, l))
                dbg("xffn%d_%d" % (l, pass_id), X[:, 0, :], [128, D])
                stop_at("ffn%d_%d" % (l, pass_id))
                layernorm(l, 1, T, l == NL - 1)
                dbg("xl%d_%d" % (l, pass_id), X[:, 0, :], [128, D])
                if stop == "l%d_%d" % (l, pass_id):
                    raise _Stop()
            mark("P%d store" % pass_id)
            dst = yp_d if pass_id == 0 else ys_d
            for t in range(T):
                load("sp", dst[t * 128:(t + 1) * 128, :], X[:, t, :])

    try:
        _main()
    except _Stop:
        pass
    fw.emit()
    return nc, fw, dbg_d


_CACHE = {}


def _inputs_per_core(inp, core):
    b = core % 2
    m = {}
    m["xp"] = np.ascontiguousarray(inp["x_prompt"][core * 4:(core + 1) * 4].reshape(1024, D))
    m["xs"] = np.ascontiguousarray(inp["x_sample"][b])
    m["cond2"] = np.ascontiguousarray(np.stack([inp["c_ctx"], inp["c"][b]], 0))
    m["sgla"] = np.ascontiguousarray(inp["state_gla"][b].reshape(16, 64, 128))
    m["sdn"] = np.ascontiguousarray(inp["state_dn"][b].reshape(32, 128, 128))
    for k in ("w_mod", "b_mod", "ln1_g", "ln1_b", "ln2_g", "ln2_b", "a_w_in", "a_w_gate", "a_b_gate", "a_norm",
              "b_proj", "b_scale", "a_w_out", "c_w_in", "c_conv", "c_a_log", "c_dt_bias", "c_norm", "c_w_out",
              "f_w_up", "f_conv", "f_w_down"):
        m[k] = np.ascontiguousarray(inp[k])
    return m


def kernel(**inputs):
    inp = {k: np.asarray(v, dtype=np.float32) for k, v in inputs.items()}
    if "nc" not in _CACHE:
        _CACHE["nc"] = build_program()[0]
    nc = _CACHE["nc"]
    n = 8
    in_maps = [_inputs_per_core(inp, c) for c in range(n)]
    res = run_bass_kernel_spmd(nc, in_maps, core_ids=list(range(n)))
    R = res.results
    y_prompt = np.concatenate([R[c]["yp"].reshape(4, 256, D) for c in range(n)], 0)
    y_sample = np.stack([R[0]["ys"], R[1]["ys"]], 0)
    ngla = np.concatenate([R[c]["ngla"].reshape(4, 2, 2, 4, 64, 128) for c in range(n)], 0)
    ndn = np.concatenate([R[c]["ndn"].reshape(4, 2, 2, 8, 128, 128) for c in range(n)], 0)
    return (y_prompt.astype(np.float32), y_sample.astype(np.float32), ngla.astype(np.float32), ndn.astype(np.float32))
```

```python
import math
import numpy as np
from concourse.bass_utils import run_bass_kernel_spmd
import concourse.bass as bass
import concourse.mybir as mybir

F32 = mybir.dt.float32
BF16 = mybir.dt.bfloat16
I32 = mybir.dt.int32
AF = mybir.ActivationFunctionType
ALU = mybir.AluOpType
AX = mybir.AxisListType

CELL = 256
_DT_SIZE = {F32: 4, BF16: 2, I32: 4}


class Region:
    def __init__(self, fw, name, handle, nbytes, cell=CELL):
        self.fw = fw
        self.name = name
        self.h = handle
        self.cell = cell
        self.ncell = (nbytes + cell - 1) // cell
        self.w = [None] * self.ncell
        self.r = [dict() for _ in range(self.ncell)]


class V:
    def __init__(self, region, ap):
        self.region = region
        self.ap = ap
        self._cells = None

    def __getitem__(self, key):
        return V(self.region, self.ap[key])

    def rr(self, pattern_, **kw):
        return V(self.region, self.ap.rearrange(pattern_, **kw))

    def bitcast(self, dt):
        return V(self.region, self.ap.bitcast(dt))

    def bc(self, shape):
        return V(self.region, self.ap.to_broadcast(shape))

    @property
    def shape(self):
        return self.ap.shape

    def cells(self):
        if self._cells is None:
            ap = self.ap
            esz = _DT_SIZE[ap.dtype]
            dims = list(ap.ap)[1:]
            base = int(ap.offset) if not isinstance(ap.offset, int) else ap.offset
            pstep = list(ap.ap)[0][0]
            if pstep > 0:
                base = base % pstep
            base_b = base * esz
            cs = set()
            CELL = self.region.cell
            dims = [(s, n) for (s, n) in dims if n > 1 or True]
            if not dims:
                dims = [(1, 1)]
            *outer, (ls, ln) = dims
            if ls in (0, 1):
                run = (esz * (ln if ls == 1 else 1))
                inner_iter = [0]
            else:
                run = esz
                inner_iter = [i * ls * esz for i in range(ln)]
            offs = [0]
            for (s, n) in outer:
                if s == 0:
                    continue
                offs = [o + i * s * esz for o in offs for i in range(n)]
            for o in offs:
                for ii in inner_iter:
                    a = base_b + o + ii
                    for c in range(a // CELL, (a + run - 1) // CELL + 1):
                        cs.add(c)
            self._cells = sorted(cs)
            assert self._cells[-1] < self.region.ncell, (self.region.name, self._cells[-1], self.region.ncell, ap)
        return self._cells


class Op:
    __slots__ = ("eng", "fn", "waits", "signal", "semval", "dma_sem", "is_dma")

    def __init__(self, eng, fn):
        self.eng = eng
        self.fn = fn
        self.waits = []
        self.signal = False
        self.semval = None
        self.dma_sem = None
        self.is_dma = False


ENGS = ("pe", "dve", "act", "pool", "sp")
N_DMA_SEMS = 12


class FW:
    def __init__(self, nc):
        self.nc = nc
        self.ops = {e: [] for e in ENGS}
        self.regions = []
        self.dma_rr = {"sp": 0, "pool": 0, "act": 0}
        self.dma_last = {}
        self._ctx = []
        self.psum_ptr = 0
        self.nops = 0

    def sbuf(self, name, shape, dt):
        g = self.nc.sbuf_tensor(name, list(shape), dt)
        h = g.__enter__()
        self._ctx.append(g)
        nb = int(np.prod(shape[1:])) * _DT_SIZE[dt]
        reg = Region(self, name, h, nb)
        self.regions.append(reg)
        return V(reg, h[:] if hasattr(h, "__getitem__") else h.ap())

    def psum_banks(self):
        self.banks = []
        for i in range(8):
            g = self.nc.psum_tensor(f"psb{i}", [128, 512], F32)
            h = g.__enter__()
            self._ctx.append(g)
            reg = Region(self, f"psb{i}", h, 2048, cell=2048)
            self.banks.append(V(reg, h[:]))

    def psum(self, ncols=512, parts=128, dt=F32):
        b = self.psum_ptr
        self.psum_ptr = (b + 1) % 8
        bank = self.banks[b] if dt == F32 else self.banks[b].bitcast(dt)
        return bank[0:parts, 0:ncols]

    def _deps(self, op, reads, writes):
        deps = {}
        for v in reads:
            reg = v.region
            for c in v.cells():
                w = reg.w[c]
                if w is not None:
                    deps[id(w)] = w
        for v in writes:
            reg = v.region
            for c in v.cells():
                w = reg.w[c]
                if w is not None:
                    deps[id(w)] = w
                for t in reg.r[c].values():
                    deps[id(t)] = t
        for t in deps.values():
            if t is op:
                continue
            if t.eng == "pe" and op.eng == "pe" and not t.is_dma and not op.is_dma:
                continue
            op.waits.append(t)
            t.signal = True
        for v in reads:
            reg = v.region
            key = op.dma_sem if op.is_dma else op.eng
            for c in v.cells():
                reg.r[c][key] = op
        for v in writes:
            reg = v.region
            for c in v.cells():
                reg.w[c] = op
                reg.r[c] = {}

    def op(self, eng, fn, reads=(), writes=()):
        if getattr(self, "halted", False):
            return None
        o = Op(eng, fn)
        self._deps(o, reads, writes)
        self.ops[eng].append(o)
        self.nops += 1
        return o

    def dma(self, queue, out, in_, reads=(), writes=(), **kw):
        if getattr(self, "halted", False):
            return None
        o = Op(queue, None)
        o.is_dma = True
        o.signal = True
        oap = out.ap if isinstance(out, V) else out
        iap = in_.ap if isinstance(in_, V) else in_
        rd = list(reads) + ([in_] if isinstance(in_, V) else [])
        wr = list(writes) + ([out] if isinstance(out, V) else [])
        k = self.dma_rr[queue]
        self.dma_rr[queue] = (k + 1) % N_DMA_SEMS
        o.dma_sem = (queue, k)
        prev = self.dma_last.get((queue, k))
        self._deps(o, rd, wr)
        if prev is not None:
            o.waits.append(prev)
        self.dma_last[(queue, k)] = o
        o.fn = lambda e: e.dma_start(out=oap, in_=iap, **kw)
        self.ops[queue].append(o)
        self.nops += 1
        return o

    def emit(self):
        nc = self.nc
        sems = {}
        semctx = []
        for e in ENGS:
            g = nc.semaphore(f"s_{e}")
            sems[e] = g.__enter__()
            semctx.append(g)
        dsems = {}
        for q in ("sp", "pool"):
            for k in range(N_DMA_SEMS):
                g = nc.semaphore(f"d_{q}{k}")
                dsems[(q, k)] = g.__enter__()
                semctx.append(g)
        for e in ENGS:
            cnt = 0
            for o in self.ops[e]:
                if o.is_dma:
                    continue
                if o.signal:
                    cnt += 1
                    o.semval = cnt
            self.maxsem = max(getattr(self, "maxsem", 0), cnt)
        dcnt = {}
        for e in ENGS:
            for o in self.ops[e]:
                if o.is_dma:
                    dcnt[o.dma_sem] = dcnt.get(o.dma_sem, 0) + 16
                    o.semval = dcnt[o.dma_sem]

        def semof(t):
            return dsems[t.dma_sem] if t.is_dma else sems[t.eng]

        def run(engname, engobj):
            known = {}
            for o in self.ops[engname]:
                need = {}
                for t in o.waits:
                    s = t.dma_sem if t.is_dma else t.eng
                    if t.semval > need.get(s, (0, None))[0]:
                        need[s] = (t.semval, t)
                for s, (val, t) in need.items():
                    if known.get(s, 0) >= val:
                        continue
                    engobj.wait_ge(semof(t), val)
                    known[s] = val
                ins = o.fn(engobj)
                if o.is_dma:
                    ins.then_inc(dsems[o.dma_sem], 16)
                elif o.signal:
                    ins.then_inc(sems[engname], 1)
            if engname in ("sp", "pool"):
                for k in range(N_DMA_SEMS):
                    if dcnt.get((engname, k), 0) > 0:
                        engobj.wait_ge(dsems[(engname, k)], dcnt[(engname, k)])

        with nc.Block() as block:
            @block.tensor
            def _(e):
                run("pe", e)

            @block.vector
            def _(e):
                run("dve", e)

            @block.scalar
            def _(e):
                run("act", e)

            @block.gpsimd
            def _(e):
                run("pool", e)

            @block.sync
            def _(e):
                run("sp", e)
        for g in reversed(semctx):
            g.__exit__(None, None, None)
        for g in reversed(self._ctx):
            g.__exit__(None, None, None)

D = 1024
NL = 4
DFF = 2816
NCH = DFF // 128
ALPHA = float(8 ** 0.25)
EPS = 1e-6
KC = 8
A_IN = 2080
C_IN = 4128
PI = float(np.pi)

DBG_SPECS = {}


class _Stop(Exception):
    pass


def build_program(debug=(), stop=None):
    nc = bass.Bass("TRN2", target_bir_lowering=False)
    fw = FW(nc)

    def din(name, shape):
        return nc.dram_tensor(name, list(shape), F32, kind="ExternalInput").ap()

    def dout(name, shape):
        return nc.dram_tensor(name, list(shape), F32, kind="ExternalOutput").ap()

    xp_d = din("xp", [1024, D])
    xs_d = din("xs", [2048, D])
    cond_d = din("cond2", [2, D])
    sgla_d = din("sgla", [16, 64, 128])
    sdn_d = din("sdn", [32, 128, 128])
    w_mod_d = din("w_mod", [NL, D, 6 * D])
    b_mod_d = din("b_mod", [NL, 6 * D])
    ln_d = {k: din(k, [NL, D]) for k in ("ln1_g", "ln1_b", "ln2_g", "ln2_b")}
    a_w_in_d = din("a_w_in", [2, D, A_IN])
    a_w_gate_d = din("a_w_gate", [2, 2, 16, 256])
    a_b_gate_d = din("a_b_gate", [2, 2, 256])
    a_norm_d = din("a_norm", [2, 128])
    b_proj_d = din("b_proj", [2, 4, 128, 128])
    b_scale_d = din("b_scale", [2, 512])
    a_w_out_d = din("a_w_out", [2, D, D])
    c_w_in_d = din("c_w_in", [2, D, C_IN])
    c_conv_d = din("c_conv", [2, 3, 3072])
    c_a_log_d = din("c_a_log", [2, 2, 8])
    c_dt_bias_d = din("c_dt_bias", [2, 2, 8])
    c_norm_d = din("c_norm", [2, 128])
    c_w_out_d = din("c_w_out", [2, D, D])
    f_w_up_d = din("f_w_up", [NL, D, 2 * DFF])
    f_conv_d = din("f_conv", [NL, 3, 2 * DFF])
    f_w_down_d = din("f_w_down", [NL, DFF, D])

    yp_d = dout("yp", [1024, D])
    ys_d = dout("ys", [2048, D])
    ngla_d = dout("ngla", [64, 64, 128])
    ndn_d = dout("ndn", [128, 128, 128])
    dbg_d = {}

    def A(v):
        return v.ap if isinstance(v, V) else v

    def rds(*xs):
        return [x for x in xs if isinstance(x, V)]

    def mm(out, lhsT, rhs, start=True, stop=True):
        fw.op("pe", lambda e: e.matmul(out.ap, lhsT=lhsT.ap, rhs=rhs.ap, start=start, stop=stop),
              reads=[lhsT, rhs], writes=[out])

    def tr(out, in_, idn):
        fw.op("pe", lambda e: e.transpose(out=out.ap, in_=in_.ap, identity=idn.ap),
              reads=[in_, idn], writes=[out])

    def act(out, in_, func, bias=None, scale=None, accum=None):
        kw = {}
        if bias is not None:
            kw["bias"] = A(bias)
        if scale is not None:
            kw["scale"] = A(scale)
        if accum is not None:
            kw["accum_out"] = accum.ap
        fw.op("act", lambda e: e.activation(out=out.ap, in_=in_.ap, func=func, **kw),
              reads=rds(in_, bias, scale), writes=rds(out, accum))

    def ts(eng, out, in0, s1, op0, s2=None, op1=None):
        kw = {"op1": op1} if op1 is not None else {}
        fw.op(eng, lambda e: e.tensor_scalar(out=out.ap, in0=in0.ap, scalar1=A(s1),
                                             scalar2=(A(s2) if s2 is not None else None), op0=op0, **kw),
              reads=rds(in0, s1, s2), writes=[out])

    def tt(eng, out, in0, in1, op):
        fw.op(eng, lambda e: e.tensor_tensor(out=out.ap, in0=in0.ap, in1=in1.ap, op=op),
              reads=[in0, in1], writes=[out])

    def stt(eng, out, in0, scalar, in1, op0, op1):
        fw.op(eng, lambda e: e.scalar_tensor_tensor(out=out.ap, in0=in0.ap, scalar=A(scalar), in1=in1.ap,
                                                    op0=op0, op1=op1),
              reads=rds(in0, scalar, in1), writes=[out])

    def cp(eng, out, in_):
        if eng == "act":
            fw.op("act", lambda e: e.copy(out=out.ap, in_=in_.ap), reads=[in_], writes=[out])
        else:
            fw.op(eng, lambda e: e.tensor_copy(out=out.ap, in_=in_.ap), reads=[in_], writes=[out])

    def memset(eng, out, val):
        fw.op(eng, lambda e: e.memset(out.ap, val), writes=[out])

    def scan(out, d0, d1, init=0.0):
        fw.op("dve", lambda e: e.tensor_tensor_scan(out=out.ap, data0=d0.ap, data1=d1.ap, initial=init,
                                                    op0=ALU.mult, op1=ALU.add),
              reads=[d0, d1], writes=[out])

    def recip(out, in_):
        fw.op("dve", lambda e: e.reciprocal(out=out.ap, in_=in_.ap), reads=[in_], writes=[out])

    def rsum(out, in_):
        fw.op("dve", lambda e: e.reduce_sum(out=out.ap, in_=in_.ap, axis=AX.X), reads=[in_], writes=[out])

    def load(q, out, src, **kw):
        fw.dma(q, out, src, **kw)

    def dbg(name, v, shape):
        if name in debug:
            d = dout("dbg_" + name, list(shape))
            dbg_d[name] = d
            fw.dma("sp" if v.ap.dtype == F32 else "pool", d, v)

    def stop_at(name):
        if stop == name:
            fw.halted = True

    fw.marks = []

    def mark(name):
        fw.marks.append((name, len(fw.ops["dve"]), len(fw.ops["pe"])))

    rr = [0]

    def evac_eng():
        rr[0] ^= 1
        return "act" if rr[0] else "dve"

    fw.psum_banks()
    X = fw.sbuf("X", [128, 16, D], F32)
    ARENA = fw.sbuf("ARENA", [128, 44032], BF16)
    ARENA_F = ARENA.bitcast(F32)
    ARENA_I = ARENA.bitcast(I32)
    WR = fw.sbuf("WR", [128, 3, 4096], BF16)
    LNB = fw.sbuf("LNB", [128, 2, D], F32)
    GB = fw.sbuf("GB", [128, D], F32)
    ones_f = fw.sbuf("ones_f", [128, 128], F32)
    ident_f = fw.sbuf("ident_f", [128, 128], F32)
    ident_b = fw.sbuf("ident_b", [128, 128], BF16)
    ones_b = fw.sbuf("ones_b", [128, 128], BF16)
    MU = fw.sbuf("MU", [128, 128], F32)
    MUs = fw.sbuf("MUs", [128, 128], F32)
    ML = fw.sbuf("ML", [128, 128], F32)
    MLs = fw.sbuf("MLs", [128, 128], F32)
    pidx = fw.sbuf("pidx", [64, 128], F32)
    BMK = fw.sbuf("BMK", [128, 4, 128], BF16)
    MDS = fw.sbuf("MDS", [128, 2, 128], BF16)
    selc = fw.sbuf("selc", [64, 256], F32)
    dncol = fw.sbuf("dncol", [128, 4, 160], F32)
    modcol = fw.sbuf("modcol", [128, NL, 48, 2], F32)
    mscale = fw.sbuf("mscale", [128, NL, 2, 8, 2], F32)
    scT = fw.sbuf("scT", [128, 8, 2], F32)
    small = fw.sbuf("small", [128, 320], F32)
    S_f = fw.sbuf("S_f", [128, 2, 128], F32)
    S_b = fw.sbuf("S_b", [128, 2, 128], BF16)
    wsm = fw.sbuf("wsm", [128, 8, 128], BF16)
    wg = fw.sbuf("wg", [48, 256], BF16)
    bproj = fw.sbuf("bproj", [128, 4, 128], BF16)
    convc = fw.sbuf("convc", [128, 160], F32)
    normB = fw.sbuf("normB", [128, 128], F32)
    junk = fw.sbuf("junk", [128, D], BF16)
    pe_c = ARENA_F[:, 0:512]
    freq = ARENA_F[:, 512:768]
    ptmp = ARENA_F[:, 768:1536].rr("p (a n) -> p a n", a=3)
    ptmpi = ARENA_I[:, 1536:1792]
    pcol = fw.sbuf("pcol", [128, 8], F32)
    dgs = fw.sbuf("dgs", [128, 128], F32)
    efix = fw.sbuf("efix", [128, 4, 16], F32)

    fw.sbuf_free = nc.sbuf_bytes_remaining
    memset("pool", ones_f, 1.0)
    memset("pool", ones_b, 1.0)

    def asel(out, in_, pattern, cmp, cm, base=0):
        fw.op("pool", lambda e: e.affine_select(out=out.ap, in_=in_.ap, pattern=pattern, compare_op=cmp, fill=0.0,
                                                base=base, channel_multiplier=cm), reads=[in_], writes=[out])

    asel(ident_f, ones_f, [[-1, 128]], ALU.is_equal, 1)
    cp("dve", ident_b, ident_f)
    asel(MU, ones_f, [[1, 128]], ALU.is_ge, -1)
    asel(MUs, ones_f, [[1, 128]], ALU.is_gt, -1)
    asel(ML, ones_f, [[-1, 128]], ALU.is_ge, 1)
    asel(MLs, ones_f, [[-1, 128]], ALU.is_gt, 1)
    fw.op("pool", lambda e: e.iota(pidx.ap, [[0, 128]], base=0, channel_multiplier=1,
                                   allow_small_or_imprecise_dtypes=True), writes=[pidx])
    mdt = ptmp.rr("p a n -> p (a n)")
    for bi_, bsz in enumerate((16, 32, 64)):
        nb_ = 128 // bsz
        Eb = ptmp[0:8, 0, 0:128]
        asel(Eb, ones_f[0:8, :], [[1, 128]], ALU.is_ge, -bsz, base=0)
        asel(Eb, Eb, [[-1, 128]], ALU.is_ge, bsz, base=bsz - 1)
        ps = fw.psum()
        mm(ps[:, 0:128], Eb[0:nb_, :], Eb[0:nb_, :])
        cp("dve", mdt[:, 256 + bi_ * 128:256 + (bi_ + 1) * 128], ps[:, 0:128])
    md16, md32, md64 = (mdt[:, 256 + i * 128:256 + (i + 1) * 128] for i in range(3))
    cp("dve", BMK[:, 0, :], md16)
    tt("dve", BMK[:, 1, :], md32, md16, ALU.subtract)
    tt("dve", BMK[:, 2, :], md64, md32, ALU.subtract)
    ts("dve", BMK[:, 3, :], md64, -1.0, ALU.mult, 1.0, ALU.add)
    tt("dve", MDS[:, 0, :], md16, MLs, ALU.mult)
    tt("dve", MDS[:, 1, :], md16, MUs, ALU.mult)

    stop_at("c1")
    condt = ARENA_F[0:2, 0:1024]
    load("sp", condt, cond_d)
    act(condt, condt, AF.Silu)
    ps = fw.psum()
    for kc in range(8):
        tr(ps[:, kc * 2:(kc + 1) * 2], condt[:, kc * 128:(kc + 1) * 128], ident_f[0:2, 0:2])
    cp("dve", scT.rr("p k c -> p (k c)"), ps[:, 0:16])

    stop_at("c2")
    bmr = ARENA_F[0:48, 1024:1152]
    bmT = small[:, 0:48]
    wm_slots = [ARENA_F[:, 2048 + i * 4096: 2048 + (i + 1) * 4096].rr("p (k n) -> p k n", k=8) for i in range(2)]
    for l in range(NL):
        load("sp", bmr, b_mod_d[l].rearrange("(b p) -> b p", p=128))
        ps = fw.psum()
        tr(ps[:, 0:48], bmr, ident_f[0:48, 0:48])
        cp("act", bmT, ps[:, 0:48])
        for g in range(12):
            slot = wm_slots[g % 2]
            load("sp", slot, w_mod_d[l].rearrange("(k p) n -> p k n", p=128)[:, :, g * 512:(g + 1) * 512])
            ps = fw.psum()
            for b4 in range(4):
                for kc in range(8):
                    mm(ps[:, b4 * 2:(b4 + 1) * 2], slot[:, kc, b4 * 128:(b4 + 1) * 128], scT[:, kc, :],
                       start=(kc == 0), stop=(kc == 7))
            ps3 = ps[:, 0:8].rr("p (b c) -> p b c", c=2)
            for c in range(2):
                tt("dve", modcol[:, l, 4 * g:4 * g + 4, c], ps3[:, :, c], bmT[:, 4 * g:4 * g + 4], ALU.add)
        for w, blk0 in enumerate((8, 32)):
            ts("dve", mscale[:, l, w], modcol[:, l, blk0:blk0 + 8, :], 1.0, ALU.add, 1.0 / ALPHA, ALU.mult)
    dbg("modcol", modcol.rr("p l b c -> p (l b c)"), [128, NL * 96])

    stop_at("c3")
    POOLW = (2, 4, 8, 16)
    for gi, w in enumerate(POOLW):
        lo = w // 2
        hi = w - 1 - lo
        fw.op("pool", lambda e, gi=gi, lo=lo, hi=hi: e.iota(efix[:, gi, 0:lo].ap, [[1, lo]], base=hi + 1, channel_multiplier=0,
                                                           allow_small_or_imprecise_dtypes=True), writes=[efix[:, gi, 0:lo]])
        if hi > 0:
            fw.op("pool", lambda e, gi=gi, hi=hi, w=w: e.iota(efix[:, gi, 8:8 + hi].ap, [[-1, hi]], base=w - 1, channel_multiplier=0,
                                                              allow_small_or_imprecise_dtypes=True), writes=[efix[:, gi, 8:8 + hi]])
        for (a, n) in ((0, lo), (8, hi)):
            if n > 0:
                recip(efix[:, gi, a:a + n], efix[:, gi, a:a + n])
                ts("dve", efix[:, gi, a:a + n], efix[:, gi, a:a + n], float(w), ALU.mult)

    def sincos(out_sin, out_cos, theta):
        for out, shift in ((out_sin, 0.0), (out_cos, 0.25)):
            t = ptmp[:, 1, :]
            gq = ptmp[:, 2, :]
            ts("dve", t, theta, 1.0 / (2 * PI), ALU.mult, shift, ALU.add)
            cp("dve", ptmpi, t)
            tt("dve", t, t, ptmpi, ALU.subtract)
            fw.op("dve", lambda e, t=t, gq=gq: e.tensor_single_scalar(out=gq.ap, in_=t.ap, scalar=0.5, op=ALU.is_ge),
                  reads=[t], writes=[gq])
            tt("dve", t, t, gq, ALU.subtract)
            fw.op("dve", lambda e, t=t, gq=gq: e.tensor_single_scalar(out=gq.ap, in_=t.ap, scalar=-0.5, op=ALU.is_lt),
                  reads=[t], writes=[gq])
            tt("dve", t, t, gq, ALU.add)
            act(out, t, AF.Sin, scale=2 * PI)

    def pe_consts():
        fw.op("pool", lambda e: e.iota(freq.ap, [[1, 256]], base=0, channel_multiplier=0, allow_small_or_imprecise_dtypes=True),
              writes=[freq])
        act(freq, freq, AF.Exp, scale=-math.log(10000.0) / 256.0)
        fw.op("pool", lambda e: e.iota(pcol[:, 0:1].ap, [[0, 1]], base=0, channel_multiplier=1, allow_small_or_imprecise_dtypes=True),
              writes=[pcol[:, 0:1]])
        fw.op("dve", lambda e: e.tensor_single_scalar(out=pcol[:, 1:2].ap, in_=pcol[:, 0:1].ap, scalar=64.0, op=ALU.is_ge),
              reads=[pcol[:, 0:1]], writes=[pcol[:, 1:2]])
        stt("dve", pcol[:, 2:3], pcol[:, 1:2], -64.0, pcol[:, 0:1], ALU.mult, ALU.add)

        ts("dve", ptmp[:, 0, :], freq, pcol[:, 2:3], ALU.mult)
        sincos(pe_c[:, 0:256], pe_c[:, 256:512], ptmp[:, 0, :])


    stop_at("c5")
    def seg512(n):
        out = []
        a = 0
        while a < n:
            out.append((a, min(512, n - a)))
            a += 512
        return out

    def load_x(pass_id, T):
        src = xp_d if pass_id == 0 else xs_d
        if pass_id == 1:
            pe_consts()
        for t in range(T):
            load("sp", X[:, t, :], src[t * 128:(t + 1) * 128, :])
            if pass_id == 1:
                ts("dve", pcol[:, 3:4], pcol[:, 1:2], float(2 * t), ALU.add)
                ts("dve", ptmp[:, 0, :], freq, pcol[:, 3:4], ALU.mult)
                pe_r = junk.bitcast(F32)
                sincos(pe_r[:, 0:256], pe_r[:, 256:512], ptmp[:, 0, :])
                tt("dve", X[:, t, 0:512], X[:, t, 0:512], pe_r, ALU.add)
                tt("dve", X[:, t, 512:1024], X[:, t, 512:1024], pe_c, ALU.add)
            act(X[:, t, :], X[:, t, :], AF.Identity, scale=ALPHA)
        stop_at("c6")

    def make_ut(UT, l, which, cond, tiles, col0, halo=None):
        shb = 0 if which == 0 else 24
        jobs = [(t, col0 + i * 128, None) for i, t in enumerate(tiles)]
        if halo is not None:
            jobs.append((halo[0], halo[2], halo[1]))
        for (t, c0, hc) in jobs:
            for half in range(2):
                ps = fw.psum()
                for q in range(4):
                    kc = half * 4 + q
                    tr(ps[:, q * 128:(q + 1) * 128], X[:, t, kc * 128:(kc + 1) * 128], ident_f)
                if which == 1:
                    stop_at("u1")
                eng_b = evac_eng()
                for q in range(4):
                    kc = half * 4 + q
                    sc = mscale[:, l, which, kc, cond:cond + 1]
                    sh = modcol[:, l, shb + kc, cond:cond + 1]
                    if hc is None:
                        src = ps[:, q * 128:(q + 1) * 128]
                        dst = UT[:, kc, c0:c0 + 128]
                    else:
                        src = ps[:, q * 128 + hc:q * 128 + hc + 1]
                        dst = UT[:, kc, c0:c0 + 1]
                    if eng_b == "act":
                        act(dst, src, AF.Identity, bias=sh, scale=sc)
                    else:
                        ts("dve", dst, src, sc, ALU.mult, sh, ALU.add)
                    if which == 1:
                        stop_at("u2")
                if which == 1:
                    stop_at("u3")
            if which == 1:
                stop_at("u4")
        if stop == "ut":
            dbg("ut", UT[:, 0, 0:1024], [128, 1024])
            for kc_ in range(8):
                dbg("ut%d" % kc_, UT[:, kc_, 0:256], [128, 256])
            dbg("x0", X[:, 0, :], [128, D])
            dbg("mscale", mscale.rr("p l w k c -> p (l w k c)"), [128, NL * 32])
            raise _Stop()

    def gbcast(l, blk0, cond):
        for half in range(2):
            ps = fw.psum()
            for q in range(4):
                kc = half * 4 + q
                dg = dgs
                ts("dve", dg, ident_f, modcol[:, l, blk0 + kc, cond:cond + 1], ALU.mult)
                mm(ps[:, q * 128:(q + 1) * 128], ones_f, dg)
            cp("act", GB[:, half * 512:(half + 1) * 512], ps)

    def layernorm(l, which, T, last):
        gname, bname = ("ln1_g", "ln1_b") if which == 0 else ("ln2_g", "ln2_b")
        load("sp", LNB[:, 0, :], ln_d[gname][l:l + 1, :].to_broadcast([128, D]))
        load("sp", LNB[:, 1, :], ln_d[bname][l:l + 1, :].to_broadcast([128, D]))
        stop_at("lna")
        if not last:
            act(LNB.rr("p a d -> p (a d)"), LNB.rr("p a d -> p (a d)"), AF.Identity, scale=ALPHA)
        stop_at("lnb")
        st = small[:, 64:72]
        for t in range(T):
            xt = X[:, t, :]
            rsum(st[:, 0:1], xt)
            stop_at("lnc")
            ts("dve", st[:, 1:2], st[:, 0:1], -1.0 / D, ALU.mult)
            act(junk, xt, AF.Square, bias=st[:, 1:2], accum=st[:, 2:3])
            stop_at("lnd")
            act(st[:, 3:4], st[:, 2:3], AF.Ln, bias=EPS, scale=1.0 / D)
            act(st[:, 4:5], st[:, 3:4], AF.Exp, scale=-0.5)
            ts("dve", xt, xt, st[:, 1:2], ALU.add, st[:, 4:5], ALU.mult)
            tt("dve", xt, xt, LNB[:, 0, :], ALU.mult)
            tt("dve", xt, xt, LNB[:, 1, :], ALU.add)

    wr_i = [0]
    hs_i = [0]

    def wslot(nring=3):
        s = WR[:, wr_i[0] % nring, :]
        wr_i[0] += 1
        return s

    WO = WR[:, 2, :]

    def xacc(t, psA, psB):
        tt("dve", X[:, t, 0:512], X[:, t, 0:512], psA, ALU.add)
        tt("dve", X[:, t, 512:1024], X[:, t, 512:1024], psB, ALU.add)

    def ffn(l, pass_id, T, cond):
        cr = ARENA_F[0:44, 0:128]
        for k in range(3):
            load("sp", cr, f_conv_d[l][k].rearrange("(c p) -> c p", p=128))
            ps = fw.psum()
            tr(ps[:, 0:44], cr, ident_f[0:44, 0:44])
            cp("act", convc[:, k * 44:(k + 1) * 44], ps[:, 0:44])
        stop_at("fa")
        gbcast(l, 40, cond)
        stop_at("fb")
        UTs = ARENA[:, 0:8224].rr("p (k n) -> p k n", k=8)
        actT = ARENA[:, 8224:8224 + 22528].rr("p (j n) -> p j n", j=NCH)
        o0 = 8224 + 22528
        hpre = [[ARENA[:, o0 + (2 * i + h) * 1032: o0 + (2 * i + h) * 1032 + 1028] for h in range(2)] for i in range(2)]
        o1 = (o0 + 4 * 1032 + 1) // 2 + 8
        accs = [[ARENA_F[:, o1 + (2 * i + h) * 1024: o1 + (2 * i + h + 1) * 1024] for h in range(2)] for i in range(2)]
        assert (o1 + 4096) <= 22016
        groups = [(0, 8, None)] if pass_id == 0 else [(0, 8, "R"), (8, 8, "L")]
        wup = f_w_up_d[l].rearrange("(k p) n -> p k n", p=128)
        wdn = f_w_down_d[l].rearrange("(c p) n -> p c n", p=128)
        for (t0, nt, hal) in groups:
            ntok = nt * 128
            halo = None
            if hal == "R":
                halo = (t0 + nt, 0, 1026)
            elif hal == "L":
                halo = (t0 - 1, 127, 1)
            make_ut(UTs, l, 1, cond, list(range(t0, t0 + nt)), 2, halo)
            stop_at("f0")
            for i in range(2):
                for h in range(2):
                    if hal != "L":
                        memset("pool", hpre[i][h][:, 0:2], 0.0)
                    if hal != "R":
                        memset("pool", hpre[i][h][:, 1026:1028], 0.0)
            segs = [(2, 512), (514, 512)]
            if hal == "R":
                segs.append((1026, 1))
            if hal == "L":
                segs.append((1, 1))
            jgs = [(j0, min(4, NCH - j0)) for j0 in range(0, NCH, 4)]
            jg2 = [(j0, 2) for j0 in range(0, NCH, 2)]
            WRH = WR.rr("p s n -> p (s n)").rr("p (s n) -> p s n", s=6)

            def up_load(gi_):
                j0_, nj_ = jg2[gi_]
                sa_ = WRH[:, hs_i[0] % 6, :].rr("p (k n) -> p k n", k=8)
                sg_ = WRH[:, (hs_i[0] + 1) % 6, :].rr("p (k n) -> p k n", k=8)
                hs_i[0] += 2
                load("pool", sa_, wup[:, :, j0_ * 128:(j0_ + nj_) * 128])
                load("pool", sg_, wup[:, :, DFF + j0_ * 128:DFF + (j0_ + nj_) * 128])
                return sa_, sg_

            nxt = up_load(0)
            for gi_, (j0, nj) in enumerate(jg2):
                sa, sg = nxt
                if gi_ + 1 < len(jg2):
                    nxt = up_load(gi_ + 1)
                for jj in range(nj):
                    j = j0 + jj
                    hp = hpre[j % 2]
                    acc = accs[j % 2]
                    for h, sw in enumerate((sa, sg)):
                        cc = h * 22 + j
                        w0 = convc[:, cc:cc + 1]
                        w1 = convc[:, 44 + cc:44 + cc + 1]
                        w2 = convc[:, 88 + cc:88 + cc + 1]
                        for (c0, n) in segs:
                            ps = fw.psum()
                            for kc in range(8):
                                mm(ps[:, 0:n], sw[:, kc, jj * 128:(jj + 1) * 128], UTs[:, kc, c0:c0 + n],
                                   start=(kc == 0), stop=(kc == 7))
                            if n > 1:
                                cp("act", hp[h][:, c0:c0 + n], ps[:, 0:n])
                                act(acc[h][:, c0 - 2:c0 - 2 + n], ps[:, 0:n], AF.Identity, scale=w1)
                            else:
                                cp("act", hp[h][:, c0:c0 + n], ps[:, 0:n])
                        hh = hp[h]
                        if pass_id == 0:
                            a3 = acc[h].rr("p (s t) -> p s t", s=4)
                            h3 = hh[:, 2:1026].rr("p (s t) -> p s t", s=4)
                            stt("dve", a3[:, :, 1:256], h3[:, :, 0:255], w0, a3[:, :, 1:256], ALU.mult, ALU.add)
                            stt("dve", a3[:, :, 0:255], h3[:, :, 1:256], w2, a3[:, :, 0:255], ALU.mult, ALU.add)
                        else:
                            stt("dve", acc[h], hh[:, 1:1025], w0, acc[h], ALU.mult, ALU.add)
                            stt("dve", acc[h], hh[:, 3:1027], w2, acc[h], ALU.mult, ALU.add)
                    act(acc[1], acc[1], AF.Silu)
                    tt("dve", actT[:, j, :], acc[1], acc[0], ALU.mult)
                    stop_at("f0b")
            if l == 0 and t0 == 0:
                dbg("actT%d" % pass_id, actT[:, 0, :], [128, 1024])
            stop_at("f1")
            def dn_load_blk(bi_):
                j0_, nj_ = jgs[bi_ % len(jgs)]
                sd_ = wslot()[:, 0:1024 * nj_].rr("p (c n) -> p c n", c=nj_)
                load("pool", sd_, wdn[:, j0_:j0_ + nj_, :])
                return sd_

            nblk = (nt // 4) * len(jgs)
            nxtd = dn_load_blk(0)
            bcnt = 0
            for q0 in range(0, nt, 4):
                for (j0, nj) in jgs:
                    sd = nxtd
                    bcnt += 1
                    if bcnt < nblk:
                        nxtd = dn_load_blk(bcnt)
                    for ti in range(4):
                        for jj in range(nj):
                            j = j0 + jj
                            for hf in range(2):
                                mm(fw.banks[ti * 2 + hf], actT[:, j, (q0 + ti) * 128:(q0 + ti + 1) * 128],
                                   sd[:, jj, hf * 512:(hf + 1) * 512], start=(j == 0), stop=(j == NCH - 1))
                for ti in range(4):
                    t_ = t0 + q0 + ti
                    for hf in range(2):
                        tmpx = accs[ti % 2][hf][:, 0:512]
                        tt("dve", tmpx, fw.banks[ti * 2 + hf], GB[:, hf * 512:(hf + 1) * 512], ALU.mult)
                        tt("dve", X[:, t_, hf * 512:(hf + 1) * 512], X[:, t_, hf * 512:(hf + 1) * 512], tmpx, ALU.add)
                stop_at("f2")

    def gla_mixer(l, pass_id, T, seqs, cond):
        e_ = l // 2
        win = a_w_in_d[e_].rearrange("(k p) n -> p k n", p=128)
        wout = a_w_out_d[e_].rearrange("(c p) n -> p c n", p=128)
        UT = ARENA[:, 0:16384].rr("p (k n) -> p k n", k=8)
        TR = ARENA[:, 16384:39936]
        TRF = ARENA_F[:, 8192:19968]
        make_ut(UT, l, 0, cond, list(range(T)), 0)
        gbcast(l, 16, cond)
        memset("pool", wsm, 0.0)
        load("pool", wsm[:, :, 0:16], win[:, :, 1536:1552])
        load("pool", wsm[:, :, 32:48], win[:, :, 1552:1568])
        load("pool", wg[0:16, :], a_w_gate_d[e_, 0])
        load("pool", wg[32:48, :], a_w_gate_d[e_, 1])
        load("pool", bproj, b_proj_d[e_].rearrange("g c d -> c g d"))
        load("sp", normB, a_norm_d[e_:e_ + 1, :].to_broadcast([128, 128]))
        bg = TRF[0:8, 0:128]
        load("sp", bg[0:8, 0:64], a_b_gate_d[e_].rearrange("z (h d) -> (z h) d", d=64))
        ps = fw.psum()
        tr(ps[0:64, 0:8], bg[0:8, 0:64], ident_f[0:8, 0:8])
        ts("dve", small[0:64, 80:88], ps[0:64, 0:8], -1.0, ALU.mult)
        ps = fw.psum()
        bg2 = TRF[0:4, 128:256]
        load("sp", bg2, b_scale_d[e_].rearrange("(g d) -> g d", d=128))
        tr(ps[:, 0:4], bg2, ident_f[0:4, 0:4])
        cp("act", small[:, 96:100], ps[:, 0:4])
        wo_h = WO.rr("p (c n) -> p c n", c=4)
        load("pool", wo_h, wout[:, 0:4, :])
        for c in range(4):
            tt("dve", wo_h[:, c, :], wo_h[:, c, :], GB, ALU.mult)
        o_f = LNB.rr("p a d -> p (a d)").rr("p (t v) -> p t v", v=128)
        gla_next = [None]
        stop_at("g1")

        for si, (t0, nt) in enumerate(seqs):
            L = nt * 128
            c0 = t0 * 128
            qT = TR[0:64, 0:2048]
            kT = TR[0:64, 2048:4096]
            v_tok = TR[:, 4096:6144].rr("p (t v) -> p t v", v=128)
            r_tok = TR[:, 6144:8192].rr("p (t v) -> p t v", v=128)
            lrT = TR[0:64, 8192:10240]
            qe = TR[0:64, 10240:11264]
            ke = TR[0:64, 11264:12288]
            kd = TR[0:64, 12288:13312]
            kd_tok = TR[:, 13312:13824].rr("p (c d) -> p c d", d=64)
            ATb = [TR[:, 13824 + i * 128:13824 + (i + 1) * 128] for i in range(2)]
            ogb = TR[:, 14080:14208]
            oTb = TR[:, 14208:14336]
            fo = 14336 // 2
            cpos = TRF[0:64, fo:fo + 1024]
            tmp = TRF[0:64, fo + 1024:fo + 2048]
            dcol = TRF[0:64, fo + 2048:fo + 2056]
            osum = TRF[:, fo + 2056:fo + 2184]
            ost = TRF[:, fo + 2184:fo + 2192]
            totc = TRF[0:64, fo + 2192:fo + 2200]
            for (a, n) in seg512(L):
                ps = fw.psum()
                for kc in range(8):
                    mm(ps[0:64, 0:n], wsm[:, kc, 0:64], UT[:, kc, c0 + a:c0 + a + n], start=(kc == 0), stop=(kc == 7))
                cp("act", lrT[:, a:a + n], ps[0:64, 0:n])
            stop_at("g2")
            def gla_load(h_):
                sw_ = wslot(2)[:, 0:8 * 384].rr("p (k n) -> p k n", k=8)
                load("pool", sw_[:, :, 0:64], win[:, :, h_ * 64:(h_ + 1) * 64])
                load("pool", sw_[:, :, 64:128], win[:, :, 256 + h_ * 64:256 + (h_ + 1) * 64])
                load("pool", sw_[:, :, 128:256], win[:, :, 512 + h_ * 128:512 + (h_ + 1) * 128])
                load("pool", sw_[:, :, 256:384], win[:, :, 1024 + h_ * 128:1024 + (h_ + 1) * 128])
                return sw_

            if si == 0:
                gla_next[0] = gla_load(0)
            for h in range(4):
                sw = gla_next[0]
                stop_at("g2a")
                for (a, n) in seg512(L):
                    ps = fw.psum()
                    for kc in range(8):
                        mm(ps[0:64, 0:n], sw[:, kc, 0:64], UT[:, kc, c0 + a:c0 + a + n], start=(kc == 0), stop=(kc == 7))
                    act(qT[:, a:a + n], ps[0:64, 0:n], AF.Identity, scale=0.125)
                    stop_at("g2b")
                    ps = fw.psum()
                    for kc in range(8):
                        mm(ps[0:64, 0:n], sw[:, kc, 64:128], UT[:, kc, c0 + a:c0 + a + n], start=(kc == 0), stop=(kc == 7))
                    cp("dve", kT[:, a:a + n], ps[0:64, 0:n])
                    stop_at("g2c")
                for t in range(nt):
                    ps = fw.psum()
                    for kc in range(8):
                        mm(ps[:, 0:128], UT[:, kc, c0 + t * 128:c0 + (t + 1) * 128], sw[:, kc, 128:256], start=(kc == 0), stop=(kc == 7))
                    cp("dve", v_tok[:, t, :], ps[:, 0:128])
                    stop_at("g2d")
                    ps = fw.psum()
                    for kc in range(8):
                        mm(ps[:, 0:128], UT[:, kc, c0 + t * 128:c0 + (t + 1) * 128], sw[:, kc, 256:384], start=(kc == 0), stop=(kc == 7))
                    act(r_tok[:, t, :], ps[:, 0:128], AF.Silu)
                    stop_at("g2e")
                stop_at("g3")
                if h + 1 < 4:
                    gla_next[0] = gla_load(h + 1)
                elif si + 1 < len(seqs):
                    gla_next[0] = gla_load(0)
                for z in range(2):
                    S = S_f[0:64, z, :]
                    Sb = S_b[0:64, z, :]
                    if pass_id == 0:
                        memset("pool", S, 0.0)
                    else:
                        load("sp", S, sgla_d[(e_ * 2 + z) * 4 + h])
                    cp("act", Sb, S)
                    blocks = [(b0, min(8, nt - b0)) for b0 in range(0, nt, 8)]
                    if z == 1:
                        blocks = blocks[::-1]
                    for (b0, nb) in blocks:
                        n = nb * 128
                        for (a, m) in seg512(n):
                            ps = fw.psum()
                            mm(ps[0:64, 0:m], wg[z * 32:z * 32 + 16, h * 64:(h + 1) * 64],
                               lrT[z * 32:z * 32 + 16, b0 * 128 + a:b0 * 128 + a + m])
                            act(tmp[:, a:a + m], ps[0:64, 0:m], AF.Exp, bias=small[0:64, 80 + z * 4 + h:81 + z * 4 + h], scale=-1.0)
                        act(tmp[:, 0:n], tmp[:, 0:n], AF.Ln, bias=1.0)
                        for c in range(nb):
                            cs = slice(c * 128, (c + 1) * 128)
                            scan(cpos[:, cs], ones_f[0:64, :], tmp[:, cs])
                        if z == 1:
                            tt("dve", tmp[:, 0:n], tmp[:, 0:n], cpos[:, 0:n], ALU.subtract)
                            cp("dve", totc[:, 0:nb], cpos[:, 0:n].rr("p (c i) -> p c i", i=128)[:, :, 127])
                            for c in range(nb):
                                cs = slice(c * 128, (c + 1) * 128)
                                ts("dve", cpos[:, cs], tmp[:, cs], totc[:, c:c + 1], ALU.add)
                        lastc = (lambda c: c * 128 + 127) if z == 0 else (lambda c: c * 128)
                        act(tmp[:, 0:n], cpos[:, 0:n], AF.Exp, scale=-1.0 / 16)
                        tt("dve", qe[:, 0:n], qT[:, b0 * 128:b0 * 128 + n], tmp[:, 0:n], ALU.mult)
                        tmp3 = tmp[:, 0:n].rr("p (c i) -> p c i", i=128)
                        cp("dve", dcol[:, 0:nb], tmp3[:, :, 127 if z == 0 else 0])
                        act(tmp[:, 0:n], cpos[:, 0:n], AF.Exp, scale=1.0 / 16)
                        tt("dve", ke[:, 0:n], kT[:, b0 * 128:b0 * 128 + n], tmp[:, 0:n], ALU.mult)
                        for c in range(nb):
                            cs = slice(c * 128, (c + 1) * 128)
                            ts("dve", tmp[:, cs], cpos[:, cs], cpos[:, lastc(c):lastc(c) + 1], ALU.subtract)
                        act(tmp[:, 0:n], tmp[:, 0:n], AF.Exp, scale=1.0 / 16)
                        tt("dve", kd[:, 0:n], kT[:, b0 * 128:b0 * 128 + n], tmp[:, 0:n], ALU.mult)
                        psk = fw.psum(dt=BF16)
                        for c in range(nb):
                            tr(psk[:, c * 64:(c + 1) * 64], kd[:, c * 128:(c + 1) * 128], ident_b[0:64, 0:64])
                        cp("act", kd_tok[:, 0:nb, :].rr("p c d -> p (c d)"), psk[:, 0:nb * 64])
                        stop_at("g4")
                        corder = range(nb) if z == 0 else range(nb - 1, -1, -1)
                        for c in corder:
                            t = b0 + c
                            cs = slice(c * 128, (c + 1) * 128)
                            ps = fw.psum()
                            mm(ps[:, 0:128], ke[:, cs], qe[:, cs])
                            AT = ATb[c % 2]
                            tt("dve", AT, ps[:, 0:128], MU if z == 0 else ML, ALU.mult)
                            ps2 = fw.psum()
                            mm(ps2[:, 0:128], AT, v_tok[:, t, :], start=True, stop=False)
                            mm(ps2[:, 0:128], qe[:, cs], Sb, start=False, stop=True)
                            ps3 = fw.psum()
                            mm(ps3[0:64, 0:128], kd_tok[:, c, :], v_tok[:, t, :])
                            stt("dve", S, S, dcol[:, c:c + 1], ps3[0:64, 0:128], ALU.mult, ALU.add)
                            cp("act", Sb, S)
                            if z == 0:
                                cp("act", o_f[:, t, :], ps2[:, 0:128])
                            else:
                                tt("dve", osum, ps2[:, 0:128], o_f[:, t, :], ALU.add)
                                act(junk[:, 0:128], osum, AF.Square, accum=ost[:, 0:1])
                                act(ost[:, 1:2], ost[:, 0:1], AF.Ln, bias=EPS, scale=1.0 / 128)
                                act(ost[:, 2:3], ost[:, 1:2], AF.Exp, scale=-0.5)
                                stt("dve", osum, osum, ost[:, 2:3], normB, ALU.mult, ALU.mult)
                                tt("dve", ogb, osum, r_tok[:, t, :], ALU.mult)
                                pst = fw.psum(dt=BF16)
                                tr(pst[:, 0:128], ogb, ident_b)
                                cp("act", oTb, pst[:, 0:128])
                                psA = fw.psum()
                                psB = fw.psum()
                                mm(psA, oTb, wo_h[:, h, 0:512])
                                mm(psB, oTb, wo_h[:, h, 512:1024])
                                xacc(t0 + t, psA, psB)
                            stop_at("g5")
                        stop_at("g6")
                    if pass_id == 0:
                        s_glob = si
                        load("sp", ngla_d[((s_glob * 2 + e_) * 2 + z) * 4 + h], S)
        wo_p = WO.rr("p (c n) -> p c n", c=4)
        load("pool", wo_p, wout[:, 4:8, :])
        for c in range(4):
            tt("dve", wo_p[:, c, :], wo_p[:, c, :], GB, ALU.mult)
        swp = wslot(2).rr("p (k n) -> p k n", k=8)
        load("pool", swp, win[:, :, 1568:2080])
        for si, (t0, nt) in enumerate(seqs):
            L = nt * 128
            c0 = t0 * 128
            pm = TR[:, 0:8192].rr("p (g n) -> p g n", g=4)
            xpad = TRF[:, 4096:4096 + 2080]
            sA = TRF[:, 6176:6176 + 2080]
            sB = TRF[:, 8256:8256 + 2080]
            pooled = TR[:, 20672:20672 + 2048]
            assert 20672 + 2048 <= 23552 and (8256 + 2080) * 2 <= 20672
            for gi, w in enumerate(POOLW):
                lo = w // 2
                hi = w - 1 - lo
                memset("pool", xpad[:, 0:16], 0.0)
                memset("pool", xpad[:, 16 + L:32 + L], 0.0)
                for (a, n) in seg512(L):
                    ps = fw.psum()
                    for kc in range(8):
                        mm(ps[:, 0:n], swp[:, kc, gi * 128:(gi + 1) * 128], UT[:, kc, c0 + a:c0 + a + n], start=(kc == 0), stop=(kc == 7))
                    cp("act", xpad[:, 16 + a:16 + a + n], ps[:, 0:n])
                src = xpad
                k = 1
                bufs = [sA, sB]
                bi = 0
                while k < w:
                    dst = bufs[bi]
                    bi ^= 1
                    n = L + 32 - 2 * k
                    tt("dve", dst[:, 0:n], src[:, 0:n], src[:, k:k + n], ALU.add)
                    src = dst
                    k *= 2
                wsum = src[:, 16 - lo:16 - lo + L]
                tt("dve", wsum[:, 0:lo], wsum[:, 0:lo], efix[:, gi, 0:lo], ALU.mult)
                if hi > 0:
                    tt("dve", wsum[:, L - hi:L], wsum[:, L - hi:L], efix[:, gi, 8:8 + hi], ALU.mult)
                stt("dve", pooled[:, 0:L], wsum, 1.0 / w, xpad[:, 16:16 + L], ALU.mult, ALU.subtract)
                for (a, n) in seg512(L):
                    ps = fw.psum()
                    mm(ps[:, 0:n], bproj[:, gi, :], pooled[:, a:a + n])
                    act(pm[:, gi, a:a + n], ps[:, 0:n], AF.Identity, scale=small[:, 96 + gi:97 + gi])
            for t in range(nt):
                psA = fw.psum()
                psB = fw.psum()
                for gi in range(4):
                    mm(psA, pm[:, gi, t * 128:(t + 1) * 128], wo_p[:, gi, 0:512], start=(gi == 0), stop=(gi == 3))
                for gi in range(4):
                    mm(psB, pm[:, gi, t * 128:(t + 1) * 128], wo_p[:, gi, 512:1024], start=(gi == 0), stop=(gi == 3))
                xacc(t0 + t, psA, psB)

    def dn_mixer(l, pass_id, T, seqs, cond):
        o_ = l // 2
        win = c_w_in_d[o_].rearrange("(k p) n -> p k n", p=128)
        wout = c_w_out_d[o_].rearrange("(c p) n -> p c n", p=128)
        UT = ARENA[:, 0:16384].rr("p (k n) -> p k n", k=8)
        TR = ARENA[:, 16384:39936]
        TRF = ARENA_F[:, 8192:19968]
        make_ut(UT, l, 0, cond, list(range(T)), 0)
        gbcast(l, 16, cond)
        memset("pool", wsm, 0.0)
        for i, off in enumerate((0, 32, 64, 96)):
            load("pool", wsm[:, :, off:off + 8], win[:, :, 4096 + i * 8:4096 + (i + 1) * 8])
        load("sp", normB, c_norm_d[o_:o_ + 1, :].to_broadcast([128, 128]))
        memset("pool", small[0:40, 104:106], 0.0)
        for z in range(2):
            load("sp", small[z * 32:z * 32 + 8, 104:105], c_a_log_d[o_, z].rearrange("(h o) -> h o", o=1))
            load("sp", small[z * 32:z * 32 + 8, 105:106], c_dt_bias_d[o_, z].rearrange("(h o) -> h o", o=1))
        act(small[0:40, 104:105], small[0:40, 104:105], AF.Exp)
        cr = TRF[0:72, 0:128]
        load("sp", cr, c_conv_d[o_].rearrange("k (c p) -> (k c) p", p=128))
        ps = fw.psum()
        tr(ps[:, 0:72], cr, ident_f[0:72, 0:72])
        cp("act", convc[:, 0:72], ps[:, 0:72])
        o_f = LNB.rr("p a d -> p (a d)").rr("p (t v) -> p t v", v=128)
        dn_next = [None]
        SLc = seqs[0][1]
        groups = [(seqs[0][0], sum(n_ for (_, n_) in seqs))]

        for si, (t0, nt) in enumerate(groups):
            L = nt * 128
            c0 = t0 * 128
            R1 = TRF[0:128, 0:L]
            R2 = TRF[0:64, L:2 * L]
            bo = 4 * L
            hpre = TR[:, bo:bo + L + 2]
            qT = TR[:, bo + L + 8:bo + 2 * L + 8]
            kT = TR[:, bo + 2 * L + 8:bo + 3 * L + 8]
            k_tok = TR[:, bo + 3 * L + 8:bo + 4 * L + 8].rr("p (t v) -> p t v", v=128)
            v_tok = TR[:, bo + 4 * L + 8:bo + 5 * L + 8].rr("p (t v) -> p t v", v=128)
            z_tok = TR[:, bo + 5 * L + 8:bo + 6 * L + 8].rr("p (t v) -> p t v", v=128)
            so = bo + 6 * L + 8
            NSLOT = 4
            slots = []
            for s_ in range(NSLOT):
                base = 16384 + so + s_ * 2176
                if base + 2176 <= 44032:
                    sbk = [ARENA[:, base + i * 128:base + (i + 1) * 128] for i in range(11)]
                    fb = (base + 11 * 128) // 2
                    fk = [ARENA_F[:, fb + i * 128:fb + (i + 1) * 128] for i in range(3)]
                else:
                    assert s_ == 3 and L + 2 >= 11 * 128
                    hb = 16384 + bo
                    sbk = [ARENA[:, hb + i * 128:hb + (i + 1) * 128] for i in range(11)]
                    jf = junk.bitcast(F32)
                    fk = [jf[:, 128 + i * 128:128 + (i + 1) * 128] for i in range(3)]
                slots.append((sbk, fk, dncol[:, s_, 0:104], dncol[:, s_, 104:144], dncol[:, s_, 144:160],
                              (fw.banks[2 * s_], fw.banks[2 * s_ + 1])))
            totr = small[0:40, 296:312]
            memset("pool", R1[:, 0:L], 0.0)
            memset("pool", R2[:, 0:L], 0.0)
            for (a, n) in seg512(L):
                ps = fw.psum()
                for kc in range(8):
                    mm(ps[0:128, 0:n], wsm[:, kc, 0:128], UT[:, kc, c0 + a:c0 + a + n], start=(kc == 0), stop=(kc == 7))
                act(R1[0:40, a:a + n], ps[0:40, 0:n], AF.Exp, bias=small[0:40, 105:106])
                act(R1[0:40, a:a + n], R1[0:40, a:a + n], AF.Ln, bias=1.0)
                ts("dve", R2[0:40, a:a + n], R1[0:40, a:a + n], small[0:40, 104:105], ALU.mult)
                act(R1[64:104, a:a + n], ps[64:104, 0:n], AF.Sigmoid)
            for c in range(nt):
                cs = slice(c * 128, (c + 1) * 128)
                scan(R1[0:40, cs], ones_f[0:40, :], R2[0:40, cs])
            tt("dve", R2[32:40, 0:L], R2[32:40, 0:L], R1[32:40, 0:L], ALU.subtract)
            cp("dve", totr[32:40, 0:nt], R1[32:40, 0:L].rr("p (c i) -> p c i", i=128)[:, :, 127])
            for c in range(nt):
                cs = slice(c * 128, (c + 1) * 128)
                ts("dve", R1[32:40, cs], R2[32:40, cs], totr[32:40, c:c + 1], ALU.add)
            for c in range(nt):
                cs = slice(c * 128, (c + 1) * 128)
                ts("dve", R2[0:8, cs], R1[0:8, cs], R1[0:8, c * 128 + 127:c * 128 + 128], ALU.subtract)
                ts("dve", R2[32:40, cs], R1[32:40, cs], R1[32:40, c * 128:c * 128 + 1], ALU.subtract)
            if l == 1 and si == 0:
                dbg("R1_%d" % pass_id, R1[0:104, 0:256], [104, 256])
                dbg("R2_%d" % pass_id, R2[0:40, 0:256], [40, 256])

            def dn_load(h_):
                sw_ = wslot(2).rr("p (k n) -> p k n", k=8)
                for i_ in range(4):
                    load("pool", sw_[:, :, i_ * 128:(i_ + 1) * 128], win[:, :, i_ * 1024 + h_ * 128:i_ * 1024 + (h_ + 1) * 128])
                return sw_

            if si == 0:
                dn_next[0] = dn_load(0)
            for h in range(8):
                if h % 4 == 0:
                    wo_h = WO.rr("p (c n) -> p c n", c=4)
                    load("pool", wo_h, wout[:, h:h + 4, :])
                sw = dn_next[0]
                memset("pool", hpre[:, 0:1], 0.0)
                memset("pool", hpre[:, L + 1:L + 2], 0.0)
                for i, dstT in enumerate((qT, kT, None)):
                    for (a, n) in seg512(L):
                        ps = fw.psum()
                        for kc in range(8):
                            mm(ps[:, 0:n], sw[:, kc, i * 128:(i + 1) * 128], UT[:, kc, c0 + a:c0 + a + n], start=(kc == 0), stop=(kc == 7))
                        cp("act", hpre[:, 1 + a:1 + a + n], ps[:, 0:n])
                    cc = i * 8 + h
                    for (a, n) in seg512(L):
                        accv = junk.bitcast(F32)[:, 0:n]
                        act(accv, hpre[:, 1 + a:1 + a + n], AF.Identity, scale=convc[:, 24 + cc:25 + cc])
                        if SLc * 128 >= L:
                            stt("dve", accv, hpre[:, a:a + n], convc[:, cc:cc + 1], accv, ALU.mult, ALU.add)
                            stt("dve", accv, hpre[:, 2 + a:2 + a + n], convc[:, 48 + cc:49 + cc], accv, ALU.mult, ALU.add)
                        else:
                            sl_ = SLc * 128
                            a3_ = accv.rr("p (s t) -> p s t", t=sl_)
                            h3_ = hpre[:, 1 + a:1 + a + n].rr("p (s t) -> p s t", t=sl_)
                            stt("dve", a3_[:, :, 1:sl_], h3_[:, :, 0:sl_ - 1], convc[:, cc:cc + 1], a3_[:, :, 1:sl_], ALU.mult, ALU.add)
                            stt("dve", a3_[:, :, 0:sl_ - 1], h3_[:, :, 1:sl_], convc[:, 48 + cc:49 + cc], a3_[:, :, 0:sl_ - 1], ALU.mult, ALU.add)
                        if dstT is not None:
                            act(dstT[:, a:a + n], accv, AF.Silu)
                            sqv = TR[:, so:so + 512]
                            act(sqv[:, 0:n], dstT[:, a:a + n], AF.Square)
                            ps = fw.psum()
                            mm(ps[:, 0:n], ones_b, sqv[:, 0:n])
                            rn = junk.bitcast(F32)[:, 0:n]
                            act(rn, ps[:, 0:n], AF.Ln, bias=EPS)
                            act(rn, rn, AF.Exp, scale=-0.5)
                            if i == 0:
                                stt("dve", dstT[:, a:a + n], dstT[:, a:a + n], float(128 ** -0.5), rn, ALU.mult, ALU.mult)
                            else:
                                tt("dve", dstT[:, a:a + n], dstT[:, a:a + n], rn, ALU.mult)
                        else:
                            vT = TR[:, so:so + 512]
                            act(vT[:, 0:n], accv, AF.Silu)
                            for tq in range(n // 128):
                                pst = fw.psum(dt=BF16)
                                tr(pst[:, 0:128], vT[:, tq * 128:(tq + 1) * 128], ident_b)
                                cp("dve", v_tok[:, a // 128 + tq, :], pst[:, 0:128])
                for t in range(nt):
                    pst = fw.psum(dt=BF16)
                    tr(pst[:, 0:128], kT[:, t * 128:(t + 1) * 128], ident_b)
                    cp("act", k_tok[:, t, :], pst[:, 0:128])
                    ps = fw.psum()
                    for kc in range(8):
                        mm(ps[:, 0:128], UT[:, kc, c0 + t * 128:c0 + (t + 1) * 128], sw[:, kc, 384:512], start=(kc == 0), stop=(kc == 7))
                    act(z_tok[:, t, :], ps[:, 0:128], AF.Silu)
                if l == 1 and si == 0 and h == 0:
                    dbg("qT_%d" % pass_id, qT[:, 0:256], [128, 256])
                    dbg("kT_%d" % pass_id, kT[:, 0:256], [128, 256])
                    dbg("vtok_%d" % pass_id, v_tok[:, 0, :], [128, 128])

                if h + 1 < 8:
                    dn_next[0] = dn_load(h + 1)
                elif si + 1 < len(groups):
                    dn_next[0] = dn_load(0)
                if h % 4 == 0:
                    for c in range(4):
                        tt("dve", wo_h[:, c, :], wo_h[:, c, :], GB, ALU.mult)
                for z in range(2):
                    S = S_f[:, z, :]
                    Sb = S_b[:, z, :]
                    if pass_id == 0:
                        memset("pool", S, 0.0)
                    else:
                        load("sp", S, sdn_d[(o_ * 2 + z) * 8 + h])
                    cp("act", Sb, S)
                    rb = z * 32 + h
                    fw.op("dve", lambda e, rb=rb, z=z: e.tensor_single_scalar(out=selc[:, z * 128:(z + 1) * 128].ap, in_=pidx.ap,
                                                                           scalar=float(rb), op=ALU.is_equal),
                          reads=[pidx], writes=[selc[:, z * 128:(z + 1) * 128]])
                rec_turn = [0, 0]
                stored = set()

                def chunk_task(z, c, zi, sl):
                    sbk, fk, cols, cold, ccol, bankpair = sl
                    nal = [0]

                    def palloc(dt=F32):
                        bnk = bankpair[nal[0] % 2]
                        nal[0] += 1
                        return bnk if dt == F32 else bnk.bitcast(dt)
                    S = S_f[:, z, :]
                    Sb = S_b[:, z, :]
                    rb = z * 32 + h
                    Msk_s = MLs if z == 0 else MUs
                    Msk_i = ML if z == 0 else MU
                    cs = slice(c * 128, (c + 1) * 128)
                    ps = palloc()
                    tr(ps[:, 0:128], R1[0:128, cs], ident_f)
                    cp("act", cols, ps[:, 0:104])
                    ps = palloc()
                    tr(ps[:, 0:64], R2[0:64, cs], ident_f[0:64, 0:64])
                    cp("dve", cold, ps[:, 0:40])
                    bcol = cols[:, rb:rb + 1]
                    betac = cols[:, 64 + rb:65 + rb]
                    act(ccol[:, 0:1], bcol, AF.Exp, scale=-1.0)
                    tt("dve", ccol[:, 1:2], ccol[:, 0:1], betac, ALU.mult)
                    ts("dve", ccol[:, 2:3], betac, -1.0, ALU.mult)
                    act(ccol[:, 3:4], cold[:, rb:rb + 1], AF.Exp)
                    psBGQ = palloc()
                    psB = psBGQ[:, 0:128]
                    psG = psBGQ[:, 128:256]
                    psQ = psBGQ[:, 256:384]
                    mm(psB[:, 0:128], selc[:, z * 128:(z + 1) * 128], R1[0:64, cs])
                    mm(psG[:, 0:128], kT[:, cs], kT[:, cs])
                    mm(psQ[:, 0:128], qT[:, cs], kT[:, cs])
                    yield
                    EB = fk[1]
                    act(EB, psB[:, 0:128], AF.Exp, scale=-1.0)
                    ld = fk[0]
                    fw.op("dve", lambda e, ld=ld, psB=psB, bcol=bcol: e.tensor_scalar(
                        out=ld.ap, in0=psB[:, 0:128].ap, scalar1=bcol.ap, scalar2=0.0, op0=ALU.subtract, op1=ALU.min),
                        reads=[psB[:, 0:128], bcol, EB], writes=[ld])
                    act(ld, ld, AF.Exp)
                    cp("act", ccol[:, 4:5], EB[:, 127:128] if z == 0 else EB[:, 0:1])
                    yield
                    t1 = fk[2]
                    tt("dve", t1, psG[:, 0:128], ld, ALU.mult)
                    P0 = sbk[0]
                    stt("dve", P0, t1, ccol[:, 2:3], Msk_s, ALU.mult, ALU.mult)
                    P = sbk[1]
                    stt("dve", P, t1, ccol[:, 2:3], MDS[:, z, :], ALU.mult, ALU.mult)
                    yield
                    pst = palloc(BF16)
                    tr(pst[:, 0:128], P, ident_b)
                    PT = sbk[2]
                    cp("act", PT, pst[:, 0:128])
                    TT = sbk[7]
                    tt("dve", TT, ident_b, PT, ALU.add)
                    tt("dve", t1, psQ[:, 0:128], ld, ALU.mult)
                    yield
                    for it in range(3):
                        Pn = sbk[3 + (it % 2) * 2]
                        PTn = sbk[4 + (it % 2) * 2]
                        ps1 = palloc()
                        mm(ps1[:, 0:128], PT, P)
                        if it < 2:
                            ps2 = palloc()
                            mm(ps2[:, 0:128], P, PT)
                        yield
                        cp("act", Pn, ps1[:, 0:128])
                        if it < 2:
                            cp("dve", PTn, ps2[:, 0:128])
                        yield
                        ps3 = palloc()
                        mm(ps3[:, 0:128], Pn, TT)
                        yield
                        TTn = sbk[8] if TT is sbk[7] else sbk[7]
                        tt("dve", TTn, ps3[:, 0:128], TT, ALU.add)
                        P, PT, TT = Pn, PTn, TTn
                        yield
                    for lv in range(3):
                        Bk = sbk[4]
                        tt("pool", Bk, P0, BMK[:, 1 + lv, :], ALU.mult)
                        pst = palloc(BF16)
                        tr(pst[:, 0:128], TT, ident_b)
                        psz = palloc()
                        mm(psz[:, 0:128], Bk, TT)
                        yield
                        Tn = sbk[3]
                        cp("act", Tn, pst[:, 0:128])
                        Zb = sbk[5]
                        cp("dve", Zb, psz[:, 0:128])
                        yield
                        psw = palloc()
                        mm(psw[:, 0:128], Tn, Zb)
                        yield
                        TTn = sbk[8] if TT is sbk[7] else sbk[7]
                        tt("dve", TTn, psw[:, 0:128], TT, ALU.add)
                        TT = TTn
                        yield
                    aq = sbk[2]
                    tt("dve", aq, t1, Msk_i, ALU.mult)
                    pst = palloc(BF16)
                    tr(pst[:, 0:128], aq, ident_b)
                    aqT = sbk[9]
                    cp("act", aqT, pst[:, 0:128])
                    qdT = sbk[10]
                    tt("dve", qdT, qT[:, cs], EB, ALU.mult)
                    kbe = sbk[3]
                    vb = sbk[4]
                    kdk = sbk[6]
                    ts("dve", kbe, k_tok[:, c, :], ccol[:, 1:2], ALU.mult)
                    act(vb, v_tok[:, c, :], AF.Identity, scale=betac)
                    act(kdk, k_tok[:, c, :], AF.Identity, scale=ccol[:, 3:4])
                    yield
                    psU = palloc()
                    mm(psU[:, 0:128], TT, vb)
                    psW = palloc()
                    mm(psW[:, 0:128], kbe, TT)
                    yield
                    u = fk[2]
                    cp("act", u, psU[:, 0:128])
                    wT = sbk[1]
                    cp("dve", wT, psW[:, 0:128])
                    yield
                    while rec_turn[z] != zi:
                        yield
                    first_in_seq = (c % SLc == 0) if z == 0 else (c % SLc == SLc - 1)
                    last_in_seq = (c % SLc == SLc - 1) if z == 0 else (c % SLc == 0)
                    if first_in_seq and zi > 0:
                        memset("pool", S, 0.0)
                        memset("pool", Sb, 0.0)
                    psS = palloc()
                    mm(psS[:, 0:128], wT, Sb)
                    yield
                    vnew = sbk[2]
                    tt("dve", vnew, u, psS[:, 0:128], ALU.subtract)
                    yield
                    psO = palloc()
                    mm(psO[:, 0:128], qdT, Sb, start=True, stop=False)
                    mm(psO[:, 0:128], aqT, vnew, start=False, stop=True)
                    psN = palloc()
                    mm(psN[:, 0:128], kdk, vnew)
                    yield
                    stt("dve", S, S, ccol[:, 4:5], psN[:, 0:128], ALU.mult, ALU.add)
                    cp("act", Sb, S)
                    if pass_id == 0 and last_in_seq:
                        load("sp", ndn_d[(((c // SLc) * 2 + o_) * 2 + z) * 8 + h], S)
                    rec_turn[z] += 1
                    t = c
                    second = (z == 1 and 2 * t < nt) or (z == 0 and 2 * t >= nt)
                    if not second:
                        cp("act", o_f[:, t, :], psO[:, 0:128])
                        stored.add(t)
                    else:
                        while t not in stored:
                            yield
                        osum = fk[0]
                        ost = ccol[:, 8:16]
                        tt("dve", osum, psO[:, 0:128], o_f[:, t, :], ALU.add)
                        act(junk[:, 0:128], osum, AF.Square, accum=ost[:, 0:1])
                        act(ost[:, 1:2], ost[:, 0:1], AF.Ln, bias=EPS, scale=1.0 / 128)
                        act(ost[:, 2:3], ost[:, 1:2], AF.Exp, scale=-0.5)
                        stt("dve", osum, osum, ost[:, 2:3], normB, ALU.mult, ALU.mult)
                        yield
                        ogb = sbk[9]
                        tt("dve", ogb, osum, z_tok[:, t, :], ALU.mult)
                        pst = palloc(BF16)
                        tr(pst[:, 0:128], ogb, ident_b)
                        yield
                        oTb = sbk[10]
                        cp("act", oTb, pst[:, 0:128])
                        psA = palloc()
                        psBk = palloc()
                        mm(psA, oTb, wo_h[:, h % 4, 0:512])
                        mm(psBk, oTb, wo_h[:, h % 4, 512:1024])
                        yield
                        xacc(t0 + t, psA, psBk)

                items = []
                for i in range(nt):
                    items.append((0, i, i))
                    items.append((1, nt - 1 - i, i))
                active = []
                free = list(range(len(slots)))
                qi = 0
                while qi < len(items) or active:
                    while free and qi < len(items):
                        z_, c_, zi_ = items[qi]
                        qi += 1
                        s_ = free.pop(0)
                        active.append([chunk_task(z_, c_, zi_, slots[s_]), s_])
                    for a_ in list(active):
                        try:
                            next(a_[0])
                        except StopIteration:
                            active.remove(a_)
                            free.append(a_[1])

    passes = [(0, 8, [(0, 2), (2, 2), (4, 2), (6, 2)], 0), (1, 16, [(0, 16)], 1)]

    def _main():
        for (pass_id, T, seqs, cond) in passes:
            mark("P%d load" % pass_id)
            load_x(pass_id, T)
            if pass_id == 1:
                dbg("x0s", X[:, 0, :], [128, D])
            for l in range(NL):
                mark("P%d L%d mixer" % (pass_id, l))
                if l % 2 == 0:
                    gla_mixer(l, pass_id, T, seqs, cond)
                else:
                    dn_mixer(l, pass_id, T, seqs, cond)
                mark("P%d L%d ln1" % (pass_id, l))
                dbg("xmix%d_%d" % (l, pass_id), X[:, 0, :], [128, D])
                if stop == "mix%d_%d" % (l, pass_id):
                    raise _Stop()
                layernorm(l, 0, T, False)
                dbg("xln%d_%d" % (l, pass_id), X[:, 0, :], [128, D])
                stop_at("ln%d_%d" % (l, pass_id))
                mark("P%d L%d ffn" % (pass_id, l))
                ffn(l, pass_id, T, cond)
                mark("P%d L%d ln2" % (pass_id, l))
                dbg("xffn%d_%d" % (l, pass_id), X[:, 0, :], [128, D])
                stop_at("ffn%d_%d" % (l, pass_id))
                layernorm(l, 1, T, l == NL - 1)
                dbg("xl%d_%d" % (l, pass_id), X[:, 0, :], [128, D])
                if stop == "l%d_%d" % (l, pass_id):
                    raise _Stop()
            mark("P%d store" % pass_id)
            dst = yp_d if pass_id == 0 else ys_d
            for t in range(T):
                load("sp", dst[t * 128:(t + 1) * 128, :], X[:, t, :])

    try:
        _main()
    except _Stop:
        pass
    fw.emit()
    return nc, fw, dbg_d


_CACHE = {}


def _inputs_per_core(inp, core):
    b = core % 2
    m = {}
    m["xp"] = np.ascontiguousarray(inp["x_prompt"][core * 4:(core + 1) * 4].reshape(1024, D))
    m["xs"] = np.ascontiguousarray(inp["x_sample"][b])
    m["cond2"] = np.ascontiguousarray(np.stack([inp["c_ctx"], inp["c"][b]], 0))
    m["sgla"] = np.ascontiguousarray(inp["state_gla"][b].reshape(16, 64, 128))
    m["sdn"] = np.ascontiguousarray(inp["state_dn"][b].reshape(32, 128, 128))
    for k in ("w_mod", "b_mod", "ln1_g", "ln1_b", "ln2_g", "ln2_b", "a_w_in", "a_w_gate", "a_b_gate", "a_norm",
              "b_proj", "b_scale", "a_w_out", "c_w_in", "c_conv", "c_a_log", "c_dt_bias", "c_norm", "c_w_out",
              "f_w_up", "f_conv", "f_w_down"):
        m[k] = np.ascontiguousarray(inp[k])
    return m


def kernel(**inputs):
    inp = {k: np.asarray(v, dtype=np.float32) for k, v in inputs.items()}
    if "nc" not in _CACHE:
        _CACHE["nc"] = build_program()[0]
    nc = _CACHE["nc"]
    n = 8
    in_maps = [_inputs_per_core(inp, c) for c in range(n)]
    res = run_bass_kernel_spmd(nc, in_maps, core_ids=list(range(n)))
    R = res.results
    y_prompt = np.concatenate([R[c]["yp"].reshape(4, 256, D) for c in range(n)], 0)
    y_sample = np.stack([R[0]["ys"], R[1]["ys"]], 0)
    ngla = np.concatenate([R[c]["ngla"].reshape(4, 2, 2, 4, 64, 128) for c in range(n)], 0)
    ndn = np.concatenate([R[c]["ndn"].reshape(4, 2, 2, 8, 128, 128) for c in range(n)], 0)
    return (y_prompt.astype(np.float32), y_sample.astype(np.float32), ngla.astype(np.float32), ndn.astype(np.float32))
```

```python
import math
import numpy as np
from concourse.bass_utils import run_bass_kernel_spmd
import concourse.bass as bass
import concourse.mybir as mybir

F32 = mybir.dt.float32
BF16 = mybir.dt.bfloat16
I32 = mybir.dt.int32
AF = mybir.ActivationFunctionType
ALU = mybir.AluOpType
AX = mybir.AxisListType

CELL = 256
_DT_SIZE = {F32: 4, BF16: 2, I32: 4}


class Region:
    def __init__(self, fw, name, handle, nbytes, cell=CELL):
        self.fw = fw
        self.name = name
        self.h = handle
        self.cell = cell
        self.ncell = (nbytes + cell - 1) // cell
        self.w = [None] * self.ncell
        self.r = [dict() for _ in range(self.ncell)]


class V:
    def __init__(self, region, ap):
        self.region = region
        self.ap = ap
        self._cells = None

    def __getitem__(self, key):
        return V(self.region, self.ap[key])

    def rr(self, pattern_, **kw):
        return V(self.region, self.ap.rearrange(pattern_, **kw))

    def bitcast(self, dt):
        return V(self.region, self.ap.bitcast(dt))

    def bc(self, shape):
        return V(self.region, self.ap.to_broadcast(shape))

    @property
    def shape(self):
        return self.ap.shape

    def cells(self):
        if self._cells is None:
            ap = self.ap
            esz = _DT_SIZE[ap.dtype]
            dims = list(ap.ap)[1:]
            base = int(ap.offset) if not isinstance(ap.offset, int) else ap.offset
            pstep = list(ap.ap)[0][0]
            if pstep > 0:
                base = base % pstep
            base_b = base * esz
            cs = set()
            CELL = self.region.cell
            dims = [(s, n) for (s, n) in dims if n > 1 or True]
            if not dims:
                dims = [(1, 1)]
            *outer, (ls, ln) = dims
            if ls in (0, 1):
                run = (esz * (ln if ls == 1 else 1))
                inner_iter = [0]
            else:
                run = esz
                inner_iter = [i * ls * esz for i in range(ln)]
            offs = [0]
            for (s, n) in outer:
                if s == 0:
                    continue
                offs = [o + i * s * esz for o in offs for i in range(n)]
            for o in offs:
                for ii in inner_iter:
                    a = base_b + o + ii
                    for c in range(a // CELL, (a + run - 1) // CELL + 1):
                        cs.add(c)
            self._cells = sorted(cs)
            assert self._cells[-1] < self.region.ncell, (self.region.name, self._cells[-1], self.region.ncell, ap)
        return self._cells


class Op:
    __slots__ = ("eng", "fn", "waits", "signal", "semval", "dma_sem", "is_dma")

    def __init__(self, eng, fn):
        self.eng = eng
        self.fn = fn
        self.waits = []
        self.signal = False
        self.semval = None
        self.dma_sem = None
        self.is_dma = False


ENGS = ("pe", "dve", "act", "pool", "sp")
N_DMA_SEMS = 12


class FW:
    def __init__(self, nc):
        self.nc = nc
        self.ops = {e: [] for e in ENGS}
        self.regions = []
        self.dma_rr = {"sp": 0, "pool": 0, "act": 0}
        self.dma_last = {}
        self._ctx = []
        self.psum_ptr = 0
        self.nops = 0

    def sbuf(self, name, shape, dt):
        g = self.nc.sbuf_tensor(name, list(shape), dt)
        h = g.__enter__()
        self._ctx.append(g)
        nb = int(np.prod(shape[1:])) * _DT_SIZE[dt]
        reg = Region(self, name, h, nb)
        self.regions.append(reg)
        return V(reg, h[:] if hasattr(h, "__getitem__") else h.ap())

    def psum_banks(self):
        self.banks = []
        for i in range(8):
            g = self.nc.psum_tensor(f"psb{i}", [128, 512], F32)
            h = g.__enter__()
            self._ctx.append(g)
            reg = Region(self, f"psb{i}", h, 2048, cell=2048)
            self.banks.append(V(reg, h[:]))

    def psum(self, ncols=512, parts=128, dt=F32):
        b = self.psum_ptr
        self.psum_ptr = (b + 1) % 8
        bank = self.banks[b] if dt == F32 else self.banks[b].bitcast(dt)
        return bank[0:parts, 0:ncols]

    def _deps(self, op, reads, writes):
        deps = {}
        for v in reads:
            reg = v.region
            for c in v.cells():
                w = reg.w[c]
                if w is not None:
                    deps[id(w)] = w
        for v in writes:
            reg = v.region
            for c in v.cells():
                w = reg.w[c]
                if w is not None:
                    deps[id(w)] = w
                for t in reg.r[c].values():
                    deps[id(t)] = t
        for t in deps.values():
            if t is op:
                continue
            if t.eng == "pe" and op.eng == "pe" and not t.is_dma and not op.is_dma:
                continue
            op.waits.append(t)
            t.signal = True
        for v in reads:
            reg = v.region
            key = op.dma_sem if op.is_dma else op.eng
            for c in v.cells():
                reg.r[c][key] = op
        for v in writes:
            reg = v.region
            for c in v.cells():
                reg.w[c] = op
                reg.r[c] = {}

    def op(self, eng, fn, reads=(), writes=()):
        if getattr(self, "halted", False):
            return None
        o = Op(eng, fn)
        self._deps(o, reads, writes)
        self.ops[eng].append(o)
        self.nops += 1
        return o

    def dma(self, queue, out, in_, reads=(), writes=(), **kw):
        if getattr(self, "halted", False):
            return None
        o = Op(queue, None)
        o.is_dma = True
        o.signal = True
        oap = out.ap if isinstance(out, V) else out
        iap = in_.ap if isinstance(in_, V) else in_
        rd = list(reads) + ([in_] if isinstance(in_, V) else [])
        wr = list(writes) + ([out] if isinstance(out, V) else [])
        k = self.dma_rr[queue]
        self.dma_rr[queue] = (k + 1) % N_DMA_SEMS
        o.dma_sem = (queue, k)
        prev = self.dma_last.get((queue, k))
        self._deps(o, rd, wr)
        if prev is not None:
            o.waits.append(prev)
        self.dma_last[(queue, k)] = o
        o.fn = lambda e: e.dma_start(out=oap, in_=iap, **kw)
        self.ops[queue].append(o)
        self.nops += 1
        return o

    def emit(self):
        nc = self.nc
        sems = {}
        semctx = []
        for e in ENGS:
            g = nc.semaphore(f"s_{e}")
            sems[e] = g.__enter__()
            semctx.append(g)
        dsems = {}
        for q in ("sp", "pool"):
            for k in range(N_DMA_SEMS):
                g = nc.semaphore(f"d_{q}{k}")
                dsems[(q, k)] = g.__enter__()
                semctx.append(g)
        for e in ENGS:
            cnt = 0
            for o in self.ops[e]:
                if o.is_dma:
                    continue
                if o.signal:
                    cnt += 1
                    o.semval = cnt
            self.maxsem = max(getattr(self, "maxsem", 0), cnt)
        dcnt = {}
        for e in ENGS:
            for o in self.ops[e]:
                if o.is_dma:
                    dcnt[o.dma_sem] = dcnt.get(o.dma_sem, 0) + 16
                    o.semval = dcnt[o.dma_sem]

        def semof(t):
            return dsems[t.dma_sem] if t.is_dma else sems[t.eng]

        def run(engname, engobj):
            known = {}
            for o in self.ops[engname]:
                need = {}
                for t in o.waits:
                    s = t.dma_sem if t.is_dma else t.eng
                    if t.semval > need.get(s, (0, None))[0]:
                        need[s] = (t.semval, t)
                for s, (val, t) in need.items():
                    if known.get(s, 0) >= val:
                        continue
                    engobj.wait_ge(semof(t), val)
                    known[s] = val
                ins = o.fn(engobj)
                if o.is_dma:
                    ins.then_inc(dsems[o.dma_sem], 16)
                elif o.signal:
                    ins.then_inc(sems[engname], 1)
            if engname in ("sp", "pool"):
                for k in range(N_DMA_SEMS):
                    if dcnt.get((engname, k), 0) > 0:
                        engobj.wait_ge(dsems[(engname, k)], dcnt[(engname, k)])

        with nc.Block() as block:
            @block.tensor
            def _(e):
                run("pe", e)

            @block.vector
            def _(e):
                run("dve", e)

            @block.scalar
            def _(e):
                run("act", e)

            @block.gpsimd
            def _(e):
                run("pool", e)

            @block.sync
            def _(e):
                run("sp", e)
        for g in reversed(semctx):
            g.__exit__(None, None, None)
        for g in reversed(self._ctx):
            g.__exit__(None, None, None)

D = 1024
NL = 4
DFF = 2816
NCH = DFF // 128
ALPHA = float(8 ** 0.25)
EPS = 1e-6
KC = 8
A_IN = 2080
C_IN = 4128
PI = float(np.pi)

DBG_SPECS = {}


class _Stop(Exception):
    pass


def build_program(debug=(), stop=None):
    nc = bass.Bass("TRN2", target_bir_lowering=False)
    fw = FW(nc)

    def din(name, shape):
        return nc.dram_tensor(name, list(shape), F32, kind="ExternalInput").ap()

    def dout(name, shape):
        return nc.dram_tensor(name, list(shape), F32, kind="ExternalOutput").ap()

    xp_d = din("xp", [1024, D])
    xs_d = din("xs", [2048, D])
    cond_d = din("cond2", [2, D])
    sgla_d = din("sgla", [16, 64, 128])
    sdn_d = din("sdn", [32, 128, 128])
    w_mod_d = din("w_mod", [NL, D, 6 * D])
    b_mod_d = din("b_mod", [NL, 6 * D])
    ln_d = {k: din(k, [NL, D]) for k in ("ln1_g", "ln1_b", "ln2_g", "ln2_b")}
    a_w_in_d = din("a_w_in", [2, D, A_IN])
    a_w_gate_d = din("a_w_gate", [2, 2, 16, 256])
    a_b_gate_d = din("a_b_gate", [2, 2, 256])
    a_norm_d = din("a_norm", [2, 128])
    b_proj_d = din("b_proj", [2, 4, 128, 128])
    b_scale_d = din("b_scale", [2, 512])
    a_w_out_d = din("a_w_out", [2, D, D])
    c_w_in_d = din("c_w_in", [2, D, C_IN])
    c_conv_d = din("c_conv", [2, 3, 3072])
    c_a_log_d = din("c_a_log", [2, 2, 8])
    c_dt_bias_d = din("c_dt_bias", [2, 2, 8])
    c_norm_d = din("c_norm", [2, 128])
    c_w_out_d = din("c_w_out", [2, D, D])
    f_w_up_d = din("f_w_up", [NL, D, 2 * DFF])
    f_conv_d = din("f_conv", [NL, 3, 2 * DFF])
    f_w_down_d = din("f_w_down", [NL, DFF, D])

    yp_d = dout("yp", [1024, D])
    ys_d = dout("ys", [2048, D])
    ngla_d = dout("ngla", [64, 64, 128])
    ndn_d = dout("ndn", [128, 128, 128])
    dbg_d = {}

    def A(v):
        return v.ap if isinstance(v, V) else v

    def rds(*xs):
        return [x for x in xs if isinstance(x, V)]

    def mm(out, lhsT, rhs, start=True, stop=True):
        fw.op("pe", lambda e: e.matmul(out.ap, lhsT=lhsT.ap, rhs=rhs.ap, start=start, stop=stop),
              reads=[lhsT, rhs], writes=[out])

    def tr(out, in_, idn):
        fw.op("pe", lambda e: e.transpose(out=out.ap, in_=in_.ap, identity=idn.ap),
              reads=[in_, idn], writes=[out])

    def act(out, in_, func, bias=None, scale=None, accum=None):
        kw = {}
        if bias is not None:
            kw["bias"] = A(bias)
        if scale is not None:
            kw["scale"] = A(scale)
        if accum is not None:
            kw["accum_out"] = accum.ap
        fw.op("act", lambda e: e.activation(out=out.ap, in_=in_.ap, func=func, **kw),
              reads=rds(in_, bias, scale), writes=rds(out, accum))

    def ts(eng, out, in0, s1, op0, s2=None, op1=None):
        kw = {"op1": op1} if op1 is not None else {}
        fw.op(eng, lambda e: e.tensor_scalar(out=out.ap, in0=in0.ap, scalar1=A(s1),
                                             scalar2=(A(s2) if s2 is not None else None), op0=op0, **kw),
              reads=rds(in0, s1, s2), writes=[out])

    def tt(eng, out, in0, in1, op):
        fw.op(eng, lambda e: e.tensor_tensor(out=out.ap, in0=in0.ap, in1=in1.ap, op=op),
              reads=[in0, in1], writes=[out])

    def stt(eng, out, in0, scalar, in1, op0, op1):
        fw.op(eng, lambda e: e.scalar_tensor_tensor(out=out.ap, in0=in0.ap, scalar=A(scalar), in1=in1.ap,
                                                    op0=op0, op1=op1),
              reads=rds(in0, scalar, in1), writes=[out])

    def cp(eng, out, in_):
        if eng == "act":
            fw.op("act", lambda e: e.copy(out=out.ap, in_=in_.ap), reads=[in_], writes=[out])
        else:
            fw.op(eng, lambda e: e.tensor_copy(out=out.ap, in_=in_.ap), reads=[in_], writes=[out])

    def memset(eng, out, val):
        fw.op(eng, lambda e: e.memset(out.ap, val), writes=[out])

    def scan(out, d0, d1, init=0.0):
        fw.op("dve", lambda e: e.tensor_tensor_scan(out=out.ap, data0=d0.ap, data1=d1.ap, initial=init,
                                                    op0=ALU.mult, op1=ALU.add),
              reads=[d0, d1], writes=[out])

    def recip(out, in_):
        fw.op("dve", lambda e: e.reciprocal(out=out.ap, in_=in_.ap), reads=[in_], writes=[out])

    def rsum(out, in_):
        fw.op("dve", lambda e: e.reduce_sum(out=out.ap, in_=in_.ap, axis=AX.X), reads=[in_], writes=[out])

    def load(q, out, src, **kw):
        fw.dma(q, out, src, **kw)

    def dbg(name, v, shape):
        if name in debug:
            d = dout("dbg_" + name, list(shape))
            dbg_d[name] = d
            fw.dma("sp" if v.ap.dtype == F32 else "pool", d, v)

    def stop_at(name):
        if stop == name:
            fw.halted = True

    fw.marks = []

    def mark(name):
        fw.marks.append((name, len(fw.ops["dve"]), len(fw.ops["pe"])))

    rr = [0]

    def evac_eng():
        rr[0] ^= 1
        return "act" if rr[0] else "dve"

    fw.psum_banks()
    X = fw.sbuf("X", [128, 16, D], F32)
    ARENA = fw.sbuf("ARENA", [128, 44032], BF16)
    ARENA_F = ARENA.bitcast(F32)
    ARENA_I = ARENA.bitcast(I32)
    WR = fw.sbuf("WR", [128, 3, 4096], BF16)
    LNB = fw.sbuf("LNB", [128, 2, D], F32)
    GB = fw.sbuf("GB", [128, D], F32)
    ones_f = fw.sbuf("ones_f", [128, 128], F32)
    ident_f = fw.sbuf("ident_f", [128, 128], F32)
    ident_b = fw.sbuf("ident_b", [128, 128], BF16)
    ones_b = fw.sbuf("ones_b", [128, 128], BF16)
    MU = fw.sbuf("MU", [128, 128], F32)
    MUs = fw.sbuf("MUs", [128, 128], F32)
    ML = fw.sbuf("ML", [128, 128], F32)
    MLs = fw.sbuf("MLs", [128, 128], F32)
    pidx = fw.sbuf("pidx", [64, 128], F32)
    BMK = fw.sbuf("BMK", [128, 4, 128], BF16)
    MDS = fw.sbuf("MDS", [128, 2, 128], BF16)
    selc = fw.sbuf("selc", [64, 256], F32)
    dncol = fw.sbuf("dncol", [128, 4, 160], F32)
    modcol = fw.sbuf("modcol", [128, NL, 48, 2], F32)
    mscale = fw.sbuf("mscale", [128, NL, 2, 8, 2], F32)
    scT = fw.sbuf("scT", [128, 8, 2], F32)
    small = fw.sbuf("small", [128, 320], F32)
    S_f = fw.sbuf("S_f", [128, 2, 128], F32)
    S_b = fw.sbuf("S_b", [128, 2, 128], BF16)
    wsm = fw.sbuf("wsm", [128, 8, 128], BF16)
    wg = fw.sbuf("wg", [48, 256], BF16)
    bproj = fw.sbuf("bproj", [128, 4, 128], BF16)
    convc = fw.sbuf("convc", [128, 160], F32)
    normB = fw.sbuf("normB", [128, 128], F32)
    junk = fw.sbuf("junk", [128, D], BF16)
    pe_c = ARENA_F[:, 0:512]
    freq = ARENA_F[:, 512:768]
    ptmp = ARENA_F[:, 768:1536].rr("p (a n) -> p a n", a=3)
    ptmpi = ARENA_I[:, 1536:1792]
    pcol = fw.sbuf("pcol", [128, 8], F32)
    dgs = fw.sbuf("dgs", [128, 128], F32)
    efix = fw.sbuf("efix", [128, 4, 16], F32)

    fw.sbuf_free = nc.sbuf_bytes_remaining
    memset("pool", ones_f, 1.0)
    memset("pool", ones_b, 1.0)

    def asel(out, in_, pattern, cmp, cm, base=0):
        fw.op("pool", lambda e: e.affine_select(out=out.ap, in_=in_.ap, pattern=pattern, compare_op=cmp, fill=0.0,
                                                base=base, channel_multiplier=cm), reads=[in_], writes=[out])

    asel(ident_f, ones_f, [[-1, 128]], ALU.is_equal, 1)
    cp("dve", ident_b, ident_f)
    asel(MU, ones_f, [[1, 128]], ALU.is_ge, -1)
    asel(MUs, ones_f, [[1, 128]], ALU.is_gt, -1)
    asel(ML, ones_f, [[-1, 128]], ALU.is_ge, 1)
    asel(MLs, ones_f, [[-1, 128]], ALU.is_gt, 1)
    fw.op("pool", lambda e: e.iota(pidx.ap, [[0, 128]], base=0, channel_multiplier=1,
                                   allow_small_or_imprecise_dtypes=True), writes=[pidx])
    mdt = ptmp.rr("p a n -> p (a n)")
    for bi_, bsz in enumerate((16, 32, 64)):
        nb_ = 128 // bsz
        Eb = ptmp[0:8, 0, 0:128]
        asel(Eb, ones_f[0:8, :], [[1, 128]], ALU.is_ge, -bsz, base=0)
        asel(Eb, Eb, [[-1, 128]], ALU.is_ge, bsz, base=bsz - 1)
        ps = fw.psum()
        mm(ps[:, 0:128], Eb[0:nb_, :], Eb[0:nb_, :])
        cp("dve", mdt[:, 256 + bi_ * 128:256 + (bi_ + 1) * 128], ps[:, 0:128])
    md16, md32, md64 = (mdt[:, 256 + i * 128:256 + (i + 1) * 128] for i in range(3))
    cp("dve", BMK[:, 0, :], md16)
    tt("dve", BMK[:, 1, :], md32, md16, ALU.subtract)
    tt("dve", BMK[:, 2, :], md64, md32, ALU.subtract)
    ts("dve", BMK[:, 3, :], md64, -1.0, ALU.mult, 1.0, ALU.add)
    tt("dve", MDS[:, 0, :], md16, MLs, ALU.mult)
    tt("dve", MDS[:, 1, :], md16, MUs, ALU.mult)

    stop_at("c1")
    condt = ARENA_F[0:2, 0:1024]
    load("sp", condt, cond_d)
    act(condt, condt, AF.Silu)
    ps = fw.psum()
    for kc in range(8):
        tr(ps[:, kc * 2:(kc + 1) * 2], condt[:, kc * 128:(kc + 1) * 128], ident_f[0:2, 0:2])
    cp("dve", scT.rr("p k c -> p (k c)"), ps[:, 0:16])

    stop_at("c2")
    bmr = ARENA_F[0:48, 1024:1152]
    bmT = small[:, 0:48]
    wm_slots = [ARENA_F[:, 2048 + i * 4096: 2048 + (i + 1) * 4096].rr("p (k n) -> p k n", k=8) for i in range(2)]
    for l in range(NL):
        load("sp", bmr, b_mod_d[l].rearrange("(b p) -> b p", p=128))
        ps = fw.psum()
        tr(ps[:, 0:48], bmr, ident_f[0:48, 0:48])
        cp("act", bmT, ps[:, 0:48])
        for g in range(12):
            slot = wm_slots[g % 2]
            load("sp", slot, w_mod_d[l].rearrange("(k p) n -> p k n", p=128)[:, :, g * 512:(g + 1) * 512])
            ps = fw.psum()
            for b4 in range(4):
                for kc in range(8):
                    mm(ps[:, b4 * 2:(b4 + 1) * 2], slot[:, kc, b4 * 128:(b4 + 1) * 128], scT[:, kc, :],
                       start=(kc == 0), stop=(kc == 7))
            ps3 = ps[:, 0:8].rr("p (b c) -> p b c", c=2)
            for c in range(2):
                tt("dve", modcol[:, l, 4 * g:4 * g + 4, c], ps3[:, :, c], bmT[:, 4 * g:4 * g + 4], ALU.add)
        for w, blk0 in enumerate((8, 32)):
            ts("dve", mscale[:, l, w], modcol[:, l, blk0:blk0 + 8, :], 1.0, ALU.add, 1.0 / ALPHA, ALU.mult)
    dbg("modcol", modcol.rr("p l b c -> p (l b c)"), [128, NL * 96])

    stop_at("c3")
    POOLW = (2, 4, 8, 16)
    for gi, w in enumerate(POOLW):
        lo = w // 2
        hi = w - 1 - lo
        fw.op("pool", lambda e, gi=gi, lo=lo, hi=hi: e.iota(efix[:, gi, 0:lo].ap, [[1, lo]], base=hi + 1, channel_multiplier=0,
                                                           allow_small_or_imprecise_dtypes=True), writes=[efix[:, gi, 0:lo]])
        if hi > 0:
            fw.op("pool", lambda e, gi=gi, hi=hi, w=w: e.iota(efix[:, gi, 8:8 + hi].ap, [[-1, hi]], base=w - 1, channel_multiplier=0,
                                                              allow_small_or_imprecise_dtypes=True), writes=[efix[:, gi, 8:8 + hi]])
        for (a, n) in ((0, lo), (8, hi)):
            if n > 0:
                recip(efix[:, gi, a:a + n], efix[:, gi, a:a + n])
                ts("dve", efix[:, gi, a:a + n], efix[:, gi, a:a + n], float(w), ALU.mult)

    def sincos(out_sin, out_cos, theta):
        for out, shift in ((out_sin, 0.0), (out_cos, 0.25)):
            t = ptmp[:, 1, :]
            gq = ptmp[:, 2, :]
            ts("dve", t, theta, 1.0 / (2 * PI), ALU.mult, shift, ALU.add)
            cp("dve", ptmpi, t)
            tt("dve", t, t, ptmpi, ALU.subtract)
            fw.op("dve", lambda e, t=t, gq=gq: e.tensor_single_scalar(out=gq.ap, in_=t.ap, scalar=0.5, op=ALU.is_ge),
                  reads=[t], writes=[gq])
            tt("dve", t, t, gq, ALU.subtract)
            fw.op("dve", lambda e, t=t, gq=gq: e.tensor_single_scalar(out=gq.ap, in_=t.ap, scalar=-0.5, op=ALU.is_lt),
                  reads=[t], writes=[gq])
            tt("dve", t, t, gq, ALU.add)
            act(out, t, AF.Sin, scale=2 * PI)

    def pe_consts():
        fw.op("pool", lambda e: e.iota(freq.ap, [[1, 256]], base=0, channel_multiplier=0, allow_small_or_imprecise_dtypes=True),
              writes=[freq])
        act(freq, freq, AF.Exp, scale=-math.log(10000.0) / 256.0)
        fw.op("pool", lambda e: e.iota(pcol[:, 0:1].ap, [[0, 1]], base=0, channel_multiplier=1, allow_small_or_imprecise_dtypes=True),
              writes=[pcol[:, 0:1]])
        fw.op("dve", lambda e: e.tensor_single_scalar(out=pcol[:, 1:2].ap, in_=pcol[:, 0:1].ap, scalar=64.0, op=ALU.is_ge),
              reads=[pcol[:, 0:1]], writes=[pcol[:, 1:2]])
        stt("dve", pcol[:, 2:3], pcol[:, 1:2], -64.0, pcol[:, 0:1], ALU.mult, ALU.add)

        ts("dve", ptmp[:, 0, :], freq, pcol[:, 2:3], ALU.mult)
        sincos(pe_c[:, 0:256], pe_c[:, 256:512], ptmp[:, 0, :])


    stop_at("c5")
    def seg512(n):
        out = []
        a = 0
        while a < n:
            out.append((a, min(512, n - a)))
            a += 512
        return out

    def load_x(pass_id, T):
        src = xp_d if pass_id == 0 else xs_d
        if pass_id == 1:
            pe_consts()
        for t in range(T):
            load("sp", X[:, t, :], src[t * 128:(t + 1) * 128, :])
            if pass_id == 1:
                ts("dve", pcol[:, 3:4], pcol[:, 1:2], float(2 * t), ALU.add)
                ts("dve", ptmp[:, 0, :], freq, pcol[:, 3:4], ALU.mult)
                pe_r = junk.bitcast(F32)
                sincos(pe_r[:, 0:256], pe_r[:, 256:512], ptmp[:, 0, :])
                tt("dve", X[:, t, 0:512], X[:, t, 0:512], pe_r, ALU.add)
                tt("dve", X[:, t, 512:1024], X[:, t, 512:1024], pe_c, ALU.add)
            act(X[:, t, :], X[:, t, :], AF.Identity, scale=ALPHA)
        stop_at("c6")

    def make_ut(UT, l, which, cond, tiles, col0, halo=None):
        shb = 0 if which == 0 else 24
        jobs = [(t, col0 + i * 128, None) for i, t in enumerate(tiles)]
        if halo is not None:
            jobs.append((halo[0], halo[2], halo[1]))
        for (t, c0, hc) in jobs:
            for half in range(2):
                ps = fw.psum()
                for q in range(4):
                    kc = half * 4 + q
                    tr(ps[:, q * 128:(q + 1) * 128], X[:, t, kc * 128:(kc + 1) * 128], ident_f)
                if which == 1:
                    stop_at("u1")
                eng_b = evac_eng()
                for q in range(4):
                    kc = half * 4 + q
                    sc = mscale[:, l, which, kc, cond:cond + 1]
                    sh = modcol[:, l, shb + kc, cond:cond + 1]
                    if hc is None:
                        src = ps[:, q * 128:(q + 1) * 128]
                        dst = UT[:, kc, c0:c0 + 128]
                    else:
                        src = ps[:, q * 128 + hc:q * 128 + hc + 1]
                        dst = UT[:, kc, c0:c0 + 1]
                    if eng_b == "act":
                        act(dst, src, AF.Identity, bias=sh, scale=sc)
                    else:
                        ts("dve", dst, src, sc, ALU.mult, sh, ALU.add)
                    if which == 1:
                        stop_at("u2")
                if which == 1:
                    stop_at("u3")
            if which == 1:
                stop_at("u4")
        if stop == "ut":
            dbg("ut", UT[:, 0, 0:1024], [128, 1024])
            for kc_ in range(8):
                dbg("ut%d" % kc_, UT[:, kc_, 0:256], [128, 256])
            dbg("x0", X[:, 0, :], [128, D])
            dbg("mscale", mscale.rr("p l w k c -> p (l w k c)"), [128, NL * 32])
            raise _Stop()

    def gbcast(l, blk0, cond):
        for half in range(2):
            ps = fw.psum()
            for q in range(4):
                kc = half * 4 + q
                dg = dgs
                ts("dve", dg, ident_f, modcol[:, l, blk0 + kc, cond:cond + 1], ALU.mult)
                mm(ps[:, q * 128:(q + 1) * 128], ones_f, dg)
            cp("act", GB[:, half * 512:(half + 1) * 512], ps)

    def layernorm(l, which, T, last):
        gname, bname = ("ln1_g", "ln1_b") if which == 0 else ("ln2_g", "ln2_b")
        load("sp", LNB[:, 0, :], ln_d[gname][l:l + 1, :].to_broadcast([128, D]))
        load("sp", LNB[:, 1, :], ln_d[bname][l:l + 1, :].to_broadcast([128, D]))
        stop_at("lna")
        if not last:
            act(LNB.rr("p a d -> p (a d)"), LNB.rr("p a d -> p (a d)"), AF.Identity, scale=ALPHA)
        stop_at("lnb")
        st = small[:, 64:72]
        for t in range(T):
            xt = X[:, t, :]
            act(junk, xt, AF.Identity, accum=st[:, 0:1])
            stop_at("lnc")
            ts("dve", st[:, 1:2], st[:, 0:1], -1.0 / D, ALU.mult)
            act(junk, xt, AF.Square, bias=st[:, 1:2], accum=st[:, 2:3])
            stop_at("lnd")
            act(st[:, 3:4], st[:, 2:3], AF.Ln, bias=EPS, scale=1.0 / D)
            act(st[:, 4:5], st[:, 3:4], AF.Exp, scale=-0.5)
            tt("dve", st[:, 5:6], st[:, 1:2], st[:, 4:5], ALU.mult)
            act(xt, xt, AF.Identity, bias=st[:, 5:6], scale=st[:, 4:5])
            tt("dve", xt, xt, LNB[:, 0, :], ALU.mult)
            tt("dve", xt, xt, LNB[:, 1, :], ALU.add)

    wr_i = [0]
    hs_i = [0]

    def wslot(nring=3):
        s = WR[:, wr_i[0] % nring, :]
        wr_i[0] += 1
        return s

    WO = WR[:, 2, :]

    def xacc(t, psA, psB):
        tt("dve", X[:, t, 0:512], X[:, t, 0:512], psA, ALU.add)
        tt("dve", X[:, t, 512:1024], X[:, t, 512:1024], psB, ALU.add)

    def ffn(l, pass_id, T, cond):
        cr = ARENA_F[0:44, 0:128]
        for k in range(3):
            load("sp", cr, f_conv_d[l][k].rearrange("(c p) -> c p", p=128))
            ps = fw.psum()
            tr(ps[:, 0:44], cr, ident_f[0:44, 0:44])
            cp("act", convc[:, k * 44:(k + 1) * 44], ps[:, 0:44])
        stop_at("fa")
        gbcast(l, 40, cond)
        stop_at("fb")
        UTs = ARENA[:, 0:8224].rr("p (k n) -> p k n", k=8)
        actT = ARENA[:, 8224:8224 + 22528].rr("p (j n) -> p j n", j=NCH)
        o0 = 8224 + 22528
        hpre = [[ARENA[:, o0 + (2 * i + h) * 1032: o0 + (2 * i + h) * 1032 + 1028] for h in range(2)] for i in range(2)]
        o1 = (o0 + 4 * 1032 + 1) // 2 + 8
        accs = [[ARENA_F[:, o1 + (2 * i + h) * 1024: o1 + (2 * i + h + 1) * 1024] for h in range(2)] for i in range(2)]
        assert (o1 + 4096) <= 22016
        groups = [(0, 8, None)] if pass_id == 0 else [(0, 8, "R"), (8, 8, "L")]
        wup = f_w_up_d[l].rearrange("(k p) n -> p k n", p=128)
        wdn = f_w_down_d[l].rearrange("(c p) n -> p c n", p=128)
        for (t0, nt, hal) in groups:
            ntok = nt * 128
            halo = None
            if hal == "R":
                halo = (t0 + nt, 0, 1026)
            elif hal == "L":
                halo = (t0 - 1, 127, 1)
            make_ut(UTs, l, 1, cond, list(range(t0, t0 + nt)), 2, halo)
            stop_at("f0")
            for i in range(2):
                for h in range(2):
                    if hal != "L":
                        memset("pool", hpre[i][h][:, 0:2], 0.0)
                    if hal != "R":
                        memset("pool", hpre[i][h][:, 1026:1028], 0.0)
            segs = [(2, 512), (514, 512)]
            if hal == "R":
                segs.append((1026, 1))
            if hal == "L":
                segs.append((1, 1))
            jgs = [(j0, min(4, NCH - j0)) for j0 in range(0, NCH, 4)]
            jg2 = [(j0, 2) for j0 in range(0, NCH, 2)]
            WRH = WR.rr("p s n -> p (s n)").rr("p (s n) -> p s n", s=6)

            def up_load(gi_):
                j0_, nj_ = jg2[gi_]
                sa_ = WRH[:, hs_i[0] % 6, :].rr("p (k n) -> p k n", k=8)
                sg_ = WRH[:, (hs_i[0] + 1) % 6, :].rr("p (k n) -> p k n", k=8)
                hs_i[0] += 2
                load("pool", sa_, wup[:, :, j0_ * 128:(j0_ + nj_) * 128])
                load("pool", sg_, wup[:, :, DFF + j0_ * 128:DFF + (j0_ + nj_) * 128])
                return sa_, sg_

            nxt = up_load(0)
            for gi_, (j0, nj) in enumerate(jg2):
                sa, sg = nxt
                if gi_ + 1 < len(jg2):
                    nxt = up_load(gi_ + 1)
                for jj in range(nj):
                    j = j0 + jj
                    hp = hpre[j % 2]
                    acc = accs[j % 2]
                    for h, sw in enumerate((sa, sg)):
                        cc = h * 22 + j
                        w0 = convc[:, cc:cc + 1]
                        w1 = convc[:, 44 + cc:44 + cc + 1]
                        w2 = convc[:, 88 + cc:88 + cc + 1]
                        for (c0, n) in segs:
                            ps = fw.psum()
                            for kc in range(8):
                                mm(ps[:, 0:n], sw[:, kc, jj * 128:(jj + 1) * 128], UTs[:, kc, c0:c0 + n],
                                   start=(kc == 0), stop=(kc == 7))
                            if n > 1:
                                cp("act", hp[h][:, c0:c0 + n], ps[:, 0:n])
                                act(acc[h][:, c0 - 2:c0 - 2 + n], ps[:, 0:n], AF.Identity, scale=w1)
                            else:
                                cp("act", hp[h][:, c0:c0 + n], ps[:, 0:n])
                        hh = hp[h]
                        if pass_id == 0:
                            a3 = acc[h].rr("p (s t) -> p s t", s=4)
                            h3 = hh[:, 2:1026].rr("p (s t) -> p s t", s=4)
                            stt("dve", a3[:, :, 1:256], h3[:, :, 0:255], w0, a3[:, :, 1:256], ALU.mult, ALU.add)
                            stt("dve", a3[:, :, 0:255], h3[:, :, 1:256], w2, a3[:, :, 0:255], ALU.mult, ALU.add)
                        else:
                            stt("dve", acc[h], hh[:, 1:1025], w0, acc[h], ALU.mult, ALU.add)
                            stt("dve", acc[h], hh[:, 3:1027], w2, acc[h], ALU.mult, ALU.add)
                    act(acc[1], acc[1], AF.Silu)
                    tt("dve", actT[:, j, :], acc[1], acc[0], ALU.mult)
                    stop_at("f0b")
            if l == 0 and t0 == 0:
                dbg("actT%d" % pass_id, actT[:, 0, :], [128, 1024])
            stop_at("f1")
            def dn_load_blk(bi_):
                j0_, nj_ = jgs[bi_ % len(jgs)]
                sd_ = wslot()[:, 0:1024 * nj_].rr("p (c n) -> p c n", c=nj_)
                load("pool", sd_, wdn[:, j0_:j0_ + nj_, :])
                return sd_

            nblk = (nt // 4) * len(jgs)
            nxtd = dn_load_blk(0)
            bcnt = 0
            for q0 in range(0, nt, 4):
                for (j0, nj) in jgs:
                    sd = nxtd
                    bcnt += 1
                    if bcnt < nblk:
                        nxtd = dn_load_blk(bcnt)
                    for ti in range(4):
                        for jj in range(nj):
                            j = j0 + jj
                            for hf in range(2):
                                mm(fw.banks[ti * 2 + hf], actT[:, j, (q0 + ti) * 128:(q0 + ti + 1) * 128],
                                   sd[:, jj, hf * 512:(hf + 1) * 512], start=(j == 0), stop=(j == NCH - 1))
                for ti in range(4):
                    t_ = t0 + q0 + ti
                    for hf in range(2):
                        tmpx = accs[ti % 2][hf][:, 0:512]
                        tt("dve", tmpx, fw.banks[ti * 2 + hf], GB[:, hf * 512:(hf + 1) * 512], ALU.mult)
                        tt("dve", X[:, t_, hf * 512:(hf + 1) * 512], X[:, t_, hf * 512:(hf + 1) * 512], tmpx, ALU.add)
                stop_at("f2")

    def gla_mixer(l, pass_id, T, seqs, cond):
        e_ = l // 2
        win = a_w_in_d[e_].rearrange("(k p) n -> p k n", p=128)
        wout = a_w_out_d[e_].rearrange("(c p) n -> p c n", p=128)
        UT = ARENA[:, 0:16384].rr("p (k n) -> p k n", k=8)
        TR = ARENA[:, 16384:39936]
        TRF = ARENA_F[:, 8192:19968]
        make_ut(UT, l, 0, cond, list(range(T)), 0)
        gbcast(l, 16, cond)
        memset("pool", wsm, 0.0)
        load("pool", wsm[:, :, 0:16], win[:, :, 1536:1552])
        load("pool", wsm[:, :, 32:48], win[:, :, 1552:1568])
        load("pool", wg[0:16, :], a_w_gate_d[e_, 0])
        load("pool", wg[32:48, :], a_w_gate_d[e_, 1])
        load("pool", bproj, b_proj_d[e_].rearrange("g c d -> c g d"))
        load("sp", normB, a_norm_d[e_:e_ + 1, :].to_broadcast([128, 128]))
        bg = TRF[0:8, 0:128]
        load("sp", bg[0:8, 0:64], a_b_gate_d[e_].rearrange("z (h d) -> (z h) d", d=64))
        ps = fw.psum()
        tr(ps[0:64, 0:8], bg[0:8, 0:64], ident_f[0:8, 0:8])
        ts("dve", small[0:64, 80:88], ps[0:64, 0:8], -1.0, ALU.mult)
        ps = fw.psum()
        bg2 = TRF[0:4, 128:256]
        load("sp", bg2, b_scale_d[e_].rearrange("(g d) -> g d", d=128))
        tr(ps[:, 0:4], bg2, ident_f[0:4, 0:4])
        cp("act", small[:, 96:100], ps[:, 0:4])
        wo_h = WO.rr("p (c n) -> p c n", c=4)
        load("pool", wo_h, wout[:, 0:4, :])
        for c in range(4):
            tt("dve", wo_h[:, c, :], wo_h[:, c, :], GB, ALU.mult)
        o_f = LNB.rr("p a d -> p (a d)").rr("p (t v) -> p t v", v=128)
        gla_next = [None]
        stop_at("g1")

        for si, (t0, nt) in enumerate(seqs):
            L = nt * 128
            c0 = t0 * 128
            qT = TR[0:64, 0:2048]
            kT = TR[0:64, 2048:4096]
            v_tok = TR[:, 4096:6144].rr("p (t v) -> p t v", v=128)
            r_tok = TR[:, 6144:8192].rr("p (t v) -> p t v", v=128)
            lrT = TR[0:64, 8192:10240]
            qe = TR[0:64, 10240:11264]
            ke = TR[0:64, 11264:12288]
            kd = TR[0:64, 12288:13312]
            kd_tok = TR[:, 13312:13824].rr("p (c d) -> p c d", d=64)
            ATb = [TR[:, 13824 + i * 128:13824 + (i + 1) * 128] for i in range(2)]
            ogb = TR[:, 14080:14208]
            oTb = TR[:, 14208:14336]
            fo = 14336 // 2
            cpos = TRF[0:64, fo:fo + 1024]
            tmp = TRF[0:64, fo + 1024:fo + 2048]
            dcol = TRF[0:64, fo + 2048:fo + 2056]
            osum = TRF[:, fo + 2056:fo + 2184]
            ost = TRF[:, fo + 2184:fo + 2192]
            totc = TRF[0:64, fo + 2192:fo + 2200]
            for (a, n) in seg512(L):
                ps = fw.psum()
                for kc in range(8):
                    mm(ps[0:64, 0:n], wsm[:, kc, 0:64], UT[:, kc, c0 + a:c0 + a + n], start=(kc == 0), stop=(kc == 7))
                cp("act", lrT[:, a:a + n], ps[0:64, 0:n])
            stop_at("g2")
            def gla_load(h_):
                sw_ = wslot(2)[:, 0:8 * 384].rr("p (k n) -> p k n", k=8)
                load("pool", sw_[:, :, 0:64], win[:, :, h_ * 64:(h_ + 1) * 64])
                load("pool", sw_[:, :, 64:128], win[:, :, 256 + h_ * 64:256 + (h_ + 1) * 64])
                load("pool", sw_[:, :, 128:256], win[:, :, 512 + h_ * 128:512 + (h_ + 1) * 128])
                load("pool", sw_[:, :, 256:384], win[:, :, 1024 + h_ * 128:1024 + (h_ + 1) * 128])
                return sw_

            if si == 0:
                gla_next[0] = gla_load(0)
            for h in range(4):
                sw = gla_next[0]
                stop_at("g2a")
                for (a, n) in seg512(L):
                    ps = fw.psum()
                    for kc in range(8):
                        mm(ps[0:64, 0:n], sw[:, kc, 0:64], UT[:, kc, c0 + a:c0 + a + n], start=(kc == 0), stop=(kc == 7))
                    act(qT[:, a:a + n], ps[0:64, 0:n], AF.Identity, scale=0.125)
                    stop_at("g2b")
                    ps = fw.psum()
                    for kc in range(8):
                        mm(ps[0:64, 0:n], sw[:, kc, 64:128], UT[:, kc, c0 + a:c0 + a + n], start=(kc == 0), stop=(kc == 7))
                    cp("dve", kT[:, a:a + n], ps[0:64, 0:n])
                    stop_at("g2c")
                for t in range(nt):
                    ps = fw.psum()
                    for kc in range(8):
                        mm(ps[:, 0:128], UT[:, kc, c0 + t * 128:c0 + (t + 1) * 128], sw[:, kc, 128:256], start=(kc == 0), stop=(kc == 7))
                    cp("dve", v_tok[:, t, :], ps[:, 0:128])
                    stop_at("g2d")
                    ps = fw.psum()
                    for kc in range(8):
                        mm(ps[:, 0:128], UT[:, kc, c0 + t * 128:c0 + (t + 1) * 128], sw[:, kc, 256:384], start=(kc == 0), stop=(kc == 7))
                    act(r_tok[:, t, :], ps[:, 0:128], AF.Silu)
                    stop_at("g2e")
                stop_at("g3")
                if h + 1 < 4:
                    gla_next[0] = gla_load(h + 1)
                elif si + 1 < len(seqs):
                    gla_next[0] = gla_load(0)
                for z in range(2):
                    S = S_f[0:64, z, :]
                    Sb = S_b[0:64, z, :]
                    if pass_id == 0:
                        memset("pool", S, 0.0)
                    else:
                        load("sp", S, sgla_d[(e_ * 2 + z) * 4 + h])
                    cp("act", Sb, S)
                    blocks = [(b0, min(8, nt - b0)) for b0 in range(0, nt, 8)]
                    if z == 1:
                        blocks = blocks[::-1]
                    for (b0, nb) in blocks:
                        n = nb * 128
                        for (a, m) in seg512(n):
                            ps = fw.psum()
                            mm(ps[0:64, 0:m], wg[z * 32:z * 32 + 16, h * 64:(h + 1) * 64],
                               lrT[z * 32:z * 32 + 16, b0 * 128 + a:b0 * 128 + a + m])
                            act(tmp[:, a:a + m], ps[0:64, 0:m], AF.Exp, bias=small[0:64, 80 + z * 4 + h:81 + z * 4 + h], scale=-1.0)
                        act(tmp[:, 0:n], tmp[:, 0:n], AF.Ln, bias=1.0)
                        for c in range(nb):
                            cs = slice(c * 128, (c + 1) * 128)
                            scan(cpos[:, cs], ones_f[0:64, :], tmp[:, cs])
                        if z == 1:
                            tt("dve", tmp[:, 0:n], tmp[:, 0:n], cpos[:, 0:n], ALU.subtract)
                            cp("dve", totc[:, 0:nb], cpos[:, 0:n].rr("p (c i) -> p c i", i=128)[:, :, 127])
                            for c in range(nb):
                                cs = slice(c * 128, (c + 1) * 128)
                                ts("dve", cpos[:, cs], tmp[:, cs], totc[:, c:c + 1], ALU.add)
                        lastc = (lambda c: c * 128 + 127) if z == 0 else (lambda c: c * 128)
                        act(tmp[:, 0:n], cpos[:, 0:n], AF.Exp, scale=-1.0 / 16)
                        tt("dve", qe[:, 0:n], qT[:, b0 * 128:b0 * 128 + n], tmp[:, 0:n], ALU.mult)
                        tmp3 = tmp[:, 0:n].rr("p (c i) -> p c i", i=128)
                        cp("dve", dcol[:, 0:nb], tmp3[:, :, 127 if z == 0 else 0])
                        act(tmp[:, 0:n], cpos[:, 0:n], AF.Exp, scale=1.0 / 16)
                        tt("dve", ke[:, 0:n], kT[:, b0 * 128:b0 * 128 + n], tmp[:, 0:n], ALU.mult)
                        for c in range(nb):
                            cs = slice(c * 128, (c + 1) * 128)
                            ts("dve", tmp[:, cs], cpos[:, cs], cpos[:, lastc(c):lastc(c) + 1], ALU.subtract)
                        act(tmp[:, 0:n], tmp[:, 0:n], AF.Exp, scale=1.0 / 16)
                        tt("dve", kd[:, 0:n], kT[:, b0 * 128:b0 * 128 + n], tmp[:, 0:n], ALU.mult)
                        psk = fw.psum(dt=BF16)
                        for c in range(nb):
                            tr(psk[:, c * 64:(c + 1) * 64], kd[:, c * 128:(c + 1) * 128], ident_b[0:64, 0:64])
                        cp("act", kd_tok[:, 0:nb, :].rr("p c d -> p (c d)"), psk[:, 0:nb * 64])
                        stop_at("g4")
                        corder = list(range(nb)) if z == 0 else list(range(nb - 1, -1, -1))

                        def emit_AT(c_):
                            cs_ = slice(c_ * 128, (c_ + 1) * 128)
                            ps_ = fw.psum()
                            mm(ps_[:, 0:128], ke[:, cs_], qe[:, cs_])
                            tt("dve", ATb[c_ % 2], ps_[:, 0:128], MU if z == 0 else ML, ALU.mult)

                        emit_AT(corder[0])
                        for ci_, c in enumerate(corder):
                            t = b0 + c
                            cs = slice(c * 128, (c + 1) * 128)
                            AT = ATb[c % 2]
                            ps2 = fw.psum()
                            mm(ps2[:, 0:128], AT, v_tok[:, t, :], start=True, stop=False)
                            mm(ps2[:, 0:128], qe[:, cs], Sb, start=False, stop=True)
                            if ci_ + 1 < len(corder):
                                emit_AT(corder[ci_ + 1])
                            ps3 = fw.psum()
                            mm(ps3[0:64, 0:128], kd_tok[:, c, :], v_tok[:, t, :])
                            stt("dve", S, S, dcol[:, c:c + 1], ps3[0:64, 0:128], ALU.mult, ALU.add)
                            cp("act", Sb, S)
                            if z == 0:
                                cp("act", o_f[:, t, :], ps2[:, 0:128])
                            else:
                                tt("dve", osum, ps2[:, 0:128], o_f[:, t, :], ALU.add)
                                act(junk[:, 0:128], osum, AF.Square, accum=ost[:, 0:1])
                                act(ost[:, 1:2], ost[:, 0:1], AF.Ln, bias=EPS, scale=1.0 / 128)
                                act(ost[:, 2:3], ost[:, 1:2], AF.Exp, scale=-0.5)
                                stt("dve", osum, osum, ost[:, 2:3], normB, ALU.mult, ALU.mult)
                                tt("dve", ogb, osum, r_tok[:, t, :], ALU.mult)
                                pst = fw.psum(dt=BF16)
                                tr(pst[:, 0:128], ogb, ident_b)
                                cp("act", oTb, pst[:, 0:128])
                                psA = fw.psum()
                                psB = fw.psum()
                                mm(psA, oTb, wo_h[:, h, 0:512])
                                mm(psB, oTb, wo_h[:, h, 512:1024])
                                xacc(t0 + t, psA, psB)
                            stop_at("g5")
                        stop_at("g6")
                    if pass_id == 0:
                        s_glob = si
                        load("sp", ngla_d[((s_glob * 2 + e_) * 2 + z) * 4 + h], S)
        wo_p = WO.rr("p (c n) -> p c n", c=4)
        load("pool", wo_p, wout[:, 4:8, :])
        for c in range(4):
            tt("dve", wo_p[:, c, :], wo_p[:, c, :], GB, ALU.mult)
        swp = wslot(2).rr("p (k n) -> p k n", k=8)
        load("pool", swp, win[:, :, 1568:2080])
        for si, (t0, nt) in enumerate(seqs):
            L = nt * 128
            c0 = t0 * 128
            pm = TR[:, 0:8192].rr("p (g n) -> p g n", g=4)
            xpad = TRF[:, 4096:4096 + 2080]
            sA = TRF[:, 6176:6176 + 2080]
            sB = TRF[:, 8256:8256 + 2080]
            pooled = TR[:, 20672:20672 + 2048]
            assert 20672 + 2048 <= 23552 and (8256 + 2080) * 2 <= 20672
            for gi, w in enumerate(POOLW):
                lo = w // 2
                hi = w - 1 - lo
                memset("pool", xpad[:, 0:16], 0.0)
                memset("pool", xpad[:, 16 + L:32 + L], 0.0)
                for (a, n) in seg512(L):
                    ps = fw.psum()
                    for kc in range(8):
                        mm(ps[:, 0:n], swp[:, kc, gi * 128:(gi + 1) * 128], UT[:, kc, c0 + a:c0 + a + n], start=(kc == 0), stop=(kc == 7))
                    cp("act", xpad[:, 16 + a:16 + a + n], ps[:, 0:n])
                src = xpad
                k = 1
                bufs = [sA, sB]
                bi = 0
                while k < w:
                    dst = bufs[bi]
                    bi ^= 1
                    n = L + 32 - 2 * k
                    tt("dve", dst[:, 0:n], src[:, 0:n], src[:, k:k + n], ALU.add)
                    src = dst
                    k *= 2
                wsum = src[:, 16 - lo:16 - lo + L]
                tt("dve", wsum[:, 0:lo], wsum[:, 0:lo], efix[:, gi, 0:lo], ALU.mult)
                if hi > 0:
                    tt("dve", wsum[:, L - hi:L], wsum[:, L - hi:L], efix[:, gi, 8:8 + hi], ALU.mult)
                stt("dve", pooled[:, 0:L], wsum, 1.0 / w, xpad[:, 16:16 + L], ALU.mult, ALU.subtract)
                for (a, n) in seg512(L):
                    ps = fw.psum()
                    mm(ps[:, 0:n], bproj[:, gi, :], pooled[:, a:a + n])
                    act(pm[:, gi, a:a + n], ps[:, 0:n], AF.Identity, scale=small[:, 96 + gi:97 + gi])
            for t in range(nt):
                psA = fw.psum()
                psB = fw.psum()
                for gi in range(4):
                    mm(psA, pm[:, gi, t * 128:(t + 1) * 128], wo_p[:, gi, 0:512], start=(gi == 0), stop=(gi == 3))
                for gi in range(4):
                    mm(psB, pm[:, gi, t * 128:(t + 1) * 128], wo_p[:, gi, 512:1024], start=(gi == 0), stop=(gi == 3))
                xacc(t0 + t, psA, psB)

    def dn_mixer(l, pass_id, T, seqs, cond):
        o_ = l // 2
        win = c_w_in_d[o_].rearrange("(k p) n -> p k n", p=128)
        wout = c_w_out_d[o_].rearrange("(c p) n -> p c n", p=128)
        UT = ARENA[:, 0:16384].rr("p (k n) -> p k n", k=8)
        TR = ARENA[:, 16384:39936]
        TRF = ARENA_F[:, 8192:19968]
        make_ut(UT, l, 0, cond, list(range(T)), 0)
        gbcast(l, 16, cond)
        memset("pool", wsm, 0.0)
        for i, off in enumerate((0, 32, 64, 96)):
            load("pool", wsm[:, :, off:off + 8], win[:, :, 4096 + i * 8:4096 + (i + 1) * 8])
        load("sp", normB, c_norm_d[o_:o_ + 1, :].to_broadcast([128, 128]))
        memset("pool", small[0:40, 104:106], 0.0)
        for z in range(2):
            load("sp", small[z * 32:z * 32 + 8, 104:105], c_a_log_d[o_, z].rearrange("(h o) -> h o", o=1))
            load("sp", small[z * 32:z * 32 + 8, 105:106], c_dt_bias_d[o_, z].rearrange("(h o) -> h o", o=1))
        act(small[0:40, 104:105], small[0:40, 104:105], AF.Exp)
        cr = TRF[0:72, 0:128]
        load("sp", cr, c_conv_d[o_].rearrange("k (c p) -> (k c) p", p=128))
        ps = fw.psum()
        tr(ps[:, 0:72], cr, ident_f[0:72, 0:72])
        cp("act", convc[:, 0:72], ps[:, 0:72])
        o_f = LNB.rr("p a d -> p (a d)").rr("p (t v) -> p t v", v=128)
        dn_next = [None]
        SLc = seqs[0][1]
        groups = [(seqs[0][0], sum(n_ for (_, n_) in seqs))]

        for si, (t0, nt) in enumerate(groups):
            L = nt * 128
            c0 = t0 * 128
            R1 = TRF[0:128, 0:L]
            R2 = TRF[0:64, L:2 * L]
            bo = 4 * L
            hpre = TR[:, bo:bo + L + 2]
            qT = TR[:, bo + L + 8:bo + 2 * L + 8]
            kT = TR[:, bo + 2 * L + 8:bo + 3 * L + 8]
            k_tok = TR[:, bo + 3 * L + 8:bo + 4 * L + 8].rr("p (t v) -> p t v", v=128)
            v_tok = TR[:, bo + 4 * L + 8:bo + 5 * L + 8].rr("p (t v) -> p t v", v=128)
            z_tok = TR[:, bo + 5 * L + 8:bo + 6 * L + 8].rr("p (t v) -> p t v", v=128)
            so = bo + 6 * L + 8
            NSLOT = 4
            slots = []
            for s_ in range(NSLOT):
                base = 16384 + so + s_ * 2176
                if base + 2176 <= 44032:
                    sbk = [ARENA[:, base + i * 128:base + (i + 1) * 128] for i in range(11)]
                    fb = (base + 11 * 128) // 2
                    fk = [ARENA_F[:, fb + i * 128:fb + (i + 1) * 128] for i in range(3)]
                else:
                    assert s_ == 3 and L + 2 >= 11 * 128
                    hb = 16384 + bo
                    sbk = [ARENA[:, hb + i * 128:hb + (i + 1) * 128] for i in range(11)]
                    jf = junk.bitcast(F32)
                    fk = [jf[:, 128 + i * 128:128 + (i + 1) * 128] for i in range(3)]
                slots.append((sbk, fk, dncol[:, s_, 0:104], dncol[:, s_, 104:144], dncol[:, s_, 144:160],
                              (fw.banks[2 * s_], fw.banks[2 * s_ + 1])))
            totr = small[0:40, 296:312]
            memset("pool", R1[:, 0:L], 0.0)
            memset("pool", R2[:, 0:L], 0.0)
            for (a, n) in seg512(L):
                ps = fw.psum()
                for kc in range(8):
                    mm(ps[0:128, 0:n], wsm[:, kc, 0:128], UT[:, kc, c0 + a:c0 + a + n], start=(kc == 0), stop=(kc == 7))
                act(R1[0:40, a:a + n], ps[0:40, 0:n], AF.Exp, bias=small[0:40, 105:106])
                act(R1[0:40, a:a + n], R1[0:40, a:a + n], AF.Ln, bias=1.0)
                ts("dve", R2[0:40, a:a + n], R1[0:40, a:a + n], small[0:40, 104:105], ALU.mult)
                act(R1[64:104, a:a + n], ps[64:104, 0:n], AF.Sigmoid)
            for c in range(nt):
                cs = slice(c * 128, (c + 1) * 128)
                scan(R1[0:40, cs], ones_f[0:40, :], R2[0:40, cs])
            tt("dve", R2[32:40, 0:L], R2[32:40, 0:L], R1[32:40, 0:L], ALU.subtract)
            cp("dve", totr[32:40, 0:nt], R1[32:40, 0:L].rr("p (c i) -> p c i", i=128)[:, :, 127])
            for c in range(nt):
                cs = slice(c * 128, (c + 1) * 128)
                ts("dve", R1[32:40, cs], R2[32:40, cs], totr[32:40, c:c + 1], ALU.add)
            for c in range(nt):
                cs = slice(c * 128, (c + 1) * 128)
                ts("dve", R2[0:8, cs], R1[0:8, cs], R1[0:8, c * 128 + 127:c * 128 + 128], ALU.subtract)
                ts("dve", R2[32:40, cs], R1[32:40, cs], R1[32:40, c * 128:c * 128 + 1], ALU.subtract)
            if l == 1 and si == 0:
                dbg("R1_%d" % pass_id, R1[0:104, 0:256], [104, 256])
                dbg("R2_%d" % pass_id, R2[0:40, 0:256], [40, 256])

            def dn_load(h_):
                sw_ = wslot(2).rr("p (k n) -> p k n", k=8)
                for i_ in range(4):
                    load("pool", sw_[:, :, i_ * 128:(i_ + 1) * 128], win[:, :, i_ * 1024 + h_ * 128:i_ * 1024 + (h_ + 1) * 128])
                return sw_

            if si == 0:
                dn_next[0] = dn_load(0)
            for h in range(8):
                if h % 4 == 0:
                    wo_h = WO.rr("p (c n) -> p c n", c=4)
                    load("pool", wo_h, wout[:, h:h + 4, :])
                sw = dn_next[0]
                memset("pool", hpre[:, 0:1], 0.0)
                memset("pool", hpre[:, L + 1:L + 2], 0.0)
                for i, dstT in enumerate((qT, kT, None)):
                    for (a, n) in seg512(L):
                        ps = fw.psum()
                        for kc in range(8):
                            mm(ps[:, 0:n], sw[:, kc, i * 128:(i + 1) * 128], UT[:, kc, c0 + a:c0 + a + n], start=(kc == 0), stop=(kc == 7))
                        cp("act", hpre[:, 1 + a:1 + a + n], ps[:, 0:n])
                    cc = i * 8 + h
                    for (a, n) in seg512(L):
                        accv = junk.bitcast(F32)[:, 0:n]
                        act(accv, hpre[:, 1 + a:1 + a + n], AF.Identity, scale=convc[:, 24 + cc:25 + cc])
                        if SLc * 128 >= L:
                            stt("dve", accv, hpre[:, a:a + n], convc[:, cc:cc + 1], accv, ALU.mult, ALU.add)
                            stt("dve", accv, hpre[:, 2 + a:2 + a + n], convc[:, 48 + cc:49 + cc], accv, ALU.mult, ALU.add)
                        else:
                            sl_ = SLc * 128
                            a3_ = accv.rr("p (s t) -> p s t", t=sl_)
                            h3_ = hpre[:, 1 + a:1 + a + n].rr("p (s t) -> p s t", t=sl_)
                            stt("dve", a3_[:, :, 1:sl_], h3_[:, :, 0:sl_ - 1], convc[:, cc:cc + 1], a3_[:, :, 1:sl_], ALU.mult, ALU.add)
                            stt("dve", a3_[:, :, 0:sl_ - 1], h3_[:, :, 1:sl_], convc[:, 48 + cc:49 + cc], a3_[:, :, 0:sl_ - 1], ALU.mult, ALU.add)
                        if dstT is not None:
                            act(dstT[:, a:a + n], accv, AF.Silu)
                            sqv = TR[:, so:so + 512]
                            act(sqv[:, 0:n], dstT[:, a:a + n], AF.Square)
                            ps = fw.psum()
                            mm(ps[:, 0:n], ones_b, sqv[:, 0:n])
                            rn = junk.bitcast(F32)[:, 0:n]
                            act(rn, ps[:, 0:n], AF.Ln, bias=EPS)
                            act(rn, rn, AF.Exp, scale=-0.5)
                            if i == 0:
                                stt("dve", dstT[:, a:a + n], dstT[:, a:a + n], float(128 ** -0.5), rn, ALU.mult, ALU.mult)
                            else:
                                tt("dve", dstT[:, a:a + n], dstT[:, a:a + n], rn, ALU.mult)
                        else:
                            vT = TR[:, so:so + 512]
                            act(vT[:, 0:n], accv, AF.Silu)
                            for tq in range(n // 128):
                                pst = fw.psum(dt=BF16)
                                tr(pst[:, 0:128], vT[:, tq * 128:(tq + 1) * 128], ident_b)
                                cp("dve", v_tok[:, a // 128 + tq, :], pst[:, 0:128])
                for t in range(nt):
                    pst = fw.psum(dt=BF16)
                    tr(pst[:, 0:128], kT[:, t * 128:(t + 1) * 128], ident_b)
                    cp("act", k_tok[:, t, :], pst[:, 0:128])
                    ps = fw.psum()
                    for kc in range(8):
                        mm(ps[:, 0:128], UT[:, kc, c0 + t * 128:c0 + (t + 1) * 128], sw[:, kc, 384:512], start=(kc == 0), stop=(kc == 7))
                    act(z_tok[:, t, :], ps[:, 0:128], AF.Silu)
                if l == 1 and si == 0 and h == 0:
                    dbg("qT_%d" % pass_id, qT[:, 0:256], [128, 256])
                    dbg("kT_%d" % pass_id, kT[:, 0:256], [128, 256])
                    dbg("vtok_%d" % pass_id, v_tok[:, 0, :], [128, 128])

                if h + 1 < 8:
                    dn_next[0] = dn_load(h + 1)
                elif si + 1 < len(groups):
                    dn_next[0] = dn_load(0)
                if h % 4 == 0:
                    for c in range(4):
                        tt("dve", wo_h[:, c, :], wo_h[:, c, :], GB, ALU.mult)
                for z in range(2):
                    S = S_f[:, z, :]
                    Sb = S_b[:, z, :]
                    if pass_id == 0:
                        memset("pool", S, 0.0)
                    else:
                        load("sp", S, sdn_d[(o_ * 2 + z) * 8 + h])
                    cp("act", Sb, S)
                    rb = z * 32 + h
                    fw.op("dve", lambda e, rb=rb, z=z: e.tensor_single_scalar(out=selc[:, z * 128:(z + 1) * 128].ap, in_=pidx.ap,
                                                                           scalar=float(rb), op=ALU.is_equal),
                          reads=[pidx], writes=[selc[:, z * 128:(z + 1) * 128]])
                rec_turn = [0, 0]
                stored = set()

                def chunk_task(z, c, zi, sl):
                    sbk, fk, cols, cold, ccol, bankpair = sl
                    nal = [0]

                    def palloc(dt=F32):
                        bnk = bankpair[nal[0] % 2]
                        nal[0] += 1
                        return bnk if dt == F32 else bnk.bitcast(dt)
                    S = S_f[:, z, :]
                    Sb = S_b[:, z, :]
                    rb = z * 32 + h
                    Msk_s = MLs if z == 0 else MUs
                    Msk_i = ML if z == 0 else MU
                    cs = slice(c * 128, (c + 1) * 128)
                    ps = palloc()
                    tr(ps[:, 0:128], R1[0:128, cs], ident_f)
                    cp("act", cols, ps[:, 0:104])
                    ps = palloc()
                    tr(ps[:, 0:64], R2[0:64, cs], ident_f[0:64, 0:64])
                    cp("dve", cold, ps[:, 0:40])
                    bcol = cols[:, rb:rb + 1]
                    betac = cols[:, 64 + rb:65 + rb]
                    act(ccol[:, 0:1], bcol, AF.Exp, scale=-1.0)
                    tt("dve", ccol[:, 1:2], ccol[:, 0:1], betac, ALU.mult)
                    ts("dve", ccol[:, 2:3], betac, -1.0, ALU.mult)
                    act(ccol[:, 3:4], cold[:, rb:rb + 1], AF.Exp)
                    psBGQ = palloc()
                    psB = psBGQ[:, 0:128]
                    psG = psBGQ[:, 128:256]
                    psQ = psBGQ[:, 256:384]
                    mm(psB[:, 0:128], selc[:, z * 128:(z + 1) * 128], R1[0:64, cs])
                    mm(psG[:, 0:128], kT[:, cs], kT[:, cs])
                    mm(psQ[:, 0:128], qT[:, cs], kT[:, cs])
                    yield
                    EB = fk[1]
                    act(EB, psB[:, 0:128], AF.Exp, scale=-1.0)
                    ld = fk[0]
                    fw.op("dve", lambda e, ld=ld, psB=psB, bcol=bcol: e.tensor_scalar(
                        out=ld.ap, in0=psB[:, 0:128].ap, scalar1=bcol.ap, scalar2=0.0, op0=ALU.subtract, op1=ALU.min),
                        reads=[psB[:, 0:128], bcol, EB], writes=[ld])
                    act(ld, ld, AF.Exp)
                    cp("act", ccol[:, 4:5], EB[:, 127:128] if z == 0 else EB[:, 0:1])
                    yield
                    t1 = fk[2]
                    tt("dve", t1, psG[:, 0:128], ld, ALU.mult)
                    P0 = sbk[0]
                    stt("dve", P0, t1, ccol[:, 2:3], Msk_s, ALU.mult, ALU.mult)
                    P = sbk[1]
                    stt("dve", P, t1, ccol[:, 2:3], MDS[:, z, :], ALU.mult, ALU.mult)
                    yield
                    pst = palloc(BF16)
                    tr(pst[:, 0:128], P, ident_b)
                    PT = sbk[2]
                    cp("act", PT, pst[:, 0:128])
                    TT = sbk[7]
                    tt("dve", TT, ident_b, PT, ALU.add)
                    tt("dve", t1, psQ[:, 0:128], ld, ALU.mult)
                    yield
                    for it in range(3):
                        Pn = sbk[3 + (it % 2) * 2]
                        PTn = sbk[4 + (it % 2) * 2]
                        ps1 = palloc()
                        mm(ps1[:, 0:128], PT, P)
                        if it < 2:
                            ps2 = palloc()
                            mm(ps2[:, 0:128], P, PT)
                        yield
                        cp("act", Pn, ps1[:, 0:128])
                        if it < 2:
                            cp("dve", PTn, ps2[:, 0:128])
                        yield
                        ps3 = palloc()
                        mm(ps3[:, 0:128], Pn, TT)
                        yield
                        TTn = sbk[8] if TT is sbk[7] else sbk[7]
                        tt("dve", TTn, ps3[:, 0:128], TT, ALU.add)
                        P, PT, TT = Pn, PTn, TTn
                        yield
                    for lv in range(3):
                        Bk = sbk[4]
                        tt("pool", Bk, P0, BMK[:, 1 + lv, :], ALU.mult)
                        pst = palloc(BF16)
                        tr(pst[:, 0:128], TT, ident_b)
                        psz = palloc()
                        mm(psz[:, 0:128], Bk, TT)
                        yield
                        Tn = sbk[3]
                        cp("act", Tn, pst[:, 0:128])
                        Zb = sbk[5]
                        cp("dve", Zb, psz[:, 0:128])
                        yield
                        psw = palloc()
                        mm(psw[:, 0:128], Tn, Zb)
                        yield
                        TTn = sbk[8] if TT is sbk[7] else sbk[7]
                        tt("dve", TTn, psw[:, 0:128], TT, ALU.add)
                        TT = TTn
                        yield
                    aq = sbk[2]
                    tt("dve", aq, t1, Msk_i, ALU.mult)
                    pst = palloc(BF16)
                    tr(pst[:, 0:128], aq, ident_b)
                    aqT = sbk[9]
                    cp("act", aqT, pst[:, 0:128])
                    qdT = sbk[10]
                    tt("dve", qdT, qT[:, cs], EB, ALU.mult)
                    kbe = sbk[3]
                    vb = sbk[4]
                    kdk = sbk[6]
                    ts("dve", kbe, k_tok[:, c, :], ccol[:, 1:2], ALU.mult)
                    act(vb, v_tok[:, c, :], AF.Identity, scale=betac)
                    act(kdk, k_tok[:, c, :], AF.Identity, scale=ccol[:, 3:4])
                    yield
                    psU = palloc()
                    mm(psU[:, 0:128], TT, vb)
                    psW = palloc()
                    mm(psW[:, 0:128], kbe, TT)
                    yield
                    u = fk[2]
                    cp("act", u, psU[:, 0:128])
                    wT = sbk[1]
                    cp("dve", wT, psW[:, 0:128])
                    yield
                    while rec_turn[z] != zi:
                        yield
                    first_in_seq = (c % SLc == 0) if z == 0 else (c % SLc == SLc - 1)
                    last_in_seq = (c % SLc == SLc - 1) if z == 0 else (c % SLc == 0)
                    if first_in_seq and zi > 0:
                        memset("pool", S, 0.0)
                        memset("pool", Sb, 0.0)
                    psS = palloc()
                    mm(psS[:, 0:128], wT, Sb)
                    yield
                    vnew = sbk[2]
                    tt("dve", vnew, u, psS[:, 0:128], ALU.subtract)
                    yield
                    psO = palloc()
                    mm(psO[:, 0:128], qdT, Sb, start=True, stop=False)
                    mm(psO[:, 0:128], aqT, vnew, start=False, stop=True)
                    psN = palloc()
                    mm(psN[:, 0:128], kdk, vnew)
                    yield
                    stt("dve", S, S, ccol[:, 4:5], psN[:, 0:128], ALU.mult, ALU.add)
                    cp("act", Sb, S)
                    if pass_id == 0 and last_in_seq:
                        load("sp", ndn_d[(((c // SLc) * 2 + o_) * 2 + z) * 8 + h], S)
                    rec_turn[z] += 1
                    t = c
                    second = (z == 1 and 2 * t < nt) or (z == 0 and 2 * t >= nt)
                    if not second:
                        cp("act", o_f[:, t, :], psO[:, 0:128])
                        stored.add(t)
                    else:
                        while t not in stored:
                            yield
                        osum = fk[0]
                        ost = ccol[:, 8:16]
                        tt("dve", osum, psO[:, 0:128], o_f[:, t, :], ALU.add)
                        act(junk[:, 0:128], osum, AF.Square, accum=ost[:, 0:1])
                        act(ost[:, 1:2], ost[:, 0:1], AF.Ln, bias=EPS, scale=1.0 / 128)
                        act(ost[:, 2:3], ost[:, 1:2], AF.Exp, scale=-0.5)
                        stt("dve", osum, osum, ost[:, 2:3], normB, ALU.mult, ALU.mult)
                        yield
                        ogb = sbk[9]
                        tt("dve", ogb, osum, z_tok[:, t, :], ALU.mult)
                        pst = palloc(BF16)
                        tr(pst[:, 0:128], ogb, ident_b)
                        yield
                        oTb = sbk[10]
                        cp("act", oTb, pst[:, 0:128])
                        psA = palloc()
                        psBk = palloc()
                        mm(psA, oTb, wo_h[:, h % 4, 0:512])
                        mm(psBk, oTb, wo_h[:, h % 4, 512:1024])
                        yield
                        xacc(t0 + t, psA, psBk)

                items = []
                for i in range(nt):
                    items.append((0, i, i))
                    items.append((1, nt - 1 - i, i))
                active = []
                free = list(range(len(slots)))
                qi = 0
                while qi < len(items) or active:
                    while free and qi < len(items):
                        z_, c_, zi_ = items[qi]
                        qi += 1
                        s_ = free.pop(0)
                        active.append([chunk_task(z_, c_, zi_, slots[s_]), s_])
                    for a_ in list(active):
                        try:
                            next(a_[0])
                        except StopIteration:
                            active.remove(a_)
                            free.append(a_[1])

    passes = [(0, 8, [(0, 2), (2, 2), (4, 2), (6, 2)], 0), (1, 16, [(0, 16)], 1)]

    def _main():
        for (pass_id, T, seqs, cond) in passes:
            mark("P%d load" % pass_id)
            load_x(pass_id, T)
            if pass_id == 1:
                dbg("x0s", X[:, 0, :], [128, D])
            for l in range(NL):
                mark("P%d L%d mixer" % (pass_id, l))
                if l % 2 == 0:
                    gla_mixer(l, pass_id, T, seqs, cond)
                else:
                    dn_mixer(l, pass_id, T, seqs, cond)
                mark("P%d L%d ln1" % (pass_id, l))
                dbg("xmix%d_%d" % (l, pass_id), X[:, 0, :], [128, D])
                if stop == "mix%d_%d" % (l, pass_id):
                    raise _Stop()
                layernorm(l, 0, T, False)
                dbg("xln%d_%d" % (l, pass_id), X[:, 0, :], [128, D])
                stop_at("ln%d_%d" % (l, pass_id))
                mark("P%d L%d ffn" % (pass_id, l))
                ffn(l, pass_id, T, cond)
                mark("P%d L%d ln2" % (pass_id, l))
                dbg("xffn%d_%d" % (l, pass_id), X[:, 0, :], [128, D])
                stop_at("ffn%d_%d" % (l, pass_id))
                layernorm(l, 1, T, l == NL - 1)
                dbg("xl%d_%d" % (l, pass_id), X[:, 0, :], [128, D])
                if stop == "l%d_%d" % (l, pass_id):
                    raise _Stop()
            mark("P%d store" % pass_id)
            dst = yp_d if pass_id == 0 else ys_d
            for t in range(T):
                load("sp", dst[t * 128:(t + 1) * 128, :], X[:, t, :])

    try:
        _main()
    except _Stop:
        pass
    fw.emit()
    return nc, fw, dbg_d


_CACHE = {}


def _inputs_per_core(inp, core):
    b = core % 2
    m = {}
    m["xp"] = np.ascontiguousarray(inp["x_prompt"][core * 4:(core + 1) * 4].reshape(1024, D))
    m["xs"] = np.ascontiguousarray(inp["x_sample"][b])
    m["cond2"] = np.ascontiguousarray(np.stack([inp["c_ctx"], inp["c"][b]], 0))
    m["sgla"] = np.ascontiguousarray(inp["state_gla"][b].reshape(16, 64, 128))
    m["sdn"] = np.ascontiguousarray(inp["state_dn"][b].reshape(32, 128, 128))
    for k in ("w_mod", "b_mod", "ln1_g", "ln1_b", "ln2_g", "ln2_b", "a_w_in", "a_w_gate", "a_b_gate", "a_norm",
              "b_proj", "b_scale", "a_w_out", "c_w_in", "c_conv", "c_a_log", "c_dt_bias", "c_norm", "c_w_out",
              "f_w_up", "f_conv", "f_w_down"):
        m[k] = np.ascontiguousarray(inp[k])
    return m


def kernel(**inputs):
    inp = {k: np.asarray(v, dtype=np.float32) for k, v in inputs.items()}
    if "nc" not in _CACHE:
        _CACHE["nc"] = build_program()[0]
    nc = _CACHE["nc"]
    n = 8
    in_maps = [_inputs_per_core(inp, c) for c in range(n)]
    res = run_bass_kernel_spmd(nc, in_maps, core_ids=list(range(n)))
    R = res.results
    y_prompt = np.concatenate([R[c]["yp"].reshape(4, 256, D) for c in range(n)], 0)
    y_sample = np.stack([R[0]["ys"], R[1]["ys"]], 0)
    ngla = np.concatenate([R[c]["ngla"].reshape(4, 2, 2, 4, 64, 128) for c in range(n)], 0)
    ndn = np.concatenate([R[c]["ndn"].reshape(4, 2, 2, 8, 128, 128) for c in range(n)], 0)
    return (y_prompt.astype(np.float32), y_sample.astype(np.float32), ngla.astype(np.float32), ndn.astype(np.float32))
```
